# Optimizing a Trainium2 kernel written in Bass

```python
import jax, jax.numpy as jnp
from jax import lax
import numpy as np

D_MODEL = 2048
BATCH = 8
SEQ = 4096
DEPTH = 4

CTX_LEN = 256
GRID_W = 64
N_MIXERS = 4
D_FF = 5632
DN_ALPHA = (2.0 * DEPTH) ** 0.25
DN_BETA = (8.0 * DEPTH) ** -0.25
N_MOD = 9
LN_EPS = 1e-5
RMS_EPS = 1e-6

HG_HEAD = 128
HG_HEADS = D_MODEL // HG_HEAD
HG_CHUNK = 32
POOL_WINDOWS = (2, 4, 8, 16)
POOL_GROUPS = len(POOL_WINDOWS)
POOL_WIDTH = D_MODEL // POOL_GROUPS
ML_HEADS = 8
ML_HEAD = D_MODEL // ML_HEADS
ML_CHUNK = 64
S5_GROUP = 16
S5_GROUPS = D_MODEL // S5_GROUP
S5_STATE = 64
S5_CHUNK = 128
S5_DT_MIN = 1e-3
S5_DT_MAX = 1e-1

kernel_name = "hybrid_interleaved_flow_backbone"


def layer_norm(t, g, b):
    tf = t.astype(jnp.float32)
    mu = tf.mean(-1, keepdims=True)
    var = jnp.mean(jnp.square(tf - mu), -1, keepdims=True)
    return ((tf - mu) * lax.rsqrt(var + LN_EPS)).astype(t.dtype) * g + b


def head_rms_norm(t, g, n_heads):
    shp = t.shape
    th = t.reshape(shp[:-1] + (n_heads, shp[-1] // n_heads)).astype(jnp.float32)
    th = th * lax.rsqrt(jnp.mean(jnp.square(th), -1, keepdims=True) + RMS_EPS)
    return th.reshape(shp).astype(t.dtype) * g


def modulate(t, shift, scale):
    return t * (1.0 + scale) + shift


def post_norm(z, out, gate, g, b):
    return layer_norm(DN_ALPHA * z + gate * out, g, b)


def swiglu(h, w13, w2):
    a, b = jnp.split(h @ w13, 2, axis=-1)
    return (jax.nn.silu(a) * b) @ w2


def to_chunks(t, size):
    b, n = t.shape[0], t.shape[1] // size
    return jnp.moveaxis(t.reshape((b, n, size) + t.shape[2:]), 1, 0)


def from_chunks(t):
    n, b, size = t.shape[:3]
    return jnp.moveaxis(t, 0, 1).reshape((b, n * size) + t.shape[3:])


def gla_scan(q, k, v, logf, s0):
    tri = jnp.tril(jnp.ones((HG_CHUNK, HG_CHUNK), bool))

    def step(S, inp):
        qc, kc, vc, gc = inp
        b = jnp.cumsum(gc, axis=1)
        qe, ke = qc * jnp.exp(b), kc * jnp.exp(-b)
        att = jnp.where(tri[None, None], jnp.einsum('blhk,bmhk->bhlm', qe, ke), 0.0)
        o = jnp.einsum('bhlm,bmhv->blhv', att, vc) + jnp.einsum('blhk,bhkv->blhv', qe, S)
        b_last = b[:, -1]
        kd = kc * jnp.exp(b_last[:, None] - b)
        S = jnp.exp(b_last)[..., None] * S + jnp.einsum('blhk,blhv->bhkv', kd, vc)
        return S, o

    xs = tuple(to_chunks(t, HG_CHUNK) for t in (q, k, v, logf))
    S, o = lax.scan(step, s0, xs)
    return from_chunks(o), S


def hgrn2_mixer(h, hc, w_in, lb, norm_g, w_o):
    def project(t):
        q, i, g, f_fw, f_bw = jnp.split(t @ w_in, 5, axis=-1)
        return jax.nn.silu(q), i, g, (f_fw, f_bw)

    def heads(t):
        return t.reshape(t.shape[:2] + (HG_HEADS, HG_HEAD)).astype(jnp.float32)

    def direction(p, d, s0):
        q, i, _, fz = p
        f = lb[d] + (1.0 - lb[d]) * jax.nn.sigmoid(fz[d].astype(jnp.float32))
        args = (heads(q), heads(1.0 - f), heads(i), heads(jnp.log(f)))
        if d == 1:
            args = tuple(a[:, ::-1] for a in args)
        o, s = gla_scan(*args, s0)
        return (o[:, ::-1] if d == 1 else o), s

    lat, ctx = project(h), project(hc)
    o_lat = o_ctx = 0.0
    for d in range(2):
        s0 = jnp.zeros((h.shape[0], HG_HEADS, HG_HEAD, HG_HEAD), jnp.float32)
        oc, s_ctx = direction(ctx, d, s0)
        ol, _ = direction(lat, d, s_ctx)
        o_ctx, o_lat = o_ctx + oc, o_lat + ol

    def readout(o, p, t):
        o = o.reshape(t.shape).astype(t.dtype)
        return (head_rms_norm(o, norm_g, HG_HEADS) * jax.nn.silu(p[2])) @ w_o

    return readout(o_lat, lat, h), readout(o_ctx, ctx, hc)


def centred_window_mean(t, w, axis):
    n = t.shape[axis]
    pos = np.arange(n)
    lo = np.clip(pos - w // 2, 0, n - 1)
    hi = np.clip(pos + w - w // 2 - 1, 0, n - 1)
    pad = [(0, 0)] * t.ndim
    pad[axis] = (1, 0)
    cs = jnp.pad(jnp.cumsum(t.astype(jnp.float32), axis=axis), pad)
    total = jnp.take(cs, hi + 1, axis=axis) - jnp.take(cs, lo, axis=axis)
    shape = [1] * t.ndim
    shape[axis] = n
    cnt = jnp.asarray((hi - lo + 1).reshape(shape), jnp.float32)
    return (total / cnt).astype(t.dtype)


def pool_mixer(h, hc, w_grp, scale):
    b, t_len, d = h.shape
    rows = t_len // GRID_W
    hg = h.reshape(b, rows, GRID_W, POOL_GROUPS, POOL_WIDTH)
    hcg = hc.reshape(b, hc.shape[1], POOL_GROUPS, POOL_WIDTH)

    def pooled(t, axis):
        p = jnp.stack([centred_window_mean(t[..., j, :], w, axis)
                       for j, w in enumerate(POOL_WINDOWS)], axis=-2)
        return p - t

    y = jnp.einsum('brwgi,gio->brwgo', pooled(hg, 2), w_grp).reshape(b, t_len, d) * scale
    yc = jnp.einsum('btgi,gio->btgo', pooled(hcg, 1), w_grp).reshape(hc.shape) * scale
    return y, yc


def mlstm_scan(q, k, v, logi, logf, state):
    tri = jnp.tril(jnp.ones((ML_CHUNK, ML_CHUNK), bool))

    def step(carry, inp):
        C, n, m = carry
        qc, kc, vc, ic, fc = inp
        b = jnp.cumsum(fc, axis=1)
        dmat = b[:, :, None, :] - b[:, None, :, :] + ic[:, None, :, :]
        dmat = jnp.where(tri[None, :, :, None], dmat, -jnp.inf)
        inter = b + m[:, None, :]
        m_t = jnp.maximum(inter, dmat.max(axis=2))
        w = jnp.exp(dmat - m_t[:, :, None, :])
        w_inter = jnp.exp(inter - m_t)
        qk = jnp.einsum('blhd,bshd->blsh', qc, kc) * w
        num = (jnp.einsum('blsh,bshv->blhv', qk, vc)
               + w_inter[..., None] * jnp.einsum('blhk,bhkv->blhv', qc, C))
        den = qk.sum(axis=2) + w_inter * jnp.einsum('blhk,bhk->blh', qc, n)
        hcur = num / jnp.maximum(jnp.abs(den), jnp.exp(-m_t))[..., None]
        b_last = b[:, -1]
        d_last = b_last[:, None, :] - b + ic
        m_new = jnp.maximum(b_last + m, d_last.max(axis=1))
        w_last = jnp.exp(d_last - m_new[:, None, :])
        decay = jnp.exp(b_last + m - m_new)
        C = decay[..., None, None] * C + jnp.einsum('blh,blhk,blhv->bhkv', w_last, kc, vc)
        n = decay[..., None] * n + jnp.einsum('blh,blhk->bhk', w_last, kc)
        return (C, n, m_new), hcur

    xs = tuple(to_chunks(t, ML_CHUNK) for t in (q, k, v, logi, logf))
    state, hs = lax.scan(step, state, xs)
    return from_chunks(hs), state


def mlstm_mixer(h, hc, w_in, gate_b, norm_g, w_o):
    d_mix = h.shape[-1]

    def project(t):
        z = t @ w_in
        sh = lambda a: a.reshape(t.shape[:2] + (ML_HEADS, ML_HEAD)).astype(jnp.float32)
        q, k, v, o = (z[..., j * d_mix:(j + 1) * d_mix] for j in range(4))
        gates = z[..., 4 * d_mix:].reshape(t.shape[:2] + (4, ML_HEADS)).astype(jnp.float32) + gate_b
        return sh(q), sh(k) * ML_HEAD ** -0.5, sh(v), o, gates

    def direction(p, d, state):
        q, k, v, _, gates = p
        args = (q, k, v, gates[:, :, 2 * d], jax.nn.log_sigmoid(gates[:, :, 2 * d + 1]))
        if d == 1:
            args = tuple(a[:, ::-1] for a in args)
        hcur, st = mlstm_scan(*args, state)
        return (hcur[:, ::-1] if d == 1 else hcur), st

    lat, ctx = project(h), project(hc)
    bsz = h.shape[0]
    h_lat = h_ctx = 0.0
    for d in range(2):
        st0 = (jnp.zeros((bsz, ML_HEADS, ML_HEAD, ML_HEAD), jnp.float32),
               jnp.zeros((bsz, ML_HEADS, ML_HEAD), jnp.float32),
               jnp.zeros((bsz, ML_HEADS), jnp.float32))
        hc_d, st_ctx = direction(ctx, d, st0)
        hl_d, _ = direction(lat, d, st_ctx)
        h_ctx, h_lat = h_ctx + hc_d, h_lat + hl_d

    def readout(hs, p, t):
        hs = hs.reshape(t.shape).astype(t.dtype)
        return (head_rms_norm(hs, norm_g, ML_HEADS) * jax.nn.sigmoid(p[3])) @ w_o

    return readout(h_lat, lat, h), readout(h_ctx, ctx, hc)


def s5_discretise(lam_re, lam_im, log_dt, b_re, b_im):
    dt = jnp.exp(log_dt)[:, None]
    mag = jnp.exp(lam_re * dt)
    ab_re, ab_im = mag * jnp.cos(lam_im * dt), mag * jnp.sin(lam_im * dt)
    den = lam_re ** 2 + lam_im ** 2
    nr, ni = ab_re - 1.0, ab_im
    fr = (nr * lam_re + ni * lam_im) / den
    fi = (ni * lam_re - nr * lam_im) / den
    bb_re = fr[..., None] * b_re - fi[..., None] * b_im
    bb_im = fr[..., None] * b_im + fi[..., None] * b_re
    return (ab_re, ab_im), (bb_re, bb_im)


def complex_affine_combine(e1, e2):
    a1r, a1i, x1r, x1i = e1
    a2r, a2i, x2r, x2i = e2
    return (a2r * a1r - a2i * a1i, a2r * a1i + a2i * a1r,
            a2r * x1r - a2i * x1i + x2r, a2r * x1i + a2i * x1r + x2i)


def s5_scan(u, abar, bbar, cmat, x0):
    ar, ai = abar
    br, bi = bbar
    cr, ci = cmat
    uc = jnp.swapaxes(to_chunks(u, S5_CHUNK), 1, 2)
    a_r = jnp.broadcast_to(ar[None, None], (S5_CHUNK, 1) + ar.shape)
    a_i = jnp.broadcast_to(ai[None, None], (S5_CHUNK, 1) + ai.shape)

    def step(carry, ub):
        xr, xi = carry
        bu_r = jnp.einsum('gpi,lbgi->lbgp', br, ub)
        bu_i = jnp.einsum('gpi,lbgi->lbgp', bi, ub)
        bu_r = bu_r.at[0].add(ar * xr - ai * xi)
        bu_i = bu_i.at[0].add(ar * xi + ai * xr)
        _, _, sr, si = lax.associative_scan(complex_affine_combine, (a_r, a_i, bu_r, bu_i), axis=0)
        y = jnp.einsum('gip,lbgp->lbgi', cr, sr) - jnp.einsum('gip,lbgp->lbgi', ci, si)
        return (sr[-1], si[-1]), y

    state, y = lax.scan(step, x0, uc)
    return from_chunks(jnp.swapaxes(y, 1, 2)), state


def s5_mixer(h, hc, lam_re, lam_im, log_dt, b_re, b_im, c_re, c_im, d_skip, w_glu):
    f32 = lambda a: a.astype(jnp.float32)
    groups = lambda t: t.reshape(t.shape[:2] + (S5_GROUPS, S5_GROUP)).astype(jnp.float32)
    u_lat, u_ctx = groups(h), groups(hc)
    bsz = h.shape[0]
    y_lat = y_ctx = 0.0
    for d in range(2):
        abar, bbar = s5_discretise(f32(lam_re[d]), f32(lam_im[d]), f32(log_dt[d]), f32(b_re[d]), f32(b_im[d]))
        cmat = (f32(c_re[d]), f32(c_im[d]))
        ul, ucx = (u_lat, u_ctx) if d == 0 else (u_lat[:, ::-1], u_ctx[:, ::-1])
        x0 = (jnp.zeros((bsz, S5_GROUPS, S5_STATE), jnp.float32),
              jnp.zeros((bsz, S5_GROUPS, S5_STATE), jnp.float32))
        yc, x_ctx = s5_scan(ucx, abar, bbar, cmat, x0)
        yl, _ = s5_scan(ul, abar, bbar, cmat, x_ctx)
        if d == 1:
            yc, yl = yc[:, ::-1], yl[:, ::-1]
        y_ctx, y_lat = y_ctx + yc, y_lat + yl

    def readout(y, t):
        y = y.reshape(t.shape).astype(t.dtype) + d_skip * t
        a, g = jnp.split(jax.nn.gelu(y) @ w_glu, 2, axis=-1)
        return a * jax.nn.sigmoid(g)

    return readout(y_lat, h), readout(y_ctx, hc)


def _n_slots(kind):
    return len(range(kind, DEPTH, N_MIXERS))


def setup_inputs(seed: int = 0) -> dict:
    key = jax.random.key(seed)
    ks = iter(jax.random.split(key, 64))
    nrm = lambda shape, s: jax.random.normal(next(ks), shape, jnp.float32) * s
    D, F, H = D_MODEL, D_FF, ML_HEADS
    G, P, I = S5_GROUPS, S5_STATE, S5_GROUP
    nA, nB, nC, nD = (_n_slots(k) for k in range(N_MIXERS))
    inp = {}
    inp['x'] = nrm((BATCH, SEQ, D), 1.0)
    inp['c'] = nrm((BATCH, D), 1.0)
    inp['ctx'] = nrm((BATCH, CTX_LEN, D), 1.0)
    inp['c_ctx'] = nrm((D,), 1.0)
    inp['mod_w'] = nrm((DEPTH, D, N_MOD * D), D ** -0.5)
    inp['mod_b'] = nrm((DEPTH, N_MOD * D), 0.01)
    inp['ln_g'] = 1.0 + nrm((DEPTH, 3, D), 0.01)
    inp['ln_b'] = nrm((DEPTH, 3, D), 0.01)
    inp['ffn1_w13'] = nrm((DEPTH, D, 2 * F), D ** -0.5)
    inp['ffn1_w2'] = nrm((DEPTH, F, D), F ** -0.5 * DN_BETA)
    inp['ffn2_w13'] = nrm((DEPTH, D, 2 * F), D ** -0.5)
    inp['ffn2_w2'] = nrm((DEPTH, F, D), F ** -0.5 * DN_BETA)
    inp['hg_w_in'] = nrm((nA, D, 5 * D), D ** -0.5)
    inp['hg_lb_logits'] = nrm((2, DEPTH, D), 0.1)
    inp['hg_norm_g'] = 1.0 + nrm((nA, D), 0.01)
    inp['hg_w_o'] = nrm((nA, D, D), D ** -0.5 * DN_BETA)
    inp['pool_w'] = nrm((nB, POOL_GROUPS, POOL_WIDTH, POOL_WIDTH), POOL_WIDTH ** -0.5 * DN_BETA)
    inp['pool_scale'] = 1.0 + nrm((nB, D), 0.01)
    inp['ml_w_in'] = nrm((nC, D, 4 * D + 4 * H), D ** -0.5)
    forget_bias = jnp.linspace(3.0, 6.0, H, dtype=jnp.float32)
    is_forget = jnp.array([0.0, 1.0, 0.0, 1.0], jnp.float32)[:, None]
    inp['ml_gate_b'] = nrm((nC, 4, H), 0.1) + is_forget * forget_bias
    inp['ml_norm_g'] = 1.0 + nrm((nC, D), 0.01)
    inp['ml_w_o'] = nrm((nC, D, D), D ** -0.5 * DN_BETA)
    inp['s5_lam_re'] = -0.5 + nrm((nD, 2, G, P), 0.01)
    inp['s5_lam_im'] = jnp.pi * jnp.arange(P, dtype=jnp.float32) + nrm((nD, 2, G, P), 0.01)
    inp['s5_log_dt'] = jax.random.uniform(next(ks), (nD, 2, G), jnp.float32,
                                          np.log(S5_DT_MIN), np.log(S5_DT_MAX))
    inp['s5_b_re'] = nrm((nD, 2, G, P, I), I ** -0.5)
    inp['s5_b_im'] = nrm((nD, 2, G, P, I), I ** -0.5)
    inp['s5_c_re'] = nrm((nD, 2, G, I, P), P ** -0.5)
    inp['s5_c_im'] = nrm((nD, 2, G, I, P), P ** -0.5)
    inp['s5_d'] = nrm((nD, D), 1.0)
    inp['s5_w_glu'] = jnp.concatenate([nrm((nD, D, D), D ** -0.5 * DN_BETA),
                                       nrm((nD, D, D), D ** -0.5)], axis=-1)
    return inp


def reference(x, c, ctx, c_ctx, mod_w, mod_b, ln_g, ln_b, ffn1_w13, ffn1_w2, ffn2_w13, ffn2_w2,
              hg_w_in, hg_lb_logits, hg_norm_g, hg_w_o, pool_w, pool_scale,
              ml_w_in, ml_gate_b, ml_norm_g, ml_w_o,
              s5_lam_re, s5_lam_im, s5_log_dt, s5_b_re, s5_b_im, s5_c_re, s5_c_im, s5_d, s5_w_glu):
    lb_all = jnp.cumsum(jax.nn.softmax(hg_lb_logits.astype(jnp.float32), axis=1), axis=1)
    for i in range(DEPTH):
        kind, slot = i % N_MIXERS, i // N_MIXERS
        ml = jnp.split((jax.nn.silu(c) @ mod_w[i] + mod_b[i])[:, None, :], N_MOD, axis=-1)
        mc = jnp.split((jax.nn.silu(c_ctx) @ mod_w[i] + mod_b[i])[None, None, :], N_MOD, axis=-1)
        x = post_norm(x, 0.5 * swiglu(modulate(x, ml[0], ml[1]), ffn1_w13[i], ffn1_w2[i]),
                      ml[2], ln_g[i, 0], ln_b[i, 0])
        ctx = post_norm(ctx, 0.5 * swiglu(modulate(ctx, mc[0], mc[1]), ffn1_w13[i], ffn1_w2[i]),
                        mc[2], ln_g[i, 0], ln_b[i, 0])
        h, hc = modulate(x, ml[3], ml[4]), modulate(ctx, mc[3], mc[4])
        if kind == 0:
            y, yc = hgrn2_mixer(h, hc, hg_w_in[slot], lb_all[:, i], hg_norm_g[slot], hg_w_o[slot])
        elif kind == 1:
            y, yc = pool_mixer(h, hc, pool_w[slot], pool_scale[slot])
        elif kind == 2:
            y, yc = mlstm_mixer(h, hc, ml_w_in[slot], ml_gate_b[slot], ml_norm_g[slot], ml_w_o[slot])
        else:
            y, yc = s5_mixer(h, hc, s5_lam_re[slot], s5_lam_im[slot], s5_log_dt[slot],
                             s5_b_re[slot], s5_b_im[slot], s5_c_re[slot], s5_c_im[slot],
                             s5_d[slot], s5_w_glu[slot])
        x = post_norm(x, y, ml[5], ln_g[i, 1], ln_b[i, 1])
        x = post_norm(x, 0.5 * swiglu(modulate(x, ml[6], ml[7]), ffn2_w13[i], ffn2_w2[i]),
                      ml[8], ln_g[i, 2], ln_b[i, 2])
        if i < DEPTH - 1:
            ctx = post_norm(ctx, yc, mc[5], ln_g[i, 1], ln_b[i, 1])
            ctx = post_norm(ctx, 0.5 * swiglu(modulate(ctx, mc[6], mc[7]), ffn2_w13[i], ffn2_w2[i]),
                            mc[8], ln_g[i, 2], ln_b[i, 2])
    return x
```

```python
import contextlib
import os
import numpy as np
import concourse.bass as bass
import concourse.mybir as mybir
from concourse.bass_utils import run_bass_kernel_spmd

F32 = mybir.dt.float32
BF16 = mybir.dt.bfloat16
I32 = mybir.dt.int32
AF = mybir.ActivationFunctionType
ALU = mybir.AluOpType
AX = mybir.AxisListType


class StopBuild(Exception):
    pass


class Buf:
    __slots__ = ("lw", "rd")

    def __init__(self):
        self.lw = None
        self.rd = {}


def bufs(n):
    return [Buf() for _ in range(n)]


class Op:
    __slots__ = ("eng", "fn", "deps", "dsem", "sig", "sigkey", "sigval")

    def __init__(self, eng, fn, deps, dsem):
        self.eng = eng
        self.fn = fn
        self.deps = deps
        self.dsem = dsem
        self.sig = False
        self.sigkey = None
        self.sigval = 0


class Prog:
    ENGS = ("pe", "act", "dve", "pool", "sp")

    def __init__(self, nc, es):
        self.nc = nc
        self.es = es
        self.h = {"pe": nc.tensor, "act": nc.scalar, "dve": nc.vector, "pool": nc.gpsimd, "sp": nc.sync}
        self.ops = []
        self.base = 0
        self.sems = {}
        for e in self.ENGS:
            self.sems[e] = es.enter_context(nc.semaphore("s_" + e))
        self.cnt = {e: 0 for e in self.ENGS}
        self.seen = {e: {} for e in self.ENGS}
        self.snap = {}
        self.last_dma = {}
        self.last_op = {}
        self.pool_keys = set()

    def _sem(self, key):
        if key not in self.sems:
            self.sems[key] = self.es.enter_context(self.nc.semaphore("d_" + str(key)))
            self.cnt[key] = 0
        return self.sems[key]

    def op(self, eng, fn, reads=(), writes=(), dsem=None, extra=()):
        idx = self.base + len(self.ops)
        dma = dsem is not None
        deps = set(extra)
        cand = []
        for b in reads:
            if b.lw is not None:
                cand.append((b.lw, True))
        for b in writes:
            if b.lw is not None:
                cand.append((b.lw, False))
            for r in b.rd.values():
                cand.append((r, False))
        for d, raw in cand:
            if d < self.base:
                continue
            od = self.ops[d - self.base]
            if (not dma) and od.dsem is None and od.eng == eng and not raw:
                continue
            deps.add(d)
        if dma:
            self._sem(dsem)
            if eng == "pool":
                self.pool_keys.add(dsem)
            p = self.last_dma.get(dsem)
            if p is not None and p >= self.base:
                deps.add(p)
            self.last_dma[dsem] = idx
        deps.discard(idx)
        self.ops.append(Op(eng, fn, deps, dsem))
        key = ("d", idx) if dma else eng
        for b in reads:
            b.rd[key] = idx
        for b in writes:
            b.lw = idx
            b.rd = {}
        if not dma:
            self.last_op[eng] = idx
        return idx

    def barrier(self):
        ext = set(v for v in self.last_op.values() if v >= self.base)
        ext |= set(v for v in self.last_dma.values() if v >= self.base)
        for e in self.ENGS:
            self.op(e, None, extra=ext)

    def flush(self):
        ops = self.ops
        base = self.base
        for o in ops:
            for d in o.deps:
                od = ops[d - base]
                if od.dsem is None:
                    od.sig = True
        for i, o in enumerate(ops):
            idx = base + i
            E = self.h[o.eng]
            sv = self.seen[o.eng]
            for d in sorted(o.deps, reverse=True):
                od = ops[d - base]
                if sv.get(od.sigkey, 0) >= od.sigval:
                    continue
                E.wait_ge(self.sems[od.sigkey], od.sigval)
                for k2, v2 in self.snap[d].items():
                    if sv.get(k2, 0) < v2:
                        sv[k2] = v2
            ins = o.fn(E) if o.fn is not None else None
            if o.dsem is not None:
                self.cnt[o.dsem] += 16
                o.sigkey, o.sigval = o.dsem, self.cnt[o.dsem]
                ins.then_inc(self.sems[o.dsem], 16)
                s = dict(sv)
                s[o.sigkey] = o.sigval
                self.snap[idx] = s
            elif o.sig:
                assert ins is not None
                self.cnt[o.eng] += 1
                o.sigkey, o.sigval = o.eng, self.cnt[o.eng]
                ins.then_inc(self.sems[o.eng], 1)
                s = dict(sv)
                s[o.sigkey] = o.sigval
                self.snap[idx] = s
        self.base += len(ops)
        self.ops = []
        self.snap = {}

    def hard_sync(self):
        self.barrier()
        self.flush()
        if os.environ.get("NOHS"):
            return
        self.nc.all_engine_barrier()
        if os.environ.get("HS") == "b":
            return
        for k, sem in self.sems.items():
            if k not in self.pool_keys:
                self.nc.sync.sem_clear(sem)
        self.nc.all_engine_barrier()
        for k in self.cnt:
            if k not in self.pool_keys:
                self.cnt[k] = 0
        self.seen = {e: {} for e in self.ENGS}

    def mm(self, out, lhsT, rhs, start, stop, reads, writes):
        return self.op("pe", lambda e: e.matmul(out, lhsT=lhsT, rhs=rhs, start=start, stop=stop,
                                                skip_group_check=(os.environ.get("NOSKIP") is None)), reads, writes)

    def tr(self, out, in_, ident, reads, writes):
        return self.op("pe", lambda e: e.transpose(out, in_, ident), reads, writes)

    def act(self, out, in_, func, reads, writes, bias=None, scale=None):
        kw = {}
        if bias is not None:
            kw["bias"] = bias
        if scale is not None:
            kw["scale"] = scale
        return self.op("act", lambda e: e.activation(out=out, in_=in_, func=func, **kw), reads, writes)

    def ts(self, eng, out, in0, s1, s2, op0, op1, reads, writes):
        if op1 is None:
            return self.op(eng, lambda e: e.tensor_scalar(out=out, in0=in0, scalar1=s1, scalar2=None, op0=op0),
                           reads, writes)
        return self.op(eng, lambda e: e.tensor_scalar(out=out, in0=in0, scalar1=s1, scalar2=s2, op0=op0, op1=op1),
                       reads, writes)

    def tt(self, eng, out, in0, in1, op, reads, writes):
        return self.op(eng, lambda e: e.tensor_tensor(out=out, in0=in0, in1=in1, op=op), reads, writes)

    def stt(self, eng, out, in0, scalar, in1, op0, op1, reads, writes):
        return self.op(eng, lambda e: e.scalar_tensor_tensor(out=out, in0=in0, scalar=scalar, in1=in1,
                                                             op0=op0, op1=op1), reads, writes)

    def cp(self, eng, out, in_, reads, writes):
        return self.op(eng, lambda e: e.tensor_copy(out=out, in_=in_), reads, writes)

    def scan(self, out, d0, d1, init, op0, op1, reads, writes):
        return self.op("dve", lambda e: e.tensor_tensor_scan(out=out, data0=d0, data1=d1, initial=init,
                                                             op0=op0, op1=op1), reads, writes)

    def memset(self, eng, ap, val, writes):
        return self.op(eng, lambda e: e.memset(ap, val), (), writes)

    def dma(self, q, out, in_, dsem, reads=(), writes=()):
        return self.op(q, lambda e: e.dma_start(out=out, in_=in_), reads, writes, dsem=dsem)


class Cfg:
    def __init__(self, D=2048, F=5632, T=4096, TC=256, kinds=(0, 1, 2, 3), ml_heads=8, nb=1, last_skip=True):
        self.D, self.F, self.T, self.TC = D, F, T, TC
        self.DC, self.FC = D // 128, F // 128
        self.kinds = tuple(kinds)
        self.L = len(kinds)
        self.TT = T + TC
        self.NT = 512
        self.ml_heads = ml_heads
        self.nb = nb
        self.last_skip = last_skip
        self.alpha = (2.0 * 4) ** 0.25
        self.tiles = [(0, TC)] + [(TC + i * self.NT, self.NT) for i in range(T // self.NT)]
        self.pool_windows = (2, 4, 8, 16)


LN_EPS = 1e-5
RMS_EPS = 1e-6
GRID_W = 64


class PN:
    def __init__(self, B, st, NT):
        self.B = B
        self.tm = [B.sb(st, "pn_tm%d" % i, [128, NT], F32) for i in range(2)]
        self.sq = [B.sb(st, "pn_sq%d" % i, [128, NT], BF16) for i in range(2)]
        self.zb = [B.sb(st, "pn_zb%d" % i, [128, NT], BF16) for i in range(2)]
        self.mean = B.sb(st, "pn_mean", [128, NT], F32)
        self.rstd = B.sb(st, "pn_rstd", [128, NT], F32)
        self.msq = B.sb(st, "pn_msq", [128, NT], F32)
        self.b_tm, self.b_sq, self.b_zb = bufs(2), bufs(2), bufs(2)
        self.b_mean, self.b_rstd, self.b_msq = Buf(), Buf(), Buf()

    def begin(self, li, s, X, bX, nt, col):
        self.li, self.s, self.X, self.bX, self.nt, self.col = li, s, X, bX, nt, col
        self.pend = None

    def _stats(self, m, first, last):
        B, P, nt = self.B, self.B.P, self.nt
        P.mm(B.ps[6][:, :nt], B.ones[:], self.zb[m % 2][:, :nt], first, last, [self.b_zb[m % 2], B.b_const], [B.psb[6]])
        P.mm(B.ps[7][:, :nt], B.ones[:], self.sq[m % 2][:, :nt], first, last, [self.b_sq[m % 2], B.b_const], [B.psb[7]])

    def add(self, m, y_ap, y_bufs, gate_ap):
        B, P, nt, X, bX = self.B, self.B.P, self.nt, self.X, self.bX
        if self.pend is not None:
            self._stats(self.pend, self.pend == 0, False)
        Tm = self.tm[m % 2]
        P.act(Tm[:, :nt], y_ap, AF.Identity, list(y_bufs) + [B.b_const], [self.b_tm[m % 2]], scale=gate_ap)
        P.stt("dve", X[:, m, :nt], X[:, m, :nt], B.cfg.alpha, Tm[:, :nt], ALU.mult, ALU.add,
              [bX[m], self.b_tm[m % 2]], [bX[m]])
        P.act(self.sq[m % 2][:, :nt], X[:, m, :nt], AF.Square, [bX[m]], [self.b_sq[m % 2]])
        P.cp("pool", self.zb[m % 2][:, :nt], X[:, m, :nt], [bX[m]], [self.b_zb[m % 2]])
        self.pend = m

    def finish(self):
        B, P, nt, X, bX, li, s = self.B, self.B.P, self.nt, self.X, self.bX, self.li, self.s
        cfg = B.cfg
        D, DC = cfg.D, cfg.DC
        self._stats(self.pend, self.pend == 0, True)
        mean, rstd, msq = self.mean, self.rstd, self.msq
        P.ts("dve", mean[:, :nt], B.ps[6][:, :nt], 1.0 / D, None, ALU.mult, None, [B.psb[6]], [self.b_mean])
        P.tt("dve", msq[:, :nt], mean[:, :nt], mean[:, :nt], ALU.mult, [self.b_mean], [self.b_msq])
        P.stt("dve", rstd[:, :nt], B.ps[7][:, :nt], 1.0 / D, msq[:, :nt], ALU.mult, ALU.subtract,
              [B.psb[7], self.b_msq], [self.b_rstd])
        P.ts("dve", rstd[:, :nt], rstd[:, :nt], LN_EPS, None, ALU.add, None, [self.b_rstd], [self.b_rstd])
        P.act(rstd[:, :nt], rstd[:, :nt], AF.Sqrt, [self.b_rstd], [self.b_rstd])
        P.op("dve", lambda e: e.reciprocal(out=rstd[:, :nt], in_=rstd[:, :nt]), [self.b_rstd], [self.b_rstd])
        for c in range(DC):
            P.tt("pool", X[:, c, :nt], X[:, c, :nt], mean[:, :nt], ALU.subtract, [bX[c], self.b_mean], [bX[c]])
            P.tt("dve", X[:, c, :nt], X[:, c, :nt], rstd[:, :nt], ALU.mult, [bX[c], self.b_rstd], [bX[c]])
            P.act(X[:, c, :nt], X[:, c, :nt], AF.Identity, [bX[c], B.b_const], [bX[c]],
                  scale=B.ln_col(B.lngT, li, s, c), bias=B.ln_col(B.lnbT, li, s, c))


class Builder:
    def __init__(self, cfg):
        self.cfg = cfg
        self.nc = bass.Bass("TRN2", target_bir_lowering=False)
        self.din = {}

    def inp(self, name, shape, dt=F32):
        t = self.nc.dram_tensor(name, list(shape), dt, kind="ExternalInput")
        self.din[name] = t
        return t.ap()

    def scratch(self, name, shape, dt):
        return self.nc.dram_tensor(name, list(shape), dt, kind="Internal").ap()

    def sb(self, stack, name, shape, dt):
        self._uid = getattr(self, "_uid", 0) + 1
        return stack.enter_context(self.nc.sbuf_tensor("%s_%d" % (name, self._uid), list(shape), dt))

    def build(self):
        cfg = self.cfg
        nc = self.nc
        D, F, T, TC, DC, FC, L, TT, nb = cfg.D, cfg.F, cfg.T, cfg.TC, cfg.DC, cfg.FC, cfg.L, cfg.TT, cfg.nb
        NJ = 2
        self.xT = self.inp("xT", [nb * D, T])
        self.ctxT = self.inp("ctxT", [nb * D, TC])
        self.cond = self.inp("cond", [nb * 128, DC * NJ])
        self.mod_w = self.inp("mod_w", [L * D, 9 * D])
        self.mod_bT = self.inp("mod_bT", [128, L * 9 * DC])
        self.ln_gT = self.inp("ln_gT", [128, L * 3 * DC])
        self.ln_bT = self.inp("ln_bT", [128, L * 3 * DC])
        self.w13 = self.inp("w13", [L * 2 * FC * 128, DC * 256])
        self.w2 = self.inp("w2", [L * 2 * DC * 128, FC * 128])
        self.outT = nc.dram_tensor("outT", [nb * D, T], F32, kind="ExternalOutput").ap()
        self.mixer_inputs()
        self.w13b = [self.scratch("w13b%d" % i, [FC * 128, DC * 256], BF16) for i in range(L * 2)]
        self.w2b = [self.scratch("w2b%d" % i, [DC * 128, FC * 128], BF16) for i in range(L * 2)]
        self.xs = [self.scratch("xs%d" % i, [D, TT], F32) for i in range(2)]
        self.mod_wb = [self.scratch("mod_wb%d" % i, [D, 9 * D], BF16) for i in range(L)]
        if self.has(1):
            self.pool_wBb = self.scratch("pool_wBb", [128, D * D // 4 // 128], BF16)
        self.mixer_scratch()

        with contextlib.ExitStack() as es:
            P = self.P = Prog(nc, es)
            self.ones = self.sb(es, "ones", [128, 128], BF16)
            self.modT = self.sb(es, "modT", [128, L * 9 * DC * NJ], F32)
            self.lngT = self.sb(es, "lngT", [128, L * 3 * DC], F32)
            self.lnbT = self.sb(es, "lnbT", [128, L * 3 * DC], F32)
            self.b_const = Buf()
            self.ps = [es.enter_context(nc.psum_tensor("ps%d" % i, [128, 512], F32)) for i in range(8)]
            self.psb = bufs(8)
            self.mixer_persistent(es)

            self.prologue_casts()
            stop = getattr(cfg, "stop_after", None)
            self._nph = 0
            if nb == 1:
                self.per_batch(0, stop)
            else:
                P.hard_sync()
                if os.environ.get("UNROLL"):
                    for bi in range(nb):
                        self.per_batch(bi, stop)
                        P.hard_sync()
                else:
                    with nc.Fori(0, nb) as bv:
                        self.per_batch(0 if os.environ.get("STATICB") else bv, stop)
                        P.hard_sync()
            P.barrier()
            P.flush()
        return nc


    def per_batch(self, b, stop):
        cfg, P = self.cfg, self.P
        D, TC, L = cfg.D, cfg.TC, cfg.L

        def dump(cur):
            src = cur.rearrange("(c p) t -> p c t", p=128)[:, :, TC:]
            P.dma("sp", self.outT[0:D, :].rearrange("(c p) t -> p c t", p=128), src, "c0")
            P.barrier()
            P.flush()
        self.prologue(b)
        cur = None
        nph = 0
        for li, kind in enumerate(cfg.kinds):
            last = (li == L - 1)
            dst = self.xs[0] if cur is not self.xs[0] else self.xs[1]
            self.ffn_phase(b, li, 0, cur, dst, cfg.tiles, None)
            cur = dst
            nph += 1
            if stop == nph:
                return dump(cur)
            dst = self.xs[0] if cur is not self.xs[0] else self.xs[1]
            try:
                self.mixer_phase(b, li, kind, cur, dst, ctx_out=not (last and cfg.last_skip))
            except StopBuild:
                return dump(cur)
            cur = dst
            nph += 1
            if stop == nph:
                return dump(cur)
            dst = self.xs[0] if cur is not self.xs[0] else self.xs[1]
            tiles = cfg.tiles[1:] if (last and cfg.last_skip) else cfg.tiles
            self.ffn_phase(b, li, 2, cur, dst, tiles, self.outT if last else None)
            cur = dst

    def mod_col(self, li, slot, c, col):
        cfg = self.cfg
        NJ = 2
        o = ((li * 9 + slot) * cfg.DC + c) * NJ + col
        return self.modT[:, o:o + 1]

    def ln_col(self, t, li, s, c):
        o = (li * 3 + s) * self.cfg.DC + c
        return t[:, o:o + 1]

    def src_ap(self, b, cur, t0, nt):
        cfg = self.cfg
        if cur is not None:
            return cur.rearrange("(c p) t -> p c t", p=128)[:, :, t0:t0 + nt]
        if t0 < cfg.TC:
            return self.ctxT.rearrange("(b c p) t -> b p c t", p=128, c=cfg.DC)[b][:, :, t0:t0 + nt]
        return self.xT.rearrange("(b c p) t -> b p c t", p=128, c=cfg.DC)[b][:, :, t0 - cfg.TC:t0 - cfg.TC + nt]

    def prologue_casts(self):
        cfg, P = self.cfg, self.P
        DC, FC, L = cfg.DC, cfg.FC, cfg.L
        for i in range(L * 2):
            for r in range(FC):
                g = i * FC + r
                P.dma("pool", self.w13b[i][r * 128:(r + 1) * 128, :], self.w13[g * 128:(g + 1) * 128, :], "cast%d" % (r % 4))
            for r in range(DC):
                g = i * DC + r
                P.dma("pool", self.w2b[i][r * 128:(r + 1) * 128, :], self.w2[g * 128:(g + 1) * 128, :], "cast%d" % (r % 4))
        D = cfg.D
        for i in range(L):
            for kc in range(DC):
                for hf in range(3):
                    P.dma("pool", self.mod_wb[i][kc * 128:(kc + 1) * 128, hf * 3 * D:(hf + 1) * 3 * D],
                          self.mod_w[i * D + kc * 128:i * D + (kc + 1) * 128, hf * 3 * D:(hf + 1) * 3 * D], "cast%d" % (kc % 4))
        if self.has(1):
            P.dma("pool", self.pool_wBb[:, :], self.pool_wB[:, :], "cast0")
        self.mixer_casts()
        P.barrier()
        P.flush()

    def prologue(self, b):
        cfg = self.cfg
        P = self.P
        nc = self.nc
        D, DC, FC, L, nb = cfg.D, cfg.DC, cfg.FC, cfg.L, cfg.nb
        NJ = 2
        with contextlib.ExitStack() as st:
            P.memset("dve", self.ones[:], 1.0, [self.b_const])
            P.dma("sp", self.lngT[:], self.ln_gT[:, :], "c0", writes=[self.b_const])
            P.dma("sp", self.lnbT[:], self.ln_bT[:, :], "c1", writes=[self.b_const])
            cnd = self.sb(st, "cnd", [128, DC * NJ], F32)
            sg = self.sb(st, "sg", [128, DC * NJ], F32)
            cb = self.sb(st, "cb", [128, DC * NJ], BF16)
            mb = self.sb(st, "mb", [128, L * 9 * DC], F32)
            n_oc = 9 * DC
            GRP = 8 if n_oc % 8 == 0 else 4
            assert GRP * NJ <= 512
            wst = [self.sb(st, "wst%d" % i, [128, DC, GRP * 128], BF16) for i in range(2)]
            b_c, b_cb, b_mb = Buf(), Buf(), Buf()
            b_w = bufs(2)
            P.dma("sp", cnd[:], self.cond.rearrange("(b p) n -> b p n", p=128)[b], "c2", writes=[b_c])
            P.dma("sp", mb[:], self.mod_bT[:, :], "c3", writes=[b_mb])
            P.act(sg[:], cnd[:], AF.Sigmoid, [b_c], [b_cb])
            P.tt("dve", cb[:], cnd[:], sg[:], ALU.mult, [b_c, b_cb], [b_cb])
            it = 0
            for li in range(L):
                for g in range(n_oc // GRP):
                    bank = self.ps[it % 2]
                    bb = self.psb[it % 2]
                    w = wst[it % 2]
                    bw = b_w[it % 2]
                    src = self.mod_wb[li][:, g * GRP * 128:(g + 1) * GRP * 128].rearrange("(k p) n -> p k n", p=128)
                    P.dma("sp", w[:], src, "mw%d" % (it % 2), writes=[bw])
                    it += 1
                    for o in range(GRP):
                        for kc in range(DC):
                            P.mm(bank[:, o * NJ:(o + 1) * NJ], w[:, kc, o * 128:(o + 1) * 128],
                                 cb[:, kc * NJ:(kc + 1) * NJ], kc == 0, kc == DC - 1, [bw, b_cb], [bb])
                    o0 = (li * n_oc + g * GRP) * NJ
                    dst = self.modT[:, o0:o0 + GRP * NJ].rearrange("p (o j) -> p o j", j=NJ)
                    srcp = bank[:, 0:GRP * NJ].rearrange("p (o j) -> p o j", j=NJ)
                    bia = mb[:, li * n_oc + g * GRP: li * n_oc + (g + 1) * GRP].unsqueeze(2).to_broadcast([128, GRP, NJ])
                    P.tt("dve", dst, srcp, bia, ALU.add, [bb, b_mb], [self.b_const])
            mv = self.modT[:].rearrange("p (l s c) -> p l s c", l=L, s=9)
            for s in (1, 4, 7):
                P.ts("dve", mv[:, :, s, :], mv[:, :, s, :], 1.0, None, ALU.add, None, [self.b_const], [self.b_const])
            for s in (2, 8):
                P.ts("dve", mv[:, :, s, :], mv[:, :, s, :], 0.5, None, ALU.mult, None, [self.b_const], [self.b_const])
            self.mixer_prologue(st)
            P.barrier()
            P.flush()

    def ffn_phase(self, b, li, s, cur, dst, tiles, final_out):
        cfg = self.cfg
        P = self.P
        D, DC, FC, TC = cfg.D, cfg.DC, cfg.FC, cfg.TC
        f = 0 if s == 0 else 1
        NT = cfg.NT
        with contextlib.ExitStack() as st:
            xt = [self.sb(st, "xt%d" % i, [128, DC, NT], F32) for i in range(2)]
            hT = self.sb(st, "hT", [128, DC, NT], BF16)
            gT = self.sb(st, "gT", [128, FC, NT], BF16)
            w13s = [self.sb(st, "w13s%d" % i, [128, DC, 256], BF16) for i in range(3)]
            w2s = [self.sb(st, "w2s%d" % i, [128, FC, 128], BF16) for i in range(2)]
            sa = [self.sb(st, "sa%d" % i, [128, NT], F32) for i in range(2)]
            b_xt, b_w13, b_w2, b_sa = bufs(2), bufs(3), bufs(2), bufs(2)
            b_h = Buf()
            b_g = bufs(FC)
            b_xc = [bufs(DC) for _ in range(2)]
            pn = PN(self, st, NT)
            PA, PB, PO, S1, S2 = (0, 1), (2, 3), (4, 5), 6, 7
            ps, psb = self.ps, self.psb
            w13v = self.w13b[li * 2 + f].rearrange("(r p) (k n) -> r p k n", p=128, n=256)
            w2v = self.w2b[li * 2 + f].rearrange("(r p) (k n) -> r p k n", p=128, n=128)
            r13 = 0
            r2 = 0
            wi13 = 0
            wi2 = 0
            for ti, (t0, nt) in enumerate(tiles):
                col = 0 if t0 >= TC else 1
                X = xt[ti % 2]
                bX = b_xc[ti % 2]
                P.dma("sp", X[:, :, :nt], self.src_ap(b, cur, t0, nt), "xl%d" % (ti % 2), writes=bX)
                for c in range(DC):
                    P.ts("pool", hT[:, c, :nt], X[:, c, :nt], self.mod_col(li, 3 * s + 1, c, col),
                         self.mod_col(li, 3 * s + 0, c, col), ALU.mult, ALU.add, [bX[c], self.b_const], [b_h])
                for j in range(FC):
                    W = w13s[wi13 % 3]
                    bW = b_w13[wi13 % 3]
                    P.dma("sp", W[:], w13v[r13 + j], "w13_%d" % (wi13 % 3), writes=[bW])
                    wi13 += 1
                    pa, pb = ps[PA[j % 2]], ps[PB[j % 2]]
                    ba, bb = psb[PA[j % 2]], psb[PB[j % 2]]
                    for kc in range(DC):
                        P.mm(pa[:, :nt], W[:, kc, 0:128], hT[:, kc, :nt], kc == 0, kc == DC - 1, [bW, b_h], [ba])
                    for kc in range(DC):
                        P.mm(pb[:, :nt], W[:, kc, 128:256], hT[:, kc, :nt], kc == 0, kc == DC - 1, [bW, b_h], [bb])
                    S = sa[j % 2]
                    P.act(S[:, :nt], pa[:, :nt], AF.Silu, [ba], [b_sa[j % 2]])
                    P.tt("dve", gT[:, j, :nt], S[:, :nt], pb[:, :nt], ALU.mult, [b_sa[j % 2], bb], [b_g[j]])
                pn.begin(li, s, X, bX, nt, col)
                for m in range(DC):
                    W = w2s[wi2 % 2]
                    bW = b_w2[wi2 % 2]
                    P.dma("sp", W[:], w2v[r2 + m], "w2_%d" % (wi2 % 2), writes=[bW])
                    wi2 += 1
                    po, bo = ps[PO[m % 2]], psb[PO[m % 2]]
                    for kc in range(FC):
                        P.mm(po[:, :nt], W[:, kc, :], gT[:, kc, :nt], kc == 0, kc == FC - 1, [bW, b_g[kc]], [bo])
                    pn.add(m, po[:, :nt], [bo], self.mod_col(li, 3 * s + 2, m, col))
                pn.finish()
                if final_out is not None:
                    oap = final_out.rearrange("(b c p) t -> b p c t", p=128, c=DC)[b][:, :, t0 - TC:t0 - TC + nt]
                else:
                    oap = dst.rearrange("(c p) t -> p c t", p=128)[:, :, t0:t0 + nt]
                P.dma("sp", oap, X[:, :, :nt], "xst%d" % (ti % 2), reads=bX)
            P.barrier()
            P.flush()

    def slot_of(self, li):
        return li // 4

    def has(self, kind):
        return kind in self.cfg.kinds

    def mixer_inputs(self):
        cfg = self.cfg
        D, DC, T, TC = cfg.D, cfg.DC, cfg.T, cfg.TC
        if self.has(1):
            self.pool_wB = self.inp("pool_wB", [128, D * D // 4 // 128])
            self.pool_scT = self.inp("pool_scT", [128, DC])
            self.pool_inv_lat = self.inp("pool_inv_lat", [128, 4 * GRID_W])
            self.pool_inv_ctx = self.inp("pool_inv_ctx", [128, 4 * TC])
        if self.has(0) or self.has(2):
            self.c_ident = self.inp("c_ident", [128, 128])
            self.c_cmask, self.c_rmask, self.c_reset = {}, {}, {}
            for CL in ((32,) if self.has(0) else ()) + ((128,) if self.has(2) else ()):
                self.c_cmask[CL] = self.inp("c_cmask%d" % CL, [128, 128])
                self.c_rmask[CL] = self.inp("c_rmask%d" % CL, [128, 128 // CL])
                self.c_reset[CL] = self.inp("c_reset%d" % CL, [128, 128])
        if self.has(0):
            self.hg_win = self.inp("hg_win", [5 * DC * 128, DC * 128])
            self.hg_wo = self.inp("hg_wo", [DC * 128, DC * 128])
            self.hg_lbT = self.inp("hg_lbT", [128, 2 * 4 * DC])
            self.hg_ngT = self.inp("hg_ngT", [128, DC])
        if self.has(2):
            H = cfg.ml_heads
            self.ml_win = self.inp("ml_win", [4 * DC * 128, DC * 128])
            self.ml_wo = self.inp("ml_wo", [DC * 128, DC * 128])
            self.ml_wg = self.inp("ml_wg", [128, DC * 4 * H])
            self.ml_gcol = self.inp("ml_gcol", [4 * H, 3])
            self.ml_sel = self.inp("ml_sel", [4 * H, 4 * H * 128])
            self.ml_ngT = self.inp("ml_ngT", [128, DC])

        if self.has(3):
            NST = 2 * DC * 4
            self.s5_lam = self.inp("s5_lam", [128, 3 * NST])
            self.s5_Bb = [self.inp("s5_Bb%d" % i, [2 * DC * 128, 4 * 128]) for i in range(2)]
            self.s5_Cb = [self.inp("s5_Cb%d" % i, [2 * DC * 128, 4 * 128]) for i in range(2)]
            self.s5_dT = self.inp("s5_dT", [128, DC])
            self.s5_iota = self.inp("s5_iota", [128, 512])
            self.s5_wglu = self.inp("s5_wglu", [2 * DC * 128, DC * 128])

    def mixer_scratch(self):
        cfg = self.cfg
        D, DC, TT = cfg.D, cfg.DC, cfg.TT
        H = cfg.ml_heads
        if self.has(3):
            self.s5_wglub = self.scratch("s5_wglub", [2 * DC * 128, DC * 128], BF16)
            self.sgy = self.scratch("sgy", [DC, 128, TT], BF16)
        if self.has(2):
            self.ml_winb = self.scratch("ml_winb", [4 * DC * 128, DC * 128], BF16)
            self.ml_wob = self.scratch("ml_wob", [DC * 128, DC * 128], BF16)
            self.mq = self.scratch("mq", [DC, 128, TT], BF16)
            self.mv = self.scratch("mv", [DC, 128, TT], BF16)
            self.mog = self.scratch("mog", [DC, 128, TT], BF16)
            self.mk = [self.scratch("mk%d" % d, [DC, 128, TT], BF16) for d in range(2)]
            self.mlf = [self.scratch("mlf%d" % d, [H, 128, TT], F32) for d in range(2)]
            self.mo = [self.scratch("mo%d" % d, [DC, 128, TT], F32) for d in range(2)]
        if self.has(0):
            self.hg_winb = self.scratch("hg_winb", [5 * DC * 128, DC * 128], BF16)
            self.hg_wob = self.scratch("hg_wob", [DC * 128, DC * 128], BF16)
            self.hq = self.scratch("hq", [DC, 128, TT], BF16)
            self.hv = self.scratch("hv", [DC, 128, TT], BF16)
            self.hgt = self.scratch("hgt", [DC, 128, TT], BF16)
            self.hk = [self.scratch("hk%d" % d, [DC, 128, TT], BF16) for d in range(2)]
            self.hlf = [self.scratch("hlf%d" % d, [DC, 128, TT], F32) for d in range(2)]
            self.ho = [self.scratch("ho%d" % d, [DC, 128, TT], F32) for d in range(2)]

    def mixer_persistent(self, es):
        pass

    def mixer_casts(self):
        P = self.P
        DC = self.cfg.DC
        if self.has(0):
            for r in range(5 * DC):
                P.dma("pool", self.hg_winb[r * 128:(r + 1) * 128, :], self.hg_win[r * 128:(r + 1) * 128, :], "cast%d" % (r % 4))
            for r in range(DC):
                P.dma("pool", self.hg_wob[r * 128:(r + 1) * 128, :], self.hg_wo[r * 128:(r + 1) * 128, :], "cast%d" % (r % 4))
        if self.has(3):
            for r in range(2 * DC):
                P.dma("pool", self.s5_wglub[r * 128:(r + 1) * 128, :], self.s5_wglu[r * 128:(r + 1) * 128, :], "cast%d" % (r % 4))
        if self.has(2):
            for r in range(4 * DC):
                P.dma("pool", self.ml_winb[r * 128:(r + 1) * 128, :], self.ml_win[r * 128:(r + 1) * 128, :], "cast%d" % (r % 4))
            for r in range(DC):
                P.dma("pool", self.ml_wob[r * 128:(r + 1) * 128, :], self.ml_wo[r * 128:(r + 1) * 128, :], "cast%d" % (r % 4))

    def mixer_prologue(self, st):
        pass

    def ml_proj_phase(self, b, li, cur):
        cfg, P = self.cfg, self.P
        D, DC, TC, NT, H = cfg.D, cfg.DC, cfg.TC, cfg.NT, cfg.ml_heads
        G4 = 4 * H
        CPH = DC // H
        s = 1
        with contextlib.ExitStack() as st:
            xt = [self.sb(st, "xt%d" % i, [128, DC, NT], F32) for i in range(2)]
            hT = self.sb(st, "hT", [128, DC, NT], BF16)
            ws = [self.sb(st, "mws%d" % i, [128, DC, 128], BF16) for i in range(3)]
            stg = [self.sb(st, "mstg%d" % i, [128, DC, NT], BF16) for i in range(2)]
            ksc = self.sb(st, "ksc", [128, 2, H, NT], F32)
            lfs = [self.sb(st, "lfs%d" % i, [128, NT], F32) for i in range(2)]
            wgf = self.sb(st, "wgf", [128, DC, G4], F32)
            wg = self.sb(st, "wg", [128, DC, G4], BF16)
            gcol = self.sb(st, "gcol", [G4, 3], F32)
            sel = self.sb(st, "sel", [G4, G4 * 128], F32)
            onec = self.sb(st, "onec", [128, 1], F32)
            xg = self.sb(st, "xg", [G4, NT], F32)
            eg = self.sb(st, "eg", [G4, NT], F32)
            g2 = self.sb(st, "g2", [G4, NT], F32)
            b_x = [bufs(DC) for _ in range(2)]
            b_h, b_c, b_xg, b_eg, b_g2, b_ksc = Buf(), Buf(), Buf(), Buf(), Buf(), Buf()
            b_ws, b_stg, b_lfs = bufs(3), bufs(2), bufs(2)
            P.dma("sp", wgf[:].rearrange("p k g -> p (k g)"), self.ml_wg[:, :], "c0", writes=[b_c])
            P.dma("sp", gcol[:], self.ml_gcol[:, :], "c1", writes=[b_c])
            P.dma("sp", sel[:], self.ml_sel[:, :], "c2", writes=[b_c])
            P.memset("dve", onec[:], 1.0, [b_c])
            P.cp("dve", wg[:], wgf[:], [b_c], [b_c])
            wv = self.ml_winb.rearrange("(r p) (k n) -> r p k n", p=128, n=128)
            wi = 0
            si = 0
            for ti, (t0, nt) in enumerate(cfg.tiles):
                col = 0 if t0 >= TC else 1
                X, bX = xt[ti % 2], b_x[ti % 2]
                P.dma("sp", X[:, :, :nt], self.src_ap(b, cur, t0, nt), "xl%d" % (ti % 2), writes=bX)
                for c in range(DC):
                    P.ts("pool", hT[:, c, :nt], X[:, c, :nt], self.mod_col(li, 3 * s + 1, c, col),
                         self.mod_col(li, 3 * s + 0, c, col), ALU.mult, ALU.add, [bX[c], self.b_const], [b_h])
                pg, bg = self.ps[7], self.psb[7]
                for kc in range(DC):
                    P.mm(pg[0:G4, :nt], wg[:, kc, :], hT[:, kc, :nt], kc == 0, kc == DC - 1, [b_c, b_h], [bg])
                P.ts("dve", xg[:, :nt], pg[0:G4, :nt], gcol[:, 0:1], None, ALU.add, None, [bg, b_c], [b_xg])
                P.act(eg[:, :nt], xg[:, :nt], AF.Exp, [b_xg], [b_eg], scale=-1.0)
                P.act(eg[:, :nt], eg[:, :nt], AF.Ln, [b_eg, b_c], [b_eg], bias=onec[0:G4, 0:1])
                P.ts("dve", g2[:, :nt], xg[:, :nt], gcol[:, 1:2], None, ALU.mult, None, [b_xg, b_c], [b_g2])
                P.stt("dve", g2[:, :nt], eg[:, :nt], gcol[:, 2:3], g2[:, :nt], ALU.mult, ALU.add, [b_eg, b_g2, b_c], [b_g2])
                for d in range(2):
                    for hd in range(H):
                        r = (2 * d) * H + hd
                        pp, bp = self.ps[(2 * hd) % 4], self.psb[(2 * hd) % 4]
                        P.mm(pp[:, :nt], sel[:, r * 128:(r + 1) * 128], g2[:, :nt], True, True, [b_c, b_g2], [bp])
                        P.act(ksc[:, d, hd, :nt], pp[:, :nt], AF.Exp, [bp], [b_ksc])
                        r = (2 * d + 1) * H + hd
                        pp, bp = self.ps[(2 * hd + 1) % 4], self.psb[(2 * hd + 1) % 4]
                        P.mm(pp[:, :nt], sel[:, r * 128:(r + 1) * 128], g2[:, :nt], True, True, [b_c, b_g2], [bp])
                        L_, bL_ = lfs[si % 2], b_lfs[si % 2]
                        P.cp("dve", L_[:, :nt], pp[:, :nt], [bp], [bL_])
                        P.dma("sp", self.mlf[d][hd, :, t0:t0 + nt], L_[:, :nt], "mlf%d" % (si % 2), reads=[bL_])
                        si += 1
                P.ts("dve", ksc[:, :, :, :nt], ksc[:, :, :, :nt], float((D // H) ** -0.5), None, ALU.mult, None, [b_ksc], [b_ksc])
                outs = [self.mq, None, self.mv, self.mog]
                for sec in range(4):
                    if sec == 1:
                        S0, S1 = stg[0], stg[1]
                    else:
                        S0 = stg[sec % 2]
                    for c in range(DC):
                        oc = sec * DC + c
                        W, bW = ws[wi % 3], b_ws[wi % 3]
                        P.dma("sp", W[:], wv[oc], "mw%d" % (wi % 3), writes=[bW])
                        wi += 1
                        pp, bp = self.ps[4 + oc % 3], self.psb[4 + oc % 3]
                        for kc in range(DC):
                            P.mm(pp[:, :nt], W[:, kc, :], hT[:, kc, :nt], kc == 0, kc == DC - 1, [bW, b_h], [bp])
                        if sec == 0 or sec == 2:
                            P.cp("dve" if c % 2 else "act", S0[:, c, :nt], pp[:, :nt], [bp], [b_stg[sec % 2]]) if False else \
                                P.act(S0[:, c, :nt], pp[:, :nt], AF.Identity, [bp], [b_stg[sec % 2]])
                        elif sec == 3:
                            P.act(S0[:, c, :nt], pp[:, :nt], AF.Sigmoid, [bp], [b_stg[sec % 2]])
                        else:
                            hd = c // CPH
                            P.tt("dve", stg[0][:, c, :nt], pp[:, :nt], ksc[:, 0, hd, :nt], ALU.mult, [bp, b_ksc], [b_stg[0]])
                            P.tt("dve", stg[1][:, c, :nt], pp[:, :nt], ksc[:, 1, hd, :nt], ALU.mult, [bp, b_ksc], [b_stg[1]])
                    if sec == 1:
                        for d in range(2):
                            P.dma("sp", self.mk[d].rearrange("c p t -> p c t")[:, :, t0:t0 + nt], stg[d][:, :, :nt],
                                  "ms%d" % d, reads=[b_stg[d]])
                    else:
                        P.dma("sp", outs[sec].rearrange("c p t -> p c t")[:, :, t0:t0 + nt], S0[:, :, :nt],
                              "ms%d" % (sec % 2), reads=[b_stg[sec % 2]])
            P.barrier()
            P.flush()

    def mixer_phase(self, b, li, kind, cur, dst, ctx_out):
        if kind == 1:
            return self.pool_phase(b, li, cur, dst, ctx_out)
        if kind == 0:
            self.hg_proj_phase(b, li, cur)
            gla_phase(self, dict(nh=self.cfg.DC, nk=1, nv=1, ones=0, CL=32, nlf=1, q=self.hq, k=self.hk, lf=self.hlf,
                                 v=self.hv, out=self.ho, dbuf=(os.environ.get("E1") is None)))
            return self.hg_readout_phase(b, li, cur, dst, ctx_out)
        if kind == 2:
            H = self.cfg.ml_heads
            self.ml_proj_phase(b, li, cur)
            if getattr(self.cfg, "dbg", None) == "proj":
                raise StopBuild()
            gla_phase(self, dict(nh=H, nk=2, nv=2, ones=int(os.environ.get("E2", "1")), CL=128, nlf=1, q=self.mq, k=self.mk, lf=self.mlf,
                                 v=self.mv, out=self.mo, dbuf=False))
            if getattr(self.cfg, "dbg", None) == "scan":
                raise StopBuild()
            return self.readout_phase(b, li, cur, dst, ctx_out, self.mo, self.mog, self.ml_ngT, self.ml_wob, self.cfg.DC // H)
        if kind == 3:
            self.s5_scan_phase(b, li, cur)
            return self.s5_glu_phase(b, li, cur, dst, ctx_out)
        raise NotImplementedError

    def s5_scan_phase(self, b, li, cur):
        cfg, P = self.cfg, self.P
        D, DC, TC, TT, T = cfg.D, cfg.DC, cfg.TC, cfg.TT, cfg.T
        NT = 512
        s = 1
        PI = float(np.pi)
        MAGIC = 12582912.0
        tiles = [(0, TC)] + [(TC + i * NT, NT) for i in range(T // NT)]
        segs = [(0, TC), (TC, TT)]
        NCH = 4
        with contextlib.ExitStack() as st:
            u32 = self.sb(st, "u32", [128, TT], F32)
            ub = self.sb(st, "ub", [128, TT], BF16)
            yacc = self.sb(st, "yacc", [128, TT], F32)
            gy = self.sb(st, "gyo", [128, TT], BF16)
            iota = self.sb(st, "iota", [128, 512], F32)
            lam = self.sb(st, "lam", [128, 3, 2, DC, 4], F32)
            dsk = self.sb(st, "dsk", [128, DC], F32)
            tab = [[self.sb(st, "tab%d_%d" % (j, k), [128, 512], F32) for k in range(4)] for j in range(4)]
            Bf = [self.sb(st, "Bf%d" % i, [128, 4, 128], F32) for i in range(2)]
            Cf = [self.sb(st, "Cf%d" % i, [128, 4, 128], F32) for i in range(2)]
            Bb = [self.sb(st, "Bb%d" % i, [128, 4, 128], BF16) for i in range(2)]
            Cb = [self.sb(st, "Cb%d" % i, [128, 4, 128], BF16) for i in range(2)]
            lp = self.sb(st, "lp", [128, 4, 16], F32)
            x0 = self.sb(st, "x0", [128, 4, 2], F32)
            tmp = [[self.sb(st, "s5t%d_%d" % (j, k), [128, 512], F32) for k in range(6)] for j in range(NCH)]
            xb = [[self.sb(st, "s5x%d_%d" % (j, k), [128, 512], BF16) for k in range(2)] for j in range(NCH)]
            b_u, b_ub, b_y, b_gy, b_c, b_lam = Buf(), Buf(), Buf(), Buf(), Buf(), Buf()
            b_tab = bufs(4)
            b_B, b_C, b_lp = Buf(), Buf(), bufs(4)
            b_x0 = bufs(4)
            b_tmp = [bufs(6) for _ in range(NCH)]
            b_xb = [bufs(2) for _ in range(NCH)]
            P.dma("sp", iota[:], self.s5_iota[:, :], "c0", writes=[b_c])
            P.dma("sp", lam[:].rearrange("p a d c j -> p (a d c j)"), self.s5_lam[:, :], "c1", writes=[b_lam])
            P.dma("sp", dsk[:], self.s5_dT[:, :], "c2", writes=[b_c])
            for fc in range(DC):
                if cur is not None:
                    P.dma("sp", u32[:], cur[fc * 128:(fc + 1) * 128, :], "s5u", writes=[b_u])
                else:
                    P.dma("sp", u32[:, 0:TC], self.ctxT.rearrange("(b c p) t -> b c p t", p=128, c=DC)[b][fc], "s5u", writes=[b_u])
                    P.dma("sp", u32[:, TC:TT], self.xT.rearrange("(b c p) t -> b c p t", p=128, c=DC)[b][fc], "s5u", writes=[b_u])
                for (a0, a1), col in zip(segs, (1, 0)):
                    P.ts("dve", u32[:, a0:a1], u32[:, a0:a1], self.mod_col(li, 4, fc, col), self.mod_col(li, 3, fc, col),
                         ALU.mult, ALU.add, [b_u, self.b_const], [b_u])
                for d in range(2):
                    for (a0, a1) in segs:
                        src = u32[:, a0:a1]
                        if d == 1:
                            src = src[:, ::-1]
                        P.cp("pool", ub[:, a0:a1], src, [b_u], [b_ub])
                    r0 = (d * DC + fc) * 128
                    for i in range(2):
                        P.dma("sp", Bf[i][:].rearrange("p j l -> p (j l)"), self.s5_Bb[i][r0:r0 + 128, :], "s5b%d" % i, writes=[b_B])
                        P.dma("sp", Cf[i][:].rearrange("p j l -> p (j l)"), self.s5_Cb[i][r0:r0 + 128, :], "s5c%d" % i, writes=[b_C])
                    P.cp("dve", Bb[0][:], Bf[0][:], [b_B], [b_B])
                    P.cp("dve", Bb[1][:], Bf[1][:], [b_B], [b_B])
                    P.cp("pool", Cb[0][:], Cf[0][:], [b_C], [b_C])
                    P.act(Cb[1][:], Cf[1][:], AF.Identity, [b_C], [b_C], scale=-1.0)
                    for j in range(4):
                        L_ = lp[:, j, :]
                        bl = b_lp[j]
                        lr = lam[:, 0, d, fc, j:j + 1]
                        lim = lam[:, 1, d, fc, j:j + 1]
                        ldt = lam[:, 2, d, fc, j:j + 1]
                        dt, th, rr = L_[:, 0:1], L_[:, 1:2], L_[:, 2:3]
                        P.act(dt, ldt, AF.Exp, [b_lam], [bl])
                        P.tt("dve", th, lim, dt, ALU.mult, [b_lam, bl], [bl])
                        P.tt("dve", rr, lr, dt, ALU.mult, [b_lam, bl], [bl])
                        P.act(rr, rr, AF.Exp, [bl], [bl])
                        Rc, Rs, Tr, Ti = tab[j]
                        bt = b_tab[j]
                        for (dstt, off) in ((Rs, 0.0), (Rc, PI / 2)):
                            P.ts("dve", Tr[:], iota[:], th, off, ALU.mult, ALU.add, [b_c, bl], [bt])
                            P.ts("dve", Ti[:], Tr[:], 1.0 / (2 * PI), MAGIC, ALU.mult, ALU.add, [bt], [bt])
                            P.ts("dve", Ti[:], Ti[:], -MAGIC, None, ALU.add, None, [bt], [bt])
                            P.stt("dve", Tr[:], Ti[:], -2 * PI, Tr[:], ALU.mult, ALU.add, [bt], [bt])
                            P.ts("dve", Tr[:], Tr[:], -PI, PI, ALU.max, ALU.min, [bt], [bt])
                            P.act(dstt[:], Tr[:], AF.Sin, [bt], [bt])
                        nr, ni, den, fr, fi, t1, t2 = (L_[:, k:k + 1] for k in range(3, 10))
                        P.tt("dve", nr, rr, Rc[:, 0:1], ALU.mult, [bl, bt], [bl])
                        P.ts("dve", nr, nr, -1.0, None, ALU.add, None, [bl], [bl])
                        P.tt("dve", ni, rr, Rs[:, 0:1], ALU.mult, [bl, bt], [bl])
                        P.tt("dve", den, lr, lr, ALU.mult, [b_lam], [bl])
                        P.tt("dve", t1, lim, lim, ALU.mult, [b_lam], [bl])
                        P.tt("dve", den, den, t1, ALU.add, [bl], [bl])
                        P.op("dve", lambda e, den=den: e.reciprocal(out=den, in_=den), [bl], [bl])
                        P.tt("dve", t1, nr, lr, ALU.mult, [bl, b_lam], [bl])
                        P.tt("dve", t2, ni, lim, ALU.mult, [bl, b_lam], [bl])
                        P.tt("dve", fr, t1, t2, ALU.add, [bl], [bl])
                        P.tt("dve", fr, fr, den, ALU.mult, [bl], [bl])
                        P.tt("dve", t1, ni, lr, ALU.mult, [bl, b_lam], [bl])
                        P.tt("dve", t2, nr, lim, ALU.mult, [bl, b_lam], [bl])
                        P.tt("dve", fi, t1, t2, ALU.subtract, [bl], [bl])
                        P.tt("dve", fi, fi, den, ALU.mult, [bl], [bl])
                        P.ts("dve", Tr[:], Rc[:], fr, None, ALU.mult, None, [bt, bl], [bt])
                        P.stt("dve", Tr[:], Rs[:], fi, Tr[:], ALU.mult, ALU.add, [bt, bl], [bt])
                        P.ts("dve", Ti[:], Rs[:], fr, None, ALU.mult, None, [bt, bl], [bt])
                        P.stt("dve", Ti[:], Rc[:], fi, Ti[:], ALU.mult, ALU.subtract, [bt, bl], [bt])
                        P.memset("dve", x0[:, j, :], 0.0, [b_x0[j]])
                    for (t0, nt) in tiles:
                        pY, bY = self.ps[7], self.psb[7]
                        for j in range(4):
                            Rc, Rs, Tr, Ti = tab[j]
                            bt, bl = b_tab[j], b_lp[j]
                            rr = lp[:, j, 2:3]
                            pr, bpr = self.ps[j % 3 * 2], self.psb[j % 3 * 2]
                            pi_, bpi = self.ps[j % 3 * 2 + 1], self.psb[j % 3 * 2 + 1]
                            P.mm(pr[:, :nt], Bb[0][:, j, :], ub[:, t0:t0 + nt], True, True, [b_B, b_ub], [bpr])
                            P.mm(pi_[:, :nt], Bb[1][:, j, :], ub[:, t0:t0 + nt], True, True, [b_B, b_ub], [bpi])
                            tm, btm = tmp[j], b_tmp[j]
                            P.tt("dve", tm[0][:, :nt], pr[:, :nt], Tr[:, :nt], ALU.mult, [bpr, bt], [btm[0]])
                            P.tt("dve", tm[1][:, :nt], pi_[:, :nt], Ti[:, :nt], ALU.mult, [bpi, bt], [btm[1]])
                            P.tt("dve", tm[2][:, :nt], pi_[:, :nt], Tr[:, :nt], ALU.mult, [bpi, bt], [btm[2]])
                            P.tt("dve", tm[3][:, :nt], pr[:, :nt], Ti[:, :nt], ALU.mult, [bpr, bt], [btm[3]])
                            P.tt("pool", tm[0][:, :nt], tm[0][:, :nt], tm[1][:, :nt], ALU.subtract, [btm[0], btm[1]], [btm[0]])
                            P.tt("pool", tm[2][:, :nt], tm[2][:, :nt], tm[3][:, :nt], ALU.add, [btm[2], btm[3]], [btm[2]])
                            rb = rr.to_broadcast([128, nt])
                            P.scan(tm[4][:, :nt], rb, tm[0][:, :nt], x0[:, j, 0:1], ALU.mult, ALU.add, [bl, btm[0], b_x0[j]], [btm[4]])
                            P.scan(tm[5][:, :nt], rb, tm[2][:, :nt], x0[:, j, 1:2], ALU.mult, ALU.add, [bl, btm[2], b_x0[j]], [btm[5]])
                            P.tt("pool", tm[0][:, :nt], tm[4][:, :nt], Rc[:, :nt], ALU.mult, [btm[4], bt], [btm[0]])
                            P.tt("pool", tm[1][:, :nt], tm[5][:, :nt], Rs[:, :nt], ALU.mult, [btm[5], bt], [btm[1]])
                            P.tt("dve", tm[2][:, :nt], tm[5][:, :nt], Rc[:, :nt], ALU.mult, [btm[5], bt], [btm[2]])
                            P.tt("dve", tm[3][:, :nt], tm[4][:, :nt], Rs[:, :nt], ALU.mult, [btm[4], bt], [btm[3]])
                            P.tt("pool", xb[j][0][:, :nt], tm[0][:, :nt], tm[1][:, :nt], ALU.subtract, [btm[0], btm[1]], [b_xb[j][0]])
                            P.tt("pool", xb[j][1][:, :nt], tm[2][:, :nt], tm[3][:, :nt], ALU.add, [btm[2], btm[3]], [b_xb[j][1]])
                            P.tt("dve", x0[:, j, 0:1], tm[0][:, nt - 1:nt], tm[1][:, nt - 1:nt], ALU.subtract, [btm[0], btm[1]], [b_x0[j]])
                            P.tt("dve", x0[:, j, 1:2], tm[2][:, nt - 1:nt], tm[3][:, nt - 1:nt], ALU.add, [btm[2], btm[3]], [b_x0[j]])
                            P.mm(pY[:, :nt], Cb[0][:, j, :], xb[j][0][:, :nt], j == 0, False, [b_C, b_xb[j][0]], [bY])
                            P.mm(pY[:, :nt], Cb[1][:, j, :], xb[j][1][:, :nt], False, j == 3, [b_C, b_xb[j][1]], [bY])
                        if d == 0:
                            P.cp("dve", yacc[:, t0:t0 + nt], pY[:, :nt], [bY], [b_y])
                        else:
                            a0, a1 = (0, TC) if t0 < TC else (TC, TT)
                            n0 = a0 + a1 - (t0 + nt)
                            dst = yacc[:, n0:n0 + nt][:, ::-1]
                            P.tt("dve", dst, dst, pY[:, :nt], ALU.add, [bY, b_y], [b_y])
                for (a0, a1) in [(0, TC)] + [(TC + i * 1024, TC + min(T, (i + 1) * 1024)) for i in range((T + 1023) // 1024)]:
                    a1 = min(a1, TT)
                    ya, ua = yacc[:, a0:a1], u32[:, a0:a1]
                    P.stt("dve", ya, ua, dsk[:, fc:fc + 1], ya, ALU.mult, ALU.add, [b_u, b_y, b_c], [b_y])
                    P.act(ua, ya, AF.Square, [b_y, b_u], [b_u])
                    P.ts("dve", ua, ua, 0.044715, 1.0, ALU.mult, ALU.add, [b_u], [b_u])
                    P.tt("dve", ua, ua, ya, ALU.mult, [b_u, b_y], [b_u])
                    P.act(ua, ua, AF.Sigmoid, [b_u], [b_u], scale=float(2.0 * np.sqrt(2.0 / np.pi)))
                    P.tt("dve", gy[:, a0:a1], ya, ua, ALU.mult, [b_y, b_u], [b_gy])
                P.dma("sp", self.sgy[fc, :, :], gy[:], "s5o", reads=[b_gy])
            P.barrier()
            P.flush()

    def s5_glu_phase(self, b, li, cur, dst, ctx_out):
        cfg, P = self.cfg, self.P
        D, DC, TC, NT = cfg.D, cfg.DC, cfg.TC, cfg.NT
        s = 1
        with contextlib.ExitStack() as st:
            xt = [self.sb(st, "xt%d" % i, [128, DC, NT], F32) for i in range(2)]
            gt = self.sb(st, "gyT", [128, DC, NT], BF16)
            ws = [self.sb(st, "gws%d" % i, [128, DC, 128], BF16) for i in range(4)]
            sg = [self.sb(st, "gsg%d" % i, [128, NT], F32) for i in range(2)]
            yv = [self.sb(st, "gyv%d" % i, [128, NT], F32) for i in range(2)]
            b_x = [bufs(DC) for _ in range(2)]
            b_gt = Buf()
            b_ws, b_sg, b_yv = bufs(4), bufs(2), bufs(2)
            pn = PN(self, st, NT)
            wv = self.s5_wglub.rearrange("(r p) (k n) -> r p k n", p=128, n=128)
            wi = 0
            tiles = cfg.tiles if ctx_out else cfg.tiles[1:]
            for ti, (t0, nt) in enumerate(tiles):
                col = 0 if t0 >= TC else 1
                X, bX = xt[ti % 2], b_x[ti % 2]
                P.dma("sp", X[:, :, :nt], self.src_ap(b, cur, t0, nt), "xl%d" % (ti % 2), writes=bX)
                P.dma("sp", gt[:, :, :nt], self.sgy.rearrange("c p t -> p c t")[:, :, t0:t0 + nt], "rg", writes=[b_gt])
                pn.begin(li, s, X, bX, nt, col)
                for m in range(DC):
                    pa, ba = self.ps[m % 2], self.psb[m % 2]
                    pg, bg = self.ps[2 + m % 2], self.psb[2 + m % 2]
                    for (oc, pp, bp) in ((m, pa, ba), (DC + m, pg, bg)):
                        W, bW = ws[wi % 4], b_ws[wi % 4]
                        P.dma("sp", W[:], wv[oc], "gw%d" % (wi % 4), writes=[bW])
                        wi += 1
                        for kc in range(DC):
                            P.mm(pp[:, :nt], W[:, kc, :], gt[:, kc, :nt], kc == 0, kc == DC - 1, [bW, b_gt], [bp])
                    P.act(sg[m % 2][:, :nt], pg[:, :nt], AF.Sigmoid, [bg], [b_sg[m % 2]])
                    P.tt("dve", yv[m % 2][:, :nt], pa[:, :nt], sg[m % 2][:, :nt], ALU.mult, [ba, b_sg[m % 2]], [b_yv[m % 2]])
                    pn.add(m, yv[m % 2][:, :nt], [b_yv[m % 2]], self.mod_col(li, 5, m, col))
                pn.finish()
                P.dma("sp", dst.rearrange("(c p) t -> p c t", p=128)[:, :, t0:t0 + nt], X[:, :, :nt],
                      "xst%d" % (ti % 2), reads=bX)
            P.barrier()
            P.flush()

    def hg_proj_phase(self, b, li, cur):
        cfg, P = self.cfg, self.P
        D, DC, TC, NT = cfg.D, cfg.DC, cfg.TC, cfg.NT
        s = 1
        with contextlib.ExitStack() as st:
            xt1 = self.sb(st, "xt", [128, DC, NT], F32)
            xt = [xt1, xt1]
            hT = self.sb(st, "hT", [128, DC, NT], BF16)
            ws = [self.sb(st, "hws%d" % i, [128, DC, 128], BF16) for i in range(3)]
            oq2 = [self.sb(st, "oq%d" % i, [128, DC, NT], BF16) for i in range(2)]
            oq = [oq2[i % 2] for i in range(5)]
            olf = [self.sb(st, "olf%d" % i, [128, DC, NT], F32) for i in range(2)]
            t1 = [self.sb(st, "ht1_%d" % i, [128, NT], F32) for i in range(2)]
            t2 = [self.sb(st, "ht2_%d" % i, [128, NT], F32) for i in range(2)]
            lg = self.sb(st, "lg", [128, 2, 4, DC], F32)
            oml = self.sb(st, "oml", [128, 2, DC], F32)
            den = self.sb(st, "lden", [128, 2, DC], F32)
            onec = self.sb(st, "onec", [128, 1], F32)
            b_x1 = bufs(DC)
            b_x = [b_x1, b_x1]
            b_h, b_lb = Buf(), Buf()
            b_ws, b_t1, b_t2 = bufs(3), bufs(2), bufs(2)
            b_oq2 = bufs(2)
            b_oq = [b_oq2[i % 2] for i in range(5)]
            b_olf = bufs(2)
            P.dma("sp", lg[:].rearrange("p d l c -> p (d l c)"), self.hg_lbT[:, :], "c0", writes=[b_lb])
            P.memset("dve", onec[:], 1.0, [b_lb])
            P.act(lg[:], lg[:], AF.Exp, [b_lb], [b_lb])
            P.cp("dve", den[:], lg[:, :, 0, :], [b_lb], [b_lb])
            for l in range(1, 4):
                P.tt("dve", den[:], den[:], lg[:, :, l, :], ALU.add, [b_lb], [b_lb])
            P.cp("dve", oml[:], lg[:, :, 0, :], [b_lb], [b_lb])
            for l in range(1, li + 1):
                P.tt("dve", oml[:], oml[:], lg[:, :, l, :], ALU.add, [b_lb], [b_lb])
            P.op("dve", lambda e: e.reciprocal(out=den[:], in_=den[:]), [b_lb], [b_lb])
            P.tt("dve", oml[:], oml[:], den[:], ALU.mult, [b_lb], [b_lb])
            P.ts("dve", oml[:], oml[:], -1.0, 1.0, ALU.mult, ALU.add, [b_lb], [b_lb])
            wv = self.hg_winb.rearrange("(r p) (k n) -> r p k n", p=128, n=128)
            wi = 0
            outs = [self.hq, self.hv, self.hgt, self.hk[0], self.hk[1]]
            for ti, (t0, nt) in enumerate(cfg.tiles):
                col = 0 if t0 >= TC else 1
                X, bX = xt[ti % 2], b_x[ti % 2]
                P.dma("sp", X[:, :, :nt], self.src_ap(b, cur, t0, nt), "xl%d" % (ti % 2), writes=bX)
                for c in range(DC):
                    P.ts("pool", hT[:, c, :nt], X[:, c, :nt], self.mod_col(li, 3 * s + 1, c, col),
                         self.mod_col(li, 3 * s + 0, c, col), ALU.mult, ALU.add, [bX[c], self.b_const], [b_h])
                for sec in range(5):
                    for c in range(DC):
                        oc = sec * DC + c
                        W, bW = ws[wi % 3], b_ws[wi % 3]
                        P.dma("sp", W[:], wv[oc], "hw%d" % (wi % 3), writes=[bW])
                        wi += 1
                        pp, bp = self.ps[oc % 4], self.psb[oc % 4]
                        for kc in range(DC):
                            P.mm(pp[:, :nt], W[:, kc, :], hT[:, kc, :nt], kc == 0, kc == DC - 1, [bW, b_h], [bp])
                        if sec == 0 or sec == 2:
                            P.act(oq[sec][:, c, :nt], pp[:, :nt], AF.Silu, [bp], [b_oq[sec]])
                        elif sec == 1:
                            P.cp("dve", oq[1][:, c, :nt], pp[:, :nt], [bp], [b_oq[1]])
                        else:
                            d = sec - 3
                            T1, T2 = t1[c % 2], t2[c % 2]
                            P.act(T1[:, :nt], pp[:, :nt], AF.Sigmoid, [bp], [b_t1[c % 2]], scale=-1.0)
                            P.ts("dve", T2[:, :nt], T1[:, :nt], oml[:, d, c:c + 1], None, ALU.mult, None,
                                 [b_t1[c % 2], b_lb], [b_t2[c % 2]])
                            P.cp("pool", oq[sec][:, c, :nt], T2[:, :nt], [b_t2[c % 2]], [b_oq[sec]])
                            P.act(olf[d][:, c, :nt], T2[:, :nt], AF.Ln, [b_t2[c % 2], b_lb], [b_olf[d]], scale=-1.0, bias=onec[:, 0:1])
                    dstT = outs[sec].rearrange("c p t -> p c t")[:, :, t0:t0 + nt]
                    P.dma("sp", dstT, oq[sec][:, :, :nt], "hs%d" % sec, reads=[b_oq[sec]])
                    if sec >= 3:
                        P.dma("sp", self.hlf[sec - 3].rearrange("c p t -> p c t")[:, :, t0:t0 + nt], olf[sec - 3][:, :, :nt],
                              "hl%d" % (sec - 3), reads=[b_olf[sec - 3]])
            P.barrier()
            P.flush()

    def hg_readout_phase(self, b, li, cur, dst, ctx_out):
        self.readout_phase(b, li, cur, dst, ctx_out, self.ho, self.hgt, self.hg_ngT, self.hg_wob, 1)

    def readout_phase(self, b, li, cur, dst, ctx_out, o_dirs, gate_s, ng_in, wob, hc):
        cfg, P = self.cfg, self.P
        D, DC, TC, NT = cfg.D, cfg.DC, cfg.TC, cfg.NT
        s = 1
        with contextlib.ExitStack() as st:
            xt1 = self.sb(st, "xt", [128, DC, NT], F32)
            xt = [xt1, xt1]
            o0 = self.sb(st, "ro0", [128, DC, NT], F32)
            o1 = self.sb(st, "ro1", [128, DC, NT], F32)
            gt = self.sb(st, "rgt", [128, DC, NT], BF16)
            yT = self.sb(st, "ryT", [128, DC, NT], BF16)
            sq = [self.sb(st, "rsq%d" % i, [128, hc, NT], BF16) for i in range(2)]
            rs = [self.sb(st, "rrs%d" % i, [128, NT], F32) for i in range(2)]
            ws = [self.sb(st, "rws%d" % i, [128, DC, 128], BF16) for i in range(3)]
            ng = self.sb(st, "rng", [128, DC], F32)
            b_x1 = bufs(DC)
            b_x = [b_x1, b_x1]
            b_o0, b_o1, b_gt = bufs(DC), bufs(DC), Buf()
            b_y = bufs(DC)
            b_sq, b_rs, b_ws = bufs(2), bufs(2), bufs(3)
            b_ng = Buf()
            pn = PN(self, st, NT)
            P.dma("sp", ng[:], ng_in[:, :], "c0", writes=[b_ng])
            wv = wob.rearrange("(r p) (k n) -> r p k n", p=128, n=128)
            wi = 0
            tiles = cfg.tiles if ctx_out else cfg.tiles[1:]
            for ti, (t0, nt) in enumerate(tiles):
                col = 0 if t0 >= TC else 1
                X, bX = xt[ti % 2], b_x[ti % 2]
                P.dma("sp", X[:, :, :nt], self.src_ap(b, cur, t0, nt), "xl%d" % (ti % 2), writes=bX)
                P.dma("sp", o0[:, :, :nt], o_dirs[0].rearrange("c p t -> p c t")[:, :, t0:t0 + nt], "ro0", writes=b_o0)
                P.dma("sp", o1[:, :, :nt], o_dirs[1].rearrange("c p t -> p c t")[:, :, t0:t0 + nt], "ro1", writes=b_o1)
                P.dma("sp", gt[:, :, :nt], gate_s.rearrange("c p t -> p c t")[:, :, t0:t0 + nt], "rg", writes=[b_gt])
                for hd in range(DC // hc):
                    r = hd % 2
                    for cc in range(hc):
                        c = hd * hc + cc
                        P.tt("dve", o0[:, c, :nt], o0[:, c, :nt], o1[:, c, :nt], ALU.add, [b_o0[c], b_o1[c]], [b_o0[c]])
                        P.act(sq[r][:, cc, :nt], o0[:, c, :nt], AF.Square, [b_o0[c]], [b_sq[r]])
                    pp, bp = self.ps[hd % 4], self.psb[hd % 4]
                    for cc in range(hc):
                        P.mm(pp[:, :nt], self.ones[:], sq[r][:, cc, :nt], cc == 0, cc == hc - 1, [b_sq[r], self.b_const], [bp])
                    P.ts("dve", rs[r][:, :nt], pp[:, :nt], 1.0 / (128 * hc), RMS_EPS, ALU.mult, ALU.add, [bp], [b_rs[r]])
                    P.act(rs[r][:, :nt], rs[r][:, :nt], AF.Sqrt, [b_rs[r]], [b_rs[r]])
                    P.op("dve", lambda e, a=rs[r], nt=nt: e.reciprocal(out=a[:, :nt], in_=a[:, :nt]), [b_rs[r]], [b_rs[r]])
                    for cc in range(hc):
                        c = hd * hc + cc
                        P.tt("pool", o0[:, c, :nt], o0[:, c, :nt], rs[r][:, :nt], ALU.mult, [b_o0[c], b_rs[r]], [b_o0[c]])
                        P.stt("dve", yT[:, c, :nt], o0[:, c, :nt], ng[:, c:c + 1], gt[:, c, :nt], ALU.mult, ALU.mult,
                              [b_o0[c], b_ng, b_gt], [b_y[c]])
                pn.begin(li, s, X, bX, nt, col)
                for m in range(DC):
                    W, bW = ws[wi % 3], b_ws[wi % 3]
                    P.dma("sp", W[:], wv[m], "rw%d" % (wi % 3), writes=[bW])
                    wi += 1
                    po, bo = self.ps[4 + m % 2], self.psb[4 + m % 2]
                    for kc in range(DC):
                        P.mm(po[:, :nt], W[:, kc, :], yT[:, kc, :nt], kc == 0, kc == DC - 1, [bW, b_y[kc]], [bo])
                    pn.add(m, po[:, :nt], [bo], self.mod_col(li, 5, m, col))
                pn.finish()
                P.dma("sp", dst.rearrange("(c p) t -> p c t", p=128)[:, :, t0:t0 + nt], X[:, :, :nt],
                      "xst%d" % (ti % 2), reads=bX)
            if not ctx_out:
                pass
            P.barrier()
            P.flush()

    def pool_phase(self, b, li, cur, dst, ctx_out):
        cfg = self.cfg
        P = self.P
        D, DC, TC, NT = cfg.D, cfg.DC, cfg.TC, cfg.NT
        NJ = 2
        CPG = DC // 4
        W = D // 4
        s = 1
        with contextlib.ExitStack() as st:
            xt = [self.sb(st, "xt%d" % i, [128, DC, NT], F32) for i in range(2)]
            PAD = 8
            HW = NT + 2 * PAD * (NT // GRID_W)
            hf = self.sb(st, "hf", [128, DC, HW], F32)
            pdT = self.sb(st, "pdT", [128, DC, NT], BF16)
            pw = self.sb(st, "pw", [128, 4, CPG, W], BF16)
            psc = self.sb(st, "psc", [128, DC], F32)
            sg = self.sb(st, "sg", [128, 2, DC], F32)
            invl = self.sb(st, "invl", [128, 4, GRID_W], F32)
            invc = self.sb(st, "invc", [128, 4, TC], F32)
            tmp = {e: [self.sb(st, "ptmp_%s%d" % (e, i), [128, HW], F32) for i in range(2)] for e in ("dve", "pool")}
            b_tmp = {e: bufs(2) for e in ("dve", "pool")}
            b_xc = [bufs(DC) for _ in range(2)]
            b_hf, b_pd = bufs(DC), bufs(DC)
            b_pw, b_psc, b_sg, b_inv = Buf(), Buf(), Buf(), Buf()
            pn = PN(self, st, NT)
            P.dma("sp", pw[:].rearrange("p g k o -> p (g k o)"), self.pool_wBb[:, :], "c0", writes=[b_pw])
            P.dma("sp", psc[:], self.pool_scT[:, :], "c1", writes=[b_psc])
            P.dma("sp", invl[:].rearrange("p g w -> p (g w)"), self.pool_inv_lat[:, :], "c2", writes=[b_inv])
            P.dma("sp", invc[:].rearrange("p g w -> p (g w)"), self.pool_inv_ctx[:, :], "c3", writes=[b_inv])
            for ci, col in enumerate((0, 1)):
                o0 = ((li * 9 + 5) * DC) * NJ
                gv = self.modT[:, o0:o0 + DC * NJ].rearrange("p (c j) -> p c j", j=NJ)[:, :, col]
                P.tt("dve", sg[:, ci, :], gv, psc[:], ALU.mult, [self.b_const, b_psc], [b_sg])
            tiles = cfg.tiles if ctx_out else cfg.tiles[1:]
            for ti, (t0, nt) in enumerate(tiles):
                lat = t0 >= TC
                col = 0 if lat else 1
                ci = 0 if lat else 1
                R = GRID_W if lat else TC
                inv = invl if lat else invc
                X = xt[ti % 2]
                bX = b_xc[ti % 2]
                P.dma("sp", X[:, :, :nt], self.src_ap(b, cur, t0, nt), "xl%d" % (ti % 2), writes=bX)
                RP = R + 2 * PAD
                rows = nt // R
                if ti < 2:
                    P.memset("dve", hf[:], 0.0, b_hf)
                for c in range(DC):
                    eng = "dve" if c % 2 == 0 else "pool"

                    def vp(ap):
                        return ap[:, :rows * RP].rearrange("p (r w) -> p r w", w=RP)

                    def v(ap):
                        return ap.rearrange("p (r w) -> p r w", w=R)
                    hp = vp(hf[:, c, :])
                    hin = hp[:, :, PAD:PAD + R]
                    P.ts(eng, hin, v(X[:, c, :nt]), self.mod_col(li, 3 * s + 1, c, col),
                         self.mod_col(li, 3 * s + 0, c, col), ALU.mult, ALU.add, [bX[c], self.b_const], [b_hf[c]])
                    wi = c // CPG
                    w = cfg.pool_windows[wi]
                    A, Bt = vp(tmp[eng][0]), vp(tmp[eng][1])
                    bA, bB = b_tmp[eng]
                    P.tt(eng, A[:, :, 1:RP], hp[:, :, 1:RP], hp[:, :, 0:RP - 1], ALU.add, [b_hf[c]], [bA])
                    srcv, bs, dstv, bd = A, bA, Bt, bB
                    d = 1
                    while 4 * d <= w:
                        P.tt(eng, dstv[:, :, d:RP - d], srcv[:, :, 0:RP - 2 * d], srcv[:, :, 2 * d:RP], ALU.add, [bs], [bd])
                        srcv, bs, dstv, bd = dstv, bd, srcv, bs
                        d *= 2
                    ib = inv[:, wi, :].unsqueeze(1).to_broadcast([128, rows, R])
                    P.tt(eng, srcv[:, :, PAD:PAD + R], srcv[:, :, PAD:PAD + R], ib, ALU.mult, [bs, b_inv], [bs])
                    P.tt(eng, v(pdT[:, c, :nt]), srcv[:, :, PAD:PAD + R], hin, ALU.subtract, [bs, b_hf[c]], [b_pd[c]])
                pn.begin(li, s, X, bX, nt, col)
                for m in range(DC):
                    g = m // CPG
                    po, bo = self.ps[4 + m % 2], self.psb[4 + m % 2]
                    for kk in range(CPG):
                        kc = g * CPG + kk
                        P.mm(po[:, :nt], pw[:, g, kk, (m % CPG) * 128:(m % CPG + 1) * 128], pdT[:, kc, :nt],
                             kk == 0, kk == CPG - 1, [b_pw, b_pd[kc]], [bo])
                    pn.add(m, po[:, :nt], [bo, b_sg], sg[:, ci, m:m + 1])
                pn.finish()
                P.dma("sp", dst.rearrange("(c p) t -> p c t", p=128)[:, :, t0:t0 + nt], X[:, :, :nt],
                      "xst%d" % (ti % 2), reads=bX)
            if not ctx_out:
                pass
            P.barrier()
            P.flush()


def gla_phase(B, spec):
    cfg, P = B.cfg, B.P
    TC, TT = cfg.TC, cfg.TT
    nh, nk, nv, ones, CL, nlf = spec["nh"], spec["nk"], spec["nv"], spec["ones"], spec["CL"], spec["nlf"]
    nvt = nv + ones
    NSUB = 128 // CL
    BL = 128
    nblk = TT // BL
    ncb = TC // BL
    NL = 2 if spec.get("dbuf", True) else 1
    order = [list(range(nblk)), list(range(ncb - 1, -1, -1)) + list(range(nblk - 1, ncb - 1, -1))]
    with contextlib.ExitStack() as st:
        ident = B.sb(st, "ident", [128, 128], BF16)
        identf = B.sb(st, "identf", [128, 128], F32)
        cmask = B.sb(st, "cmask", [128, 128], F32)
        rmask = B.sb(st, "rmask", [128, NSUB], F32)
        reset = B.sb(st, "reset", [128, 128], F32)
        b_cst = Buf()
        P.dma("sp", identf[:], B.c_ident[:, :], "c0", writes=[b_cst])
        P.dma("sp", cmask[:], B.c_cmask[CL][:, :], "c1", writes=[b_cst])
        P.dma("sp", rmask[:], B.c_rmask[CL][:, :], "c2", writes=[b_cst])
        P.dma("sp", reset[:], B.c_reset[CL][:, :], "c3", writes=[b_cst])
        P.cp("dve", ident[:], identf[:], [b_cst], [b_cst])
        D_ = {}
        R2 = 2
        for d in range(2):
            t = {}
            t["Lq"] = [B.sb(st, "Lq%d_%d" % (d, i), [128, nh * nk, BL], BF16) for i in range(NL)]
            t["Lk"] = [B.sb(st, "Lk%d_%d" % (d, i), [128, nh * nk, BL], BF16) for i in range(NL)]
            t["Ll"] = [B.sb(st, "Ll%d_%d" % (d, i), [128, nh * nlf, BL], F32) for i in range(NL)]
            t["Lv"] = [B.sb(st, "Lv%d_%d" % (d, i), [128, nh * nv, BL], BF16) for i in range(NL)]
            t["bL"] = [bufs(4) for _ in range(NL)]
            t["Oo"] = [B.sb(st, "Oo%d_%d" % (d, i), [128, nh * nv, BL], F32) for i in range(NL)]
            t["bOo"] = bufs(NL)
            shapes = {"b": ([128, nk, BL], F32), "eb": ([128, nk, BL], F32), "enb": ([128, nk, BL], F32),
                      "qe": ([128, nk, BL], BF16), "ke": ([128, nk, BL], BF16), "vb": ([128, nv, BL], BF16),
                      "attm": ([128, BL], BF16), "vtok": ([128, nvt * 128], BF16), "ktok": ([128, NSUB, nk * 128], BF16),
                      "ud": ([128, nvt * 128], F32), "rc": ([128, BL], F32)}
            for nm, (shp, dt) in shapes.items():
                t[nm] = [B.sb(st, "g%s%d_%d" % (nm, d, i), shp, dt) for i in range(R2)]
                t["B" + nm] = bufs(R2)
            t["S"] = B.sb(st, "gS%d" % d, [128, nh, nk, nvt * 128], F32)
            t["Sb"] = B.sb(st, "gSb%d" % d, [128, nh, NSUB, nk * nvt * 128], BF16)
            t["bS"] = bufs(nh)
            t["bSb"] = bufs(nh)
            P.memset("dve", t["S"][:], 0.0, t["bS"])
            P.memset("pool", t["Sb"][:], 0.0, t["bSb"])
            if ones:
                for i in range(R2):
                    P.memset("pool", t["vtok"][i][:, nv * 128:], 1.0, [t["Bvtok"][i]])
            t["pb"] = d * 4
            t["bAatt"] = t["bAkt"] = B.psb[d * 4]
            t["bBv"] = B.psb[d * 4 + 1]
            t["bO"] = [B.psb[d * 4 + 1], B.psb[d * 4 + 1]]
            t["bU"] = [B.psb[d * 4 + 2], B.psb[d * 4 + 3]]
            t["oi"] = 0
            D_[d] = t
        STG = int(os.environ.get("E4", "9"))
        for step in range(min(nblk, int(os.environ.get("E3", "100000")))):
            for d in range(2):
                t = D_[d]
                blk = order[d][step]
                t0 = blk * BL
                rev = (d == 1)
                li = step % NL
                Lq, Lk, Ll, Lv = t["Lq"][li], t["Lk"][li], t["Ll"][li], t["Lv"][li]
                bLq, bLk, bLl, bLv = t["bL"][li]
                P.dma("sp", Lq[:], spec["q"].rearrange("c p t -> p c t")[:, :, t0:t0 + BL], "gq%d%d" % (d, li), writes=[bLq])
                P.dma("sp", Lk[:], spec["k"][d].rearrange("c p t -> p c t")[:, :, t0:t0 + BL], "gk%d%d" % (d, li), writes=[bLk])
                P.dma("sp", Ll[:], spec["lf"][d].rearrange("c p t -> p c t")[:, :, t0:t0 + BL], "gl%d%d" % (d, li), writes=[bLl])
                P.dma("sp", Lv[:], spec["v"].rearrange("c p t -> p c t")[:, :, t0:t0 + BL], "gv%d%d" % (d, li), writes=[bLv])
                Oo, bOo = t["Oo"][li], t["bOo"][li]

                def rv(ap, rev=rev):
                    return ap[:, ::-1] if rev else ap
                pb = t["pb"]
                pA, pBk, pU = B.ps[pb], B.ps[pb + 1], [B.ps[pb + 2], B.ps[pb + 3]]
                bAatt, bAkt, bBv, bO, bU = t["bAatt"], t["bAkt"], t["bBv"], t["bO"], t["bU"]
                S, Sb = t["S"], t["Sb"]
                for h in range(nh):
                    r = h % R2
                    g = {nm: t[nm][r] for nm in ("b", "eb", "enb", "qe", "ke", "vb", "attm", "vtok", "ktok", "ud", "rc")}
                    G = {nm: t["B" + nm][r] for nm in g}
                    bS, bSb = t["bS"][h], t["bSb"][h]
                    b_, eb, enb, qe, ke, vb = g["b"], g["eb"], g["enb"], g["qe"], g["ke"], g["vb"]
                    attm, vtok, ktok, ud, rc = g["attm"], g["vtok"], g["ktok"], g["ud"], g["rc"]
                    if STG < 1:
                        continue
                    for kc in range(nk):
                        lfi = h * nlf + (kc if nlf == nk else 0)
                        if kc == 0 or nlf == nk:
                            P.scan(b_[:, kc, :], reset[:], rv(Ll[:, lfi, :]), 0.0, ALU.mult, ALU.add, [b_cst, bLl], [G["b"]])
                            src_b = b_[:, kc, :]
                        P.act(eb[:, kc, :], src_b, AF.Exp, [G["b"]], [G["eb"]])
                        P.act(enb[:, kc, :], src_b, AF.Exp, [G["b"]], [G["enb"]], scale=-1.0)
                        P.tt("dve", qe[:, kc, :], rv(Lq[:, h * nk + kc, :]), eb[:, kc, :], ALU.mult, [bLq, G["eb"]], [G["qe"]])
                        P.tt("pool", ke[:, kc, :], rv(Lk[:, h * nk + kc, :]), enb[:, kc, :], ALU.mult, [bLk, G["enb"]], [G["ke"]])
                    for vc in range(nv):
                        P.cp("pool", vb[:, vc, :], rv(Lv[:, h * nv + vc, :]), [bLv], [G["vb"]])
                    if STG < 2:
                        continue
                    for kc in range(nk):
                        P.mm(pA[:, 0:128], ke[:, kc, :], qe[:, kc, :], kc == 0, kc == nk - 1, [G["ke"], G["qe"]], [bAatt])
                    for kc in range(nk):
                        P.mm(pA[:, 128 + kc * 128:256 + kc * 128], ke[:, kc, :], ident[:], True, True, [G["ke"], b_cst], [bAkt])
                    for vc in range(nv):
                        P.mm(pBk[:, vc * 128:(vc + 1) * 128], vb[:, vc, :], ident[:], True, True, [G["vb"], b_cst], [bBv])
                    if STG < 3:
                        continue
                    P.tt("dve", attm[:], pA[:, 0:128], cmask[:], ALU.mult, [bAatt, b_cst], [G["attm"]])
                    P.act(vtok[:, 0:nv * 128], pBk[:, 0:nv * 128], AF.Identity, [bBv], [G["vtok"]])
                    for j in range(NSUB):
                        P.act(ktok[:, j, :], pA[:, 128:128 + nk * 128], AF.Identity, [bAkt, b_cst], [G["ktok"]],
                              scale=rmask[:, j:j + 1])

                    def chain(j):
                        for kc in range(nk):
                            P.mm(pU[kc][:, 0:nvt * 128], ktok[:, j, kc * 128:(kc + 1) * 128], vtok[:], True, True,
                                 [G["ktok"], G["vtok"]], [bU[kc]])
                        for kc in range(nk):
                            dec = eb[:, kc, (j + 1) * CL - 1:(j + 1) * CL]
                            P.act(ud[:], pU[kc][:, 0:nvt * 128], AF.Identity, [bU[kc], G["eb"]], [G["ud"]], scale=dec)
                            P.stt("dve", S[:, h, kc, :], S[:, h, kc, :], dec, ud[:], ALU.mult, ALU.add,
                                  [bS, G["ud"], G["eb"]], [bS])
                            jn = (j + 1) % NSUB
                            P.cp("pool", Sb[:, h, jn, kc * nvt * 128:(kc + 1) * nvt * 128], S[:, h, kc, :], [bS], [bSb])

                    if STG < 4:
                        continue
                    for j in range(NSUB - 1):
                        chain(j)
                    vorder = ([nv] if ones else []) + list(range(nv))
                    for vc in vorder:
                        oi = t["oi"] % 2
                        t["oi"] += 1
                        pO = pBk[:, 256 + oi * 128:384 + oi * 128]
                        E7 = os.environ.get("E7", "")
                        P.mm(pO, vtok[:, vc * 128:(vc + 1) * 128], attm[:], True, "i" in E7, [G["vtok"], G["attm"]], [bO[oi]])
                        n_in = NSUB * nk
                        ii = 0
                        for j in range(0 if "i" in E7 else NSUB):
                            for kc in range(nk):
                                ii += 1
                                o0 = kc * nvt * 128 + vc * 128
                                lh = ident[:] if "L" in E7 else Sb[:, h, j, o0:o0 + 128]
                                rh = ident[:, j * CL:(j + 1) * CL] if "R" in E7 else qe[:, kc, j * CL:(j + 1) * CL]
                                P.mm(pBk[:, 256 + oi * 128 + j * CL:256 + oi * 128 + (j + 1) * CL],
                                     lh, rh, False, ii == n_in,
                                     [bSb, G["qe"]], [bO[oi]])
                        if ones and vc == nv:
                            P.act(rc[:], pO, AF.Abs, [bO[oi]], [G["rc"]])
                            P.ts("dve", rc[:], rc[:], 1.0, None, ALU.max, None, [G["rc"]], [G["rc"]])
                            P.op("dve", lambda e, rc=rc: e.reciprocal(out=rc[:], in_=rc[:]), [G["rc"]], [G["rc"]])
                        elif ones:
                            P.tt("dve", rv(Oo[:, h * nv + vc, :]), pO, rc[:], ALU.mult, [bO[oi], G["rc"]], [bOo])
                        else:
                            P.cp("dve", rv(Oo[:, h * nv + vc, :]), pO, [bO[oi]], [bOo])
                    if STG >= 5:
                        chain(NSUB - 1)
                P.dma("sp", spec["out"][d].rearrange("c p t -> p c t")[:, :, t0:t0 + BL], Oo[:], "go%d%d" % (d, li), reads=[bOo])
        P.barrier()
        P.flush()


def fm_cols(v):
    v = np.asarray(v, np.float32)
    lead = v.shape[:-1]
    n = v.shape[-1] // 128
    v = v.reshape(lead + (n, 128))
    v = np.moveaxis(v, -1, 0)
    return np.ascontiguousarray(v.reshape(128, -1))


def pool_inv_table(n, windows):
    out = np.zeros((len(windows), n), np.float32)
    pos = np.arange(n)
    for i, w in enumerate(windows):
        lo = np.clip(pos - w // 2, 0, n - 1)
        hi = np.clip(pos + w - w // 2 - 1, 0, n - 1)
        out[i] = 1.0 / (hi - lo + 1)
    return out


def prep_shared(cfg, inp):
    D, F, DC, FC, L = cfg.D, cfg.F, cfg.DC, cfg.FC, cfg.L
    f32 = lambda a: np.asarray(a, np.float32)
    m = {}
    m["mod_w"] = np.ascontiguousarray(f32(inp["mod_w"]).reshape(L * D, 9 * D))
    m["mod_bT"] = fm_cols(f32(inp["mod_b"]))
    m["ln_gT"] = fm_cols(f32(inp["ln_g"]))
    m["ln_bT"] = fm_cols(f32(inp["ln_b"]))
    w13 = np.stack([f32(inp["ffn1_w13"]), f32(inp["ffn2_w13"])], 1)
    w13 = w13.reshape(L, 2, DC, 128, 2, FC, 128)
    w13 = w13.transpose(0, 1, 5, 3, 2, 4, 6)
    m["w13"] = np.ascontiguousarray(w13).reshape(L * 2 * FC * 128, DC * 256)
    w2 = np.stack([f32(inp["ffn1_w2"]), f32(inp["ffn2_w2"])], 1)
    w2 = w2.reshape(L, 2, FC, 128, DC, 128)
    w2 = w2.transpose(0, 1, 4, 3, 2, 5)
    m["w2"] = np.ascontiguousarray(w2).reshape(L * 2 * DC * 128, FC * 128)
    if 1 in cfg.kinds:
        W = D // 4
        CPG = DC // 4
        pw = f32(inp["pool_w"])[0].reshape(4, CPG, 128, W).transpose(2, 0, 1, 3)
        m["pool_wB"] = np.ascontiguousarray(pw).reshape(128, -1)
        m["pool_scT"] = fm_cols(f32(inp["pool_scale"])[0])
        m["pool_inv_lat"] = np.ascontiguousarray(np.broadcast_to(pool_inv_table(GRID_W, cfg.pool_windows).reshape(1, -1), (128, 4 * GRID_W)))
        m["pool_inv_ctx"] = np.ascontiguousarray(np.broadcast_to(pool_inv_table(cfg.TC, cfg.pool_windows).reshape(1, -1), (128, 4 * cfg.TC)))
    if 0 in cfg.kinds or 2 in cfg.kinds:
        m["c_ident"] = np.eye(128, dtype=np.float32)
        for CL in ((32,) if 0 in cfg.kinds else ()) + ((128,) if 2 in cfg.kinds else ()):
            i = np.arange(128)
            same = (i[:, None] // CL) == (i[None, :] // CL)
            m["c_cmask%d" % CL] = (same & (i[:, None] <= i[None, :])).astype(np.float32)
            m["c_rmask%d" % CL] = ((i[:, None] // CL) == np.arange(128 // CL)[None, :]).astype(np.float32)
            m["c_reset%d" % CL] = np.ascontiguousarray(np.broadcast_to(((i % CL) != 0).astype(np.float32)[None, :], (128, 128)))
    if 0 in cfg.kinds:
        w = f32(inp["hg_w_in"])[0].reshape(DC, 128, 5 * DC, 128).transpose(2, 1, 0, 3)
        m["hg_win"] = np.ascontiguousarray(w).reshape(5 * DC * 128, DC * 128)
        w = f32(inp["hg_w_o"])[0].reshape(DC, 128, DC, 128).transpose(2, 1, 0, 3)
        m["hg_wo"] = np.ascontiguousarray(w).reshape(DC * 128, DC * 128)
        m["hg_lbT"] = fm_cols(f32(inp["hg_lb_logits"]))
        m["hg_ngT"] = fm_cols(f32(inp["hg_norm_g"])[0])
    if 2 in cfg.kinds:
        H = cfg.ml_heads
        win = f32(inp["ml_w_in"])[0]
        w = win[:, :4 * D].reshape(DC, 128, 4 * DC, 128).transpose(2, 1, 0, 3)
        m["ml_win"] = np.ascontiguousarray(w).reshape(4 * DC * 128, DC * 128)
        w = f32(inp["ml_w_o"])[0].reshape(DC, 128, DC, 128).transpose(2, 1, 0, 3)
        m["ml_wo"] = np.ascontiguousarray(w).reshape(DC * 128, DC * 128)
        wg = win[:, 4 * D:].reshape(DC, 128, 4 * H).transpose(1, 0, 2)
        m["ml_wg"] = np.ascontiguousarray(wg).reshape(128, DC * 4 * H)
        gb = f32(inp["ml_gate_b"])[0].reshape(4 * H)
        isf = np.repeat(np.array([0.0, 1.0, 0.0, 1.0], np.float32), H)
        m["ml_gcol"] = np.ascontiguousarray(np.stack([gb, np.float32(1.0) - isf, -isf], 1).astype(np.float32))
        sel = np.zeros((4 * H, 4 * H, 128), np.float32)
        for r in range(4 * H):
            sel[r, r, :] = 1.0
        m["ml_sel"] = sel.reshape(4 * H, 4 * H * 128)
        m["ml_ngT"] = fm_cols(f32(inp["ml_norm_g"])[0])
    if 3 in cfg.kinds:
        G = D // 16

        def lanes(a):
            a = f32(a).reshape(2, DC, 4, 2, 64).transpose(3, 4, 0, 1, 2).reshape(128, 2, DC, 4)
            return a
        lre, lim = lanes(inp["s5_lam_re"][0]), lanes(inp["s5_lam_im"][0])
        ldt = lanes(np.repeat(f32(inp["s5_log_dt"][0])[:, :, None], 64, axis=2))
        m["s5_lam"] = np.ascontiguousarray(np.stack([lre, lim, ldt], 1)).reshape(128, -1)

        def bblk(bm):
            bm = f32(bm).reshape(2, DC, 4, 2, 64, 16)
            out = np.zeros((2, DC, 8, 16, 4, 2, 64), np.float32)
            for j in range(4):
                for gl in range(2):
                    out[:, :, 2 * j + gl, :, j, gl, :] = bm[:, :, j, gl].transpose(0, 1, 3, 2)
            return out.reshape(2 * DC * 128, 4 * 128)

        def cblk(cm):
            cm = f32(cm).reshape(2, DC, 4, 2, 16, 64)
            out = np.zeros((2, DC, 2, 64, 4, 8, 16), np.float32)
            for j in range(4):
                for gl in range(2):
                    out[:, :, gl, :, j, 2 * j + gl, :] = cm[:, :, j, gl].transpose(0, 1, 3, 2)
            return out.reshape(2 * DC * 128, 4 * 128)
        m["s5_Bb0"], m["s5_Bb1"] = bblk(inp["s5_b_re"][0]), bblk(inp["s5_b_im"][0])
        m["s5_Cb0"], m["s5_Cb1"] = cblk(inp["s5_c_re"][0]), cblk(inp["s5_c_im"][0])
        m["s5_dT"] = fm_cols(f32(inp["s5_d"])[0])
        m["s5_iota"] = np.ascontiguousarray(np.broadcast_to(np.arange(1, 513, dtype=np.float32)[None, :], (128, 512)))
        w = f32(inp["s5_w_glu"])[0].reshape(DC, 128, 2 * DC, 128).transpose(2, 1, 0, 3)
        m["s5_wglu"] = np.ascontiguousarray(w).reshape(2 * DC * 128, DC * 128)
    return m


def prep_core(cfg, inp, batches):
    D, DC = cfg.D, cfg.DC
    f32 = lambda a: np.asarray(a, np.float32)
    m = {}
    m["xT"] = np.ascontiguousarray(np.concatenate([f32(inp["x"][b]).T for b in batches], 0))
    m["ctxT"] = np.ascontiguousarray(np.concatenate([f32(inp["ctx"][b]).T for b in batches], 0))
    cnds = []
    for b in batches:
        cs = [f32(inp["c"][b]), f32(inp["c_ctx"])]
        cnds.append(np.stack(cs, -1).reshape(DC, 128, 2).transpose(1, 0, 2).reshape(128, -1))
    m["cond"] = np.ascontiguousarray(np.concatenate(cnds, 0))
    return m


def run(cfg, inp, n_cores):
    nb = cfg.nb
    bld = Builder(cfg)
    nc = bld.build()
    shared = prep_shared(cfg, inp)
    in_maps = []
    for c in range(n_cores):
        m = dict(shared)
        m.update(prep_core(cfg, inp, list(range(c * nb, (c + 1) * nb))))
        in_maps.append({k: m[k] for k in bld.din})
    res = run_bass_kernel_spmd(nc, in_maps, core_ids=list(range(n_cores)))
    outs = []
    for c in range(n_cores):
        o = res.results[c]["outT"].reshape(nb, cfg.D, cfg.T)
        outs.append(np.transpose(o, (0, 2, 1)))
    return np.ascontiguousarray(np.concatenate(outs, 0))


N_CORES = 4


def kernel(**inputs):
    n_cores = N_CORES
    cfg = Cfg(nb=8 // n_cores)
    return run(cfg, inputs, n_cores).astype(np.float32)
```

```python
import contextlib
import os
import numpy as np
import concourse.bass as bass
import concourse.mybir as mybir
from concourse.bass_utils import run_bass_kernel_spmd

F32 = mybir.dt.float32
BF16 = mybir.dt.bfloat16
I32 = mybir.dt.int32
AF = mybir.ActivationFunctionType
ALU = mybir.AluOpType
AX = mybir.AxisListType


class StopBuild(Exception):
    pass


class Buf:
    __slots__ = ("lw", "rd")

    def __init__(self):
        self.lw = None
        self.rd = {}


def bufs(n):
    return [Buf() for _ in range(n)]


class Op:
    __slots__ = ("eng", "fn", "deps", "dsem", "sig", "sigkey", "sigval")

    def __init__(self, eng, fn, deps, dsem):
        self.eng = eng
        self.fn = fn
        self.deps = deps
        self.dsem = dsem
        self.sig = False
        self.sigkey = None
        self.sigval = 0


class Prog:
    ENGS = ("pe", "act", "dve", "pool", "sp")

    def __init__(self, nc, es):
        self.nc = nc
        self.es = es
        self.h = {"pe": nc.tensor, "act": nc.scalar, "dve": nc.vector, "pool": nc.gpsimd, "sp": nc.sync}
        self.ops = []
        self.base = 0
        self.sems = {}
        for e in self.ENGS:
            self.sems[e] = es.enter_context(nc.semaphore("s_" + e))
        self.cnt = {e: 0 for e in self.ENGS}
        self.seen = {e: {} for e in self.ENGS}
        self.snap = {}
        self.last_dma = {}
        self.last_op = {}
        self.pool_keys = set()

    def _sem(self, key):
        if key not in self.sems:
            self.sems[key] = self.es.enter_context(self.nc.semaphore("d_" + str(key)))
            self.cnt[key] = 0
        return self.sems[key]

    def op(self, eng, fn, reads=(), writes=(), dsem=None, extra=()):
        idx = self.base + len(self.ops)
        dma = dsem is not None
        deps = set(extra)
        cand = []
        for b in reads:
            if b.lw is not None:
                cand.append((b.lw, True))
        for b in writes:
            if b.lw is not None:
                cand.append((b.lw, False))
            for r in b.rd.values():
                cand.append((r, False))
        for d, raw in cand:
            if d < self.base:
                continue
            od = self.ops[d - self.base]
            if (not dma) and od.dsem is None and od.eng == eng and not raw:
                continue
            deps.add(d)
        if dma:
            self._sem(dsem)
            if eng == "pool":
                self.pool_keys.add(dsem)
            p = self.last_dma.get(dsem)
            if p is not None and p >= self.base:
                deps.add(p)
            self.last_dma[dsem] = idx
        deps.discard(idx)
        self.ops.append(Op(eng, fn, deps, dsem))
        key = ("d", idx) if dma else eng
        for b in reads:
            b.rd[key] = idx
        for b in writes:
            b.lw = idx
            b.rd = {}
        if not dma:
            self.last_op[eng] = idx
        return idx

    def barrier(self):
        ext = set(v for v in self.last_op.values() if v >= self.base)
        ext |= set(v for v in self.last_dma.values() if v >= self.base)
        for e in self.ENGS:
            self.op(e, None, extra=ext)

    def flush(self):
        ops = self.ops
        base = self.base
        for o in ops:
            for d in o.deps:
                od = ops[d - base]
                if od.dsem is None:
                    od.sig = True
        for i, o in enumerate(ops):
            idx = base + i
            E = self.h[o.eng]
            sv = self.seen[o.eng]
            for d in sorted(o.deps, reverse=True):
                od = ops[d - base]
                if sv.get(od.sigkey, 0) >= od.sigval:
                    continue
                E.wait_ge(self.sems[od.sigkey], od.sigval)
                for k2, v2 in self.snap[d].items():
                    if sv.get(k2, 0) < v2:
                        sv[k2] = v2
            ins = o.fn(E) if o.fn is not None else None
            if o.dsem is not None:
                self.cnt[o.dsem] += 16
                o.sigkey, o.sigval = o.dsem, self.cnt[o.dsem]
                ins.then_inc(self.sems[o.dsem], 16)
                s = dict(sv)
                s[o.sigkey] = o.sigval
                self.snap[idx] = s
            elif o.sig:
                assert ins is not None
                self.cnt[o.eng] += 1
                o.sigkey, o.sigval = o.eng, self.cnt[o.eng]
                ins.then_inc(self.sems[o.eng], 1)
                s = dict(sv)
                s[o.sigkey] = o.sigval
                self.snap[idx] = s
        self.base += len(ops)
        self.ops = []
        self.snap = {}

    def hard_sync(self):
        self.barrier()
        self.flush()
        if os.environ.get("NOHS"):
            return
        self.nc.all_engine_barrier()
        if os.environ.get("HS") == "b":
            return
        for k, sem in self.sems.items():
            if k not in self.pool_keys:
                self.nc.sync.sem_clear(sem)
        self.nc.all_engine_barrier()
        for k in self.cnt:
            if k not in self.pool_keys:
                self.cnt[k] = 0
        self.seen = {e: {} for e in self.ENGS}

    def mm(self, out, lhsT, rhs, start, stop, reads, writes):
        return self.op("pe", lambda e: e.matmul(out, lhsT=lhsT, rhs=rhs, start=start, stop=stop,
                                                skip_group_check=(os.environ.get("NOSKIP") is None)), reads, writes)

    def tr(self, out, in_, ident, reads, writes):
        return self.op("pe", lambda e: e.transpose(out, in_, ident), reads, writes)

    def act(self, out, in_, func, reads, writes, bias=None, scale=None):
        kw = {}
        if bias is not None:
            kw["bias"] = bias
        if scale is not None:
            kw["scale"] = scale
        return self.op("act", lambda e: e.activation(out=out, in_=in_, func=func, **kw), reads, writes)

    def ts(self, eng, out, in0, s1, s2, op0, op1, reads, writes):
        if op1 is None:
            return self.op(eng, lambda e: e.tensor_scalar(out=out, in0=in0, scalar1=s1, scalar2=None, op0=op0),
                           reads, writes)
        return self.op(eng, lambda e: e.tensor_scalar(out=out, in0=in0, scalar1=s1, scalar2=s2, op0=op0, op1=op1),
                       reads, writes)

    def tt(self, eng, out, in0, in1, op, reads, writes):
        return self.op(eng, lambda e: e.tensor_tensor(out=out, in0=in0, in1=in1, op=op), reads, writes)

    def stt(self, eng, out, in0, scalar, in1, op0, op1, reads, writes):
        return self.op(eng, lambda e: e.scalar_tensor_tensor(out=out, in0=in0, scalar=scalar, in1=in1,
                                                             op0=op0, op1=op1), reads, writes)

    def cp(self, eng, out, in_, reads, writes):
        return self.op(eng, lambda e: e.tensor_copy(out=out, in_=in_), reads, writes)

    def scan(self, out, d0, d1, init, op0, op1, reads, writes):
        return self.op("dve", lambda e: e.tensor_tensor_scan(out=out, data0=d0, data1=d1, initial=init,
                                                             op0=op0, op1=op1), reads, writes)

    def memset(self, eng, ap, val, writes):
        return self.op(eng, lambda e: e.memset(ap, val), (), writes)

    def dma(self, q, out, in_, dsem, reads=(), writes=()):
        return self.op(q, lambda e: e.dma_start(out=out, in_=in_), reads, writes, dsem=dsem)


class Cfg:
    def __init__(self, D=2048, F=5632, T=4096, TC=256, kinds=(0, 1, 2, 3), ml_heads=8, nb=1, last_skip=True):
        self.D, self.F, self.T, self.TC = D, F, T, TC
        self.DC, self.FC = D // 128, F // 128
        self.kinds = tuple(kinds)
        self.L = len(kinds)
        self.TT = T + TC
        self.NT = 512
        self.ml_heads = ml_heads
        self.nb = nb
        self.last_skip = last_skip
        self.alpha = (2.0 * 4) ** 0.25
        self.tiles = [(0, TC)] + [(TC + i * self.NT, self.NT) for i in range(T // self.NT)]
        self.pool_windows = (2, 4, 8, 16)


LN_EPS = 1e-5
RMS_EPS = 1e-6
GRID_W = 64


class PN:
    def __init__(self, B, st, NT):
        self.B = B
        self.tm = [B.sb(st, "pn_tm%d" % i, [128, NT], F32) for i in range(2)]
        self.sq = [B.sb(st, "pn_sq%d" % i, [128, NT], BF16) for i in range(2)]
        self.zb = [B.sb(st, "pn_zb%d" % i, [128, NT], BF16) for i in range(2)]
        self.mean = B.sb(st, "pn_mean", [128, NT], F32)
        self.rstd = B.sb(st, "pn_rstd", [128, NT], F32)
        self.msq = B.sb(st, "pn_msq", [128, NT], F32)
        self.b_tm, self.b_sq, self.b_zb = bufs(2), bufs(2), bufs(2)
        self.b_mean, self.b_rstd, self.b_msq = Buf(), Buf(), Buf()

    def begin(self, li, s, X, bX, nt, col):
        self.li, self.s, self.X, self.bX, self.nt, self.col = li, s, X, bX, nt, col
        self.pend = None

    def _stats(self, m, first, last):
        B, P, nt = self.B, self.B.P, self.nt
        P.mm(B.ps[6][:, :nt], B.ones[:], self.zb[m % 2][:, :nt], first, last, [self.b_zb[m % 2], B.b_const], [B.psb[6]])
        P.mm(B.ps[7][:, :nt], B.ones[:], self.sq[m % 2][:, :nt], first, last, [self.b_sq[m % 2], B.b_const], [B.psb[7]])

    def add(self, m, y_ap, y_bufs, gate_ap):
        B, P, nt, X, bX = self.B, self.B.P, self.nt, self.X, self.bX
        if self.pend is not None:
            self._stats(self.pend, self.pend == 0, False)
        Tm = self.tm[m % 2]
        P.act(Tm[:, :nt], y_ap, AF.Identity, list(y_bufs) + [B.b_const], [self.b_tm[m % 2]], scale=gate_ap)
        P.stt("dve", X[:, m, :nt], X[:, m, :nt], B.cfg.alpha, Tm[:, :nt], ALU.mult, ALU.add,
              [bX[m], self.b_tm[m % 2]], [bX[m]])
        P.act(self.sq[m % 2][:, :nt], X[:, m, :nt], AF.Square, [bX[m]], [self.b_sq[m % 2]])
        P.cp("pool", self.zb[m % 2][:, :nt], X[:, m, :nt], [bX[m]], [self.b_zb[m % 2]])
        self.pend = m

    def finish(self):
        B, P, nt, X, bX, li, s = self.B, self.B.P, self.nt, self.X, self.bX, self.li, self.s
        cfg = B.cfg
        D, DC = cfg.D, cfg.DC
        self._stats(self.pend, self.pend == 0, True)
        mean, rstd, msq = self.mean, self.rstd, self.msq
        P.ts("dve", mean[:, :nt], B.ps[6][:, :nt], 1.0 / D, None, ALU.mult, None, [B.psb[6]], [self.b_mean])
        P.tt("dve", msq[:, :nt], mean[:, :nt], mean[:, :nt], ALU.mult, [self.b_mean], [self.b_msq])
        P.stt("dve", rstd[:, :nt], B.ps[7][:, :nt], 1.0 / D, msq[:, :nt], ALU.mult, ALU.subtract,
              [B.psb[7], self.b_msq], [self.b_rstd])
        P.ts("dve", rstd[:, :nt], rstd[:, :nt], LN_EPS, None, ALU.add, None, [self.b_rstd], [self.b_rstd])
        P.act(rstd[:, :nt], rstd[:, :nt], AF.Sqrt, [self.b_rstd], [self.b_rstd])
        P.op("dve", lambda e: e.reciprocal(out=rstd[:, :nt], in_=rstd[:, :nt]), [self.b_rstd], [self.b_rstd])
        for c in range(DC):
            P.tt("pool", X[:, c, :nt], X[:, c, :nt], mean[:, :nt], ALU.subtract, [bX[c], self.b_mean], [bX[c]])
            P.tt("dve", X[:, c, :nt], X[:, c, :nt], rstd[:, :nt], ALU.mult, [bX[c], self.b_rstd], [bX[c]])
            P.act(X[:, c, :nt], X[:, c, :nt], AF.Identity, [bX[c], B.b_const], [bX[c]],
                  scale=B.ln_col(B.lngT, li, s, c), bias=B.ln_col(B.lnbT, li, s, c))


class Builder:
    def __init__(self, cfg):
        self.cfg = cfg
        self.nc = bass.Bass("TRN2", target_bir_lowering=False)
        self.din = {}

    def inp(self, name, shape, dt=F32):
        t = self.nc.dram_tensor(name, list(shape), dt, kind="ExternalInput")
        self.din[name] = t
        return t.ap()

    def scratch(self, name, shape, dt):
        return self.nc.dram_tensor(name, list(shape), dt, kind="Internal").ap()

    def sb(self, stack, name, shape, dt):
        self._uid = getattr(self, "_uid", 0) + 1
        return stack.enter_context(self.nc.sbuf_tensor("%s_%d" % (name, self._uid), list(shape), dt))

    def build(self):
        cfg = self.cfg
        nc = self.nc
        D, F, T, TC, DC, FC, L, TT, nb = cfg.D, cfg.F, cfg.T, cfg.TC, cfg.DC, cfg.FC, cfg.L, cfg.TT, cfg.nb
        NJ = 2
        self.xT = self.inp("xT", [nb * D, T])
        self.ctxT = self.inp("ctxT", [nb * D, TC])
        self.cond = self.inp("cond", [nb * 128, DC * NJ])
        self.mod_w = self.inp("mod_w", [L * D, 9 * D])
        self.mod_bT = self.inp("mod_bT", [128, L * 9 * DC])
        self.ln_gT = self.inp("ln_gT", [128, L * 3 * DC])
        self.ln_bT = self.inp("ln_bT", [128, L * 3 * DC])
        self.w13 = self.inp("w13", [L * 2 * FC * 128, DC * 256])
        self.w2 = self.inp("w2", [L * 2 * DC * 128, FC * 128])
        self.outT = nc.dram_tensor("outT", [nb * D, T], F32, kind="ExternalOutput").ap()
        self.mixer_inputs()
        self.w13b = [self.scratch("w13b%d" % i, [FC * 128, DC * 256], BF16) for i in range(L * 2)]
        self.w2b = [self.scratch("w2b%d" % i, [DC * 128, FC * 128], BF16) for i in range(L * 2)]
        self.xs = [self.scratch("xs%d" % i, [D, TT], F32) for i in range(2)]
        self.mod_wb = [self.scratch("mod_wb%d" % i, [D, 9 * D], BF16) for i in range(L)]
        if self.has(1):
            self.pool_wBb = self.scratch("pool_wBb", [128, D * D // 4 // 128], BF16)
        self.mixer_scratch()

        with contextlib.ExitStack() as es:
            P = self.P = Prog(nc, es)
            self.ones = self.sb(es, "ones", [128, 128], BF16)
            self.modT = self.sb(es, "modT", [128, L * 9 * DC * NJ], F32)
            self.lngT = self.sb(es, "lngT", [128, L * 3 * DC], F32)
            self.lnbT = self.sb(es, "lnbT", [128, L * 3 * DC], F32)
            self.b_const = Buf()
            self.ps = [es.enter_context(nc.psum_tensor("ps%d" % i, [128, 512], F32)) for i in range(8)]
            self.psb = bufs(8)
            self.mixer_persistent(es)

            self.prologue_casts()
            stop = getattr(cfg, "stop_after", None)
            self._nph = 0
            if nb == 1:
                self.per_batch(0, stop)
            else:
                P.hard_sync()
                if os.environ.get("UNROLL"):
                    for bi in range(nb):
                        self.per_batch(bi, stop)
                        P.hard_sync()
                else:
                    with nc.Fori(0, nb) as bv:
                        self.per_batch(0 if os.environ.get("STATICB") else bv, stop)
                        P.hard_sync()
            P.barrier()
            P.flush()
        return nc


    def per_batch(self, b, stop):
        cfg, P = self.cfg, self.P
        D, TC, L = cfg.D, cfg.TC, cfg.L

        def dump(cur):
            src = cur.rearrange("(c p) t -> p c t", p=128)[:, :, TC:]
            P.dma("sp", self.outT[0:D, :].rearrange("(c p) t -> p c t", p=128), src, "c0")
            P.barrier()
            P.flush()
        self.prologue(b)
        cur = None
        nph = 0
        for li, kind in enumerate(cfg.kinds):
            last = (li == L - 1)
            dst = self.xs[0] if cur is not self.xs[0] else self.xs[1]
            self.ffn_phase(b, li, 0, cur, dst, cfg.tiles, None)
            cur = dst
            nph += 1
            if stop == nph:
                return dump(cur)
            dst = self.xs[0] if cur is not self.xs[0] else self.xs[1]
            try:
                self.mixer_phase(b, li, kind, cur, dst, ctx_out=not (last and cfg.last_skip))
            except StopBuild:
                return dump(cur)
            cur = dst
            nph += 1
            if stop == nph:
                return dump(cur)
            dst = self.xs[0] if cur is not self.xs[0] else self.xs[1]
            tiles = cfg.tiles[1:] if (last and cfg.last_skip) else cfg.tiles
            self.ffn_phase(b, li, 2, cur, dst, tiles, self.outT if last else None)
            cur = dst

    def mod_col(self, li, slot, c, col):
        cfg = self.cfg
        NJ = 2
        o = ((li * 9 + slot) * cfg.DC + c) * NJ + col
        return self.modT[:, o:o + 1]

    def ln_col(self, t, li, s, c):
        o = (li * 3 + s) * self.cfg.DC + c
        return t[:, o:o + 1]

    def src_ap(self, b, cur, t0, nt):
        cfg = self.cfg
        if cur is not None:
            return cur.rearrange("(c p) t -> p c t", p=128)[:, :, t0:t0 + nt]
        if t0 < cfg.TC:
            return self.ctxT.rearrange("(b c p) t -> b p c t", p=128, c=cfg.DC)[b][:, :, t0:t0 + nt]
        return self.xT.rearrange("(b c p) t -> b p c t", p=128, c=cfg.DC)[b][:, :, t0 - cfg.TC:t0 - cfg.TC + nt]

    def prologue_casts(self):
        cfg, P = self.cfg, self.P
        DC, FC, L = cfg.DC, cfg.FC, cfg.L
        for i in range(L * 2):
            for r in range(FC):
                g = i * FC + r
                P.dma("pool", self.w13b[i][r * 128:(r + 1) * 128, :], self.w13[g * 128:(g + 1) * 128, :], "cast%d" % (r % 4))
            for r in range(DC):
                g = i * DC + r
                P.dma("pool", self.w2b[i][r * 128:(r + 1) * 128, :], self.w2[g * 128:(g + 1) * 128, :], "cast%d" % (r % 4))
        D = cfg.D
        for i in range(L):
            for kc in range(DC):
                for hf in range(3):
                    P.dma("pool", self.mod_wb[i][kc * 128:(kc + 1) * 128, hf * 3 * D:(hf + 1) * 3 * D],
                          self.mod_w[i * D + kc * 128:i * D + (kc + 1) * 128, hf * 3 * D:(hf + 1) * 3 * D], "cast%d" % (kc % 4))
        if self.has(1):
            P.dma("pool", self.pool_wBb[:, :], self.pool_wB[:, :], "cast0")
        self.mixer_casts()
        P.barrier()
        P.flush()

    def prologue(self, b):
        cfg = self.cfg
        P = self.P
        nc = self.nc
        D, DC, FC, L, nb = cfg.D, cfg.DC, cfg.FC, cfg.L, cfg.nb
        NJ = 2
        with contextlib.ExitStack() as st:
            P.memset("dve", self.ones[:], 1.0, [self.b_const])
            P.dma("sp", self.lngT[:], self.ln_gT[:, :], "c0", writes=[self.b_const])
            P.dma("sp", self.lnbT[:], self.ln_bT[:, :], "c1", writes=[self.b_const])
            cnd = self.sb(st, "cnd", [128, DC * NJ], F32)
            sg = self.sb(st, "sg", [128, DC * NJ], F32)
            cb = self.sb(st, "cb", [128, DC * NJ], BF16)
            mb = self.sb(st, "mb", [128, L * 9 * DC], F32)
            n_oc = 9 * DC
            GRP = 8 if n_oc % 8 == 0 else 4
            assert GRP * NJ <= 512
            wst = [self.sb(st, "wst%d" % i, [128, DC, GRP * 128], BF16) for i in range(2)]
            b_c, b_cb, b_mb = Buf(), Buf(), Buf()
            b_w = bufs(2)
            P.dma("sp", cnd[:], self.cond.rearrange("(b p) n -> b p n", p=128)[b], "c2", writes=[b_c])
            P.dma("sp", mb[:], self.mod_bT[:, :], "c3", writes=[b_mb])
            P.act(sg[:], cnd[:], AF.Sigmoid, [b_c], [b_cb])
            P.tt("dve", cb[:], cnd[:], sg[:], ALU.mult, [b_c, b_cb], [b_cb])
            it = 0
            for li in range(L):
                for g in range(n_oc // GRP):
                    bank = self.ps[it % 2]
                    bb = self.psb[it % 2]
                    w = wst[it % 2]
                    bw = b_w[it % 2]
                    src = self.mod_wb[li][:, g * GRP * 128:(g + 1) * GRP * 128].rearrange("(k p) n -> p k n", p=128)
                    P.dma("sp", w[:], src, "mw%d" % (it % 2), writes=[bw])
                    it += 1
                    for o in range(GRP):
                        for kc in range(DC):
                            P.mm(bank[:, o * NJ:(o + 1) * NJ], w[:, kc, o * 128:(o + 1) * 128],
                                 cb[:, kc * NJ:(kc + 1) * NJ], kc == 0, kc == DC - 1, [bw, b_cb], [bb])
                    o0 = (li * n_oc + g * GRP) * NJ
                    dst = self.modT[:, o0:o0 + GRP * NJ].rearrange("p (o j) -> p o j", j=NJ)
                    srcp = bank[:, 0:GRP * NJ].rearrange("p (o j) -> p o j", j=NJ)
                    bia = mb[:, li * n_oc + g * GRP: li * n_oc + (g + 1) * GRP].unsqueeze(2).to_broadcast([128, GRP, NJ])
                    P.tt("dve", dst, srcp, bia, ALU.add, [bb, b_mb], [self.b_const])
            mv = self.modT[:].rearrange("p (l s c) -> p l s c", l=L, s=9)
            for s in (1, 4, 7):
                P.ts("dve", mv[:, :, s, :], mv[:, :, s, :], 1.0, None, ALU.add, None, [self.b_const], [self.b_const])
            for s in (2, 8):
                P.ts("dve", mv[:, :, s, :], mv[:, :, s, :], 0.5, None, ALU.mult, None, [self.b_const], [self.b_const])
            self.mixer_prologue(st)
            P.barrier()
            P.flush()

    def ffn_phase(self, b, li, s, cur, dst, tiles, final_out):
        cfg = self.cfg
        P = self.P
        D, DC, FC, TC = cfg.D, cfg.DC, cfg.FC, cfg.TC
        f = 0 if s == 0 else 1
        NT = cfg.NT
        with contextlib.ExitStack() as st:
            xt = [self.sb(st, "xt%d" % i, [128, DC, NT], F32) for i in range(2)]
            hT = self.sb(st, "hT", [128, DC, NT], BF16)
            gT = self.sb(st, "gT", [128, FC, NT], BF16)
            w13s = [self.sb(st, "w13s%d" % i, [128, DC, 256], BF16) for i in range(3)]
            w2s = [self.sb(st, "w2s%d" % i, [128, FC, 128], BF16) for i in range(2)]
            sa = [self.sb(st, "sa%d" % i, [128, NT], F32) for i in range(2)]
            b_xt, b_w13, b_w2, b_sa = bufs(2), bufs(3), bufs(2), bufs(2)
            b_h = Buf()
            b_g = bufs(FC)
            b_xc = [bufs(DC) for _ in range(2)]
            pn = PN(self, st, NT)
            PA, PB, PO, S1, S2 = (0, 1), (2, 3), (4, 5), 6, 7
            ps, psb = self.ps, self.psb
            w13v = self.w13b[li * 2 + f].rearrange("(r p) (k n) -> r p k n", p=128, n=256)
            w2v = self.w2b[li * 2 + f].rearrange("(r p) (k n) -> r p k n", p=128, n=128)
            r13 = 0
            r2 = 0
            wi13 = 0
            wi2 = 0
            for ti, (t0, nt) in enumerate(tiles):
                col = 0 if t0 >= TC else 1
                X = xt[ti % 2]
                bX = b_xc[ti % 2]
                P.dma("sp", X[:, :, :nt], self.src_ap(b, cur, t0, nt), "xl%d" % (ti % 2), writes=bX)
                for c in range(DC):
                    P.ts("pool", hT[:, c, :nt], X[:, c, :nt], self.mod_col(li, 3 * s + 1, c, col),
                         self.mod_col(li, 3 * s + 0, c, col), ALU.mult, ALU.add, [bX[c], self.b_const], [b_h])
                for j in range(FC):
                    W = w13s[wi13 % 3]
                    bW = b_w13[wi13 % 3]
                    P.dma("sp", W[:], w13v[r13 + j], "w13_%d" % (wi13 % 3), writes=[bW])
                    wi13 += 1
                    pa, pb = ps[PA[j % 2]], ps[PB[j % 2]]
                    ba, bb = psb[PA[j % 2]], psb[PB[j % 2]]
                    for kc in range(DC):
                        P.mm(pa[:, :nt], W[:, kc, 0:128], hT[:, kc, :nt], kc == 0, kc == DC - 1, [bW, b_h], [ba])
                    for kc in range(DC):
                        P.mm(pb[:, :nt], W[:, kc, 128:256], hT[:, kc, :nt], kc == 0, kc == DC - 1, [bW, b_h], [bb])
                    S = sa[j % 2]
                    P.act(S[:, :nt], pa[:, :nt], AF.Silu, [ba], [b_sa[j % 2]])
                    P.tt("dve", gT[:, j, :nt], S[:, :nt], pb[:, :nt], ALU.mult, [b_sa[j % 2], bb], [b_g[j]])
                pn.begin(li, s, X, bX, nt, col)
                for m in range(DC):
                    W = w2s[wi2 % 2]
                    bW = b_w2[wi2 % 2]
                    P.dma("sp", W[:], w2v[r2 + m], "w2_%d" % (wi2 % 2), writes=[bW])
                    wi2 += 1
                    po, bo = ps[PO[m % 2]], psb[PO[m % 2]]
                    for kc in range(FC):
                        P.mm(po[:, :nt], W[:, kc, :], gT[:, kc, :nt], kc == 0, kc == FC - 1, [bW, b_g[kc]], [bo])
                    pn.add(m, po[:, :nt], [bo], self.mod_col(li, 3 * s + 2, m, col))
                pn.finish()
                if final_out is not None:
                    oap = final_out.rearrange("(b c p) t -> b p c t", p=128, c=DC)[b][:, :, t0 - TC:t0 - TC + nt]
                else:
                    oap = dst.rearrange("(c p) t -> p c t", p=128)[:, :, t0:t0 + nt]
                P.dma("sp", oap, X[:, :, :nt], "xst%d" % (ti % 2), reads=bX)
            P.barrier()
            P.flush()

    def slot_of(self, li):
        return li // 4

    def has(self, kind):
        return kind in self.cfg.kinds

    def mixer_inputs(self):
        cfg = self.cfg
        D, DC, T, TC = cfg.D, cfg.DC, cfg.T, cfg.TC
        if self.has(1):
            self.pool_wB = self.inp("pool_wB", [128, D * D // 4 // 128])
            self.pool_scT = self.inp("pool_scT", [128, DC])
            self.pool_inv_lat = self.inp("pool_inv_lat", [128, 4 * GRID_W])
            self.pool_inv_ctx = self.inp("pool_inv_ctx", [128, 4 * TC])
        if self.has(0) or self.has(2):
            self.c_ident = self.inp("c_ident", [128, 128])
            self.c_cmask, self.c_rmask, self.c_reset = {}, {}, {}
            for CL in ((32,) if self.has(0) else ()) + ((128,) if self.has(2) else ()):
                self.c_cmask[CL] = self.inp("c_cmask%d" % CL, [128, 128])
                self.c_rmask[CL] = self.inp("c_rmask%d" % CL, [128, 128 // CL])
                self.c_reset[CL] = self.inp("c_reset%d" % CL, [128, 128])
        if self.has(0):
            self.hg_win = self.inp("hg_win", [5 * DC * 128, DC * 128])
            self.hg_wo = self.inp("hg_wo", [DC * 128, DC * 128])
            self.hg_lbT = self.inp("hg_lbT", [128, 2 * 4 * DC])
            self.hg_ngT = self.inp("hg_ngT", [128, DC])
        if self.has(2):
            H = cfg.ml_heads
            self.ml_win = self.inp("ml_win", [4 * DC * 128, DC * 128])
            self.ml_wo = self.inp("ml_wo", [DC * 128, DC * 128])
            self.ml_wg = self.inp("ml_wg", [128, DC * 4 * H])
            self.ml_gcol = self.inp("ml_gcol", [4 * H, 3])
            self.ml_sel = self.inp("ml_sel", [4 * H, 4 * H * 128])
            self.ml_ngT = self.inp("ml_ngT", [128, DC])

        if self.has(3):
            NST = 2 * DC * 4
            self.s5_lam = self.inp("s5_lam", [128, 3 * NST])
            self.s5_Bb = [self.inp("s5_Bb%d" % i, [2 * DC * 128, 4 * 128]) for i in range(2)]
            self.s5_Cb = [self.inp("s5_Cb%d" % i, [2 * DC * 128, 4 * 128]) for i in range(2)]
            self.s5_dT = self.inp("s5_dT", [128, DC])
            self.s5_iota = self.inp("s5_iota", [128, 512])
            self.s5_wglu = self.inp("s5_wglu", [2 * DC * 128, DC * 128])

    def mixer_scratch(self):
        cfg = self.cfg
        D, DC, TT = cfg.D, cfg.DC, cfg.TT
        H = cfg.ml_heads
        if self.has(3):
            self.s5_wglub = self.scratch("s5_wglub", [2 * DC * 128, DC * 128], BF16)
            self.sgy = self.scratch("sgy", [DC, 128, TT], BF16)
        if self.has(2):
            self.ml_winb = self.scratch("ml_winb", [4 * DC * 128, DC * 128], BF16)
            self.ml_wob = self.scratch("ml_wob", [DC * 128, DC * 128], BF16)
            self.mq = self.scratch("mq", [DC, 128, TT], BF16)
            self.mv = self.scratch("mv", [DC, 128, TT], BF16)
            self.mog = self.scratch("mog", [DC, 128, TT], BF16)
            self.mk = [self.scratch("mk%d" % d, [DC, 128, TT], BF16) for d in range(2)]
            self.mlf = [self.scratch("mlf%d" % d, [H, 128, TT], F32) for d in range(2)]
            self.mo = [self.scratch("mo%d" % d, [DC, 128, TT], F32) for d in range(2)]
        if self.has(0):
            self.hg_winb = self.scratch("hg_winb", [5 * DC * 128, DC * 128], BF16)
            self.hg_wob = self.scratch("hg_wob", [DC * 128, DC * 128], BF16)
            self.hq = self.scratch("hq", [DC, 128, TT], BF16)
            self.hv = self.scratch("hv", [DC, 128, TT], BF16)
            self.hgt = self.scratch("hgt", [DC, 128, TT], BF16)
            self.hk = [self.scratch("hk%d" % d, [DC, 128, TT], BF16) for d in range(2)]
            self.hlf = [self.scratch("hlf%d" % d, [DC, 128, TT], F32) for d in range(2)]
            self.ho = [self.scratch("ho%d" % d, [DC, 128, TT], F32) for d in range(2)]

    def mixer_persistent(self, es):
        pass

    def mixer_casts(self):
        P = self.P
        DC = self.cfg.DC
        if self.has(0):
            for r in range(5 * DC):
                P.dma("pool", self.hg_winb[r * 128:(r + 1) * 128, :], self.hg_win[r * 128:(r + 1) * 128, :], "cast%d" % (r % 4))
            for r in range(DC):
                P.dma("pool", self.hg_wob[r * 128:(r + 1) * 128, :], self.hg_wo[r * 128:(r + 1) * 128, :], "cast%d" % (r % 4))
        if self.has(3):
            for r in range(2 * DC):
                P.dma("pool", self.s5_wglub[r * 128:(r + 1) * 128, :], self.s5_wglu[r * 128:(r + 1) * 128, :], "cast%d" % (r % 4))
        if self.has(2):
            for r in range(4 * DC):
                P.dma("pool", self.ml_winb[r * 128:(r + 1) * 128, :], self.ml_win[r * 128:(r + 1) * 128, :], "cast%d" % (r % 4))
            for r in range(DC):
                P.dma("pool", self.ml_wob[r * 128:(r + 1) * 128, :], self.ml_wo[r * 128:(r + 1) * 128, :], "cast%d" % (r % 4))

    def mixer_prologue(self, st):
        pass

    def ml_proj_phase(self, b, li, cur):
        cfg, P = self.cfg, self.P
        D, DC, TC, NT, H = cfg.D, cfg.DC, cfg.TC, cfg.NT, cfg.ml_heads
        G4 = 4 * H
        CPH = DC // H
        s = 1
        with contextlib.ExitStack() as st:
            xt = [self.sb(st, "xt%d" % i, [128, DC, NT], F32) for i in range(2)]
            hT = self.sb(st, "hT", [128, DC, NT], BF16)
            ws = [self.sb(st, "mws%d" % i, [128, DC, 128], BF16) for i in range(3)]
            stg = [self.sb(st, "mstg%d" % i, [128, DC, NT], BF16) for i in range(2)]
            ksc = self.sb(st, "ksc", [128, 2, H, NT], F32)
            lfs = [self.sb(st, "lfs%d" % i, [128, NT], F32) for i in range(2)]
            wgf = self.sb(st, "wgf", [128, DC, G4], F32)
            wg = self.sb(st, "wg", [128, DC, G4], BF16)
            gcol = self.sb(st, "gcol", [G4, 3], F32)
            sel = self.sb(st, "sel", [G4, G4 * 128], F32)
            onec = self.sb(st, "onec", [128, 1], F32)
            xg = self.sb(st, "xg", [G4, NT], F32)
            eg = self.sb(st, "eg", [G4, NT], F32)
            g2 = self.sb(st, "g2", [G4, NT], F32)
            b_x = [bufs(DC) for _ in range(2)]
            b_h, b_c, b_xg, b_eg, b_g2, b_ksc = Buf(), Buf(), Buf(), Buf(), Buf(), Buf()
            b_ws, b_stg, b_lfs = bufs(3), bufs(2), bufs(2)
            P.dma("sp", wgf[:].rearrange("p k g -> p (k g)"), self.ml_wg[:, :], "c0", writes=[b_c])
            P.dma("sp", gcol[:], self.ml_gcol[:, :], "c1", writes=[b_c])
            P.dma("sp", sel[:], self.ml_sel[:, :], "c2", writes=[b_c])
            P.memset("dve", onec[:], 1.0, [b_c])
            P.cp("dve", wg[:], wgf[:], [b_c], [b_c])
            wv = self.ml_winb.rearrange("(r p) (k n) -> r p k n", p=128, n=128)
            wi = 0
            si = 0
            for ti, (t0, nt) in enumerate(cfg.tiles):
                col = 0 if t0 >= TC else 1
                X, bX = xt[ti % 2], b_x[ti % 2]
                P.dma("sp", X[:, :, :nt], self.src_ap(b, cur, t0, nt), "xl%d" % (ti % 2), writes=bX)
                for c in range(DC):
                    P.ts("pool", hT[:, c, :nt], X[:, c, :nt], self.mod_col(li, 3 * s + 1, c, col),
                         self.mod_col(li, 3 * s + 0, c, col), ALU.mult, ALU.add, [bX[c], self.b_const], [b_h])
                pg, bg = self.ps[7], self.psb[7]
                for kc in range(DC):
                    P.mm(pg[0:G4, :nt], wg[:, kc, :], hT[:, kc, :nt], kc == 0, kc == DC - 1, [b_c, b_h], [bg])
                P.ts("dve", xg[:, :nt], pg[0:G4, :nt], gcol[:, 0:1], None, ALU.add, None, [bg, b_c], [b_xg])
                P.act(eg[:, :nt], xg[:, :nt], AF.Exp, [b_xg], [b_eg], scale=-1.0)
                P.act(eg[:, :nt], eg[:, :nt], AF.Ln, [b_eg, b_c], [b_eg], bias=onec[0:G4, 0:1])
                P.ts("dve", g2[:, :nt], xg[:, :nt], gcol[:, 1:2], None, ALU.mult, None, [b_xg, b_c], [b_g2])
                P.stt("dve", g2[:, :nt], eg[:, :nt], gcol[:, 2:3], g2[:, :nt], ALU.mult, ALU.add, [b_eg, b_g2, b_c], [b_g2])
                for d in range(2):
                    for hd in range(H):
                        r = (2 * d) * H + hd
                        pp, bp = self.ps[(2 * hd) % 4], self.psb[(2 * hd) % 4]
                        P.mm(pp[:, :nt], sel[:, r * 128:(r + 1) * 128], g2[:, :nt], True, True, [b_c, b_g2], [bp])
                        P.act(ksc[:, d, hd, :nt], pp[:, :nt], AF.Exp, [bp], [b_ksc])
                        r = (2 * d + 1) * H + hd
                        pp, bp = self.ps[(2 * hd + 1) % 4], self.psb[(2 * hd + 1) % 4]
                        P.mm(pp[:, :nt], sel[:, r * 128:(r + 1) * 128], g2[:, :nt], True, True, [b_c, b_g2], [bp])
                        L_, bL_ = lfs[si % 2], b_lfs[si % 2]
                        P.cp("dve", L_[:, :nt], pp[:, :nt], [bp], [bL_])
                        P.dma("sp", self.mlf[d][hd, :, t0:t0 + nt], L_[:, :nt], "mlf%d" % (si % 2), reads=[bL_])
                        si += 1
                P.ts("dve", ksc[:, :, :, :nt], ksc[:, :, :, :nt], float((D // H) ** -0.5), None, ALU.mult, None, [b_ksc], [b_ksc])
                outs = [self.mq, None, self.mv, self.mog]
                for sec in range(4):
                    if sec == 1:
                        S0, S1 = stg[0], stg[1]
                    else:
                        S0 = stg[sec % 2]
                    for c in range(DC):
                        oc = sec * DC + c
                        W, bW = ws[wi % 3], b_ws[wi % 3]
                        P.dma("sp", W[:], wv[oc], "mw%d" % (wi % 3), writes=[bW])
                        wi += 1
                        pp, bp = self.ps[4 + oc % 3], self.psb[4 + oc % 3]
                        for kc in range(DC):
                            P.mm(pp[:, :nt], W[:, kc, :], hT[:, kc, :nt], kc == 0, kc == DC - 1, [bW, b_h], [bp])
                        if sec == 0 or sec == 2:
                            P.cp("dve" if c % 2 else "act", S0[:, c, :nt], pp[:, :nt], [bp], [b_stg[sec % 2]]) if False else \
                                P.act(S0[:, c, :nt], pp[:, :nt], AF.Identity, [bp], [b_stg[sec % 2]])
                        elif sec == 3:
                            P.act(S0[:, c, :nt], pp[:, :nt], AF.Sigmoid, [bp], [b_stg[sec % 2]])
                        else:
                            hd = c // CPH
                            P.tt("dve", stg[0][:, c, :nt], pp[:, :nt], ksc[:, 0, hd, :nt], ALU.mult, [bp, b_ksc], [b_stg[0]])
                            P.tt("dve", stg[1][:, c, :nt], pp[:, :nt], ksc[:, 1, hd, :nt], ALU.mult, [bp, b_ksc], [b_stg[1]])
                    if sec == 1:
                        for d in range(2):
                            P.dma("sp", self.mk[d].rearrange("c p t -> p c t")[:, :, t0:t0 + nt], stg[d][:, :, :nt],
                                  "ms%d" % d, reads=[b_stg[d]])
                    else:
                        P.dma("sp", outs[sec].rearrange("c p t -> p c t")[:, :, t0:t0 + nt], S0[:, :, :nt],
                              "ms%d" % (sec % 2), reads=[b_stg[sec % 2]])
            P.barrier()
            P.flush()

    def mixer_phase(self, b, li, kind, cur, dst, ctx_out):
        if kind == 1:
            return self.pool_phase(b, li, cur, dst, ctx_out)
        if kind == 0:
            self.hg_proj_phase(b, li, cur)
            gla_phase(self, dict(nh=self.cfg.DC, nk=1, nv=1, ones=0, CL=32, nlf=1, q=self.hq, k=self.hk, lf=self.hlf,
                                 v=self.hv, out=self.ho, dbuf=(os.environ.get("E1") is None)))
            return self.hg_readout_phase(b, li, cur, dst, ctx_out)
        if kind == 2:
            H = self.cfg.ml_heads
            self.ml_proj_phase(b, li, cur)
            if getattr(self.cfg, "dbg", None) == "proj":
                raise StopBuild()
            gla_phase(self, dict(nh=H, nk=2, nv=2, ones=int(os.environ.get("E2", "1")), CL=128, nlf=1, q=self.mq, k=self.mk, lf=self.mlf,
                                 v=self.mv, out=self.mo, dbuf=False))
            if getattr(self.cfg, "dbg", None) == "scan":
                raise StopBuild()
            return self.readout_phase(b, li, cur, dst, ctx_out, self.mo, self.mog, self.ml_ngT, self.ml_wob, self.cfg.DC // H)
        if kind == 3:
            self.s5_scan_phase(b, li, cur)
            return self.s5_glu_phase(b, li, cur, dst, ctx_out)
        raise NotImplementedError

    def s5_scan_phase(self, b, li, cur):
        cfg, P = self.cfg, self.P
        D, DC, TC, TT, T = cfg.D, cfg.DC, cfg.TC, cfg.TT, cfg.T
        NT = 512
        s = 1
        PI = float(np.pi)
        MAGIC = 12582912.0
        tiles = [(0, TC)] + [(TC + i * NT, NT) for i in range(T // NT)]
        segs = [(0, TC), (TC, TT)]
        NCH = 4
        with contextlib.ExitStack() as st:
            u32 = self.sb(st, "u32", [128, TT], F32)
            ub = self.sb(st, "ub", [128, TT], BF16)
            yacc = self.sb(st, "yacc", [128, TT], F32)
            gy = self.sb(st, "gyo", [128, TT], BF16)
            iota = self.sb(st, "iota", [128, 512], F32)
            lam = self.sb(st, "lam", [128, 3, 2, DC, 4], F32)
            dsk = self.sb(st, "dsk", [128, DC], F32)
            tab = [[self.sb(st, "tab%d_%d" % (j, k), [128, 512], F32) for k in range(4)] for j in range(4)]
            Bf = [self.sb(st, "Bf%d" % i, [128, 4, 128], F32) for i in range(2)]
            Cf = [self.sb(st, "Cf%d" % i, [128, 4, 128], F32) for i in range(2)]
            Bb = [self.sb(st, "Bb%d" % i, [128, 4, 128], BF16) for i in range(2)]
            Cb = [self.sb(st, "Cb%d" % i, [128, 4, 128], BF16) for i in range(2)]
            lp = self.sb(st, "lp", [128, 4, 16], F32)
            x0 = self.sb(st, "x0", [128, 4, 2], F32)
            tmp = [[self.sb(st, "s5t%d_%d" % (j, k), [128, 512], F32) for k in range(6)] for j in range(NCH)]
            xb = [[self.sb(st, "s5x%d_%d" % (j, k), [128, 512], BF16) for k in range(2)] for j in range(NCH)]
            b_u, b_ub, b_y, b_gy, b_c, b_lam = Buf(), Buf(), Buf(), Buf(), Buf(), Buf()
            b_tab = bufs(4)
            b_B, b_C, b_lp = Buf(), Buf(), bufs(4)
            b_x0 = bufs(4)
            b_tmp = [bufs(6) for _ in range(NCH)]
            b_xb = [bufs(2) for _ in range(NCH)]
            P.dma("sp", iota[:], self.s5_iota[:, :], "c0", writes=[b_c])
            P.dma("sp", lam[:].rearrange("p a d c j -> p (a d c j)"), self.s5_lam[:, :], "c1", writes=[b_lam])
            P.dma("sp", dsk[:], self.s5_dT[:, :], "c2", writes=[b_c])
            for fc in range(DC):
                if cur is not None:
                    P.dma("sp", u32[:], cur[fc * 128:(fc + 1) * 128, :], "s5u", writes=[b_u])
                else:
                    P.dma("sp", u32[:, 0:TC], self.ctxT.rearrange("(b c p) t -> b c p t", p=128, c=DC)[b][fc], "s5u", writes=[b_u])
                    P.dma("sp", u32[:, TC:TT], self.xT.rearrange("(b c p) t -> b c p t", p=128, c=DC)[b][fc], "s5u", writes=[b_u])
                for (a0, a1), col in zip(segs, (1, 0)):
                    P.ts("dve", u32[:, a0:a1], u32[:, a0:a1], self.mod_col(li, 4, fc, col), self.mod_col(li, 3, fc, col),
                         ALU.mult, ALU.add, [b_u, self.b_const], [b_u])
                for d in range(2):
                    for (a0, a1) in segs:
                        src = u32[:, a0:a1]
                        if d == 1:
                            src = src[:, ::-1]
                        P.cp("pool", ub[:, a0:a1], src, [b_u], [b_ub])
                    r0 = (d * DC + fc) * 128
                    for i in range(2):
                        P.dma("sp", Bf[i][:].rearrange("p j l -> p (j l)"), self.s5_Bb[i][r0:r0 + 128, :], "s5b%d" % i, writes=[b_B])
                        P.dma("sp", Cf[i][:].rearrange("p j l -> p (j l)"), self.s5_Cb[i][r0:r0 + 128, :], "s5c%d" % i, writes=[b_C])
                    P.cp("dve", Bb[0][:], Bf[0][:], [b_B], [b_B])
                    P.cp("dve", Bb[1][:], Bf[1][:], [b_B], [b_B])
                    P.cp("pool", Cb[0][:], Cf[0][:], [b_C], [b_C])
                    P.act(Cb[1][:], Cf[1][:], AF.Identity, [b_C], [b_C], scale=-1.0)
                    for j in range(4):
                        L_ = lp[:, j, :]
                        bl = b_lp[j]
                        lr = lam[:, 0, d, fc, j:j + 1]
                        lim = lam[:, 1, d, fc, j:j + 1]
                        ldt = lam[:, 2, d, fc, j:j + 1]
                        dt, th, rr = L_[:, 0:1], L_[:, 1:2], L_[:, 2:3]
                        P.act(dt, ldt, AF.Exp, [b_lam], [bl])
                        P.tt("dve", th, lim, dt, ALU.mult, [b_lam, bl], [bl])
                        P.tt("dve", rr, lr, dt, ALU.mult, [b_lam, bl], [bl])
                        P.act(rr, rr, AF.Exp, [bl], [bl])
                        Rc, Rs, Tr, Ti = tab[j]
                        bt = b_tab[j]
                        for (dstt, off) in ((Rs, 0.0), (Rc, PI / 2)):
                            P.ts("dve", Tr[:], iota[:], th, off, ALU.mult, ALU.add, [b_c, bl], [bt])
                            P.ts("dve", Ti[:], Tr[:], 1.0 / (2 * PI), MAGIC, ALU.mult, ALU.add, [bt], [bt])
                            P.ts("dve", Ti[:], Ti[:], -MAGIC, None, ALU.add, None, [bt], [bt])
                            P.stt("dve", Tr[:], Ti[:], -2 * PI, Tr[:], ALU.mult, ALU.add, [bt], [bt])
                            P.ts("dve", Tr[:], Tr[:], -PI, PI, ALU.max, ALU.min, [bt], [bt])
                            P.act(dstt[:], Tr[:], AF.Sin, [bt], [bt])
                        nr, ni, den, fr, fi, t1, t2 = (L_[:, k:k + 1] for k in range(3, 10))
                        P.tt("dve", nr, rr, Rc[:, 0:1], ALU.mult, [bl, bt], [bl])
                        P.ts("dve", nr, nr, -1.0, None, ALU.add, None, [bl], [bl])
                        P.tt("dve", ni, rr, Rs[:, 0:1], ALU.mult, [bl, bt], [bl])
                        P.tt("dve", den, lr, lr, ALU.mult, [b_lam], [bl])
                        P.tt("dve", t1, lim, lim, ALU.mult, [b_lam], [bl])
                        P.tt("dve", den, den, t1, ALU.add, [bl], [bl])
                        P.op("dve", lambda e, den=den: e.reciprocal(out=den, in_=den), [bl], [bl])
                        P.tt("dve", t1, nr, lr, ALU.mult, [bl, b_lam], [bl])
                        P.tt("dve", t2, ni, lim, ALU.mult, [bl, b_lam], [bl])
                        P.tt("dve", fr, t1, t2, ALU.add, [bl], [bl])
                        P.tt("dve", fr, fr, den, ALU.mult, [bl], [bl])
                        P.tt("dve", t1, ni, lr, ALU.mult, [bl, b_lam], [bl])
                        P.tt("dve", t2, nr, lim, ALU.mult, [bl, b_lam], [bl])
                        P.tt("dve", fi, t1, t2, ALU.subtract, [bl], [bl])
                        P.tt("dve", fi, fi, den, ALU.mult, [bl], [bl])
                        P.ts("dve", Tr[:], Rc[:], fr, None, ALU.mult, None, [bt, bl], [bt])
                        P.stt("dve", Tr[:], Rs[:], fi, Tr[:], ALU.mult, ALU.add, [bt, bl], [bt])
                        P.ts("dve", Ti[:], Rs[:], fr, None, ALU.mult, None, [bt, bl], [bt])
                        P.stt("dve", Ti[:], Rc[:], fi, Ti[:], ALU.mult, ALU.subtract, [bt, bl], [bt])
                        P.memset("dve", x0[:, j, :], 0.0, [b_x0[j]])
                    for (t0, nt) in tiles:
                        pY, bY = self.ps[7], self.psb[7]
                        for j in range(4):
                            Rc, Rs, Tr, Ti = tab[j]
                            bt, bl = b_tab[j], b_lp[j]
                            rr = lp[:, j, 2:3]
                            pr, bpr = self.ps[j % 3 * 2], self.psb[j % 3 * 2]
                            pi_, bpi = self.ps[j % 3 * 2 + 1], self.psb[j % 3 * 2 + 1]
                            P.mm(pr[:, :nt], Bb[0][:, j, :], ub[:, t0:t0 + nt], True, True, [b_B, b_ub], [bpr])
                            P.mm(pi_[:, :nt], Bb[1][:, j, :], ub[:, t0:t0 + nt], True, True, [b_B, b_ub], [bpi])
                            tm, btm = tmp[j], b_tmp[j]
                            P.tt("dve", tm[0][:, :nt], pr[:, :nt], Tr[:, :nt], ALU.mult, [bpr, bt], [btm[0]])
                            P.tt("dve", tm[1][:, :nt], pi_[:, :nt], Ti[:, :nt], ALU.mult, [bpi, bt], [btm[1]])
                            P.tt("dve", tm[2][:, :nt], pi_[:, :nt], Tr[:, :nt], ALU.mult, [bpi, bt], [btm[2]])
                            P.tt("dve", tm[3][:, :nt], pr[:, :nt], Ti[:, :nt], ALU.mult, [bpr, bt], [btm[3]])
                            P.tt("pool", tm[0][:, :nt], tm[0][:, :nt], tm[1][:, :nt], ALU.subtract, [btm[0], btm[1]], [btm[0]])
                            P.tt("pool", tm[2][:, :nt], tm[2][:, :nt], tm[3][:, :nt], ALU.add, [btm[2], btm[3]], [btm[2]])
                            rb = rr.to_broadcast([128, nt])
                            P.scan(tm[4][:, :nt], rb, tm[0][:, :nt], x0[:, j, 0:1], ALU.mult, ALU.add, [bl, btm[0], b_x0[j]], [btm[4]])
                            P.scan(tm[5][:, :nt], rb, tm[2][:, :nt], x0[:, j, 1:2], ALU.mult, ALU.add, [bl, btm[2], b_x0[j]], [btm[5]])
                            P.tt("pool", tm[0][:, :nt], tm[4][:, :nt], Rc[:, :nt], ALU.mult, [btm[4], bt], [btm[0]])
                            P.tt("pool", tm[1][:, :nt], tm[5][:, :nt], Rs[:, :nt], ALU.mult, [btm[5], bt], [btm[1]])
                            P.tt("dve", tm[2][:, :nt], tm[5][:, :nt], Rc[:, :nt], ALU.mult, [btm[5], bt], [btm[2]])
                            P.tt("dve", tm[3][:, :nt], tm[4][:, :nt], Rs[:, :nt], ALU.mult, [btm[4], bt], [btm[3]])
                            P.tt("pool", xb[j][0][:, :nt], tm[0][:, :nt], tm[1][:, :nt], ALU.subtract, [btm[0], btm[1]], [b_xb[j][0]])
                            P.tt("pool", xb[j][1][:, :nt], tm[2][:, :nt], tm[3][:, :nt], ALU.add, [btm[2], btm[3]], [b_xb[j][1]])
                            P.tt("dve", x0[:, j, 0:1], tm[0][:, nt - 1:nt], tm[1][:, nt - 1:nt], ALU.subtract, [btm[0], btm[1]], [b_x0[j]])
                            P.tt("dve", x0[:, j, 1:2], tm[2][:, nt - 1:nt], tm[3][:, nt - 1:nt], ALU.add, [btm[2], btm[3]], [b_x0[j]])
                            P.mm(pY[:, :nt], Cb[0][:, j, :], xb[j][0][:, :nt], j == 0, False, [b_C, b_xb[j][0]], [bY])
                            P.mm(pY[:, :nt], Cb[1][:, j, :], xb[j][1][:, :nt], False, j == 3, [b_C, b_xb[j][1]], [bY])
                        if d == 0:
                            P.cp("dve", yacc[:, t0:t0 + nt], pY[:, :nt], [bY], [b_y])
                        else:
                            a0, a1 = (0, TC) if t0 < TC else (TC, TT)
                            n0 = a0 + a1 - (t0 + nt)
                            dst = yacc[:, n0:n0 + nt][:, ::-1]
                            P.tt("dve", dst, dst, pY[:, :nt], ALU.add, [bY, b_y], [b_y])
                for (a0, a1) in [(0, TC)] + [(TC + i * 1024, TC + min(T, (i + 1) * 1024)) for i in range((T + 1023) // 1024)]:
                    a1 = min(a1, TT)
                    ya, ua = yacc[:, a0:a1], u32[:, a0:a1]
                    P.stt("dve", ya, ua, dsk[:, fc:fc + 1], ya, ALU.mult, ALU.add, [b_u, b_y, b_c], [b_y])
                    P.act(ua, ya, AF.Square, [b_y, b_u], [b_u])
                    P.ts("dve", ua, ua, 0.044715, 1.0, ALU.mult, ALU.add, [b_u], [b_u])
                    P.tt("dve", ua, ua, ya, ALU.mult, [b_u, b_y], [b_u])
                    P.act(ua, ua, AF.Sigmoid, [b_u], [b_u], scale=float(2.0 * np.sqrt(2.0 / np.pi)))
                    P.tt("dve", gy[:, a0:a1], ya, ua, ALU.mult, [b_y, b_u], [b_gy])
                P.dma("sp", self.sgy[fc, :, :], gy[:], "s5o", reads=[b_gy])
            P.barrier()
            P.flush()

    def s5_glu_phase(self, b, li, cur, dst, ctx_out):
        cfg, P = self.cfg, self.P
        D, DC, TC, NT = cfg.D, cfg.DC, cfg.TC, cfg.NT
        s = 1
        with contextlib.ExitStack() as st:
            xt = [self.sb(st, "xt%d" % i, [128, DC, NT], F32) for i in range(2)]
            gt = self.sb(st, "gyT", [128, DC, NT], BF16)
            ws = [self.sb(st, "gws%d" % i, [128, DC, 128], BF16) for i in range(4)]
            sg = [self.sb(st, "gsg%d" % i, [128, NT], F32) for i in range(2)]
            yv = [self.sb(st, "gyv%d" % i, [128, NT], F32) for i in range(2)]
            b_x = [bufs(DC) for _ in range(2)]
            b_gt = Buf()
            b_ws, b_sg, b_yv = bufs(4), bufs(2), bufs(2)
            pn = PN(self, st, NT)
            wv = self.s5_wglub.rearrange("(r p) (k n) -> r p k n", p=128, n=128)
            wi = 0
            tiles = cfg.tiles if ctx_out else cfg.tiles[1:]
            for ti, (t0, nt) in enumerate(tiles):
                col = 0 if t0 >= TC else 1
                X, bX = xt[ti % 2], b_x[ti % 2]
                P.dma("sp", X[:, :, :nt], self.src_ap(b, cur, t0, nt), "xl%d" % (ti % 2), writes=bX)
                P.dma("sp", gt[:, :, :nt], self.sgy.rearrange("c p t -> p c t")[:, :, t0:t0 + nt], "rg", writes=[b_gt])
                pn.begin(li, s, X, bX, nt, col)
                for m in range(DC):
                    pa, ba = self.ps[m % 2], self.psb[m % 2]
                    pg, bg = self.ps[2 + m % 2], self.psb[2 + m % 2]
                    for (oc, pp, bp) in ((m, pa, ba), (DC + m, pg, bg)):
                        W, bW = ws[wi % 4], b_ws[wi % 4]
                        P.dma("sp", W[:], wv[oc], "gw%d" % (wi % 4), writes=[bW])
                        wi += 1
                        for kc in range(DC):
                            P.mm(pp[:, :nt], W[:, kc, :], gt[:, kc, :nt], kc == 0, kc == DC - 1, [bW, b_gt], [bp])
                    P.act(sg[m % 2][:, :nt], pg[:, :nt], AF.Sigmoid, [bg], [b_sg[m % 2]])
                    P.tt("dve", yv[m % 2][:, :nt], pa[:, :nt], sg[m % 2][:, :nt], ALU.mult, [ba, b_sg[m % 2]], [b_yv[m % 2]])
                    pn.add(m, yv[m % 2][:, :nt], [b_yv[m % 2]], self.mod_col(li, 5, m, col))
                pn.finish()
                P.dma("sp", dst.rearrange("(c p) t -> p c t", p=128)[:, :, t0:t0 + nt], X[:, :, :nt],
                      "xst%d" % (ti % 2), reads=bX)
            P.barrier()
            P.flush()

    def hg_proj_phase(self, b, li, cur):
        cfg, P = self.cfg, self.P
        D, DC, TC, NT = cfg.D, cfg.DC, cfg.TC, cfg.NT
        s = 1
        with contextlib.ExitStack() as st:
            xt1 = self.sb(st, "xt", [128, DC, NT], F32)
            xt = [xt1, xt1]
            hT = self.sb(st, "hT", [128, DC, NT], BF16)
            ws = [self.sb(st, "hws%d" % i, [128, DC, 128], BF16) for i in range(3)]
            oq2 = [self.sb(st, "oq%d" % i, [128, DC, NT], BF16) for i in range(2)]
            oq = [oq2[i % 2] for i in range(5)]
            olf = [self.sb(st, "olf%d" % i, [128, DC, NT], F32) for i in range(2)]
            t1 = [self.sb(st, "ht1_%d" % i, [128, NT], F32) for i in range(2)]
            t2 = [self.sb(st, "ht2_%d" % i, [128, NT], F32) for i in range(2)]
            lg = self.sb(st, "lg", [128, 2, 4, DC], F32)
            oml = self.sb(st, "oml", [128, 2, DC], F32)
            den = self.sb(st, "lden", [128, 2, DC], F32)
            onec = self.sb(st, "onec", [128, 1], F32)
            b_x1 = bufs(DC)
            b_x = [b_x1, b_x1]
            b_h, b_lb = Buf(), Buf()
            b_ws, b_t1, b_t2 = bufs(3), bufs(2), bufs(2)
            b_oq2 = bufs(2)
            b_oq = [b_oq2[i % 2] for i in range(5)]
            b_olf = bufs(2)
            P.dma("sp", lg[:].rearrange("p d l c -> p (d l c)"), self.hg_lbT[:, :], "c0", writes=[b_lb])
            P.memset("dve", onec[:], 1.0, [b_lb])
            P.act(lg[:], lg[:], AF.Exp, [b_lb], [b_lb])
            P.cp("dve", den[:], lg[:, :, 0, :], [b_lb], [b_lb])
            for l in range(1, 4):
                P.tt("dve", den[:], den[:], lg[:, :, l, :], ALU.add, [b_lb], [b_lb])
            P.cp("dve", oml[:], lg[:, :, 0, :], [b_lb], [b_lb])
            for l in range(1, li + 1):
                P.tt("dve", oml[:], oml[:], lg[:, :, l, :], ALU.add, [b_lb], [b_lb])
            P.op("dve", lambda e: e.reciprocal(out=den[:], in_=den[:]), [b_lb], [b_lb])
            P.tt("dve", oml[:], oml[:], den[:], ALU.mult, [b_lb], [b_lb])
            P.ts("dve", oml[:], oml[:], -1.0, 1.0, ALU.mult, ALU.add, [b_lb], [b_lb])
            wv = self.hg_winb.rearrange("(r p) (k n) -> r p k n", p=128, n=128)
            wi = 0
            outs = [self.hq, self.hv, self.hgt, self.hk[0], self.hk[1]]
            for ti, (t0, nt) in enumerate(cfg.tiles):
                col = 0 if t0 >= TC else 1
                X, bX = xt[ti % 2], b_x[ti % 2]
                P.dma("sp", X[:, :, :nt], self.src_ap(b, cur, t0, nt), "xl%d" % (ti % 2), writes=bX)
                for c in range(DC):
                    P.ts("pool", hT[:, c, :nt], X[:, c, :nt], self.mod_col(li, 3 * s + 1, c, col),
                         self.mod_col(li, 3 * s + 0, c, col), ALU.mult, ALU.add, [bX[c], self.b_const], [b_h])
                for sec in range(5):
                    for c in range(DC):
                        oc = sec * DC + c
                        W, bW = ws[wi % 3], b_ws[wi % 3]
                        P.dma("sp", W[:], wv[oc], "hw%d" % (wi % 3), writes=[bW])
                        wi += 1
                        pp, bp = self.ps[oc % 4], self.psb[oc % 4]
                        for kc in range(DC):
                            P.mm(pp[:, :nt], W[:, kc, :], hT[:, kc, :nt], kc == 0, kc == DC - 1, [bW, b_h], [bp])
                        if sec == 0 or sec == 2:
                            P.act(oq[sec][:, c, :nt], pp[:, :nt], AF.Silu, [bp], [b_oq[sec]])
                        elif sec == 1:
                            P.cp("dve", oq[1][:, c, :nt], pp[:, :nt], [bp], [b_oq[1]])
                        else:
                            d = sec - 3
                            T1, T2 = t1[c % 2], t2[c % 2]
                            P.act(T1[:, :nt], pp[:, :nt], AF.Sigmoid, [bp], [b_t1[c % 2]], scale=-1.0)
                            P.ts("dve", T2[:, :nt], T1[:, :nt], oml[:, d, c:c + 1], None, ALU.mult, None,
                                 [b_t1[c % 2], b_lb], [b_t2[c % 2]])
                            P.cp("pool", oq[sec][:, c, :nt], T2[:, :nt], [b_t2[c % 2]], [b_oq[sec]])
                            P.act(olf[d][:, c, :nt], T2[:, :nt], AF.Ln, [b_t2[c % 2], b_lb], [b_olf[d]], scale=-1.0, bias=onec[:, 0:1])
                    dstT = outs[sec].rearrange("c p t -> p c t")[:, :, t0:t0 + nt]
                    P.dma("sp", dstT, oq[sec][:, :, :nt], "hs%d" % sec, reads=[b_oq[sec]])
                    if sec >= 3:
                        P.dma("sp", self.hlf[sec - 3].rearrange("c p t -> p c t")[:, :, t0:t0 + nt], olf[sec - 3][:, :, :nt],
                              "hl%d" % (sec - 3), reads=[b_olf[sec - 3]])
            P.barrier()
            P.flush()

    def hg_readout_phase(self, b, li, cur, dst, ctx_out):
        self.readout_phase(b, li, cur, dst, ctx_out, self.ho, self.hgt, self.hg_ngT, self.hg_wob, 1)

    def readout_phase(self, b, li, cur, dst, ctx_out, o_dirs, gate_s, ng_in, wob, hc):
        cfg, P = self.cfg, self.P
        D, DC, TC, NT = cfg.D, cfg.DC, cfg.TC, cfg.NT
        s = 1
        with contextlib.ExitStack() as st:
            xt1 = self.sb(st, "xt", [128, DC, NT], F32)
            xt = [xt1, xt1]
            o0 = self.sb(st, "ro0", [128, DC, NT], F32)
            o1 = self.sb(st, "ro1", [128, DC, NT], F32)
            gt = self.sb(st, "rgt", [128, DC, NT], BF16)
            yT = self.sb(st, "ryT", [128, DC, NT], BF16)
            sq = [self.sb(st, "rsq%d" % i, [128, hc, NT], BF16) for i in range(2)]
            rs = [self.sb(st, "rrs%d" % i, [128, NT], F32) for i in range(2)]
            ws = [self.sb(st, "rws%d" % i, [128, DC, 128], BF16) for i in range(3)]
            ng = self.sb(st, "rng", [128, DC], F32)
            b_x1 = bufs(DC)
            b_x = [b_x1, b_x1]
            b_o0, b_o1, b_gt = bufs(DC), bufs(DC), Buf()
            b_y = bufs(DC)
            b_sq, b_rs, b_ws = bufs(2), bufs(2), bufs(3)
            b_ng = Buf()
            pn = PN(self, st, NT)
            P.dma("sp", ng[:], ng_in[:, :], "c0", writes=[b_ng])
            wv = wob.rearrange("(r p) (k n) -> r p k n", p=128, n=128)
            wi = 0
            tiles = cfg.tiles if ctx_out else cfg.tiles[1:]
            for ti, (t0, nt) in enumerate(tiles):
                col = 0 if t0 >= TC else 1
                X, bX = xt[ti % 2], b_x[ti % 2]
                P.dma("sp", X[:, :, :nt], self.src_ap(b, cur, t0, nt), "xl%d" % (ti % 2), writes=bX)
                P.dma("sp", o0[:, :, :nt], o_dirs[0].rearrange("c p t -> p c t")[:, :, t0:t0 + nt], "ro0", writes=b_o0)
                P.dma("sp", o1[:, :, :nt], o_dirs[1].rearrange("c p t -> p c t")[:, :, t0:t0 + nt], "ro1", writes=b_o1)
                P.dma("sp", gt[:, :, :nt], gate_s.rearrange("c p t -> p c t")[:, :, t0:t0 + nt], "rg", writes=[b_gt])
                for hd in range(DC // hc):
                    r = hd % 2
                    for cc in range(hc):
                        c = hd * hc + cc
                        P.tt("dve", o0[:, c, :nt], o0[:, c, :nt], o1[:, c, :nt], ALU.add, [b_o0[c], b_o1[c]], [b_o0[c]])
                        P.act(sq[r][:, cc, :nt], o0[:, c, :nt], AF.Square, [b_o0[c]], [b_sq[r]])
                    pp, bp = self.ps[hd % 4], self.psb[hd % 4]
                    for cc in range(hc):
                        P.mm(pp[:, :nt], self.ones[:], sq[r][:, cc, :nt], cc == 0, cc == hc - 1, [b_sq[r], self.b_const], [bp])
                    P.ts("dve", rs[r][:, :nt], pp[:, :nt], 1.0 / (128 * hc), RMS_EPS, ALU.mult, ALU.add, [bp], [b_rs[r]])
                    P.act(rs[r][:, :nt], rs[r][:, :nt], AF.Sqrt, [b_rs[r]], [b_rs[r]])
                    P.op("dve", lambda e, a=rs[r], nt=nt: e.reciprocal(out=a[:, :nt], in_=a[:, :nt]), [b_rs[r]], [b_rs[r]])
                    for cc in range(hc):
                        c = hd * hc + cc
                        P.tt("pool", o0[:, c, :nt], o0[:, c, :nt], rs[r][:, :nt], ALU.mult, [b_o0[c], b_rs[r]], [b_o0[c]])
                        P.stt("dve", yT[:, c, :nt], o0[:, c, :nt], ng[:, c:c + 1], gt[:, c, :nt], ALU.mult, ALU.mult,
                              [b_o0[c], b_ng, b_gt], [b_y[c]])
                pn.begin(li, s, X, bX, nt, col)
                for m in range(DC):
                    W, bW = ws[wi % 3], b_ws[wi % 3]
                    P.dma("sp", W[:], wv[m], "rw%d" % (wi % 3), writes=[bW])
                    wi += 1
                    po, bo = self.ps[4 + m % 2], self.psb[4 + m % 2]
                    for kc in range(DC):
                        P.mm(po[:, :nt], W[:, kc, :], yT[:, kc, :nt], kc == 0, kc == DC - 1, [bW, b_y[kc]], [bo])
                    pn.add(m, po[:, :nt], [bo], self.mod_col(li, 5, m, col))
                pn.finish()
                P.dma("sp", dst.rearrange("(c p) t -> p c t", p=128)[:, :, t0:t0 + nt], X[:, :, :nt],
                      "xst%d" % (ti % 2), reads=bX)
            if not ctx_out:
                pass
            P.barrier()
            P.flush()

    def pool_phase(self, b, li, cur, dst, ctx_out):
        cfg = self.cfg
        P = self.P
        D, DC, TC, NT = cfg.D, cfg.DC, cfg.TC, cfg.NT
        NJ = 2
        CPG = DC // 4
        W = D // 4
        s = 1
        with contextlib.ExitStack() as st:
            xt = [self.sb(st, "xt%d" % i, [128, DC, NT], F32) for i in range(2)]
            PAD = 8
            HW = NT + 2 * PAD * (NT // GRID_W)
            hf = self.sb(st, "hf", [128, DC, HW], F32)
            pdT = self.sb(st, "pdT", [128, DC, NT], BF16)
            pw = self.sb(st, "pw", [128, 4, CPG, W], BF16)
            psc = self.sb(st, "psc", [128, DC], F32)
            sg = self.sb(st, "sg", [128, 2, DC], F32)
            invl = self.sb(st, "invl", [128, 4, GRID_W], F32)
            invc = self.sb(st, "invc", [128, 4, TC], F32)
            tmp = {e: [self.sb(st, "ptmp_%s%d" % (e, i), [128, HW], F32) for i in range(2)] for e in ("dve", "pool")}
            b_tmp = {e: bufs(2) for e in ("dve", "pool")}
            b_xc = [bufs(DC) for _ in range(2)]
            b_hf, b_pd = bufs(DC), bufs(DC)
            b_pw, b_psc, b_sg, b_inv = Buf(), Buf(), Buf(), Buf()
            pn = PN(self, st, NT)
            P.dma("sp", pw[:].rearrange("p g k o -> p (g k o)"), self.pool_wBb[:, :], "c0", writes=[b_pw])
            P.dma("sp", psc[:], self.pool_scT[:, :], "c1", writes=[b_psc])
            P.dma("sp", invl[:].rearrange("p g w -> p (g w)"), self.pool_inv_lat[:, :], "c2", writes=[b_inv])
            P.dma("sp", invc[:].rearrange("p g w -> p (g w)"), self.pool_inv_ctx[:, :], "c3", writes=[b_inv])
            for ci, col in enumerate((0, 1)):
                o0 = ((li * 9 + 5) * DC) * NJ
                gv = self.modT[:, o0:o0 + DC * NJ].rearrange("p (c j) -> p c j", j=NJ)[:, :, col]
                P.tt("dve", sg[:, ci, :], gv, psc[:], ALU.mult, [self.b_const, b_psc], [b_sg])
            tiles = cfg.tiles if ctx_out else cfg.tiles[1:]
            for ti, (t0, nt) in enumerate(tiles):
                lat = t0 >= TC
                col = 0 if lat else 1
                ci = 0 if lat else 1
                R = GRID_W if lat else TC
                inv = invl if lat else invc
                X = xt[ti % 2]
                bX = b_xc[ti % 2]
                P.dma("sp", X[:, :, :nt], self.src_ap(b, cur, t0, nt), "xl%d" % (ti % 2), writes=bX)
                RP = R + 2 * PAD
                rows = nt // R
                if ti < 2:
                    P.memset("dve", hf[:], 0.0, b_hf)
                for c in range(DC):
                    eng = "dve" if c % 2 == 0 else "pool"

                    def vp(ap):
                        return ap[:, :rows * RP].rearrange("p (r w) -> p r w", w=RP)

                    def v(ap):
                        return ap.rearrange("p (r w) -> p r w", w=R)
                    hp = vp(hf[:, c, :])
                    hin = hp[:, :, PAD:PAD + R]
                    P.ts(eng, hin, v(X[:, c, :nt]), self.mod_col(li, 3 * s + 1, c, col),
                         self.mod_col(li, 3 * s + 0, c, col), ALU.mult, ALU.add, [bX[c], self.b_const], [b_hf[c]])
                    wi = c // CPG
                    w = cfg.pool_windows[wi]
                    A, Bt = vp(tmp[eng][0]), vp(tmp[eng][1])
                    bA, bB = b_tmp[eng]
                    P.tt(eng, A[:, :, 1:RP], hp[:, :, 1:RP], hp[:, :, 0:RP - 1], ALU.add, [b_hf[c]], [bA])
                    srcv, bs, dstv, bd = A, bA, Bt, bB
                    d = 1
                    while 4 * d <= w:
                        P.tt(eng, dstv[:, :, d:RP - d], srcv[:, :, 0:RP - 2 * d], srcv[:, :, 2 * d:RP], ALU.add, [bs], [bd])
                        srcv, bs, dstv, bd = dstv, bd, srcv, bs
                        d *= 2
                    ib = inv[:, wi, :].unsqueeze(1).to_broadcast([128, rows, R])
                    P.tt(eng, srcv[:, :, PAD:PAD + R], srcv[:, :, PAD:PAD + R], ib, ALU.mult, [bs, b_inv], [bs])
                    P.tt(eng, v(pdT[:, c, :nt]), srcv[:, :, PAD:PAD + R], hin, ALU.subtract, [bs, b_hf[c]], [b_pd[c]])
                pn.begin(li, s, X, bX, nt, col)
                for m in range(DC):
                    g = m // CPG
                    po, bo = self.ps[4 + m % 2], self.psb[4 + m % 2]
                    for kk in range(CPG):
                        kc = g * CPG + kk
                        P.mm(po[:, :nt], pw[:, g, kk, (m % CPG) * 128:(m % CPG + 1) * 128], pdT[:, kc, :nt],
                             kk == 0, kk == CPG - 1, [b_pw, b_pd[kc]], [bo])
                    pn.add(m, po[:, :nt], [bo, b_sg], sg[:, ci, m:m + 1])
                pn.finish()
                P.dma("sp", dst.rearrange("(c p) t -> p c t", p=128)[:, :, t0:t0 + nt], X[:, :, :nt],
                      "xst%d" % (ti % 2), reads=bX)
            if not ctx_out:
                pass
            P.barrier()
            P.flush()


def gla_phase(B, spec):
    cfg, P = B.cfg, B.P
    TC, TT = cfg.TC, cfg.TT
    nh, nk, nv, ones, CL, nlf = spec["nh"], spec["nk"], spec["nv"], spec["ones"], spec["CL"], spec["nlf"]
    nvt = nv + ones
    NSUB = 128 // CL
    BL = 128
    nblk = TT // BL
    ncb = TC // BL
    NL = 2 if spec.get("dbuf", True) else 1
    order = [list(range(nblk)), list(range(ncb - 1, -1, -1)) + list(range(nblk - 1, ncb - 1, -1))]
    with contextlib.ExitStack() as st:
        ident = B.sb(st, "ident", [128, 128], BF16)
        identf = B.sb(st, "identf", [128, 128], F32)
        cmask = B.sb(st, "cmask", [128, 128], F32)
        rmask = B.sb(st, "rmask", [128, NSUB], F32)
        reset = B.sb(st, "reset", [128, 128], F32)
        b_cst = Buf()
        P.dma("sp", identf[:], B.c_ident[:, :], "c0", writes=[b_cst])
        P.dma("sp", cmask[:], B.c_cmask[CL][:, :], "c1", writes=[b_cst])
        P.dma("sp", rmask[:], B.c_rmask[CL][:, :], "c2", writes=[b_cst])
        P.dma("sp", reset[:], B.c_reset[CL][:, :], "c3", writes=[b_cst])
        P.cp("dve", ident[:], identf[:], [b_cst], [b_cst])
        D_ = {}
        R2 = 2
        for d in range(2):
            t = {}
            t["Lq"] = [B.sb(st, "Lq%d_%d" % (d, i), [128, nh * nk, BL], BF16) for i in range(NL)]
            t["Lk"] = [B.sb(st, "Lk%d_%d" % (d, i), [128, nh * nk, BL], BF16) for i in range(NL)]
            t["Ll"] = [B.sb(st, "Ll%d_%d" % (d, i), [128, nh * nlf, BL], F32) for i in range(NL)]
            t["Lv"] = [B.sb(st, "Lv%d_%d" % (d, i), [128, nh * nv, BL], BF16) for i in range(NL)]
            t["bL"] = [bufs(4) for _ in range(NL)]
            t["Oo"] = [B.sb(st, "Oo%d_%d" % (d, i), [128, nh * nv, BL], F32) for i in range(NL)]
            t["bOo"] = bufs(NL)
            shapes = {"b": ([128, nk, BL], F32), "eb": ([128, nk, BL], F32), "enb": ([128, nk, BL], F32),
                      "qe": ([128, nk, BL], BF16), "ke": ([128, nk, BL], BF16), "vb": ([128, nv, BL], BF16),
                      "attm": ([128, BL], BF16), "vtok": ([128, nvt * 128], BF16), "ktok": ([128, NSUB, nk * 128], BF16),
                      "ud": ([128, nvt * 128], F32), "rc": ([128, BL], F32)}
            for nm, (shp, dt) in shapes.items():
                t[nm] = [B.sb(st, "g%s%d_%d" % (nm, d, i), shp, dt) for i in range(R2)]
                t["B" + nm] = bufs(R2)
            t["S"] = B.sb(st, "gS%d" % d, [128, nh, nk, nvt * 128], F32)
            t["Sb"] = B.sb(st, "gSb%d" % d, [128, nh, NSUB, nk * nvt * 128], BF16)
            t["bS"] = bufs(nh)
            t["bSb"] = bufs(nh)
            P.memset("dve", t["S"][:], 0.0, t["bS"])
            P.memset("pool", t["Sb"][:], 0.0, t["bSb"])
            if ones:
                for i in range(R2):
                    P.memset("pool", t["vtok"][i][:, nv * 128:], 1.0, [t["Bvtok"][i]])
            t["pb"] = d * 4
            t["bAatt"] = t["bAkt"] = B.psb[d * 4]
            t["bBv"] = B.psb[d * 4 + 1]
            t["bO"] = [B.psb[d * 4 + 1], B.psb[d * 4 + 1]]
            t["bU"] = [B.psb[d * 4 + 2], B.psb[d * 4 + 3]]
            t["oi"] = 0
            D_[d] = t
        STG = int(os.environ.get("E4", "9"))
        for step in range(min(nblk, int(os.environ.get("E3", "100000")))):
            for d in range(2):
                t = D_[d]
                blk = order[d][step]
                t0 = blk * BL
                rev = (d == 1)
                li = step % NL
                Lq, Lk, Ll, Lv = t["Lq"][li], t["Lk"][li], t["Ll"][li], t["Lv"][li]
                bLq, bLk, bLl, bLv = t["bL"][li]
                P.dma("sp", Lq[:], spec["q"].rearrange("c p t -> p c t")[:, :, t0:t0 + BL], "gq%d%d" % (d, li), writes=[bLq])
                P.dma("sp", Lk[:], spec["k"][d].rearrange("c p t -> p c t")[:, :, t0:t0 + BL], "gk%d%d" % (d, li), writes=[bLk])
                P.dma("sp", Ll[:], spec["lf"][d].rearrange("c p t -> p c t")[:, :, t0:t0 + BL], "gl%d%d" % (d, li), writes=[bLl])
                P.dma("sp", Lv[:], spec["v"].rearrange("c p t -> p c t")[:, :, t0:t0 + BL], "gv%d%d" % (d, li), writes=[bLv])
                Oo, bOo = t["Oo"][li], t["bOo"][li]

                def rv(ap, rev=rev):
                    return ap[:, ::-1] if rev else ap
                pb = t["pb"]
                pA, pBk, pU = B.ps[pb], B.ps[pb + 1], [B.ps[pb + 2], B.ps[pb + 3]]
                bAatt, bAkt, bBv, bO, bU = t["bAatt"], t["bAkt"], t["bBv"], t["bO"], t["bU"]
                S, Sb = t["S"], t["Sb"]
                for h in range(nh):
                    r = h % R2
                    g = {nm: t[nm][r] for nm in ("b", "eb", "enb", "qe", "ke", "vb", "attm", "vtok", "ktok", "ud", "rc")}
                    G = {nm: t["B" + nm][r] for nm in g}
                    bS, bSb = t["bS"][h], t["bSb"][h]
                    b_, eb, enb, qe, ke, vb = g["b"], g["eb"], g["enb"], g["qe"], g["ke"], g["vb"]
                    attm, vtok, ktok, ud, rc = g["attm"], g["vtok"], g["ktok"], g["ud"], g["rc"]
                    if STG < 1:
                        continue
                    for kc in range(nk):
                        lfi = h * nlf + (kc if nlf == nk else 0)
                        if kc == 0 or nlf == nk:
                            P.scan(b_[:, kc, :], reset[:], rv(Ll[:, lfi, :]), 0.0, ALU.mult, ALU.add, [b_cst, bLl], [G["b"]])
                            src_b = b_[:, kc, :]
                        P.act(eb[:, kc, :], src_b, AF.Exp, [G["b"]], [G["eb"]])
                        P.act(enb[:, kc, :], src_b, AF.Exp, [G["b"]], [G["enb"]], scale=-1.0)
                        P.tt("dve", qe[:, kc, :], rv(Lq[:, h * nk + kc, :]), eb[:, kc, :], ALU.mult, [bLq, G["eb"]], [G["qe"]])
                        P.tt("pool", ke[:, kc, :], rv(Lk[:, h * nk + kc, :]), enb[:, kc, :], ALU.mult, [bLk, G["enb"]], [G["ke"]])
                    for vc in range(nv):
                        P.cp("pool", vb[:, vc, :], rv(Lv[:, h * nv + vc, :]), [bLv], [G["vb"]])
                    if STG < 2:
                        continue
                    for kc in range(nk):
                        P.mm(pA[:, 0:128], ke[:, kc, :], qe[:, kc, :], kc == 0, kc == nk - 1, [G["ke"], G["qe"]], [bAatt])
                    for kc in range(nk):
                        P.mm(pA[:, 128 + kc * 128:256 + kc * 128], ke[:, kc, :], ident[:], True, True, [G["ke"], b_cst], [bAkt])
                    for vc in range(nv):
                        P.mm(pBk[:, vc * 128:(vc + 1) * 128], vb[:, vc, :], ident[:], True, True, [G["vb"], b_cst], [bBv])
                    if STG < 3:
                        continue
                    P.tt("dve", attm[:], pA[:, 0:128], cmask[:], ALU.mult, [bAatt, b_cst], [G["attm"]])
                    P.act(vtok[:, 0:nv * 128], pBk[:, 0:nv * 128], AF.Identity, [bBv], [G["vtok"]])
                    for j in range(NSUB):
                        P.act(ktok[:, j, :], pA[:, 128:128 + nk * 128], AF.Identity, [bAkt, b_cst], [G["ktok"]],
                              scale=rmask[:, j:j + 1])

                    def chain(j):
                        for kc in range(nk):
                            P.mm(pU[kc][:, 0:nvt * 128], ktok[:, j, kc * 128:(kc + 1) * 128], vtok[:], True, True,
                                 [G["ktok"], G["vtok"]], [bU[kc]])
                        for kc in range(nk):
                            dec = eb[:, kc, (j + 1) * CL - 1:(j + 1) * CL]
                            P.act(ud[:], pU[kc][:, 0:nvt * 128], AF.Identity, [bU[kc], G["eb"]], [G["ud"]], scale=dec)
                            P.stt("dve", S[:, h, kc, :], S[:, h, kc, :], dec, ud[:], ALU.mult, ALU.add,
                                  [bS, G["ud"], G["eb"]], [bS])
                            jn = (j + 1) % NSUB
                            P.cp("pool", Sb[:, h, jn, kc * nvt * 128:(kc + 1) * nvt * 128], S[:, h, kc, :], [bS], [bSb])

                    if STG < 4:
                        continue
                    for j in range(NSUB - 1):
                        chain(j)
                    vorder = ([nv] if ones else []) + list(range(nv))
                    for vc in vorder:
                        oi = t["oi"] % 2
                        t["oi"] += 1
                        pO = pBk[:, 256 + oi * 128:384 + oi * 128]
                        E7 = os.environ.get("E7", "")
                        P.mm(pO, vtok[:, vc * 128:(vc + 1) * 128], attm[:], True, "i" in E7, [G["vtok"], G["attm"]], [bO[oi]])
                        n_in = NSUB * nk
                        ii = 0
                        for j in range(0 if "i" in E7 else NSUB):
                            for kc in range(nk):
                                ii += 1
                                o0 = kc * nvt * 128 + vc * 128
                                lh = ident[:] if "L" in E7 else Sb[:, h, j, o0:o0 + 128]
                                rh = ident[:, j * CL:(j + 1) * CL] if "R" in E7 else qe[:, kc, j * CL:(j + 1) * CL]
                                P.mm(pBk[:, 256 + oi * 128 + j * CL:256 + oi * 128 + (j + 1) * CL],
                                     lh, rh, False, ii == n_in,
                                     [bSb, G["qe"]], [bO[oi]])
                        if ones and vc == nv:
                            P.act(rc[:], pO, AF.Abs, [bO[oi]], [G["rc"]])
                            P.ts("dve", rc[:], rc[:], 1.0, None, ALU.max, None, [G["rc"]], [G["rc"]])
                            P.op("dve", lambda e, rc=rc: e.reciprocal(out=rc[:], in_=rc[:]), [G["rc"]], [G["rc"]])
                        elif ones:
                            P.tt("dve", rv(Oo[:, h * nv + vc, :]), pO, rc[:], ALU.mult, [bO[oi], G["rc"]], [bOo])
                        else:
                            P.cp("dve", rv(Oo[:, h * nv + vc, :]), pO, [bO[oi]], [bOo])
                    if STG >= 5:
                        chain(NSUB - 1)
                P.dma("sp", spec["out"][d].rearrange("c p t -> p c t")[:, :, t0:t0 + BL], Oo[:], "go%d%d" % (d, li), reads=[bOo])
        P.barrier()
        P.flush()


def fm_cols(v):
    v = np.asarray(v, np.float32)
    lead = v.shape[:-1]
    n = v.shape[-1] // 128
    v = v.reshape(lead + (n, 128))
    v = np.moveaxis(v, -1, 0)
    return np.ascontiguousarray(v.reshape(128, -1))


def pool_inv_table(n, windows):
    out = np.zeros((len(windows), n), np.float32)
    pos = np.arange(n)
    for i, w in enumerate(windows):
        lo = np.clip(pos - w // 2, 0, n - 1)
        hi = np.clip(pos + w - w // 2 - 1, 0, n - 1)
        out[i] = 1.0 / (hi - lo + 1)
    return out


def prep_shared(cfg, inp):
    D, F, DC, FC, L = cfg.D, cfg.F, cfg.DC, cfg.FC, cfg.L
    f32 = lambda a: np.asarray(a, np.float32)
    m = {}
    m["mod_w"] = np.ascontiguousarray(f32(inp["mod_w"]).reshape(L * D, 9 * D))
    m["mod_bT"] = fm_cols(f32(inp["mod_b"]))
    m["ln_gT"] = fm_cols(f32(inp["ln_g"]))
    m["ln_bT"] = fm_cols(f32(inp["ln_b"]))
    w13 = np.stack([f32(inp["ffn1_w13"]), f32(inp["ffn2_w13"])], 1)
    w13 = w13.reshape(L, 2, DC, 128, 2, FC, 128)
    w13 = w13.transpose(0, 1, 5, 3, 2, 4, 6)
    m["w13"] = np.ascontiguousarray(w13).reshape(L * 2 * FC * 128, DC * 256)
    w2 = np.stack([f32(inp["ffn1_w2"]), f32(inp["ffn2_w2"])], 1)
    w2 = w2.reshape(L, 2, FC, 128, DC, 128)
    w2 = w2.transpose(0, 1, 4, 3, 2, 5)
    m["w2"] = np.ascontiguousarray(w2).reshape(L * 2 * DC * 128, FC * 128)
    if 1 in cfg.kinds:
        W = D // 4
        CPG = DC // 4
        pw = f32(inp["pool_w"])[0].reshape(4, CPG, 128, W).transpose(2, 0, 1, 3)
        m["pool_wB"] = np.ascontiguousarray(pw).reshape(128, -1)
        m["pool_scT"] = fm_cols(f32(inp["pool_scale"])[0])
        m["pool_inv_lat"] = np.ascontiguousarray(np.broadcast_to(pool_inv_table(GRID_W, cfg.pool_windows).reshape(1, -1), (128, 4 * GRID_W)))
        m["pool_inv_ctx"] = np.ascontiguousarray(np.broadcast_to(pool_inv_table(cfg.TC, cfg.pool_windows).reshape(1, -1), (128, 4 * cfg.TC)))
    if 0 in cfg.kinds or 2 in cfg.kinds:
        m["c_ident"] = np.eye(128, dtype=np.float32)
        for CL in ((32,) if 0 in cfg.kinds else ()) + ((128,) if 2 in cfg.kinds else ()):
            i = np.arange(128)
            same = (i[:, None] // CL) == (i[None, :] // CL)
            m["c_cmask%d" % CL] = (same & (i[:, None] <= i[None, :])).astype(np.float32)
            m["c_rmask%d" % CL] = ((i[:, None] // CL) == np.arange(128 // CL)[None, :]).astype(np.float32)
            m["c_reset%d" % CL] = np.ascontiguousarray(np.broadcast_to(((i % CL) != 0).astype(np.float32)[None, :], (128, 128)))
    if 0 in cfg.kinds:
        w = f32(inp["hg_w_in"])[0].reshape(DC, 128, 5 * DC, 128).transpose(2, 1, 0, 3)
        m["hg_win"] = np.ascontiguousarray(w).reshape(5 * DC * 128, DC * 128)
        w = f32(inp["hg_w_o"])[0].reshape(DC, 128, DC, 128).transpose(2, 1, 0, 3)
        m["hg_wo"] = np.ascontiguousarray(w).reshape(DC * 128, DC * 128)
        m["hg_lbT"] = fm_cols(f32(inp["hg_lb_logits"]))
        m["hg_ngT"] = fm_cols(f32(inp["hg_norm_g"])[0])
    if 2 in cfg.kinds:
        H = cfg.ml_heads
        win = f32(inp["ml_w_in"])[0]
        w = win[:, :4 * D].reshape(DC, 128, 4 * DC, 128).transpose(2, 1, 0, 3)
        m["ml_win"] = np.ascontiguousarray(w).reshape(4 * DC * 128, DC * 128)
        w = f32(inp["ml_w_o"])[0].reshape(DC, 128, DC, 128).transpose(2, 1, 0, 3)
        m["ml_wo"] = np.ascontiguousarray(w).reshape(DC * 128, DC * 128)
        wg = win[:, 4 * D:].reshape(DC, 128, 4 * H).transpose(1, 0, 2)
        m["ml_wg"] = np.ascontiguousarray(wg).reshape(128, DC * 4 * H)
        gb = f32(inp["ml_gate_b"])[0].reshape(4 * H)
        isf = np.repeat(np.array([0.0, 1.0, 0.0, 1.0], np.float32), H)
        m["ml_gcol"] = np.ascontiguousarray(np.stack([gb, np.float32(1.0) - isf, -isf], 1).astype(np.float32))
        sel = np.zeros((4 * H, 4 * H, 128), np.float32)
        for r in range(4 * H):
            sel[r, r, :] = 1.0
        m["ml_sel"] = sel.reshape(4 * H, 4 * H * 128)
        m["ml_ngT"] = fm_cols(f32(inp["ml_norm_g"])[0])
    if 3 in cfg.kinds:
        G = D // 16

        def lanes(a):
            a = f32(a).reshape(2, DC, 4, 2, 64).transpose(3, 4, 0, 1, 2).reshape(128, 2, DC, 4)
            return a
        lre, lim = lanes(inp["s5_lam_re"][0]), lanes(inp["s5_lam_im"][0])
        ldt = lanes(np.repeat(f32(inp["s5_log_dt"][0])[:, :, None], 64, axis=2))
        m["s5_lam"] = np.ascontiguousarray(np.stack([lre, lim, ldt], 1)).reshape(128, -1)

        def bblk(bm):
            bm = f32(bm).reshape(2, DC, 4, 2, 64, 16)
            out = np.zeros((2, DC, 8, 16, 4, 2, 64), np.float32)
            for j in range(4):
                for gl in range(2):
                    out[:, :, 2 * j + gl, :, j, gl, :] = bm[:, :, j, gl].transpose(0, 1, 3, 2)
            return out.reshape(2 * DC * 128, 4 * 128)

        def cblk(cm):
            cm = f32(cm).reshape(2, DC, 4, 2, 16, 64)
            out = np.zeros((2, DC, 2, 64, 4, 8, 16), np.float32)
            for j in range(4):
                for gl in range(2):
                    out[:, :, gl, :, j, 2 * j + gl, :] = cm[:, :, j, gl].transpose(0, 1, 3, 2)
            return out.reshape(2 * DC * 128, 4 * 128)
        m["s5_Bb0"], m["s5_Bb1"] = bblk(inp["s5_b_re"][0]), bblk(inp["s5_b_im"][0])
        m["s5_Cb0"], m["s5_Cb1"] = cblk(inp["s5_c_re"][0]), cblk(inp["s5_c_im"][0])
        m["s5_dT"] = fm_cols(f32(inp["s5_d"])[0])
        m["s5_iota"] = np.ascontiguousarray(np.broadcast_to(np.arange(1, 513, dtype=np.float32)[None, :], (128, 512)))
        w = f32(inp["s5_w_glu"])[0].reshape(DC, 128, 2 * DC, 128).transpose(2, 1, 0, 3)
        m["s5_wglu"] = np.ascontiguousarray(w).reshape(2 * DC * 128, DC * 128)
    return m


def prep_core(cfg, inp, batches):
    D, DC = cfg.D, cfg.DC
    f32 = lambda a: np.asarray(a, np.float32)
    m = {}
    m["xT"] = np.ascontiguousarray(np.concatenate([f32(inp["x"][b]).T for b in batches], 0))
    m["ctxT"] = np.ascontiguousarray(np.concatenate([f32(inp["ctx"][b]).T for b in batches], 0))
    cnds = []
    for b in batches:
        cs = [f32(inp["c"][b]), f32(inp["c_ctx"])]
        cnds.append(np.stack(cs, -1).reshape(DC, 128, 2).transpose(1, 0, 2).reshape(128, -1))
    m["cond"] = np.ascontiguousarray(np.concatenate(cnds, 0))
    return m


def run(cfg, inp, n_cores):
    nb = cfg.nb
    bld = Builder(cfg)
    nc = bld.build()
    shared = prep_shared(cfg, inp)
    in_maps = []
    for c in range(n_cores):
        m = dict(shared)
        m.update(prep_core(cfg, inp, list(range(c * nb, (c + 1) * nb))))
        in_maps.append({k: m[k] for k in bld.din})
    res = run_bass_kernel_spmd(nc, in_maps, core_ids=list(range(n_cores)))
    outs = []
    for c in range(n_cores):
        o = res.results[c]["outT"].reshape(nb, cfg.D, cfg.T)
        outs.append(np.transpose(o, (0, 2, 1)))
    return np.ascontiguousarray(np.concatenate(outs, 0))


N_CORES = 8


def kernel(**inputs):
    n_cores = N_CORES
    cfg = Cfg(nb=8 // n_cores)
    return run(cfg, inputs, n_cores).astype(np.float32)
```

```python
import contextlib
import os
import numpy as np
import concourse.bass as bass
import concourse.mybir as mybir
from concourse.bass_utils import run_bass_kernel_spmd

F32 = mybir.dt.float32
BF16 = mybir.dt.bfloat16
I32 = mybir.dt.int32
AF = mybir.ActivationFunctionType
ALU = mybir.AluOpType
AX = mybir.AxisListType


class StopBuild(Exception):
    pass


class Buf:
    __slots__ = ("lw", "rd")

    def __init__(self):
        self.lw = None
        self.rd = {}


def bufs(n):
    return [Buf() for _ in range(n)]


class Op:
    __slots__ = ("eng", "fn", "deps", "dsem", "sig", "sigkey", "sigval")

    def __init__(self, eng, fn, deps, dsem):
        self.eng = eng
        self.fn = fn
        self.deps = deps
        self.dsem = dsem
        self.sig = False
        self.sigkey = None
        self.sigval = 0


class Prog:
    ENGS = ("pe", "act", "dve", "pool", "sp")

    def __init__(self, nc, es):
        self.nc = nc
        self.es = es
        self.h = {"pe": nc.tensor, "act": nc.scalar, "dve": nc.vector, "pool": nc.gpsimd, "sp": nc.sync}
        self.ops = []
        self.base = 0
        self.sems = {}
        for e in self.ENGS:
            self.sems[e] = es.enter_context(nc.semaphore("s_" + e))
        self.cnt = {e: 0 for e in self.ENGS}
        self.seen = {e: {} for e in self.ENGS}
        self.snap = {}
        self.last_dma = {}
        self.last_op = {}
        self.pool_keys = set()

    def _sem(self, key):
        if key not in self.sems:
            self.sems[key] = self.es.enter_context(self.nc.semaphore("d_" + str(key)))
            self.cnt[key] = 0
        return self.sems[key]

    def op(self, eng, fn, reads=(), writes=(), dsem=None, extra=()):
        idx = self.base + len(self.ops)
        dma = dsem is not None
        deps = set(extra)
        cand = []
        for b in reads:
            if b.lw is not None:
                cand.append((b.lw, True))
        for b in writes:
            if b.lw is not None:
                cand.append((b.lw, False))
            for r in b.rd.values():
                cand.append((r, False))
        for d, raw in cand:
            if d < self.base:
                continue
            od = self.ops[d - self.base]
            if (not dma) and od.dsem is None and od.eng == eng and not raw:
                continue
            deps.add(d)
        if dma:
            self._sem(dsem)
            if eng == "pool":
                self.pool_keys.add(dsem)
            p = self.last_dma.get(dsem)
            if p is not None and p >= self.base:
                deps.add(p)
            self.last_dma[dsem] = idx
        deps.discard(idx)
        self.ops.append(Op(eng, fn, deps, dsem))
        key = ("d", idx) if dma else eng
        for b in reads:
            b.rd[key] = idx
        for b in writes:
            b.lw = idx
            b.rd = {}
        if not dma:
            self.last_op[eng] = idx
        return idx

    def barrier(self):
        ext = set(v for v in self.last_op.values() if v >= self.base)
        ext |= set(v for v in self.last_dma.values() if v >= self.base)
        for e in self.ENGS:
            self.op(e, None, extra=ext)

    def flush(self):
        ops = self.ops
        base = self.base
        for o in ops:
            for d in o.deps:
                od = ops[d - base]
                if od.dsem is None:
                    od.sig = True
        for i, o in enumerate(ops):
            idx = base + i
            E = self.h[o.eng]
            sv = self.seen[o.eng]
            for d in sorted(o.deps, reverse=True):
                od = ops[d - base]
                if sv.get(od.sigkey, 0) >= od.sigval:
                    continue
                E.wait_ge(self.sems[od.sigkey], od.sigval)
                for k2, v2 in self.snap[d].items():
                    if sv.get(k2, 0) < v2:
                        sv[k2] = v2
            ins = o.fn(E) if o.fn is not None else None
            if o.dsem is not None:
                self.cnt[o.dsem] += 16
                o.sigkey, o.sigval = o.dsem, self.cnt[o.dsem]
                ins.then_inc(self.sems[o.dsem], 16)
                s = dict(sv)
                s[o.sigkey] = o.sigval
                self.snap[idx] = s
            elif o.sig:
                assert ins is not None
                self.cnt[o.eng] += 1
                o.sigkey, o.sigval = o.eng, self.cnt[o.eng]
                ins.then_inc(self.sems[o.eng], 1)
                s = dict(sv)
                s[o.sigkey] = o.sigval
                self.snap[idx] = s
        self.base += len(ops)
        self.ops = []
        self.snap = {}

    def hard_sync(self):
        self.barrier()
        self.flush()
        if os.environ.get("NOHS"):
            return
        self.nc.all_engine_barrier()
        if os.environ.get("HS") == "b":
            return
        for k, sem in self.sems.items():
            if k not in self.pool_keys:
                self.nc.sync.sem_clear(sem)
        self.nc.all_engine_barrier()
        for k in self.cnt:
            if k not in self.pool_keys:
                self.cnt[k] = 0
        self.seen = {e: {} for e in self.ENGS}

    def mm(self, out, lhsT, rhs, start, stop, reads, writes):
        return self.op("pe", lambda e: e.matmul(out, lhsT=lhsT, rhs=rhs, start=start, stop=stop,
                                                skip_group_check=(os.environ.get("NOSKIP") is None)), reads, writes)

    def tr(self, out, in_, ident, reads, writes):
        return self.op("pe", lambda e: e.transpose(out, in_, ident), reads, writes)

    def act(self, out, in_, func, reads, writes, bias=None, scale=None):
        kw = {}
        if bias is not None:
            kw["bias"] = bias
        if scale is not None:
            kw["scale"] = scale
        return self.op("act", lambda e: e.activation(out=out, in_=in_, func=func, **kw), reads, writes)

    def ts(self, eng, out, in0, s1, s2, op0, op1, reads, writes):
        if op1 is None:
            return self.op(eng, lambda e: e.tensor_scalar(out=out, in0=in0, scalar1=s1, scalar2=None, op0=op0),
                           reads, writes)
        return self.op(eng, lambda e: e.tensor_scalar(out=out, in0=in0, scalar1=s1, scalar2=s2, op0=op0, op1=op1),
                       reads, writes)

    def tt(self, eng, out, in0, in1, op, reads, writes):
        return self.op(eng, lambda e: e.tensor_tensor(out=out, in0=in0, in1=in1, op=op), reads, writes)

    def stt(self, eng, out, in0, scalar, in1, op0, op1, reads, writes):
        return self.op(eng, lambda e: e.scalar_tensor_tensor(out=out, in0=in0, scalar=scalar, in1=in1,
                                                             op0=op0, op1=op1), reads, writes)

    def cp(self, eng, out, in_, reads, writes):
        return self.op(eng, lambda e: e.tensor_copy(out=out, in_=in_), reads, writes)

    def scan(self, out, d0, d1, init, op0, op1, reads, writes):
        return self.op("dve", lambda e: e.tensor_tensor_scan(out=out, data0=d0, data1=d1, initial=init,
                                                             op0=op0, op1=op1), reads, writes)

    def memset(self, eng, ap, val, writes):
        return self.op(eng, lambda e: e.memset(ap, val), (), writes)

    def dma(self, q, out, in_, dsem, reads=(), writes=()):
        return self.op(q, lambda e: e.dma_start(out=out, in_=in_), reads, writes, dsem=dsem)


class Cfg:
    def __init__(self, D=2048, F=5632, T=4096, TC=256, kinds=(0, 1, 2, 3), ml_heads=8, nb=1, last_skip=True):
        self.D, self.F, self.T, self.TC = D, F, T, TC
        self.DC, self.FC = D // 128, F // 128
        self.kinds = tuple(kinds)
        self.L = len(kinds)
        self.TT = T + TC
        self.NT = 512
        self.ml_heads = ml_heads
        self.nb = nb
        self.last_skip = last_skip
        self.alpha = (2.0 * 4) ** 0.25
        self.tiles = [(0, TC)] + [(TC + i * self.NT, self.NT) for i in range(T // self.NT)]
        self.pool_windows = (2, 4, 8, 16)


LN_EPS = 1e-5
RMS_EPS = 1e-6
GRID_W = 64


class PN:
    def __init__(self, B, st, NT):
        self.B = B
        self.tm = [B.sb(st, "pn_tm%d" % i, [128, NT], F32) for i in range(2)]
        self.sq = [B.sb(st, "pn_sq%d" % i, [128, NT], BF16) for i in range(2)]
        self.zb = [B.sb(st, "pn_zb%d" % i, [128, NT], BF16) for i in range(2)]
        self.mean = B.sb(st, "pn_mean", [128, NT], F32)
        self.rstd = B.sb(st, "pn_rstd", [128, NT], F32)
        self.msq = B.sb(st, "pn_msq", [128, NT], F32)
        self.b_tm, self.b_sq, self.b_zb = bufs(2), bufs(2), bufs(2)
        self.b_mean, self.b_rstd, self.b_msq = Buf(), Buf(), Buf()

    def begin(self, li, s, X, bX, nt, col):
        self.li, self.s, self.X, self.bX, self.nt, self.col = li, s, X, bX, nt, col
        self.pend = None

    def _stats(self, m, first, last):
        B, P, nt = self.B, self.B.P, self.nt
        P.mm(B.ps[6][:, :nt], B.ones[:], self.zb[m % 2][:, :nt], first, last, [self.b_zb[m % 2], B.b_const], [B.psb[6]])
        P.mm(B.ps[7][:, :nt], B.ones[:], self.sq[m % 2][:, :nt], first, last, [self.b_sq[m % 2], B.b_const], [B.psb[7]])

    def add(self, m, y_ap, y_bufs, gate_ap):
        B, P, nt, X, bX = self.B, self.B.P, self.nt, self.X, self.bX
        if self.pend is not None:
            self._stats(self.pend, self.pend == 0, False)
        Tm = self.tm[m % 2]
        P.act(Tm[:, :nt], y_ap, AF.Identity, list(y_bufs) + [B.b_const], [self.b_tm[m % 2]], scale=gate_ap)
        P.stt("dve", X[:, m, :nt], X[:, m, :nt], B.cfg.alpha, Tm[:, :nt], ALU.mult, ALU.add,
              [bX[m], self.b_tm[m % 2]], [bX[m]])
        P.act(self.sq[m % 2][:, :nt], X[:, m, :nt], AF.Square, [bX[m]], [self.b_sq[m % 2]])
        P.cp("pool", self.zb[m % 2][:, :nt], X[:, m, :nt], [bX[m]], [self.b_zb[m % 2]])
        self.pend = m

    def finish(self):
        B, P, nt, X, bX, li, s = self.B, self.B.P, self.nt, self.X, self.bX, self.li, self.s
        cfg = B.cfg
        D, DC = cfg.D, cfg.DC
        self._stats(self.pend, self.pend == 0, True)
        mean, rstd, msq = self.mean, self.rstd, self.msq
        P.ts("dve", mean[:, :nt], B.ps[6][:, :nt], 1.0 / D, None, ALU.mult, None, [B.psb[6]], [self.b_mean])
        P.tt("dve", msq[:, :nt], mean[:, :nt], mean[:, :nt], ALU.mult, [self.b_mean], [self.b_msq])
        P.stt("dve", rstd[:, :nt], B.ps[7][:, :nt], 1.0 / D, msq[:, :nt], ALU.mult, ALU.subtract,
              [B.psb[7], self.b_msq], [self.b_rstd])
        P.ts("dve", rstd[:, :nt], rstd[:, :nt], LN_EPS, None, ALU.add, None, [self.b_rstd], [self.b_rstd])
        P.act(rstd[:, :nt], rstd[:, :nt], AF.Sqrt, [self.b_rstd], [self.b_rstd])
        P.op("dve", lambda e: e.reciprocal(out=rstd[:, :nt], in_=rstd[:, :nt]), [self.b_rstd], [self.b_rstd])
        for c in range(DC):
            P.tt("pool", X[:, c, :nt], X[:, c, :nt], mean[:, :nt], ALU.subtract, [bX[c], self.b_mean], [bX[c]])
            P.tt("dve", X[:, c, :nt], X[:, c, :nt], rstd[:, :nt], ALU.mult, [bX[c], self.b_rstd], [bX[c]])
            P.act(X[:, c, :nt], X[:, c, :nt], AF.Identity, [bX[c], B.b_const], [bX[c]],
                  scale=B.ln_col(B.lngT, li, s, c), bias=B.ln_col(B.lnbT, li, s, c))


class Builder:
    def __init__(self, cfg):
        self.cfg = cfg
        self.nc = bass.Bass("TRN2", target_bir_lowering=False)
        self.din = {}

    def inp(self, name, shape, dt=F32):
        t = self.nc.dram_tensor(name, list(shape), dt, kind="ExternalInput")
        self.din[name] = t
        return t.ap()

    def scratch(self, name, shape, dt):
        return self.nc.dram_tensor(name, list(shape), dt, kind="Internal").ap()

    def sb(self, stack, name, shape, dt):
        self._uid = getattr(self, "_uid", 0) + 1
        return stack.enter_context(self.nc.sbuf_tensor("%s_%d" % (name, self._uid), list(shape), dt))

    def build(self):
        cfg = self.cfg
        nc = self.nc
        D, F, T, TC, DC, FC, L, TT, nb = cfg.D, cfg.F, cfg.T, cfg.TC, cfg.DC, cfg.FC, cfg.L, cfg.TT, cfg.nb
        NJ = 2
        self.xT = self.inp("xT", [nb * D, T])
        self.ctxT = self.inp("ctxT", [nb * D, TC])
        self.cond = self.inp("cond", [nb * 128, DC * NJ])
        self.mod_w = self.inp("mod_w", [L * D, 9 * D])
        self.mod_bT = self.inp("mod_bT", [128, L * 9 * DC])
        self.ln_gT = self.inp("ln_gT", [128, L * 3 * DC])
        self.ln_bT = self.inp("ln_bT", [128, L * 3 * DC])
        self.w13 = self.inp("w13", [L * 2 * FC * 128, DC * 256])
        self.w2 = self.inp("w2", [L * 2 * DC * 128, FC * 128])
        self.outT = nc.dram_tensor("outT", [nb * D, T], F32, kind="ExternalOutput").ap()
        self.mixer_inputs()
        self.w13b = [self.scratch("w13b%d" % i, [FC * 128, DC * 256], BF16) for i in range(L * 2)]
        self.w2b = [self.scratch("w2b%d" % i, [DC * 128, FC * 128], BF16) for i in range(L * 2)]
        self.xs = [self.scratch("xs%d" % i, [D, TT], F32) for i in range(2)]
        self.mod_wb = [self.scratch("mod_wb%d" % i, [D, 9 * D], BF16) for i in range(L)]
        if self.has(1):
            self.pool_wBb = self.scratch("pool_wBb", [128, D * D // 4 // 128], BF16)
        self.mixer_scratch()

        with contextlib.ExitStack() as es:
            P = self.P = Prog(nc, es)
            self.ones = self.sb(es, "ones", [128, 128], BF16)
            self.modT = self.sb(es, "modT", [128, L * 9 * DC * NJ], F32)
            self.lngT = self.sb(es, "lngT", [128, L * 3 * DC], F32)
            self.lnbT = self.sb(es, "lnbT", [128, L * 3 * DC], F32)
            self.b_const = Buf()
            self.ps = [es.enter_context(nc.psum_tensor("ps%d" % i, [128, 512], F32)) for i in range(8)]
            self.psb = bufs(8)
            self.mixer_persistent(es)

            self.prologue_casts()
            stop = getattr(cfg, "stop_after", None)
            self._nph = 0
            if nb == 1:
                self.per_batch(0, stop)
            else:
                P.hard_sync()
                if os.environ.get("UNROLL"):
                    for bi in range(nb):
                        self.per_batch(bi, stop)
                        P.hard_sync()
                else:
                    with nc.Fori(0, nb) as bv:
                        self.per_batch(0 if os.environ.get("STATICB") else bv, stop)
                        P.hard_sync()
            P.barrier()
            P.flush()
        return nc


    def per_batch(self, b, stop):
        cfg, P = self.cfg, self.P
        D, TC, L = cfg.D, cfg.TC, cfg.L

        def dump(cur):
            src = cur.rearrange("(c p) t -> p c t", p=128)[:, :, TC:]
            P.dma("sp", self.outT[0:D, :].rearrange("(c p) t -> p c t", p=128), src, "c0")
            P.barrier()
            P.flush()
        self.prologue(b)
        cur = None
        nph = 0
        for li, kind in enumerate(cfg.kinds):
            last = (li == L - 1)
            dst = self.xs[0] if cur is not self.xs[0] else self.xs[1]
            self.ffn_phase(b, li, 0, cur, dst, cfg.tiles, None)
            cur = dst
            nph += 1
            if stop == nph:
                return dump(cur)
            dst = self.xs[0] if cur is not self.xs[0] else self.xs[1]
            try:
                self.mixer_phase(b, li, kind, cur, dst, ctx_out=not (last and cfg.last_skip))
            except StopBuild:
                return dump(cur)
            cur = dst
            nph += 1
            if stop == nph:
                return dump(cur)
            dst = self.xs[0] if cur is not self.xs[0] else self.xs[1]
            tiles = cfg.tiles[1:] if (last and cfg.last_skip) else cfg.tiles
            self.ffn_phase(b, li, 2, cur, dst, tiles, self.outT if last else None)
            cur = dst

    def mod_col(self, li, slot, c, col):
        cfg = self.cfg
        NJ = 2
        o = ((li * 9 + slot) * cfg.DC + c) * NJ + col
        return self.modT[:, o:o + 1]

    def ln_col(self, t, li, s, c):
        o = (li * 3 + s) * self.cfg.DC + c
        return t[:, o:o + 1]

    def src_ap(self, b, cur, t0, nt):
        cfg = self.cfg
        if cur is not None:
            return cur.rearrange("(c p) t -> p c t", p=128)[:, :, t0:t0 + nt]
        if t0 < cfg.TC:
            return self.ctxT.rearrange("(b c p) t -> b p c t", p=128, c=cfg.DC)[b][:, :, t0:t0 + nt]
        return self.xT.rearrange("(b c p) t -> b p c t", p=128, c=cfg.DC)[b][:, :, t0 - cfg.TC:t0 - cfg.TC + nt]

    def prologue_casts(self):
        cfg, P = self.cfg, self.P
        DC, FC, L = cfg.DC, cfg.FC, cfg.L
        for i in range(L * 2):
            for r in range(FC):
                g = i * FC + r
                P.dma("pool", self.w13b[i][r * 128:(r + 1) * 128, :], self.w13[g * 128:(g + 1) * 128, :], "cast%d" % (r % 4))
            for r in range(DC):
                g = i * DC + r
                P.dma("pool", self.w2b[i][r * 128:(r + 1) * 128, :], self.w2[g * 128:(g + 1) * 128, :], "cast%d" % (r % 4))
        D = cfg.D
        for i in range(L):
            for kc in range(DC):
                for hf in range(3):
                    P.dma("pool", self.mod_wb[i][kc * 128:(kc + 1) * 128, hf * 3 * D:(hf + 1) * 3 * D],
                          self.mod_w[i * D + kc * 128:i * D + (kc + 1) * 128, hf * 3 * D:(hf + 1) * 3 * D], "cast%d" % (kc % 4))
        if self.has(1):
            P.dma("pool", self.pool_wBb[:, :], self.pool_wB[:, :], "cast0")
        self.mixer_casts()
        P.barrier()
        P.flush()

    def prologue(self, b):
        cfg = self.cfg
        P = self.P
        nc = self.nc
        D, DC, FC, L, nb = cfg.D, cfg.DC, cfg.FC, cfg.L, cfg.nb
        NJ = 2
        with contextlib.ExitStack() as st:
            P.memset("dve", self.ones[:], 1.0, [self.b_const])
            P.dma("sp", self.lngT[:], self.ln_gT[:, :], "c0", writes=[self.b_const])
            P.dma("sp", self.lnbT[:], self.ln_bT[:, :], "c1", writes=[self.b_const])
            cnd = self.sb(st, "cnd", [128, DC * NJ], F32)
            sg = self.sb(st, "sg", [128, DC * NJ], F32)
            cb = self.sb(st, "cb", [128, DC * NJ], BF16)
            mb = self.sb(st, "mb", [128, L * 9 * DC], F32)
            n_oc = 9 * DC
            GRP = 8 if n_oc % 8 == 0 else 4
            assert GRP * NJ <= 512
            wst = [self.sb(st, "wst%d" % i, [128, DC, GRP * 128], BF16) for i in range(2)]
            b_c, b_cb, b_mb = Buf(), Buf(), Buf()
            b_w = bufs(2)
            P.dma("sp", cnd[:], self.cond.rearrange("(b p) n -> b p n", p=128)[b], "c2", writes=[b_c])
            P.dma("sp", mb[:], self.mod_bT[:, :], "c3", writes=[b_mb])
            P.act(sg[:], cnd[:], AF.Sigmoid, [b_c], [b_cb])
            P.tt("dve", cb[:], cnd[:], sg[:], ALU.mult, [b_c, b_cb], [b_cb])
            it = 0
            for li in range(L):
                for g in range(n_oc // GRP):
                    bank = self.ps[it % 2]
                    bb = self.psb[it % 2]
                    w = wst[it % 2]
                    bw = b_w[it % 2]
                    src = self.mod_wb[li][:, g * GRP * 128:(g + 1) * GRP * 128].rearrange("(k p) n -> p k n", p=128)
                    P.dma("sp", w[:], src, "mw%d" % (it % 2), writes=[bw])
                    it += 1
                    for o in range(GRP):
                        for kc in range(DC):
                            P.mm(bank[:, o * NJ:(o + 1) * NJ], w[:, kc, o * 128:(o + 1) * 128],
                                 cb[:, kc * NJ:(kc + 1) * NJ], kc == 0, kc == DC - 1, [bw, b_cb], [bb])
                    o0 = (li * n_oc + g * GRP) * NJ
                    dst = self.modT[:, o0:o0 + GRP * NJ].rearrange("p (o j) -> p o j", j=NJ)
                    srcp = bank[:, 0:GRP * NJ].rearrange("p (o j) -> p o j", j=NJ)
                    bia = mb[:, li * n_oc + g * GRP: li * n_oc + (g + 1) * GRP].unsqueeze(2).to_broadcast([128, GRP, NJ])
                    P.tt("dve", dst, srcp, bia, ALU.add, [bb, b_mb], [self.b_const])
            mv = self.modT[:].rearrange("p (l s c) -> p l s c", l=L, s=9)
            for s in (1, 4, 7):
                P.ts("dve", mv[:, :, s, :], mv[:, :, s, :], 1.0, None, ALU.add, None, [self.b_const], [self.b_const])
            for s in (2, 8):
                P.ts("dve", mv[:, :, s, :], mv[:, :, s, :], 0.5, None, ALU.mult, None, [self.b_const], [self.b_const])
            self.mixer_prologue(st)
            P.barrier()
            P.flush()

    def ffn_phase(self, b, li, s, cur, dst, tiles, final_out):
        cfg = self.cfg
        P = self.P
        D, DC, FC, TC = cfg.D, cfg.DC, cfg.FC, cfg.TC
        f = 0 if s == 0 else 1
        NT = cfg.NT
        with contextlib.ExitStack() as st:
            xt = [self.sb(st, "xt%d" % i, [128, DC, NT], F32) for i in range(2)]
            hT = self.sb(st, "hT", [128, DC, NT], BF16)
            gT = self.sb(st, "gT", [128, FC, NT], BF16)
            w13s = [self.sb(st, "w13s%d" % i, [128, DC, 256], BF16) for i in range(3)]
            w2s = [self.sb(st, "w2s%d" % i, [128, FC, 128], BF16) for i in range(2)]
            sa = [self.sb(st, "sa%d" % i, [128, NT], F32) for i in range(2)]
            b_xt, b_w13, b_w2, b_sa = bufs(2), bufs(3), bufs(2), bufs(2)
            b_h = Buf()
            b_g = bufs(FC)
            b_xc = [bufs(DC) for _ in range(2)]
            pn = PN(self, st, NT)
            PA, PB, PO, S1, S2 = (0, 1), (2, 3), (4, 5), 6, 7
            ps, psb = self.ps, self.psb
            w13v = self.w13b[li * 2 + f].rearrange("(r p) (k n) -> r p k n", p=128, n=256)
            w2v = self.w2b[li * 2 + f].rearrange("(r p) (k n) -> r p k n", p=128, n=128)
            r13 = 0
            r2 = 0
            wi13 = 0
            wi2 = 0
            for ti, (t0, nt) in enumerate(tiles):
                col = 0 if t0 >= TC else 1
                X = xt[ti % 2]
                bX = b_xc[ti % 2]
                P.dma("sp", X[:, :, :nt], self.src_ap(b, cur, t0, nt), "xl%d" % (ti % 2), writes=bX)
                for c in range(DC):
                    P.ts("pool", hT[:, c, :nt], X[:, c, :nt], self.mod_col(li, 3 * s + 1, c, col),
                         self.mod_col(li, 3 * s + 0, c, col), ALU.mult, ALU.add, [bX[c], self.b_const], [b_h])
                for j in range(FC):
                    W = w13s[wi13 % 3]
                    bW = b_w13[wi13 % 3]
                    P.dma("sp", W[:], w13v[r13 + j], "w13_%d" % (wi13 % 3), writes=[bW])
                    wi13 += 1
                    pa, pb = ps[PA[j % 2]], ps[PB[j % 2]]
                    ba, bb = psb[PA[j % 2]], psb[PB[j % 2]]
                    for kc in range(DC):
                        P.mm(pa[:, :nt], W[:, kc, 0:128], hT[:, kc, :nt], kc == 0, kc == DC - 1, [bW, b_h], [ba])
                    for kc in range(DC):
                        P.mm(pb[:, :nt], W[:, kc, 128:256], hT[:, kc, :nt], kc == 0, kc == DC - 1, [bW, b_h], [bb])
                    S = sa[j % 2]
                    P.act(S[:, :nt], pa[:, :nt], AF.Silu, [ba], [b_sa[j % 2]])
                    P.tt("dve", gT[:, j, :nt], S[:, :nt], pb[:, :nt], ALU.mult, [b_sa[j % 2], bb], [b_g[j]])
                pn.begin(li, s, X, bX, nt, col)
                for m in range(DC):
                    W = w2s[wi2 % 2]
                    bW = b_w2[wi2 % 2]
                    P.dma("sp", W[:], w2v[r2 + m], "w2_%d" % (wi2 % 2), writes=[bW])
                    wi2 += 1
                    po, bo = ps[PO[m % 2]], psb[PO[m % 2]]
                    for kc in range(FC):
                        P.mm(po[:, :nt], W[:, kc, :], gT[:, kc, :nt], kc == 0, kc == FC - 1, [bW, b_g[kc]], [bo])
                    pn.add(m, po[:, :nt], [bo], self.mod_col(li, 3 * s + 2, m, col))
                pn.finish()
                if final_out is not None:
                    oap = final_out.rearrange("(b c p) t -> b p c t", p=128, c=DC)[b][:, :, t0 - TC:t0 - TC + nt]
                else:
                    oap = dst.rearrange("(c p) t -> p c t", p=128)[:, :, t0:t0 + nt]
                P.dma("sp", oap, X[:, :, :nt], "xst%d" % (ti % 2), reads=bX)
            P.barrier()
            P.flush()

    def slot_of(self, li):
        return li // 4

    def has(self, kind):
        return kind in self.cfg.kinds

    def mixer_inputs(self):
        cfg = self.cfg
        D, DC, T, TC = cfg.D, cfg.DC, cfg.T, cfg.TC
        if self.has(1):
            self.pool_wB = self.inp("pool_wB", [128, D * D // 4 // 128])
            self.pool_scT = self.inp("pool_scT", [128, DC])
            self.pool_inv_lat = self.inp("pool_inv_lat", [128, 4 * GRID_W])
            self.pool_inv_ctx = self.inp("pool_inv_ctx", [128, 4 * TC])
        if self.has(0) or self.has(2):
            self.c_ident = self.inp("c_ident", [128, 128])
            self.c_cmask, self.c_rmask, self.c_reset = {}, {}, {}
            for CL in ((32,) if self.has(0) else ()) + ((128,) if self.has(2) else ()):
                self.c_cmask[CL] = self.inp("c_cmask%d" % CL, [128, 128])
                self.c_rmask[CL] = self.inp("c_rmask%d" % CL, [128, 128 // CL])
                self.c_reset[CL] = self.inp("c_reset%d" % CL, [128, 128])
        if self.has(0):
            self.hg_win = self.inp("hg_win", [5 * DC * 128, DC * 128])
            self.hg_wo = self.inp("hg_wo", [DC * 128, DC * 128])
            self.hg_lbT = self.inp("hg_lbT", [128, 2 * 4 * DC])
            self.hg_ngT = self.inp("hg_ngT", [128, DC])
        if self.has(2):
            H = cfg.ml_heads
            self.ml_win = self.inp("ml_win", [4 * DC * 128, DC * 128])
            self.ml_wo = self.inp("ml_wo", [DC * 128, DC * 128])
            self.ml_wg = self.inp("ml_wg", [128, DC * 4 * H])
            self.ml_gcol = self.inp("ml_gcol", [4 * H, 3])
            self.ml_sel = self.inp("ml_sel", [4 * H, 4 * H * 128])
            self.ml_ngT = self.inp("ml_ngT", [128, DC])

        if self.has(3):
            NST = 2 * DC * 4
            self.s5_lam = self.inp("s5_lam", [128, 3 * NST])
            self.s5_Bb = [self.inp("s5_Bb%d" % i, [2 * DC * 128, 4 * 128]) for i in range(2)]
            self.s5_Cb = [self.inp("s5_Cb%d" % i, [2 * DC * 128, 4 * 128]) for i in range(2)]
            self.s5_dT = self.inp("s5_dT", [128, DC])
            self.s5_iota = self.inp("s5_iota", [128, 512])
            self.s5_wglu = self.inp("s5_wglu", [2 * DC * 128, DC * 128])

    def mixer_scratch(self):
        cfg = self.cfg
        D, DC, TT = cfg.D, cfg.DC, cfg.TT
        H = cfg.ml_heads
        if self.has(3):
            self.s5_wglub = self.scratch("s5_wglub", [2 * DC * 128, DC * 128], BF16)
            self.sgy = self.scratch("sgy", [DC, 128, TT], BF16)
        if self.has(2):
            self.ml_winb = self.scratch("ml_winb", [4 * DC * 128, DC * 128], BF16)
            self.ml_wob = self.scratch("ml_wob", [DC * 128, DC * 128], BF16)
            self.mq = self.scratch("mq", [DC, 128, TT], BF16)
            self.mv = self.scratch("mv", [DC, 128, TT], BF16)
            self.mog = self.scratch("mog", [DC, 128, TT], BF16)
            self.mk = [self.scratch("mk%d" % d, [DC, 128, TT], BF16) for d in range(2)]
            self.mlf = [self.scratch("mlf%d" % d, [H, 128, TT], F32) for d in range(2)]
            self.mo = [self.scratch("mo%d" % d, [DC, 128, TT], F32) for d in range(2)]
        if self.has(0):
            self.hg_winb = self.scratch("hg_winb", [5 * DC * 128, DC * 128], BF16)
            self.hg_wob = self.scratch("hg_wob", [DC * 128, DC * 128], BF16)
            self.hq = self.scratch("hq", [DC, 128, TT], BF16)
            self.hv = self.scratch("hv", [DC, 128, TT], BF16)
            self.hgt = self.scratch("hgt", [DC, 128, TT], BF16)
            self.hk = [self.scratch("hk%d" % d, [DC, 128, TT], BF16) for d in range(2)]
            self.hlf = [self.scratch("hlf%d" % d, [DC, 128, TT], F32) for d in range(2)]
            self.ho = [self.scratch("ho%d" % d, [DC, 128, TT], F32) for d in range(2)]

    def mixer_persistent(self, es):
        pass

    def mixer_casts(self):
        P = self.P
        DC = self.cfg.DC
        if self.has(0):
            for r in range(5 * DC):
                P.dma("pool", self.hg_winb[r * 128:(r + 1) * 128, :], self.hg_win[r * 128:(r + 1) * 128, :], "cast%d" % (r % 4))
            for r in range(DC):
                P.dma("pool", self.hg_wob[r * 128:(r + 1) * 128, :], self.hg_wo[r * 128:(r + 1) * 128, :], "cast%d" % (r % 4))
        if self.has(3):
            for r in range(2 * DC):
                P.dma("pool", self.s5_wglub[r * 128:(r + 1) * 128, :], self.s5_wglu[r * 128:(r + 1) * 128, :], "cast%d" % (r % 4))
        if self.has(2):
            for r in range(4 * DC):
                P.dma("pool", self.ml_winb[r * 128:(r + 1) * 128, :], self.ml_win[r * 128:(r + 1) * 128, :], "cast%d" % (r % 4))
            for r in range(DC):
                P.dma("pool", self.ml_wob[r * 128:(r + 1) * 128, :], self.ml_wo[r * 128:(r + 1) * 128, :], "cast%d" % (r % 4))

    def mixer_prologue(self, st):
        pass

    def ml_proj_phase(self, b, li, cur):
        cfg, P = self.cfg, self.P
        D, DC, TC, NT, H = cfg.D, cfg.DC, cfg.TC, cfg.NT, cfg.ml_heads
        G4 = 4 * H
        CPH = DC // H
        s = 1
        with contextlib.ExitStack() as st:
            xt = [self.sb(st, "xt%d" % i, [128, DC, NT], F32) for i in range(2)]
            hT = self.sb(st, "hT", [128, DC, NT], BF16)
            ws = [self.sb(st, "mws%d" % i, [128, DC, 128], BF16) for i in range(3)]
            stg = [self.sb(st, "mstg%d" % i, [128, DC, NT], BF16) for i in range(2)]
            ksc = self.sb(st, "ksc", [128, 2, H, NT], F32)
            lfs = [self.sb(st, "lfs%d" % i, [128, NT], F32) for i in range(2)]
            wgf = self.sb(st, "wgf", [128, DC, G4], F32)
            wg = self.sb(st, "wg", [128, DC, G4], BF16)
            gcol = self.sb(st, "gcol", [G4, 3], F32)
            sel = self.sb(st, "sel", [G4, G4 * 128], F32)
            onec = self.sb(st, "onec", [128, 1], F32)
            xg = self.sb(st, "xg", [G4, NT], F32)
            eg = self.sb(st, "eg", [G4, NT], F32)
            g2 = self.sb(st, "g2", [G4, NT], F32)
            b_x = [bufs(DC) for _ in range(2)]
            b_h, b_c, b_xg, b_eg, b_g2, b_ksc = Buf(), Buf(), Buf(), Buf(), Buf(), Buf()
            b_ws, b_stg, b_lfs = bufs(3), bufs(2), bufs(2)
            P.dma("sp", wgf[:].rearrange("p k g -> p (k g)"), self.ml_wg[:, :], "c0", writes=[b_c])
            P.dma("sp", gcol[:], self.ml_gcol[:, :], "c1", writes=[b_c])
            P.dma("sp", sel[:], self.ml_sel[:, :], "c2", writes=[b_c])
            P.memset("dve", onec[:], 1.0, [b_c])
            P.cp("dve", wg[:], wgf[:], [b_c], [b_c])
            wv = self.ml_winb.rearrange("(r p) (k n) -> r p k n", p=128, n=128)
            wi = 0
            si = 0
            for ti, (t0, nt) in enumerate(cfg.tiles):
                col = 0 if t0 >= TC else 1
                X, bX = xt[ti % 2], b_x[ti % 2]
                P.dma("sp", X[:, :, :nt], self.src_ap(b, cur, t0, nt), "xl%d" % (ti % 2), writes=bX)
                for c in range(DC):
                    P.ts("pool", hT[:, c, :nt], X[:, c, :nt], self.mod_col(li, 3 * s + 1, c, col),
                         self.mod_col(li, 3 * s + 0, c, col), ALU.mult, ALU.add, [bX[c], self.b_const], [b_h])
                pg, bg = self.ps[7], self.psb[7]
                for kc in range(DC):
                    P.mm(pg[0:G4, :nt], wg[:, kc, :], hT[:, kc, :nt], kc == 0, kc == DC - 1, [b_c, b_h], [bg])
                P.ts("dve", xg[:, :nt], pg[0:G4, :nt], gcol[:, 0:1], None, ALU.add, None, [bg, b_c], [b_xg])
                P.act(eg[:, :nt], xg[:, :nt], AF.Exp, [b_xg], [b_eg], scale=-1.0)
                P.act(eg[:, :nt], eg[:, :nt], AF.Ln, [b_eg, b_c], [b_eg], bias=onec[0:G4, 0:1])
                P.ts("dve", g2[:, :nt], xg[:, :nt], gcol[:, 1:2], None, ALU.mult, None, [b_xg, b_c], [b_g2])
                P.stt("dve", g2[:, :nt], eg[:, :nt], gcol[:, 2:3], g2[:, :nt], ALU.mult, ALU.add, [b_eg, b_g2, b_c], [b_g2])
                for d in range(2):
                    for hd in range(H):
                        r = (2 * d) * H + hd
                        pp, bp = self.ps[(2 * hd) % 4], self.psb[(2 * hd) % 4]
                        P.mm(pp[:, :nt], sel[:, r * 128:(r + 1) * 128], g2[:, :nt], True, True, [b_c, b_g2], [bp])
                        P.act(ksc[:, d, hd, :nt], pp[:, :nt], AF.Exp, [bp], [b_ksc])
                        r = (2 * d + 1) * H + hd
                        pp, bp = self.ps[(2 * hd + 1) % 4], self.psb[(2 * hd + 1) % 4]
                        P.mm(pp[:, :nt], sel[:, r * 128:(r + 1) * 128], g2[:, :nt], True, True, [b_c, b_g2], [bp])
                        L_, bL_ = lfs[si % 2], b_lfs[si % 2]
                        P.cp("dve", L_[:, :nt], pp[:, :nt], [bp], [bL_])
                        P.dma("sp", self.mlf[d][hd, :, t0:t0 + nt], L_[:, :nt], "mlf%d" % (si % 2), reads=[bL_])
                        si += 1
                P.ts("dve", ksc[:, :, :, :nt], ksc[:, :, :, :nt], float((D // H) ** -0.5), None, ALU.mult, None, [b_ksc], [b_ksc])
                outs = [self.mq, None, self.mv, self.mog]
                for sec in range(4):
                    if sec == 1:
                        S0, S1 = stg[0], stg[1]
                    else:
                        S0 = stg[sec % 2]
                    for c in range(DC):
                        oc = sec * DC + c
                        W, bW = ws[wi % 3], b_ws[wi % 3]
                        P.dma("sp", W[:], wv[oc], "mw%d" % (wi % 3), writes=[bW])
                        wi += 1
                        pp, bp = self.ps[4 + oc % 3], self.psb[4 + oc % 3]
                        for kc in range(DC):
                            P.mm(pp[:, :nt], W[:, kc, :], hT[:, kc, :nt], kc == 0, kc == DC - 1, [bW, b_h], [bp])
                        if sec == 0 or sec == 2:
                            P.cp("dve" if c % 2 else "act", S0[:, c, :nt], pp[:, :nt], [bp], [b_stg[sec % 2]]) if False else \
                                P.act(S0[:, c, :nt], pp[:, :nt], AF.Identity, [bp], [b_stg[sec % 2]])
                        elif sec == 3:
                            P.act(S0[:, c, :nt], pp[:, :nt], AF.Sigmoid, [bp], [b_stg[sec % 2]])
                        else:
                            hd = c // CPH
                            P.tt("dve", stg[0][:, c, :nt], pp[:, :nt], ksc[:, 0, hd, :nt], ALU.mult, [bp, b_ksc], [b_stg[0]])
                            P.tt("dve", stg[1][:, c, :nt], pp[:, :nt], ksc[:, 1, hd, :nt], ALU.mult, [bp, b_ksc], [b_stg[1]])
                    if sec == 1:
                        for d in range(2):
                            P.dma("sp", self.mk[d].rearrange("c p t -> p c t")[:, :, t0:t0 + nt], stg[d][:, :, :nt],
                                  "ms%d" % d, reads=[b_stg[d]])
                    else:
                        P.dma("sp", outs[sec].rearrange("c p t -> p c t")[:, :, t0:t0 + nt], S0[:, :, :nt],
                              "ms%d" % (sec % 2), reads=[b_stg[sec % 2]])
            P.barrier()
            P.flush()

    def mixer_phase(self, b, li, kind, cur, dst, ctx_out):
        if kind == 1:
            return self.pool_phase(b, li, cur, dst, ctx_out)
        if kind == 0:
            self.hg_proj_phase(b, li, cur)
            gla_phase(self, dict(nh=self.cfg.DC, nk=1, nv=1, ones=0, CL=32, nlf=1, q=self.hq, k=self.hk, lf=self.hlf,
                                 v=self.hv, out=self.ho, dbuf=(os.environ.get("E1") is None)))
            return self.hg_readout_phase(b, li, cur, dst, ctx_out)
        if kind == 2:
            H = self.cfg.ml_heads
            self.ml_proj_phase(b, li, cur)
            if getattr(self.cfg, "dbg", None) == "proj":
                raise StopBuild()
            gla_phase(self, dict(nh=H, nk=2, nv=2, ones=int(os.environ.get("E2", "1")), CL=128, nlf=1, q=self.mq, k=self.mk, lf=self.mlf,
                                 v=self.mv, out=self.mo, dbuf=False))
            if getattr(self.cfg, "dbg", None) == "scan":
                raise StopBuild()
            return self.readout_phase(b, li, cur, dst, ctx_out, self.mo, self.mog, self.ml_ngT, self.ml_wob, self.cfg.DC // H)
        if kind == 3:
            self.s5_scan_phase(b, li, cur)
            return self.s5_glu_phase(b, li, cur, dst, ctx_out)
        raise NotImplementedError

    def s5_scan_phase(self, b, li, cur):
        cfg, P = self.cfg, self.P
        D, DC, TC, TT, T = cfg.D, cfg.DC, cfg.TC, cfg.TT, cfg.T
        NT = 512
        s = 1
        PI = float(np.pi)
        MAGIC = 12582912.0
        tiles = [(0, TC)] + [(TC + i * NT, NT) for i in range(T // NT)]
        segs = [(0, TC), (TC, TT)]
        NCH = 4
        with contextlib.ExitStack() as st:
            u32 = self.sb(st, "u32", [128, TT], F32)
            ub = self.sb(st, "ub", [128, TT], BF16)
            yacc = self.sb(st, "yacc", [128, TT], F32)
            gy = self.sb(st, "gyo", [128, TT], BF16)
            iota = self.sb(st, "iota", [128, 512], F32)
            lam = self.sb(st, "lam", [128, 3, 2, DC, 4], F32)
            dsk = self.sb(st, "dsk", [128, DC], F32)
            tab = [[self.sb(st, "tab%d_%d" % (j, k), [128, 512], F32) for k in range(4)] for j in range(4)]
            Bf = [self.sb(st, "Bf%d" % i, [128, 4, 128], F32) for i in range(2)]
            Cf = [self.sb(st, "Cf%d" % i, [128, 4, 128], F32) for i in range(2)]
            Bb = [self.sb(st, "Bb%d" % i, [128, 4, 128], BF16) for i in range(2)]
            Cb = [self.sb(st, "Cb%d" % i, [128, 4, 128], BF16) for i in range(2)]
            lp = self.sb(st, "lp", [128, 4, 16], F32)
            x0 = self.sb(st, "x0", [128, 4, 2], F32)
            tmp = [[self.sb(st, "s5t%d_%d" % (j, k), [128, 512], F32) for k in range(6)] for j in range(NCH)]
            xb = [[self.sb(st, "s5x%d_%d" % (j, k), [128, 512], BF16) for k in range(2)] for j in range(NCH)]
            b_u, b_ub, b_y, b_gy, b_c, b_lam = Buf(), Buf(), Buf(), Buf(), Buf(), Buf()
            b_tab = bufs(4)
            b_B, b_C, b_lp = Buf(), Buf(), bufs(4)
            b_x0 = bufs(4)
            b_tmp = [bufs(6) for _ in range(NCH)]
            b_xb = [bufs(2) for _ in range(NCH)]
            P.dma("sp", iota[:], self.s5_iota[:, :], "c0", writes=[b_c])
            P.dma("sp", lam[:].rearrange("p a d c j -> p (a d c j)"), self.s5_lam[:, :], "c1", writes=[b_lam])
            P.dma("sp", dsk[:], self.s5_dT[:, :], "c2", writes=[b_c])
            for fc in range(DC):
                if cur is not None:
                    P.dma("sp", u32[:], cur[fc * 128:(fc + 1) * 128, :], "s5u", writes=[b_u])
                else:
                    P.dma("sp", u32[:, 0:TC], self.ctxT.rearrange("(b c p) t -> b c p t", p=128, c=DC)[b][fc], "s5u", writes=[b_u])
                    P.dma("sp", u32[:, TC:TT], self.xT.rearrange("(b c p) t -> b c p t", p=128, c=DC)[b][fc], "s5u", writes=[b_u])
                for (a0, a1), col in zip(segs, (1, 0)):
                    P.ts("dve", u32[:, a0:a1], u32[:, a0:a1], self.mod_col(li, 4, fc, col), self.mod_col(li, 3, fc, col),
                         ALU.mult, ALU.add, [b_u, self.b_const], [b_u])
                for d in range(2):
                    for (a0, a1) in segs:
                        src = u32[:, a0:a1]
                        if d == 1:
                            src = src[:, ::-1]
                        P.cp("pool", ub[:, a0:a1], src, [b_u], [b_ub])
                    r0 = (d * DC + fc) * 128
                    for i in range(2):
                        P.dma("sp", Bf[i][:].rearrange("p j l -> p (j l)"), self.s5_Bb[i][r0:r0 + 128, :], "s5b%d" % i, writes=[b_B])
                        P.dma("sp", Cf[i][:].rearrange("p j l -> p (j l)"), self.s5_Cb[i][r0:r0 + 128, :], "s5c%d" % i, writes=[b_C])
                    P.cp("dve", Bb[0][:], Bf[0][:], [b_B], [b_B])
                    P.cp("dve", Bb[1][:], Bf[1][:], [b_B], [b_B])
                    P.cp("pool", Cb[0][:], Cf[0][:], [b_C], [b_C])
                    P.act(Cb[1][:], Cf[1][:], AF.Identity, [b_C], [b_C], scale=-1.0)
                    for j in range(4):
                        L_ = lp[:, j, :]
                        bl = b_lp[j]
                        lr = lam[:, 0, d, fc, j:j + 1]
                        lim = lam[:, 1, d, fc, j:j + 1]
                        ldt = lam[:, 2, d, fc, j:j + 1]
                        dt, th, rr = L_[:, 0:1], L_[:, 1:2], L_[:, 2:3]
                        P.act(dt, ldt, AF.Exp, [b_lam], [bl])
                        P.tt("dve", th, lim, dt, ALU.mult, [b_lam, bl], [bl])
                        P.tt("dve", rr, lr, dt, ALU.mult, [b_lam, bl], [bl])
                        P.act(rr, rr, AF.Exp, [bl], [bl])
                        Rc, Rs, Tr, Ti = tab[j]
                        bt = b_tab[j]
                        for (dstt, off) in ((Rs, 0.0), (Rc, PI / 2)):
                            P.ts("dve", Tr[:], iota[:], th, off, ALU.mult, ALU.add, [b_c, bl], [bt])
                            P.ts("dve", Ti[:], Tr[:], 1.0 / (2 * PI), MAGIC, ALU.mult, ALU.add, [bt], [bt])
                            P.ts("dve", Ti[:], Ti[:], -MAGIC, None, ALU.add, None, [bt], [bt])
                            P.stt("dve", Tr[:], Ti[:], -2 * PI, Tr[:], ALU.mult, ALU.add, [bt], [bt])
                            P.ts("dve", Tr[:], Tr[:], -PI, PI, ALU.max, ALU.min, [bt], [bt])
                            P.act(dstt[:], Tr[:], AF.Sin, [bt], [bt])
                        nr, ni, den, fr, fi, t1, t2 = (L_[:, k:k + 1] for k in range(3, 10))
                        P.tt("dve", nr, rr, Rc[:, 0:1], ALU.mult, [bl, bt], [bl])
                        P.ts("dve", nr, nr, -1.0, None, ALU.add, None, [bl], [bl])
                        P.tt("dve", ni, rr, Rs[:, 0:1], ALU.mult, [bl, bt], [bl])
                        P.tt("dve", den, lr, lr, ALU.mult, [b_lam], [bl])
                        P.tt("dve", t1, lim, lim, ALU.mult, [b_lam], [bl])
                        P.tt("dve", den, den, t1, ALU.add, [bl], [bl])
                        P.op("dve", lambda e, den=den: e.reciprocal(out=den, in_=den), [bl], [bl])
                        P.tt("dve", t1, nr, lr, ALU.mult, [bl, b_lam], [bl])
                        P.tt("dve", t2, ni, lim, ALU.mult, [bl, b_lam], [bl])
                        P.tt("dve", fr, t1, t2, ALU.add, [bl], [bl])
                        P.tt("dve", fr, fr, den, ALU.mult, [bl], [bl])
                        P.tt("dve", t1, ni, lr, ALU.mult, [bl, b_lam], [bl])
                        P.tt("dve", t2, nr, lim, ALU.mult, [bl, b_lam], [bl])
                        P.tt("dve", fi, t1, t2, ALU.subtract, [bl], [bl])
                        P.tt("dve", fi, fi, den, ALU.mult, [bl], [bl])
                        P.ts("dve", Tr[:], Rc[:], fr, None, ALU.mult, None, [bt, bl], [bt])
                        P.stt("dve", Tr[:], Rs[:], fi, Tr[:], ALU.mult, ALU.add, [bt, bl], [bt])
                        P.ts("dve", Ti[:], Rs[:], fr, None, ALU.mult, None, [bt, bl], [bt])
                        P.stt("dve", Ti[:], Rc[:], fi, Ti[:], ALU.mult, ALU.subtract, [bt, bl], [bt])
                        P.memset("dve", x0[:, j, :], 0.0, [b_x0[j]])
                    for (t0, nt) in tiles:
                        pY, bY = self.ps[7], self.psb[7]
                        def make_chain(j, t0=t0, nt=nt, pY=pY, bY=bY):
                            Rc, Rs, Tr, Ti = tab[j]
                            bt, bl = b_tab[j], b_lp[j]
                            rr = lp[:, j, 2:3]
                            pr, bpr = self.ps[j % 3 * 2], self.psb[j % 3 * 2]
                            pi_, bpi = self.ps[j % 3 * 2 + 1], self.psb[j % 3 * 2 + 1]
                            tm, btm = tmp[j], b_tmp[j]
                            rb = rr.to_broadcast([128, nt])

                            def sA():
                                P.mm(pr[:, :nt], Bb[0][:, j, :], ub[:, t0:t0 + nt], True, True, [b_B, b_ub], [bpr])
                                P.mm(pi_[:, :nt], Bb[1][:, j, :], ub[:, t0:t0 + nt], True, True, [b_B, b_ub], [bpi])

                            def sB():
                                P.tt("dve", tm[0][:, :nt], pr[:, :nt], Tr[:, :nt], ALU.mult, [bpr, bt], [btm[0]])
                                P.tt("dve", tm[1][:, :nt], pi_[:, :nt], Ti[:, :nt], ALU.mult, [bpi, bt], [btm[1]])
                                P.tt("dve", tm[2][:, :nt], pi_[:, :nt], Tr[:, :nt], ALU.mult, [bpi, bt], [btm[2]])
                                P.tt("dve", tm[3][:, :nt], pr[:, :nt], Ti[:, :nt], ALU.mult, [bpr, bt], [btm[3]])

                            def sC():
                                P.tt("pool", tm[0][:, :nt], tm[0][:, :nt], tm[1][:, :nt], ALU.subtract, [btm[0], btm[1]], [btm[0]])
                                P.tt("pool", tm[2][:, :nt], tm[2][:, :nt], tm[3][:, :nt], ALU.add, [btm[2], btm[3]], [btm[2]])

                            def sD():
                                P.scan(tm[4][:, :nt], rb, tm[0][:, :nt], x0[:, j, 0:1], ALU.mult, ALU.add, [bl, btm[0], b_x0[j]], [btm[4]])
                                P.scan(tm[5][:, :nt], rb, tm[2][:, :nt], x0[:, j, 1:2], ALU.mult, ALU.add, [bl, btm[2], b_x0[j]], [btm[5]])

                            def sE():
                                P.tt("pool", tm[0][:, :nt], tm[4][:, :nt], Rc[:, :nt], ALU.mult, [btm[4], bt], [btm[0]])
                                P.tt("pool", tm[1][:, :nt], tm[5][:, :nt], Rs[:, :nt], ALU.mult, [btm[5], bt], [btm[1]])
                                P.tt("dve", tm[2][:, :nt], tm[5][:, :nt], Rc[:, :nt], ALU.mult, [btm[5], bt], [btm[2]])
                                P.tt("dve", tm[3][:, :nt], tm[4][:, :nt], Rs[:, :nt], ALU.mult, [btm[4], bt], [btm[3]])

                            def sF():
                                P.tt("pool", xb[j][0][:, :nt], tm[0][:, :nt], tm[1][:, :nt], ALU.subtract, [btm[0], btm[1]], [b_xb[j][0]])
                                P.tt("dve", xb[j][1][:, :nt], tm[2][:, :nt], tm[3][:, :nt], ALU.add, [btm[2], btm[3]], [b_xb[j][1]])
                                P.tt("dve", x0[:, j, 0:1], tm[0][:, nt - 1:nt], tm[1][:, nt - 1:nt], ALU.subtract, [btm[0], btm[1]], [b_x0[j]])
                                P.tt("dve", x0[:, j, 1:2], tm[2][:, nt - 1:nt], tm[3][:, nt - 1:nt], ALU.add, [btm[2], btm[3]], [b_x0[j]])

                            def sG():
                                P.mm(pY[:, :nt], Cb[0][:, j, :], xb[j][0][:, :nt], j == 0, False, [b_C, b_xb[j][0]], [bY])
                                P.mm(pY[:, :nt], Cb[1][:, j, :], xb[j][1][:, :nt], False, j == 3, [b_C, b_xb[j][1]], [bY])
                            return [sA, sB, sC, sD, sE, sF, sG]

                        for j0 in (0, 2):
                            pair = [make_chain(j0), make_chain(j0 + 1)]
                            for si in range(len(pair[0])):
                                for ch in pair:
                                    ch[si]()
                        if d == 0:
                            P.cp("dve", yacc[:, t0:t0 + nt], pY[:, :nt], [bY], [b_y])
                        else:
                            a0, a1 = (0, TC) if t0 < TC else (TC, TT)
                            n0 = a0 + a1 - (t0 + nt)
                            dst = yacc[:, n0:n0 + nt][:, ::-1]
                            P.tt("dve", dst, dst, pY[:, :nt], ALU.add, [bY, b_y], [b_y])
                for (a0, a1) in [(0, TC)] + [(TC + i * 1024, TC + min(T, (i + 1) * 1024)) for i in range((T + 1023) // 1024)]:
                    a1 = min(a1, TT)
                    ya, ua = yacc[:, a0:a1], u32[:, a0:a1]
                    P.stt("dve", ya, ua, dsk[:, fc:fc + 1], ya, ALU.mult, ALU.add, [b_u, b_y, b_c], [b_y])
                    P.act(ua, ya, AF.Square, [b_y, b_u], [b_u])
                    P.ts("dve", ua, ua, 0.044715, 1.0, ALU.mult, ALU.add, [b_u], [b_u])
                    P.tt("dve", ua, ua, ya, ALU.mult, [b_u, b_y], [b_u])
                    P.act(ua, ua, AF.Sigmoid, [b_u], [b_u], scale=float(2.0 * np.sqrt(2.0 / np.pi)))
                    P.tt("dve", gy[:, a0:a1], ya, ua, ALU.mult, [b_y, b_u], [b_gy])
                P.dma("sp", self.sgy[fc, :, :], gy[:], "s5o", reads=[b_gy])
            P.barrier()
            P.flush()

    def s5_glu_phase(self, b, li, cur, dst, ctx_out):
        cfg, P = self.cfg, self.P
        D, DC, TC, NT = cfg.D, cfg.DC, cfg.TC, cfg.NT
        s = 1
        with contextlib.ExitStack() as st:
            xt = [self.sb(st, "xt%d" % i, [128, DC, NT], F32) for i in range(2)]
            gt = self.sb(st, "gyT", [128, DC, NT], BF16)
            ws = [self.sb(st, "gws%d" % i, [128, DC, 128], BF16) for i in range(4)]
            sg = [self.sb(st, "gsg%d" % i, [128, NT], F32) for i in range(2)]
            yv = [self.sb(st, "gyv%d" % i, [128, NT], F32) for i in range(2)]
            b_x = [bufs(DC) for _ in range(2)]
            b_gt = Buf()
            b_ws, b_sg, b_yv = bufs(4), bufs(2), bufs(2)
            pn = PN(self, st, NT)
            wv = self.s5_wglub.rearrange("(r p) (k n) -> r p k n", p=128, n=128)
            wi = 0
            tiles = cfg.tiles if ctx_out else cfg.tiles[1:]
            for ti, (t0, nt) in enumerate(tiles):
                col = 0 if t0 >= TC else 1
                X, bX = xt[ti % 2], b_x[ti % 2]
                P.dma("sp", X[:, :, :nt], self.src_ap(b, cur, t0, nt), "xl%d" % (ti % 2), writes=bX)
                P.dma("sp", gt[:, :, :nt], self.sgy.rearrange("c p t -> p c t")[:, :, t0:t0 + nt], "rg", writes=[b_gt])
                pn.begin(li, s, X, bX, nt, col)
                for m in range(DC):
                    pa, ba = self.ps[m % 2], self.psb[m % 2]
                    pg, bg = self.ps[2 + m % 2], self.psb[2 + m % 2]
                    for (oc, pp, bp) in ((m, pa, ba), (DC + m, pg, bg)):
                        W, bW = ws[wi % 4], b_ws[wi % 4]
                        P.dma("sp", W[:], wv[oc], "gw%d" % (wi % 4), writes=[bW])
                        wi += 1
                        for kc in range(DC):
                            P.mm(pp[:, :nt], W[:, kc, :], gt[:, kc, :nt], kc == 0, kc == DC - 1, [bW, b_gt], [bp])
                    P.act(sg[m % 2][:, :nt], pg[:, :nt], AF.Sigmoid, [bg], [b_sg[m % 2]])
                    P.tt("dve", yv[m % 2][:, :nt], pa[:, :nt], sg[m % 2][:, :nt], ALU.mult, [ba, b_sg[m % 2]], [b_yv[m % 2]])
                    pn.add(m, yv[m % 2][:, :nt], [b_yv[m % 2]], self.mod_col(li, 5, m, col))
                pn.finish()
                P.dma("sp", dst.rearrange("(c p) t -> p c t", p=128)[:, :, t0:t0 + nt], X[:, :, :nt],
                      "xst%d" % (ti % 2), reads=bX)
            P.barrier()
            P.flush()

    def hg_proj_phase(self, b, li, cur):
        cfg, P = self.cfg, self.P
        D, DC, TC, NT = cfg.D, cfg.DC, cfg.TC, cfg.NT
        s = 1
        with contextlib.ExitStack() as st:
            xt1 = self.sb(st, "xt", [128, DC, NT], F32)
            xt = [xt1, xt1]
            hT = self.sb(st, "hT", [128, DC, NT], BF16)
            ws = [self.sb(st, "hws%d" % i, [128, DC, 128], BF16) for i in range(3)]
            oq2 = [self.sb(st, "oq%d" % i, [128, DC, NT], BF16) for i in range(2)]
            oq = [oq2[i % 2] for i in range(5)]
            olf = [self.sb(st, "olf%d" % i, [128, DC, NT], F32) for i in range(2)]
            t1 = [self.sb(st, "ht1_%d" % i, [128, NT], F32) for i in range(2)]
            t2 = [self.sb(st, "ht2_%d" % i, [128, NT], F32) for i in range(2)]
            lg = self.sb(st, "lg", [128, 2, 4, DC], F32)
            oml = self.sb(st, "oml", [128, 2, DC], F32)
            den = self.sb(st, "lden", [128, 2, DC], F32)
            onec = self.sb(st, "onec", [128, 1], F32)
            b_x1 = bufs(DC)
            b_x = [b_x1, b_x1]
            b_h, b_lb = Buf(), Buf()
            b_ws, b_t1, b_t2 = bufs(3), bufs(2), bufs(2)
            b_oq2 = bufs(2)
            b_oq = [b_oq2[i % 2] for i in range(5)]
            b_olf = bufs(2)
            P.dma("sp", lg[:].rearrange("p d l c -> p (d l c)"), self.hg_lbT[:, :], "c0", writes=[b_lb])
            P.memset("dve", onec[:], 1.0, [b_lb])
            P.act(lg[:], lg[:], AF.Exp, [b_lb], [b_lb])
            P.cp("dve", den[:], lg[:, :, 0, :], [b_lb], [b_lb])
            for l in range(1, 4):
                P.tt("dve", den[:], den[:], lg[:, :, l, :], ALU.add, [b_lb], [b_lb])
            P.cp("dve", oml[:], lg[:, :, 0, :], [b_lb], [b_lb])
            for l in range(1, li + 1):
                P.tt("dve", oml[:], oml[:], lg[:, :, l, :], ALU.add, [b_lb], [b_lb])
            P.op("dve", lambda e: e.reciprocal(out=den[:], in_=den[:]), [b_lb], [b_lb])
            P.tt("dve", oml[:], oml[:], den[:], ALU.mult, [b_lb], [b_lb])
            P.ts("dve", oml[:], oml[:], -1.0, 1.0, ALU.mult, ALU.add, [b_lb], [b_lb])
            wv = self.hg_winb.rearrange("(r p) (k n) -> r p k n", p=128, n=128)
            wi = 0
            outs = [self.hq, self.hv, self.hgt, self.hk[0], self.hk[1]]
            for ti, (t0, nt) in enumerate(cfg.tiles):
                col = 0 if t0 >= TC else 1
                X, bX = xt[ti % 2], b_x[ti % 2]
                P.dma("sp", X[:, :, :nt], self.src_ap(b, cur, t0, nt), "xl%d" % (ti % 2), writes=bX)
                for c in range(DC):
                    P.ts("pool", hT[:, c, :nt], X[:, c, :nt], self.mod_col(li, 3 * s + 1, c, col),
                         self.mod_col(li, 3 * s + 0, c, col), ALU.mult, ALU.add, [bX[c], self.b_const], [b_h])
                for sec in range(5):
                    for c in range(DC):
                        oc = sec * DC + c
                        W, bW = ws[wi % 3], b_ws[wi % 3]
                        P.dma("sp", W[:], wv[oc], "hw%d" % (wi % 3), writes=[bW])
                        wi += 1
                        pp, bp = self.ps[oc % 4], self.psb[oc % 4]
                        for kc in range(DC):
                            P.mm(pp[:, :nt], W[:, kc, :], hT[:, kc, :nt], kc == 0, kc == DC - 1, [bW, b_h], [bp])
                        if sec == 0 or sec == 2:
                            P.act(oq[sec][:, c, :nt], pp[:, :nt], AF.Silu, [bp], [b_oq[sec]])
                        elif sec == 1:
                            P.cp("dve", oq[1][:, c, :nt], pp[:, :nt], [bp], [b_oq[1]])
                        else:
                            d = sec - 3
                            T1, T2 = t1[c % 2], t2[c % 2]
                            P.act(T1[:, :nt], pp[:, :nt], AF.Sigmoid, [bp], [b_t1[c % 2]], scale=-1.0)
                            P.ts("dve", T2[:, :nt], T1[:, :nt], oml[:, d, c:c + 1], None, ALU.mult, None,
                                 [b_t1[c % 2], b_lb], [b_t2[c % 2]])
                            P.cp("pool", oq[sec][:, c, :nt], T2[:, :nt], [b_t2[c % 2]], [b_oq[sec]])
                            P.act(olf[d][:, c, :nt], T2[:, :nt], AF.Ln, [b_t2[c % 2], b_lb], [b_olf[d]], scale=-1.0, bias=onec[:, 0:1])
                    dstT = outs[sec].rearrange("c p t -> p c t")[:, :, t0:t0 + nt]
                    P.dma("sp", dstT, oq[sec][:, :, :nt], "hs%d" % sec, reads=[b_oq[sec]])
                    if sec >= 3:
                        P.dma("sp", self.hlf[sec - 3].rearrange("c p t -> p c t")[:, :, t0:t0 + nt], olf[sec - 3][:, :, :nt],
                              "hl%d" % (sec - 3), reads=[b_olf[sec - 3]])
            P.barrier()
            P.flush()

    def hg_readout_phase(self, b, li, cur, dst, ctx_out):
        self.readout_phase(b, li, cur, dst, ctx_out, self.ho, self.hgt, self.hg_ngT, self.hg_wob, 1)

    def readout_phase(self, b, li, cur, dst, ctx_out, o_dirs, gate_s, ng_in, wob, hc):
        cfg, P = self.cfg, self.P
        D, DC, TC, NT = cfg.D, cfg.DC, cfg.TC, cfg.NT
        s = 1
        with contextlib.ExitStack() as st:
            xt1 = self.sb(st, "xt", [128, DC, NT], F32)
            xt = [xt1, xt1]
            o0 = self.sb(st, "ro0", [128, DC, NT], F32)
            o1 = self.sb(st, "ro1", [128, DC, NT], F32)
            gt = self.sb(st, "rgt", [128, DC, NT], BF16)
            yT = self.sb(st, "ryT", [128, DC, NT], BF16)
            sq = [self.sb(st, "rsq%d" % i, [128, hc, NT], BF16) for i in range(2)]
            rs = [self.sb(st, "rrs%d" % i, [128, NT], F32) for i in range(2)]
            ws = [self.sb(st, "rws%d" % i, [128, DC, 128], BF16) for i in range(3)]
            ng = self.sb(st, "rng", [128, DC], F32)
            b_x1 = bufs(DC)
            b_x = [b_x1, b_x1]
            b_o0, b_o1, b_gt = bufs(DC), bufs(DC), Buf()
            b_y = bufs(DC)
            b_sq, b_rs, b_ws = bufs(2), bufs(2), bufs(3)
            b_ng = Buf()
            pn = PN(self, st, NT)
            P.dma("sp", ng[:], ng_in[:, :], "c0", writes=[b_ng])
            wv = wob.rearrange("(r p) (k n) -> r p k n", p=128, n=128)
            wi = 0
            tiles = cfg.tiles if ctx_out else cfg.tiles[1:]
            for ti, (t0, nt) in enumerate(tiles):
                col = 0 if t0 >= TC else 1
                X, bX = xt[ti % 2], b_x[ti % 2]
                P.dma("sp", X[:, :, :nt], self.src_ap(b, cur, t0, nt), "xl%d" % (ti % 2), writes=bX)
                P.dma("sp", o0[:, :, :nt], o_dirs[0].rearrange("c p t -> p c t")[:, :, t0:t0 + nt], "ro0", writes=b_o0)
                P.dma("sp", o1[:, :, :nt], o_dirs[1].rearrange("c p t -> p c t")[:, :, t0:t0 + nt], "ro1", writes=b_o1)
                P.dma("sp", gt[:, :, :nt], gate_s.rearrange("c p t -> p c t")[:, :, t0:t0 + nt], "rg", writes=[b_gt])
                for hd in range(DC // hc):
                    r = hd % 2
                    for cc in range(hc):
                        c = hd * hc + cc
                        P.tt("dve", o0[:, c, :nt], o0[:, c, :nt], o1[:, c, :nt], ALU.add, [b_o0[c], b_o1[c]], [b_o0[c]])
                        P.act(sq[r][:, cc, :nt], o0[:, c, :nt], AF.Square, [b_o0[c]], [b_sq[r]])
                    pp, bp = self.ps[hd % 4], self.psb[hd % 4]
                    for cc in range(hc):
                        P.mm(pp[:, :nt], self.ones[:], sq[r][:, cc, :nt], cc == 0, cc == hc - 1, [b_sq[r], self.b_const], [bp])
                    P.ts("dve", rs[r][:, :nt], pp[:, :nt], 1.0 / (128 * hc), RMS_EPS, ALU.mult, ALU.add, [bp], [b_rs[r]])
                    P.act(rs[r][:, :nt], rs[r][:, :nt], AF.Sqrt, [b_rs[r]], [b_rs[r]])
                    P.op("dve", lambda e, a=rs[r], nt=nt: e.reciprocal(out=a[:, :nt], in_=a[:, :nt]), [b_rs[r]], [b_rs[r]])
                    for cc in range(hc):
                        c = hd * hc + cc
                        P.tt("pool", o0[:, c, :nt], o0[:, c, :nt], rs[r][:, :nt], ALU.mult, [b_o0[c], b_rs[r]], [b_o0[c]])
                        P.stt("dve", yT[:, c, :nt], o0[:, c, :nt], ng[:, c:c + 1], gt[:, c, :nt], ALU.mult, ALU.mult,
                              [b_o0[c], b_ng, b_gt], [b_y[c]])
                pn.begin(li, s, X, bX, nt, col)
                for m in range(DC):
                    W, bW = ws[wi % 3], b_ws[wi % 3]
                    P.dma("sp", W[:], wv[m], "rw%d" % (wi % 3), writes=[bW])
                    wi += 1
                    po, bo = self.ps[4 + m % 2], self.psb[4 + m % 2]
                    for kc in range(DC):
                        P.mm(po[:, :nt], W[:, kc, :], yT[:, kc, :nt], kc == 0, kc == DC - 1, [bW, b_y[kc]], [bo])
                    pn.add(m, po[:, :nt], [bo], self.mod_col(li, 5, m, col))
                pn.finish()
                P.dma("sp", dst.rearrange("(c p) t -> p c t", p=128)[:, :, t0:t0 + nt], X[:, :, :nt],
                      "xst%d" % (ti % 2), reads=bX)
            if not ctx_out:
                pass
            P.barrier()
            P.flush()

    def pool_phase(self, b, li, cur, dst, ctx_out):
        cfg = self.cfg
        P = self.P
        D, DC, TC, NT = cfg.D, cfg.DC, cfg.TC, cfg.NT
        NJ = 2
        CPG = DC // 4
        W = D // 4
        s = 1
        with contextlib.ExitStack() as st:
            xt = [self.sb(st, "xt%d" % i, [128, DC, NT], F32) for i in range(2)]
            PAD = 8
            HW = NT + 2 * PAD * (NT // GRID_W)
            hf = self.sb(st, "hf", [128, DC, HW], F32)
            pdT = self.sb(st, "pdT", [128, DC, NT], BF16)
            pw = self.sb(st, "pw", [128, 4, CPG, W], BF16)
            psc = self.sb(st, "psc", [128, DC], F32)
            sg = self.sb(st, "sg", [128, 2, DC], F32)
            invl = self.sb(st, "invl", [128, 4, GRID_W], F32)
            invc = self.sb(st, "invc", [128, 4, TC], F32)
            tmp = {e: [self.sb(st, "ptmp_%s%d" % (e, i), [128, HW], F32) for i in range(2)] for e in ("dve", "pool")}
            b_tmp = {e: bufs(2) for e in ("dve", "pool")}
            b_xc = [bufs(DC) for _ in range(2)]
            b_hf, b_pd = bufs(DC), bufs(DC)
            b_pw, b_psc, b_sg, b_inv = Buf(), Buf(), Buf(), Buf()
            pn = PN(self, st, NT)
            P.dma("sp", pw[:].rearrange("p g k o -> p (g k o)"), self.pool_wBb[:, :], "c0", writes=[b_pw])
            P.dma("sp", psc[:], self.pool_scT[:, :], "c1", writes=[b_psc])
            P.dma("sp", invl[:].rearrange("p g w -> p (g w)"), self.pool_inv_lat[:, :], "c2", writes=[b_inv])
            P.dma("sp", invc[:].rearrange("p g w -> p (g w)"), self.pool_inv_ctx[:, :], "c3", writes=[b_inv])
            for ci, col in enumerate((0, 1)):
                o0 = ((li * 9 + 5) * DC) * NJ
                gv = self.modT[:, o0:o0 + DC * NJ].rearrange("p (c j) -> p c j", j=NJ)[:, :, col]
                P.tt("dve", sg[:, ci, :], gv, psc[:], ALU.mult, [self.b_const, b_psc], [b_sg])
            tiles = cfg.tiles if ctx_out else cfg.tiles[1:]
            for ti, (t0, nt) in enumerate(tiles):
                lat = t0 >= TC
                col = 0 if lat else 1
                ci = 0 if lat else 1
                R = GRID_W if lat else TC
                inv = invl if lat else invc
                X = xt[ti % 2]
                bX = b_xc[ti % 2]
                P.dma("sp", X[:, :, :nt], self.src_ap(b, cur, t0, nt), "xl%d" % (ti % 2), writes=bX)
                RP = R + 2 * PAD
                rows = nt // R
                if ti < 2:
                    P.memset("dve", hf[:], 0.0, b_hf)
                for c in range(DC):
                    eng = "dve" if c % 2 == 0 else "pool"

                    def vp(ap):
                        return ap[:, :rows * RP].rearrange("p (r w) -> p r w", w=RP)

                    def v(ap):
                        return ap.rearrange("p (r w) -> p r w", w=R)
                    hp = vp(hf[:, c, :])
                    hin = hp[:, :, PAD:PAD + R]
                    P.ts(eng, hin, v(X[:, c, :nt]), self.mod_col(li, 3 * s + 1, c, col),
                         self.mod_col(li, 3 * s + 0, c, col), ALU.mult, ALU.add, [bX[c], self.b_const], [b_hf[c]])
                    wi = c // CPG
                    w = cfg.pool_windows[wi]
                    A, Bt = vp(tmp[eng][0]), vp(tmp[eng][1])
                    bA, bB = b_tmp[eng]
                    P.tt(eng, A[:, :, 1:RP], hp[:, :, 1:RP], hp[:, :, 0:RP - 1], ALU.add, [b_hf[c]], [bA])
                    srcv, bs, dstv, bd = A, bA, Bt, bB
                    d = 1
                    while 4 * d <= w:
                        P.tt(eng, dstv[:, :, d:RP - d], srcv[:, :, 0:RP - 2 * d], srcv[:, :, 2 * d:RP], ALU.add, [bs], [bd])
                        srcv, bs, dstv, bd = dstv, bd, srcv, bs
                        d *= 2
                    ib = inv[:, wi, :].unsqueeze(1).to_broadcast([128, rows, R])
                    P.tt(eng, srcv[:, :, PAD:PAD + R], srcv[:, :, PAD:PAD + R], ib, ALU.mult, [bs, b_inv], [bs])
                    P.tt(eng, v(pdT[:, c, :nt]), srcv[:, :, PAD:PAD + R], hin, ALU.subtract, [bs, b_hf[c]], [b_pd[c]])
                pn.begin(li, s, X, bX, nt, col)
                for m in range(DC):
                    g = m // CPG
                    po, bo = self.ps[4 + m % 2], self.psb[4 + m % 2]
                    for kk in range(CPG):
                        kc = g * CPG + kk
                        P.mm(po[:, :nt], pw[:, g, kk, (m % CPG) * 128:(m % CPG + 1) * 128], pdT[:, kc, :nt],
                             kk == 0, kk == CPG - 1, [b_pw, b_pd[kc]], [bo])
                    pn.add(m, po[:, :nt], [bo, b_sg], sg[:, ci, m:m + 1])
                pn.finish()
                P.dma("sp", dst.rearrange("(c p) t -> p c t", p=128)[:, :, t0:t0 + nt], X[:, :, :nt],
                      "xst%d" % (ti % 2), reads=bX)
            if not ctx_out:
                pass
            P.barrier()
            P.flush()


def gla_phase(B, spec):
    cfg, P = B.cfg, B.P
    TC, TT = cfg.TC, cfg.TT
    nh, nk, nv, ones, CL, nlf = spec["nh"], spec["nk"], spec["nv"], spec["ones"], spec["CL"], spec["nlf"]
    nvt = nv + ones
    NSUB = 128 // CL
    BL = 128
    nblk = TT // BL
    ncb = TC // BL
    NL = 2 if spec.get("dbuf", True) else 1
    order = [list(range(nblk)), list(range(ncb - 1, -1, -1)) + list(range(nblk - 1, ncb - 1, -1))]
    with contextlib.ExitStack() as st:
        ident = B.sb(st, "ident", [128, 128], BF16)
        identf = B.sb(st, "identf", [128, 128], F32)
        cmask = B.sb(st, "cmask", [128, 128], F32)
        rmask = B.sb(st, "rmask", [128, NSUB], F32)
        reset = B.sb(st, "reset", [128, 128], F32)
        b_cst = Buf()
        P.dma("sp", identf[:], B.c_ident[:, :], "c0", writes=[b_cst])
        P.dma("sp", cmask[:], B.c_cmask[CL][:, :], "c1", writes=[b_cst])
        P.dma("sp", rmask[:], B.c_rmask[CL][:, :], "c2", writes=[b_cst])
        P.dma("sp", reset[:], B.c_reset[CL][:, :], "c3", writes=[b_cst])
        P.cp("dve", ident[:], identf[:], [b_cst], [b_cst])
        D_ = {}
        R2 = 2
        for d in range(2):
            t = {}
            t["Lq"] = [B.sb(st, "Lq%d_%d" % (d, i), [128, nh * nk, BL], BF16) for i in range(NL)]
            t["Lk"] = [B.sb(st, "Lk%d_%d" % (d, i), [128, nh * nk, BL], BF16) for i in range(NL)]
            t["Ll"] = [B.sb(st, "Ll%d_%d" % (d, i), [128, nh * nlf, BL], F32) for i in range(NL)]
            t["Lv"] = [B.sb(st, "Lv%d_%d" % (d, i), [128, nh * nv, BL], BF16) for i in range(NL)]
            t["bL"] = [bufs(4) for _ in range(NL)]
            t["Oo"] = [B.sb(st, "Oo%d_%d" % (d, i), [128, nh * nv, BL], F32) for i in range(NL)]
            t["bOo"] = bufs(NL)
            shapes = {"b": ([128, nk, BL], F32), "eb": ([128, nk, BL], F32), "enb": ([128, nk, BL], F32),
                      "qe": ([128, nk, BL], BF16), "ke": ([128, nk, BL], BF16), "vb": ([128, nv, BL], BF16),
                      "attm": ([128, BL], BF16), "vtok": ([128, nvt * 128], BF16), "ktok": ([128, NSUB, nk * 128], BF16),
                      "ud": ([128, nvt * 128], F32), "rc": ([128, BL], F32)}
            for nm, (shp, dt) in shapes.items():
                t[nm] = [B.sb(st, "g%s%d_%d" % (nm, d, i), shp, dt) for i in range(R2)]
                t["B" + nm] = bufs(R2)
            t["S"] = B.sb(st, "gS%d" % d, [128, nh, nk, nvt * 128], F32)
            t["Sb"] = B.sb(st, "gSb%d" % d, [128, nh, NSUB, nk * nvt * 128], BF16)
            t["bS"] = bufs(nh)
            t["bSb"] = bufs(nh)
            P.memset("dve", t["S"][:], 0.0, t["bS"])
            P.memset("pool", t["Sb"][:], 0.0, t["bSb"])
            if ones:
                for i in range(R2):
                    P.memset("pool", t["vtok"][i][:, nv * 128:], 1.0, [t["Bvtok"][i]])
            t["pb"] = d * 4
            t["bAatt"] = t["bAkt"] = B.psb[d * 4]
            t["bBv"] = B.psb[d * 4 + 1]
            t["bO"] = [B.psb[d * 4 + 1], B.psb[d * 4 + 1]]
            t["bU"] = [B.psb[d * 4 + 2], B.psb[d * 4 + 3]]
            t["oi"] = 0
            D_[d] = t
        STG = int(os.environ.get("E4", "9"))
        for step in range(min(nblk, int(os.environ.get("E3", "100000")))):
            for d in range(2):
                t = D_[d]
                blk = order[d][step]
                t0 = blk * BL
                rev = (d == 1)
                li = step % NL
                Lq, Lk, Ll, Lv = t["Lq"][li], t["Lk"][li], t["Ll"][li], t["Lv"][li]
                bLq, bLk, bLl, bLv = t["bL"][li]
                P.dma("sp", Lq[:], spec["q"].rearrange("c p t -> p c t")[:, :, t0:t0 + BL], "gq%d%d" % (d, li), writes=[bLq])
                P.dma("sp", Lk[:], spec["k"][d].rearrange("c p t -> p c t")[:, :, t0:t0 + BL], "gk%d%d" % (d, li), writes=[bLk])
                P.dma("sp", Ll[:], spec["lf"][d].rearrange("c p t -> p c t")[:, :, t0:t0 + BL], "gl%d%d" % (d, li), writes=[bLl])
                P.dma("sp", Lv[:], spec["v"].rearrange("c p t -> p c t")[:, :, t0:t0 + BL], "gv%d%d" % (d, li), writes=[bLv])
                Oo, bOo = t["Oo"][li], t["bOo"][li]

                def rv(ap, rev=rev):
                    return ap[:, ::-1] if rev else ap
                pb = t["pb"]
                pA, pBk, pU = B.ps[pb], B.ps[pb + 1], [B.ps[pb + 2], B.ps[pb + 3]]
                bAatt, bAkt, bBv, bO, bU = t["bAatt"], t["bAkt"], t["bBv"], t["bO"], t["bU"]
                S, Sb = t["S"], t["Sb"]
                def make_head(h, t=t, rv=rv, Lq=Lq, Lk=Lk, Ll=Ll, Lv=Lv, bLq=bLq, bLk=bLk, bLl=bLl, bLv=bLv, Oo=Oo, bOo=bOo, pb=pb):
                    r = h % R2
                    g = {nm: t[nm][r] for nm in ("b", "eb", "enb", "qe", "ke", "vb", "attm", "vtok", "ktok", "ud", "rc")}
                    G = {nm: t["B" + nm][r] for nm in g}
                    bS, bSb = t["bS"][h], t["bSb"][h]
                    S, Sb = t["S"], t["Sb"]
                    b_, eb, enb, qe, ke, vb = g["b"], g["eb"], g["enb"], g["qe"], g["ke"], g["vb"]
                    attm, vtok, ktok, ud, rc = g["attm"], g["vtok"], g["ktok"], g["ud"], g["rc"]
                    if nk == 1:
                        ia, ib = pb + 2 * r, pb + 2 * r + 1
                        pA, pBk = B.ps[ia], B.ps[ib]
                        bA, bBk = B.psb[ia], B.psb[ib]
                        pUr = [pA[:, 384:512]]
                        bU = [bA]
                    else:
                        pA, pBk = B.ps[pb], B.ps[pb + 1]
                        bA, bBk = B.psb[pb], B.psb[pb + 1]
                        pUr = [B.ps[pb + 2][:, 0:nvt * 128], B.ps[pb + 3][:, 0:nvt * 128]]
                        bU = [B.psb[pb + 2], B.psb[pb + 3]]

                    def s_prep():
                        for kc in range(nk):
                            lfi = h * nlf + (kc if nlf == nk else 0)
                            if kc == 0 or nlf == nk:
                                P.scan(b_[:, kc, :], reset[:], rv(Ll[:, lfi, :]), 0.0, ALU.mult, ALU.add, [b_cst, bLl], [G["b"]])
                            src_b = b_[:, kc if nlf == nk else 0, :]
                            P.act(eb[:, kc, :], src_b, AF.Exp, [G["b"]], [G["eb"]])
                            P.act(enb[:, kc, :], src_b, AF.Exp, [G["b"]], [G["enb"]], scale=-1.0)
                            P.tt("dve", qe[:, kc, :], rv(Lq[:, h * nk + kc, :]), eb[:, kc, :], ALU.mult, [bLq, G["eb"]], [G["qe"]])
                            P.tt("pool", ke[:, kc, :], rv(Lk[:, h * nk + kc, :]), enb[:, kc, :], ALU.mult, [bLk, G["enb"]], [G["ke"]])
                        for vc in range(nv):
                            P.cp("pool", vb[:, vc, :], rv(Lv[:, h * nv + vc, :]), [bLv], [G["vb"]])

                    def s_pe1():
                        for kc in range(nk):
                            P.mm(pA[:, 0:128], ke[:, kc, :], qe[:, kc, :], kc == 0, kc == nk - 1, [G["ke"], G["qe"]], [bA])
                        for kc in range(nk):
                            P.mm(pA[:, 128 + kc * 128:256 + kc * 128], ke[:, kc, :], ident[:], True, True, [G["ke"], b_cst], [bA])
                        for vc in range(nv):
                            P.mm(pBk[:, vc * 128:(vc + 1) * 128], vb[:, vc, :], ident[:], True, True, [G["vb"], b_cst], [bBk])

                    def s_evac():
                        P.tt("dve", attm[:], pA[:, 0:128], cmask[:], ALU.mult, [bA, b_cst], [G["attm"]])
                        P.act(vtok[:, 0:nv * 128], pBk[:, 0:nv * 128], AF.Identity, [bBk], [G["vtok"]])
                        for j in range(NSUB):
                            P.act(ktok[:, j, :], pA[:, 128:128 + nk * 128], AF.Identity, [bA, b_cst], [G["ktok"]],
                                  scale=rmask[:, j:j + 1])

                    def chain(j):
                        for kc in range(nk):
                            P.mm(pUr[kc], ktok[:, j, kc * 128:(kc + 1) * 128], vtok[:], True, True,
                                 [G["ktok"], G["vtok"]], [bU[kc]])
                        for kc in range(nk):
                            dec = eb[:, kc, (j + 1) * CL - 1:(j + 1) * CL]
                            P.act(ud[:], pUr[kc], AF.Identity, [bU[kc], G["eb"]], [G["ud"]], scale=dec)
                            P.stt("dve", S[:, h, kc, :], S[:, h, kc, :], dec, ud[:], ALU.mult, ALU.add,
                                  [bS, G["ud"], G["eb"]], [bS])
                            jn = (j + 1) % NSUB
                            P.cp("pool", Sb[:, h, jn, kc * nvt * 128:(kc + 1) * nvt * 128], S[:, h, kc, :], [bS], [bSb])

                    def s_out():
                        vorder = ([nv] if ones else []) + list(range(nv))
                        for vc in vorder:
                            oi = t["oi"] % 2
                            t["oi"] += 1
                            pO = pBk[:, 256 + oi * 128:384 + oi * 128]
                            P.mm(pO, vtok[:, vc * 128:(vc + 1) * 128], attm[:], True, False, [G["vtok"], G["attm"]], [bBk])
                            n_in = NSUB * nk
                            ii = 0
                            for j in range(NSUB):
                                for kc in range(nk):
                                    ii += 1
                                    o0 = kc * nvt * 128 + vc * 128
                                    P.mm(pBk[:, 256 + oi * 128 + j * CL:256 + oi * 128 + (j + 1) * CL],
                                         Sb[:, h, j, o0:o0 + 128], qe[:, kc, j * CL:(j + 1) * CL], False, ii == n_in,
                                         [bSb, G["qe"]], [bBk])
                            if ones and vc == nv:
                                P.act(rc[:], pO, AF.Abs, [bBk], [G["rc"]])
                                P.ts("dve", rc[:], rc[:], 1.0, None, ALU.max, None, [G["rc"]], [G["rc"]])
                                P.op("dve", lambda e, rc=rc: e.reciprocal(out=rc[:], in_=rc[:]), [G["rc"]], [G["rc"]])
                            elif ones:
                                P.tt("dve", rv(Oo[:, h * nv + vc, :]), pO, rc[:], ALU.mult, [bBk, G["rc"]], [bOo])
                            else:
                                P.cp("dve", rv(Oo[:, h * nv + vc, :]), pO, [bBk], [bOo])

                    stages = [s_prep, s_pe1, s_evac]
                    for j in range(NSUB - 1):
                        stages.append(lambda j=j: chain(j))
                    stages.append(s_out)
                    stages.append(lambda: chain(NSUB - 1))
                    return stages

                PW = 2 if nk == 1 else 1
                for h0 in range(0, nh, PW):
                    pair = [make_head(h) for h in range(h0, min(nh, h0 + PW))]
                    for si in range(len(pair[0])):
                        for stg in pair:
                            stg[si]()
                P.dma("sp", spec["out"][d].rearrange("c p t -> p c t")[:, :, t0:t0 + BL], Oo[:], "go%d%d" % (d, li), reads=[bOo])
        P.barrier()
        P.flush()


def fm_cols(v):
    v = np.asarray(v, np.float32)
    lead = v.shape[:-1]
    n = v.shape[-1] // 128
    v = v.reshape(lead + (n, 128))
    v = np.moveaxis(v, -1, 0)
    return np.ascontiguousarray(v.reshape(128, -1))


def pool_inv_table(n, windows):
    out = np.zeros((len(windows), n), np.float32)
    pos = np.arange(n)
    for i, w in enumerate(windows):
        lo = np.clip(pos - w // 2, 0, n - 1)
        hi = np.clip(pos + w - w // 2 - 1, 0, n - 1)
        out[i] = 1.0 / (hi - lo + 1)
    return out


def prep_shared(cfg, inp):
    D, F, DC, FC, L = cfg.D, cfg.F, cfg.DC, cfg.FC, cfg.L
    f32 = lambda a: np.asarray(a, np.float32)
    m = {}
    m["mod_w"] = np.ascontiguousarray(f32(inp["mod_w"]).reshape(L * D, 9 * D))
    m["mod_bT"] = fm_cols(f32(inp["mod_b"]))
    m["ln_gT"] = fm_cols(f32(inp["ln_g"]))
    m["ln_bT"] = fm_cols(f32(inp["ln_b"]))
    w13 = np.stack([f32(inp["ffn1_w13"]), f32(inp["ffn2_w13"])], 1)
    w13 = w13.reshape(L, 2, DC, 128, 2, FC, 128)
    w13 = w13.transpose(0, 1, 5, 3, 2, 4, 6)
    m["w13"] = np.ascontiguousarray(w13).reshape(L * 2 * FC * 128, DC * 256)
    w2 = np.stack([f32(inp["ffn1_w2"]), f32(inp["ffn2_w2"])], 1)
    w2 = w2.reshape(L, 2, FC, 128, DC, 128)
    w2 = w2.transpose(0, 1, 4, 3, 2, 5)
    m["w2"] = np.ascontiguousarray(w2).reshape(L * 2 * DC * 128, FC * 128)
    if 1 in cfg.kinds:
        W = D // 4
        CPG = DC // 4
        pw = f32(inp["pool_w"])[0].reshape(4, CPG, 128, W).transpose(2, 0, 1, 3)
        m["pool_wB"] = np.ascontiguousarray(pw).reshape(128, -1)
        m["pool_scT"] = fm_cols(f32(inp["pool_scale"])[0])
        m["pool_inv_lat"] = np.ascontiguousarray(np.broadcast_to(pool_inv_table(GRID_W, cfg.pool_windows).reshape(1, -1), (128, 4 * GRID_W)))
        m["pool_inv_ctx"] = np.ascontiguousarray(np.broadcast_to(pool_inv_table(cfg.TC, cfg.pool_windows).reshape(1, -1), (128, 4 * cfg.TC)))
    if 0 in cfg.kinds or 2 in cfg.kinds:
        m["c_ident"] = np.eye(128, dtype=np.float32)
        for CL in ((32,) if 0 in cfg.kinds else ()) + ((128,) if 2 in cfg.kinds else ()):
            i = np.arange(128)
            same = (i[:, None] // CL) == (i[None, :] // CL)
            m["c_cmask%d" % CL] = (same & (i[:, None] <= i[None, :])).astype(np.float32)
            m["c_rmask%d" % CL] = ((i[:, None] // CL) == np.arange(128 // CL)[None, :]).astype(np.float32)
            m["c_reset%d" % CL] = np.ascontiguousarray(np.broadcast_to(((i % CL) != 0).astype(np.float32)[None, :], (128, 128)))
    if 0 in cfg.kinds:
        w = f32(inp["hg_w_in"])[0].reshape(DC, 128, 5 * DC, 128).transpose(2, 1, 0, 3)
        m["hg_win"] = np.ascontiguousarray(w).reshape(5 * DC * 128, DC * 128)
        w = f32(inp["hg_w_o"])[0].reshape(DC, 128, DC, 128).transpose(2, 1, 0, 3)
        m["hg_wo"] = np.ascontiguousarray(w).reshape(DC * 128, DC * 128)
        m["hg_lbT"] = fm_cols(f32(inp["hg_lb_logits"]))
        m["hg_ngT"] = fm_cols(f32(inp["hg_norm_g"])[0])
    if 2 in cfg.kinds:
        H = cfg.ml_heads
        win = f32(inp["ml_w_in"])[0]
        w = win[:, :4 * D].reshape(DC, 128, 4 * DC, 128).transpose(2, 1, 0, 3)
        m["ml_win"] = np.ascontiguousarray(w).reshape(4 * DC * 128, DC * 128)
        w = f32(inp["ml_w_o"])[0].reshape(DC, 128, DC, 128).transpose(2, 1, 0, 3)
        m["ml_wo"] = np.ascontiguousarray(w).reshape(DC * 128, DC * 128)
        wg = win[:, 4 * D:].reshape(DC, 128, 4 * H).transpose(1, 0, 2)
        m["ml_wg"] = np.ascontiguousarray(wg).reshape(128, DC * 4 * H)
        gb = f32(inp["ml_gate_b"])[0].reshape(4 * H)
        isf = np.repeat(np.array([0.0, 1.0, 0.0, 1.0], np.float32), H)
        m["ml_gcol"] = np.ascontiguousarray(np.stack([gb, np.float32(1.0) - isf, -isf], 1).astype(np.float32))
        sel = np.zeros((4 * H, 4 * H, 128), np.float32)
        for r in range(4 * H):
            sel[r, r, :] = 1.0
        m["ml_sel"] = sel.reshape(4 * H, 4 * H * 128)
        m["ml_ngT"] = fm_cols(f32(inp["ml_norm_g"])[0])
    if 3 in cfg.kinds:
        G = D // 16

        def lanes(a):
            a = f32(a).reshape(2, DC, 4, 2, 64).transpose(3, 4, 0, 1, 2).reshape(128, 2, DC, 4)
            return a
        lre, lim = lanes(inp["s5_lam_re"][0]), lanes(inp["s5_lam_im"][0])
        ldt = lanes(np.repeat(f32(inp["s5_log_dt"][0])[:, :, None], 64, axis=2))
        m["s5_lam"] = np.ascontiguousarray(np.stack([lre, lim, ldt], 1)).reshape(128, -1)

        def bblk(bm):
            bm = f32(bm).reshape(2, DC, 4, 2, 64, 16)
            out = np.zeros((2, DC, 8, 16, 4, 2, 64), np.float32)
            for j in range(4):
                for gl in range(2):
                    out[:, :, 2 * j + gl, :, j, gl, :] = bm[:, :, j, gl].transpose(0, 1, 3, 2)
            return out.reshape(2 * DC * 128, 4 * 128)

        def cblk(cm):
            cm = f32(cm).reshape(2, DC, 4, 2, 16, 64)
            out = np.zeros((2, DC, 2, 64, 4, 8, 16), np.float32)
            for j in range(4):
                for gl in range(2):
                    out[:, :, gl, :, j, 2 * j + gl, :] = cm[:, :, j, gl].transpose(0, 1, 3, 2)
            return out.reshape(2 * DC * 128, 4 * 128)
        m["s5_Bb0"], m["s5_Bb1"] = bblk(inp["s5_b_re"][0]), bblk(inp["s5_b_im"][0])
        m["s5_Cb0"], m["s5_Cb1"] = cblk(inp["s5_c_re"][0]), cblk(inp["s5_c_im"][0])
        m["s5_dT"] = fm_cols(f32(inp["s5_d"])[0])
        m["s5_iota"] = np.ascontiguousarray(np.broadcast_to(np.arange(1, 513, dtype=np.float32)[None, :], (128, 512)))
        w = f32(inp["s5_w_glu"])[0].reshape(DC, 128, 2 * DC, 128).transpose(2, 1, 0, 3)
        m["s5_wglu"] = np.ascontiguousarray(w).reshape(2 * DC * 128, DC * 128)
    return m


def prep_core(cfg, inp, batches):
    D, DC = cfg.D, cfg.DC
    f32 = lambda a: np.asarray(a, np.float32)
    m = {}
    m["xT"] = np.ascontiguousarray(np.concatenate([f32(inp["x"][b]).T for b in batches], 0))
    m["ctxT"] = np.ascontiguousarray(np.concatenate([f32(inp["ctx"][b]).T for b in batches], 0))
    cnds = []
    for b in batches:
        cs = [f32(inp["c"][b]), f32(inp["c_ctx"])]
        cnds.append(np.stack(cs, -1).reshape(DC, 128, 2).transpose(1, 0, 2).reshape(128, -1))
    m["cond"] = np.ascontiguousarray(np.concatenate(cnds, 0))
    return m


def run(cfg, inp, n_cores):
    nb = cfg.nb
    bld = Builder(cfg)
    nc = bld.build()
    shared = prep_shared(cfg, inp)
    in_maps = []
    for c in range(n_cores):
        m = dict(shared)
        m.update(prep_core(cfg, inp, list(range(c * nb, (c + 1) * nb))))
        in_maps.append({k: m[k] for k in bld.din})
    res = run_bass_kernel_spmd(nc, in_maps, core_ids=list(range(n_cores)))
    outs = []
    for c in range(n_cores):
        o = res.results[c]["outT"].reshape(nb, cfg.D, cfg.T)
        outs.append(np.transpose(o, (0, 2, 1)))
    return np.ascontiguousarray(np.concatenate(outs, 0))


N_CORES = 8


def kernel(**inputs):
    n_cores = N_CORES
    cfg = Cfg(nb=8 // n_cores)
    return run(cfg, inputs, n_cores).astype(np.float32)
```

```python
import contextlib
import os
import numpy as np
import concourse.bass as bass
import concourse.mybir as mybir
from concourse.bass_utils import run_bass_kernel_spmd

F32 = mybir.dt.float32
BF16 = mybir.dt.bfloat16
I32 = mybir.dt.int32
AF = mybir.ActivationFunctionType
ALU = mybir.AluOpType
AX = mybir.AxisListType


class StopBuild(Exception):
    pass


class Buf:
    __slots__ = ("lw", "rd")

    def __init__(self):
        self.lw = None
        self.rd = {}


def bufs(n):
    return [Buf() for _ in range(n)]


class Op:
    __slots__ = ("eng", "fn", "deps", "dsem", "sig", "sigkey", "sigval", "raw")

    def __init__(self, eng, fn, deps, dsem):
        self.eng = eng
        self.fn = fn
        self.deps = deps
        self.dsem = dsem
        self.sig = False
        self.sigkey = None
        self.sigval = 0
        self.raw = ()


class Prog:
    ENGS = ("pe", "act", "dve", "pool", "sp")

    def __init__(self, nc, es):
        self.nc = nc
        self.es = es
        self.h = {"pe": nc.tensor, "act": nc.scalar, "dve": nc.vector, "pool": nc.gpsimd, "sp": nc.sync}
        self.ops = []
        self.base = 0
        self.sems = {}
        for e in self.ENGS:
            self.sems[e] = es.enter_context(nc.semaphore("s_" + e))
        self.cnt = {e: 0 for e in self.ENGS}
        self.seen = {e: {} for e in self.ENGS}
        self.snap = {}
        self.last_dma = {}
        self.last_op = {}
        self.pool_keys = set()

    def _sem(self, key):
        if key not in self.sems:
            self.sems[key] = self.es.enter_context(self.nc.semaphore("d_" + str(key)))
            self.cnt[key] = 0
        return self.sems[key]

    BG = "bgcast"

    def wait_bg(self):
        total = self.cnt.get(self.BG, 0)
        if total == 0:
            return
        for e in self.ENGS:
            i = self.op(e, None)
            self.ops[i - self.base].raw = ((self.BG, total),)

    def op(self, eng, fn, reads=(), writes=(), dsem=None, extra=(), serial=True):
        idx = self.base + len(self.ops)
        dma = dsem is not None
        deps = set(extra)
        cand = []
        for b in reads:
            if b.lw is not None:
                cand.append((b.lw, True))
        for b in writes:
            if b.lw is not None:
                cand.append((b.lw, False))
            for r in b.rd.values():
                cand.append((r, False))
        for d, raw in cand:
            if d < self.base:
                continue
            od = self.ops[d - self.base]
            if (not dma) and od.dsem is None and od.eng == eng and not raw:
                continue
            deps.add(d)
        if dma:
            self._sem(dsem)
            if eng == "pool":
                self.pool_keys.add(dsem)
            p = self.last_dma.get(dsem)
            if serial and p is not None and p >= self.base:
                deps.add(p)
            if serial:
                self.last_dma[dsem] = idx
        deps.discard(idx)
        self.ops.append(Op(eng, fn, deps, dsem))
        key = ("d", idx) if dma else eng
        for b in reads:
            b.rd[key] = idx
        for b in writes:
            b.lw = idx
            b.rd = {}
        if not dma and fn is not None:
            self.last_op[eng] = idx
        return idx

    def barrier(self):
        ext = set(v for v in self.last_op.values() if v >= self.base)
        ext |= set(v for v in self.last_dma.values() if v >= self.base)
        for e in self.ENGS:
            self.op(e, None, extra=ext)

    def flush(self):
        ops = self.ops
        base = self.base
        for o in ops:
            for d in o.deps:
                od = ops[d - base]
                if od.dsem is None:
                    od.sig = True
        for i, o in enumerate(ops):
            idx = base + i
            E = self.h[o.eng]
            sv = self.seen[o.eng]
            for key, val in o.raw:
                if sv.get(key, 0) < val:
                    E.wait_ge(self.sems[key], val)
                    sv[key] = val
            for d in sorted(o.deps, reverse=True):
                od = ops[d - base]
                if sv.get(od.sigkey, 0) >= od.sigval:
                    continue
                E.wait_ge(self.sems[od.sigkey], od.sigval)
                for k2, v2 in self.snap[d].items():
                    if sv.get(k2, 0) < v2:
                        sv[k2] = v2
            ins = o.fn(E) if o.fn is not None else None
            if o.dsem is not None:
                self.cnt[o.dsem] += 16
                o.sigkey, o.sigval = o.dsem, self.cnt[o.dsem]
                ins.then_inc(self.sems[o.dsem], 16)
                s = dict(sv)
                s[o.sigkey] = o.sigval
                self.snap[idx] = s
            elif o.sig:
                assert ins is not None
                self.cnt[o.eng] += 1
                o.sigkey, o.sigval = o.eng, self.cnt[o.eng]
                ins.then_inc(self.sems[o.eng], 1)
                s = dict(sv)
                s[o.sigkey] = o.sigval
                self.snap[idx] = s
        self.base += len(ops)
        self.ops = []
        self.snap = {}

    def hard_sync(self):
        self.barrier()
        self.flush()
        if os.environ.get("NOHS"):
            return
        self.nc.all_engine_barrier()
        if os.environ.get("HS") == "b":
            return
        for k, sem in self.sems.items():
            if k not in self.pool_keys:
                self.nc.sync.sem_clear(sem)
        self.nc.all_engine_barrier()
        for k in self.cnt:
            if k not in self.pool_keys:
                self.cnt[k] = 0
        self.seen = {e: {} for e in self.ENGS}

    def mm(self, out, lhsT, rhs, start, stop, reads, writes):
        return self.op("pe", lambda e: e.matmul(out, lhsT=lhsT, rhs=rhs, start=start, stop=stop,
                                                skip_group_check=(os.environ.get("NOSKIP") is None)), reads, writes)

    def tr(self, out, in_, ident, reads, writes):
        return self.op("pe", lambda e: e.transpose(out, in_, ident), reads, writes)

    def act(self, out, in_, func, reads, writes, bias=None, scale=None):
        kw = {}
        if bias is not None:
            kw["bias"] = bias
        if scale is not None:
            kw["scale"] = scale
        return self.op("act", lambda e: e.activation(out=out, in_=in_, func=func, **kw), reads, writes)

    def ts(self, eng, out, in0, s1, s2, op0, op1, reads, writes):
        if op1 is None:
            return self.op(eng, lambda e: e.tensor_scalar(out=out, in0=in0, scalar1=s1, scalar2=None, op0=op0),
                           reads, writes)
        return self.op(eng, lambda e: e.tensor_scalar(out=out, in0=in0, scalar1=s1, scalar2=s2, op0=op0, op1=op1),
                       reads, writes)

    def tt(self, eng, out, in0, in1, op, reads, writes):
        return self.op(eng, lambda e: e.tensor_tensor(out=out, in0=in0, in1=in1, op=op), reads, writes)

    def stt(self, eng, out, in0, scalar, in1, op0, op1, reads, writes):
        return self.op(eng, lambda e: e.scalar_tensor_tensor(out=out, in0=in0, scalar=scalar, in1=in1,
                                                             op0=op0, op1=op1), reads, writes)

    def cp(self, eng, out, in_, reads, writes):
        return self.op(eng, lambda e: e.tensor_copy(out=out, in_=in_), reads, writes)

    def scan(self, out, d0, d1, init, op0, op1, reads, writes):
        return self.op("dve", lambda e: e.tensor_tensor_scan(out=out, data0=d0, data1=d1, initial=init,
                                                             op0=op0, op1=op1), reads, writes)

    def memset(self, eng, ap, val, writes):
        return self.op(eng, lambda e: e.memset(ap, val), (), writes)

    def dma(self, q, out, in_, dsem, reads=(), writes=(), serial=True):
        return self.op(q, lambda e: e.dma_start(out=out, in_=in_), reads, writes, dsem=dsem, serial=serial)

    def cast(self, out, in_, r, bg):
        if bg:
            return self.dma("pool", out, in_, self.BG, serial=False)
        return self.dma("pool", out, in_, "cast%d" % (r % 4))


class Cfg:
    def __init__(self, D=2048, F=5632, T=4096, TC=256, kinds=(0, 1, 2, 3), ml_heads=8, nb=1, last_skip=True):
        self.D, self.F, self.T, self.TC = D, F, T, TC
        self.DC, self.FC = D // 128, F // 128
        self.kinds = tuple(kinds)
        self.L = len(kinds)
        self.TT = T + TC
        self.NT = 512
        self.ml_heads = ml_heads
        self.nb = nb
        self.last_skip = last_skip
        self.alpha = (2.0 * 4) ** 0.25
        self.tiles = [(0, TC)] + [(TC + i * self.NT, self.NT) for i in range(T // self.NT)]
        self.pool_windows = (2, 4, 8, 16)


LN_EPS = 1e-5
RMS_EPS = 1e-6
GRID_W = 64


class PN:
    def __init__(self, B, st, NT):
        self.B = B
        self.tm = [B.sb(st, "pn_tm%d" % i, [128, NT], F32) for i in range(2)]
        self.sq = [B.sb(st, "pn_sq%d" % i, [128, NT], BF16) for i in range(2)]
        self.zb = [B.sb(st, "pn_zb%d" % i, [128, NT], BF16) for i in range(2)]
        self.mean = B.sb(st, "pn_mean", [128, NT], F32)
        self.rstd = B.sb(st, "pn_rstd", [128, NT], F32)
        self.msq = B.sb(st, "pn_msq", [128, NT], F32)
        self.b_tm, self.b_sq, self.b_zb = bufs(2), bufs(2), bufs(2)
        self.b_mean, self.b_rstd, self.b_msq = Buf(), Buf(), Buf()

    def begin(self, li, s, X, bX, nt, col):
        self.li, self.s, self.X, self.bX, self.nt, self.col = li, s, X, bX, nt, col
        self.pend = None

    def _stats(self, m, first, last):
        B, P, nt = self.B, self.B.P, self.nt
        P.mm(B.ps[6][:, :nt], B.ones[:], self.zb[m % 2][:, :nt], first, last, [self.b_zb[m % 2], B.b_const], [B.psb[6]])
        P.mm(B.ps[7][:, :nt], B.ones[:], self.sq[m % 2][:, :nt], first, last, [self.b_sq[m % 2], B.b_const], [B.psb[7]])

    def add(self, m, y_ap, y_bufs, gate_ap):
        B, P, nt, X, bX = self.B, self.B.P, self.nt, self.X, self.bX
        if self.pend is not None:
            self._stats(self.pend, self.pend == 0, False)
        Tm = self.tm[m % 2]
        P.act(Tm[:, :nt], y_ap, AF.Identity, list(y_bufs) + [B.b_const], [self.b_tm[m % 2]], scale=gate_ap)
        P.stt("dve", X[:, m, :nt], X[:, m, :nt], B.cfg.alpha, Tm[:, :nt], ALU.mult, ALU.add,
              [bX[m], self.b_tm[m % 2]], [bX[m]])
        P.act(self.sq[m % 2][:, :nt], X[:, m, :nt], AF.Square, [bX[m]], [self.b_sq[m % 2]])
        P.cp("pool", self.zb[m % 2][:, :nt], X[:, m, :nt], [bX[m]], [self.b_zb[m % 2]])
        self.pend = m

    def finish(self):
        B, P, nt, X, bX, li, s = self.B, self.B.P, self.nt, self.X, self.bX, self.li, self.s
        cfg = B.cfg
        D, DC = cfg.D, cfg.DC
        self._stats(self.pend, self.pend == 0, True)
        mean, rstd, msq = self.mean, self.rstd, self.msq
        P.ts("dve", mean[:, :nt], B.ps[6][:, :nt], 1.0 / D, None, ALU.mult, None, [B.psb[6]], [self.b_mean])
        P.tt("dve", msq[:, :nt], mean[:, :nt], mean[:, :nt], ALU.mult, [self.b_mean], [self.b_msq])
        P.stt("dve", rstd[:, :nt], B.ps[7][:, :nt], 1.0 / D, msq[:, :nt], ALU.mult, ALU.subtract,
              [B.psb[7], self.b_msq], [self.b_rstd])
        P.ts("dve", rstd[:, :nt], rstd[:, :nt], LN_EPS, None, ALU.add, None, [self.b_rstd], [self.b_rstd])
        P.act(rstd[:, :nt], rstd[:, :nt], AF.Sqrt, [self.b_rstd], [self.b_rstd])
        P.op("dve", lambda e: e.reciprocal(out=rstd[:, :nt], in_=rstd[:, :nt]), [self.b_rstd], [self.b_rstd])
        for c in range(DC):
            P.tt("pool", X[:, c, :nt], X[:, c, :nt], mean[:, :nt], ALU.subtract, [bX[c], self.b_mean], [bX[c]])
            P.tt("dve", X[:, c, :nt], X[:, c, :nt], rstd[:, :nt], ALU.mult, [bX[c], self.b_rstd], [bX[c]])
            P.act(X[:, c, :nt], X[:, c, :nt], AF.Identity, [bX[c], B.b_const], [bX[c]],
                  scale=B.ln_col(B.lngT, li, s, c), bias=B.ln_col(B.lnbT, li, s, c))


class Builder:
    def __init__(self, cfg):
        self.cfg = cfg
        self.nc = bass.Bass("TRN2", target_bir_lowering=False)
        self.din = {}

    def inp(self, name, shape, dt=F32):
        t = self.nc.dram_tensor(name, list(shape), dt, kind="ExternalInput")
        self.din[name] = t
        return t.ap()

    def scratch(self, name, shape, dt):
        return self.nc.dram_tensor(name, list(shape), dt, kind="Internal").ap()

    def sb(self, stack, name, shape, dt):
        self._uid = getattr(self, "_uid", 0) + 1
        return stack.enter_context(self.nc.sbuf_tensor("%s_%d" % (name, self._uid), list(shape), dt))

    def build(self):
        cfg = self.cfg
        nc = self.nc
        D, F, T, TC, DC, FC, L, TT, nb = cfg.D, cfg.F, cfg.T, cfg.TC, cfg.DC, cfg.FC, cfg.L, cfg.TT, cfg.nb
        NJ = 2
        self.xT = self.inp("xT", [nb * D, T])
        self.ctxT = self.inp("ctxT", [nb * D, TC])
        self.cond = self.inp("cond", [nb * 128, DC * NJ])
        self.mod_w = self.inp("mod_w", [L * D, 9 * D])
        self.mod_bT = self.inp("mod_bT", [128, L * 9 * DC])
        self.ln_gT = self.inp("ln_gT", [128, L * 3 * DC])
        self.ln_bT = self.inp("ln_bT", [128, L * 3 * DC])
        self.w13 = self.inp("w13", [L * 2 * FC * 128, DC * 256])
        self.w2 = self.inp("w2", [L * 2 * DC * 128, FC * 128])
        self.outT = nc.dram_tensor("outT", [nb * D, T], F32, kind="ExternalOutput").ap()
        self.mixer_inputs()
        self.w13b = [self.scratch("w13b%d" % i, [FC * 128, DC * 256], BF16) for i in range(L * 2)]
        self.w2b = [self.scratch("w2b%d" % i, [DC * 128, FC * 128], BF16) for i in range(L * 2)]
        self.xs = [self.scratch("xs%d" % i, [D, TT], F32) for i in range(2)]
        self.mod_wb = [self.scratch("mod_wb%d" % i, [D, 9 * D], BF16) for i in range(L)]
        if self.has(1):
            self.pool_wBb = self.scratch("pool_wBb", [128, D * D // 4 // 128], BF16)
        self.mixer_scratch()

        with contextlib.ExitStack() as es:
            P = self.P = Prog(nc, es)
            self.ones = self.sb(es, "ones", [128, 128], BF16)
            self.modT = self.sb(es, "modT", [128, L * 9 * DC * NJ], F32)
            self.lngT = self.sb(es, "lngT", [128, L * 3 * DC], F32)
            self.lnbT = self.sb(es, "lnbT", [128, L * 3 * DC], F32)
            self.b_const = Buf()
            self.ps = [es.enter_context(nc.psum_tensor("ps%d" % i, [128, 512], F32)) for i in range(8)]
            self.psb = bufs(8)
            self.mixer_persistent(es)

            self.prologue_casts()
            stop = getattr(cfg, "stop_after", None)
            self._nph = 0
            if nb == 1:
                self.per_batch(0, stop)
            else:
                P.hard_sync()
                if os.environ.get("UNROLL"):
                    for bi in range(nb):
                        self.per_batch(bi, stop)
                        P.hard_sync()
                else:
                    with nc.Fori(0, nb) as bv:
                        self.per_batch(0 if os.environ.get("STATICB") else bv, stop)
                        P.hard_sync()
            P.barrier()
            P.flush()
        return nc


    def per_batch(self, b, stop):
        cfg, P = self.cfg, self.P
        D, TC, L = cfg.D, cfg.TC, cfg.L

        def dump(cur):
            src = cur.rearrange("(c p) t -> p c t", p=128)[:, :, TC:]
            P.dma("sp", self.outT[0:D, :].rearrange("(c p) t -> p c t", p=128), src, "c0")
            P.barrier()
            P.flush()
        self.prologue(b)
        cur = None
        nph = 0
        for li, kind in enumerate(cfg.kinds):
            last = (li == L - 1)
            if li == 1:
                P.wait_bg()
            dst = self.xs[0] if cur is not self.xs[0] else self.xs[1]
            self.ffn_phase(b, li, 0, cur, dst, cfg.tiles, None)
            cur = dst
            nph += 1
            if stop == nph:
                return dump(cur)
            dst = self.xs[0] if cur is not self.xs[0] else self.xs[1]
            try:
                self.mixer_phase(b, li, kind, cur, dst, ctx_out=not (last and cfg.last_skip))
            except StopBuild:
                return dump(cur)
            cur = dst
            nph += 1
            if stop == nph:
                return dump(cur)
            dst = self.xs[0] if cur is not self.xs[0] else self.xs[1]
            tiles = cfg.tiles[1:] if (last and cfg.last_skip) else cfg.tiles
            self.ffn_phase(b, li, 2, cur, dst, tiles, self.outT if last else None)
            cur = dst

    def mod_col(self, li, slot, c, col):
        cfg = self.cfg
        NJ = 2
        o = ((li * 9 + slot) * cfg.DC + c) * NJ + col
        return self.modT[:, o:o + 1]

    def ln_col(self, t, li, s, c):
        o = (li * 3 + s) * self.cfg.DC + c
        return t[:, o:o + 1]

    def src_ap(self, b, cur, t0, nt):
        cfg = self.cfg
        if cur is not None:
            return cur.rearrange("(c p) t -> p c t", p=128)[:, :, t0:t0 + nt]
        if t0 < cfg.TC:
            return self.ctxT.rearrange("(b c p) t -> b p c t", p=128, c=cfg.DC)[b][:, :, t0:t0 + nt]
        return self.xT.rearrange("(b c p) t -> b p c t", p=128, c=cfg.DC)[b][:, :, t0 - cfg.TC:t0 - cfg.TC + nt]

    def prologue_casts(self):
        cfg, P = self.cfg, self.P
        D, DC, FC, L = cfg.D, cfg.DC, cfg.FC, cfg.L
        use_bg = (cfg.nb == 1 and L > 1)
        for i in range(L):
            for kc in range(DC):
                for hf in range(3):
                    P.cast(self.mod_wb[i][kc * 128:(kc + 1) * 128, hf * 3 * D:(hf + 1) * 3 * D],
                           self.mod_w[i * D + kc * 128:i * D + (kc + 1) * 128, hf * 3 * D:(hf + 1) * 3 * D], kc, False)
        for bg in ((False, True) if use_bg else (False,)):
            for i in range(L * 2):
                if use_bg and ((i // 2 == 0) == bg):
                    continue
                for r in range(FC):
                    g = i * FC + r
                    P.cast(self.w13b[i][r * 128:(r + 1) * 128, :], self.w13[g * 128:(g + 1) * 128, :], r, bg)
                for r in range(DC):
                    g = i * DC + r
                    P.cast(self.w2b[i][r * 128:(r + 1) * 128, :], self.w2[g * 128:(g + 1) * 128, :], r, bg)
            k0 = cfg.kinds[0]
            kset = [k for k in (0, 1, 2, 3) if self.has(k) and ((not use_bg) or ((k == k0) != bg))]
            self.mixer_casts(kset, bg)
        P.barrier()
        P.flush()

    def prologue(self, b):
        cfg = self.cfg
        P = self.P
        nc = self.nc
        D, DC, FC, L, nb = cfg.D, cfg.DC, cfg.FC, cfg.L, cfg.nb
        NJ = 2
        with contextlib.ExitStack() as st:
            P.memset("dve", self.ones[:], 1.0, [self.b_const])
            P.dma("sp", self.lngT[:], self.ln_gT[:, :], "c0", writes=[self.b_const])
            P.dma("sp", self.lnbT[:], self.ln_bT[:, :], "c1", writes=[self.b_const])
            cnd = self.sb(st, "cnd", [128, DC * NJ], F32)
            sg = self.sb(st, "sg", [128, DC * NJ], F32)
            cb = self.sb(st, "cb", [128, DC * NJ], BF16)
            mb = self.sb(st, "mb", [128, L * 9 * DC], F32)
            n_oc = 9 * DC
            GRP = 8 if n_oc % 8 == 0 else 4
            assert GRP * NJ <= 512
            wst = [self.sb(st, "wst%d" % i, [128, DC, GRP * 128], BF16) for i in range(2)]
            b_c, b_cb, b_mb = Buf(), Buf(), Buf()
            b_w = bufs(2)
            P.dma("sp", cnd[:], self.cond.rearrange("(b p) n -> b p n", p=128)[b], "c2", writes=[b_c])
            P.dma("sp", mb[:], self.mod_bT[:, :], "c3", writes=[b_mb])
            P.act(sg[:], cnd[:], AF.Sigmoid, [b_c], [b_cb])
            P.tt("dve", cb[:], cnd[:], sg[:], ALU.mult, [b_c, b_cb], [b_cb])
            it = 0
            for li in range(L):
                for g in range(n_oc // GRP):
                    bank = self.ps[it % 2]
                    bb = self.psb[it % 2]
                    w = wst[it % 2]
                    bw = b_w[it % 2]
                    src = self.mod_wb[li][:, g * GRP * 128:(g + 1) * GRP * 128].rearrange("(k p) n -> p k n", p=128)
                    P.dma("sp", w[:], src, "mw%d" % (it % 2), writes=[bw])
                    it += 1
                    for o in range(GRP):
                        for kc in range(DC):
                            P.mm(bank[:, o * NJ:(o + 1) * NJ], w[:, kc, o * 128:(o + 1) * 128],
                                 cb[:, kc * NJ:(kc + 1) * NJ], kc == 0, kc == DC - 1, [bw, b_cb], [bb])
                    o0 = (li * n_oc + g * GRP) * NJ
                    dst = self.modT[:, o0:o0 + GRP * NJ].rearrange("p (o j) -> p o j", j=NJ)
                    srcp = bank[:, 0:GRP * NJ].rearrange("p (o j) -> p o j", j=NJ)
                    bia = mb[:, li * n_oc + g * GRP: li * n_oc + (g + 1) * GRP].unsqueeze(2).to_broadcast([128, GRP, NJ])
                    P.tt("dve", dst, srcp, bia, ALU.add, [bb, b_mb], [self.b_const])
            mv = self.modT[:].rearrange("p (l s c) -> p l s c", l=L, s=9)
            for s in (1, 4, 7):
                P.ts("dve", mv[:, :, s, :], mv[:, :, s, :], 1.0, None, ALU.add, None, [self.b_const], [self.b_const])
            for s in (2, 8):
                P.ts("dve", mv[:, :, s, :], mv[:, :, s, :], 0.5, None, ALU.mult, None, [self.b_const], [self.b_const])
            self.mixer_prologue(st)
            P.barrier()
            P.flush()

    def ffn_phase(self, b, li, s, cur, dst, tiles, final_out):
        cfg = self.cfg
        P = self.P
        D, DC, FC, TC = cfg.D, cfg.DC, cfg.FC, cfg.TC
        f = 0 if s == 0 else 1
        NT = cfg.NT
        with contextlib.ExitStack() as st:
            xt = [self.sb(st, "xt%d" % i, [128, DC, NT], F32) for i in range(2)]
            hT = self.sb(st, "hT", [128, DC, NT], BF16)
            gT = self.sb(st, "gT", [128, FC, NT], BF16)
            w13s = [self.sb(st, "w13s%d" % i, [128, DC, 256], BF16) for i in range(3)]
            w2s = [self.sb(st, "w2s%d" % i, [128, FC, 128], BF16) for i in range(2)]
            sa = [self.sb(st, "sa%d" % i, [128, NT], F32) for i in range(2)]
            b_xt, b_w13, b_w2, b_sa = bufs(2), bufs(3), bufs(2), bufs(2)
            b_h = Buf()
            b_g = bufs(FC)
            b_xc = [bufs(DC) for _ in range(2)]
            pn = PN(self, st, NT)
            PA, PB, PO, S1, S2 = (0, 1), (2, 3), (4, 5), 6, 7
            ps, psb = self.ps, self.psb
            w13v = self.w13b[li * 2 + f].rearrange("(r p) (k n) -> r p k n", p=128, n=256)
            w2v = self.w2b[li * 2 + f].rearrange("(r p) (k n) -> r p k n", p=128, n=128)
            r13 = 0
            r2 = 0
            wi13 = 0
            wi2 = 0
            for ti, (t0, nt) in enumerate(tiles):
                col = 0 if t0 >= TC else 1
                X = xt[ti % 2]
                bX = b_xc[ti % 2]
                P.dma("sp", X[:, :, :nt], self.src_ap(b, cur, t0, nt), "xl%d" % (ti % 2), writes=bX)
                for c in range(DC):
                    P.ts("pool", hT[:, c, :nt], X[:, c, :nt], self.mod_col(li, 3 * s + 1, c, col),
                         self.mod_col(li, 3 * s + 0, c, col), ALU.mult, ALU.add, [bX[c], self.b_const], [b_h])
                for j in range(FC):
                    W = w13s[wi13 % 3]
                    bW = b_w13[wi13 % 3]
                    P.dma("sp", W[:], w13v[r13 + j], "w13_%d" % (wi13 % 3), writes=[bW])
                    wi13 += 1
                    pa, pb = ps[PA[j % 2]], ps[PB[j % 2]]
                    ba, bb = psb[PA[j % 2]], psb[PB[j % 2]]
                    for kc in range(DC):
                        P.mm(pa[:, :nt], W[:, kc, 0:128], hT[:, kc, :nt], kc == 0, kc == DC - 1, [bW, b_h], [ba])
                    for kc in range(DC):
                        P.mm(pb[:, :nt], W[:, kc, 128:256], hT[:, kc, :nt], kc == 0, kc == DC - 1, [bW, b_h], [bb])
                    S = sa[j % 2]
                    P.act(S[:, :nt], pa[:, :nt], AF.Silu, [ba], [b_sa[j % 2]])
                    P.tt("dve", gT[:, j, :nt], S[:, :nt], pb[:, :nt], ALU.mult, [b_sa[j % 2], bb], [b_g[j]])
                pn.begin(li, s, X, bX, nt, col)
                for m in range(DC):
                    W = w2s[wi2 % 2]
                    bW = b_w2[wi2 % 2]
                    P.dma("sp", W[:], w2v[r2 + m], "w2_%d" % (wi2 % 2), writes=[bW])
                    wi2 += 1
                    po, bo = ps[PO[m % 2]], psb[PO[m % 2]]
                    for kc in range(FC):
                        P.mm(po[:, :nt], W[:, kc, :], gT[:, kc, :nt], kc == 0, kc == FC - 1, [bW, b_g[kc]], [bo])
                    pn.add(m, po[:, :nt], [bo], self.mod_col(li, 3 * s + 2, m, col))
                pn.finish()
                if final_out is not None:
                    oap = final_out.rearrange("(b c p) t -> b p c t", p=128, c=DC)[b][:, :, t0 - TC:t0 - TC + nt]
                else:
                    oap = dst.rearrange("(c p) t -> p c t", p=128)[:, :, t0:t0 + nt]
                P.dma("sp", oap, X[:, :, :nt], "xst%d" % (ti % 2), reads=bX)
            P.barrier()
            P.flush()

    def slot_of(self, li):
        return li // 4

    def has(self, kind):
        return kind in self.cfg.kinds

    def mixer_inputs(self):
        cfg = self.cfg
        D, DC, T, TC = cfg.D, cfg.DC, cfg.T, cfg.TC
        if self.has(1):
            self.pool_wB = self.inp("pool_wB", [128, D * D // 4 // 128])
            self.pool_scT = self.inp("pool_scT", [128, DC])
            self.pool_inv_lat = self.inp("pool_inv_lat", [128, 4 * GRID_W])
            self.pool_inv_ctx = self.inp("pool_inv_ctx", [128, 4 * TC])
        if self.has(0) or self.has(2):
            self.c_ident = self.inp("c_ident", [128, 128])
            self.c_cmask, self.c_rmask, self.c_reset = {}, {}, {}
            for CL in ((32,) if self.has(0) else ()) + ((128,) if self.has(2) else ()):
                self.c_cmask[CL] = self.inp("c_cmask%d" % CL, [128, 128])
                self.c_rmask[CL] = self.inp("c_rmask%d" % CL, [128, 128 // CL])
                self.c_reset[CL] = self.inp("c_reset%d" % CL, [128, 128])
        if self.has(0):
            self.hg_win = self.inp("hg_win", [5 * DC * 128, DC * 128])
            self.hg_wo = self.inp("hg_wo", [DC * 128, DC * 128])
            self.hg_lbT = self.inp("hg_lbT", [128, 2 * 4 * DC])
            self.hg_ngT = self.inp("hg_ngT", [128, DC])
        if self.has(2):
            H = cfg.ml_heads
            self.ml_win = self.inp("ml_win", [4 * DC * 128, DC * 128])
            self.ml_wo = self.inp("ml_wo", [DC * 128, DC * 128])
            self.ml_wg = self.inp("ml_wg", [128, DC * 4 * H])
            self.ml_gcol = self.inp("ml_gcol", [4 * H, 3])
            self.ml_sel = self.inp("ml_sel", [4 * H, 4 * H * 128])
            self.ml_ngT = self.inp("ml_ngT", [128, DC])

        if self.has(3):
            NST = 2 * DC * 4
            self.s5_lam = self.inp("s5_lam", [128, 3 * NST])
            self.s5_Bb = [self.inp("s5_Bb%d" % i, [2 * DC * 128, 4 * 128]) for i in range(2)]
            self.s5_Cb = [self.inp("s5_Cb%d" % i, [2 * DC * 128, 4 * 128]) for i in range(2)]
            self.s5_dT = self.inp("s5_dT", [128, DC])
            self.s5_iota = self.inp("s5_iota", [128, 512])
            self.s5_wglu = self.inp("s5_wglu", [2 * DC * 128, DC * 128])

    def mixer_scratch(self):
        cfg = self.cfg
        D, DC, TT = cfg.D, cfg.DC, cfg.TT
        H = cfg.ml_heads
        if self.has(3):
            self.s5_wglub = self.scratch("s5_wglub", [2 * DC * 128, DC * 128], BF16)
            self.sgy = self.scratch("sgy", [DC, 128, TT], BF16)
        if self.has(2):
            self.ml_winb = self.scratch("ml_winb", [4 * DC * 128, DC * 128], BF16)
            self.ml_wob = self.scratch("ml_wob", [DC * 128, DC * 128], BF16)
            self.mq = self.scratch("mq", [DC, 128, TT], BF16)
            self.mv = self.scratch("mv", [DC, 128, TT], BF16)
            self.mog = self.scratch("mog", [DC, 128, TT], BF16)
            self.mk = [self.scratch("mk%d" % d, [DC, 128, TT], BF16) for d in range(2)]
            self.mlf = [self.scratch("mlf%d" % d, [H, 128, TT], F32) for d in range(2)]
            self.mo = [self.scratch("mo%d" % d, [DC, 128, TT], F32) for d in range(2)]
        if self.has(0):
            self.hg_winb = self.scratch("hg_winb", [5 * DC * 128, DC * 128], BF16)
            self.hg_wob = self.scratch("hg_wob", [DC * 128, DC * 128], BF16)
            self.hq = self.scratch("hq", [DC, 128, TT], BF16)
            self.hv = self.scratch("hv", [DC, 128, TT], BF16)
            self.hgt = self.scratch("hgt", [DC, 128, TT], BF16)
            self.hk = [self.scratch("hk%d" % d, [DC, 128, TT], BF16) for d in range(2)]
            self.hlf = [self.scratch("hlf%d" % d, [DC, 128, TT], F32) for d in range(2)]
            self.ho = [self.scratch("ho%d" % d, [DC, 128, TT], F32) for d in range(2)]

    def mixer_persistent(self, es):
        pass

    def mixer_casts(self, kset, bg):
        P = self.P
        DC = self.cfg.DC
        if 1 in kset:
            P.cast(self.pool_wBb[:, :], self.pool_wB[:, :], 0, bg)
        if 0 in kset:
            for r in range(5 * DC):
                P.cast(self.hg_winb[r * 128:(r + 1) * 128, :], self.hg_win[r * 128:(r + 1) * 128, :], r, bg)
            for r in range(DC):
                P.cast(self.hg_wob[r * 128:(r + 1) * 128, :], self.hg_wo[r * 128:(r + 1) * 128, :], r, bg)
        if 3 in kset:
            for r in range(2 * DC):
                P.cast(self.s5_wglub[r * 128:(r + 1) * 128, :], self.s5_wglu[r * 128:(r + 1) * 128, :], r, bg)
        if 2 in kset:
            for r in range(4 * DC):
                P.cast(self.ml_winb[r * 128:(r + 1) * 128, :], self.ml_win[r * 128:(r + 1) * 128, :], r, bg)
            for r in range(DC):
                P.cast(self.ml_wob[r * 128:(r + 1) * 128, :], self.ml_wo[r * 128:(r + 1) * 128, :], r, bg)

    def mixer_prologue(self, st):
        pass

    def ml_proj_phase(self, b, li, cur):
        cfg, P = self.cfg, self.P
        D, DC, TC, NT, H = cfg.D, cfg.DC, cfg.TC, cfg.NT, cfg.ml_heads
        G4 = 4 * H
        CPH = DC // H
        s = 1
        with contextlib.ExitStack() as st:
            xt = [self.sb(st, "xt%d" % i, [128, DC, NT], F32) for i in range(2)]
            hT = self.sb(st, "hT", [128, DC, NT], BF16)
            ws = [self.sb(st, "mws%d" % i, [128, DC, 128], BF16) for i in range(3)]
            stg = [self.sb(st, "mstg%d" % i, [128, DC, NT], BF16) for i in range(2)]
            ksc = self.sb(st, "ksc", [128, 2, H, NT], F32)
            lfs = [self.sb(st, "lfs%d" % i, [128, NT], F32) for i in range(2)]
            wgf = self.sb(st, "wgf", [128, DC, G4], F32)
            wg = self.sb(st, "wg", [128, DC, G4], BF16)
            gcol = self.sb(st, "gcol", [G4, 3], F32)
            sel = self.sb(st, "sel", [G4, G4 * 128], F32)
            onec = self.sb(st, "onec", [128, 1], F32)
            xg = self.sb(st, "xg", [G4, NT], F32)
            eg = self.sb(st, "eg", [G4, NT], F32)
            g2 = self.sb(st, "g2", [G4, NT], F32)
            b_x = [bufs(DC) for _ in range(2)]
            b_h, b_c, b_xg, b_eg, b_g2, b_ksc = Buf(), Buf(), Buf(), Buf(), Buf(), Buf()
            b_ws, b_stg, b_lfs = bufs(3), bufs(2), bufs(2)
            P.dma("sp", wgf[:].rearrange("p k g -> p (k g)"), self.ml_wg[:, :], "c0", writes=[b_c])
            P.dma("sp", gcol[:], self.ml_gcol[:, :], "c1", writes=[b_c])
            P.dma("sp", sel[:], self.ml_sel[:, :], "c2", writes=[b_c])
            P.memset("dve", onec[:], 1.0, [b_c])
            P.cp("dve", wg[:], wgf[:], [b_c], [b_c])
            wv = self.ml_winb.rearrange("(r p) (k n) -> r p k n", p=128, n=128)
            wi = 0
            si = 0
            for ti, (t0, nt) in enumerate(cfg.tiles):
                col = 0 if t0 >= TC else 1
                X, bX = xt[ti % 2], b_x[ti % 2]
                P.dma("sp", X[:, :, :nt], self.src_ap(b, cur, t0, nt), "xl%d" % (ti % 2), writes=bX)
                for c in range(DC):
                    P.ts("pool", hT[:, c, :nt], X[:, c, :nt], self.mod_col(li, 3 * s + 1, c, col),
                         self.mod_col(li, 3 * s + 0, c, col), ALU.mult, ALU.add, [bX[c], self.b_const], [b_h])
                pg, bg = self.ps[7], self.psb[7]
                for kc in range(DC):
                    P.mm(pg[0:G4, :nt], wg[:, kc, :], hT[:, kc, :nt], kc == 0, kc == DC - 1, [b_c, b_h], [bg])
                P.ts("dve", xg[:, :nt], pg[0:G4, :nt], gcol[:, 0:1], None, ALU.add, None, [bg, b_c], [b_xg])
                P.act(eg[:, :nt], xg[:, :nt], AF.Exp, [b_xg], [b_eg], scale=-1.0)
                P.act(eg[:, :nt], eg[:, :nt], AF.Ln, [b_eg, b_c], [b_eg], bias=onec[0:G4, 0:1])
                P.ts("dve", g2[:, :nt], xg[:, :nt], gcol[:, 1:2], None, ALU.mult, None, [b_xg, b_c], [b_g2])
                P.stt("dve", g2[:, :nt], eg[:, :nt], gcol[:, 2:3], g2[:, :nt], ALU.mult, ALU.add, [b_eg, b_g2, b_c], [b_g2])
                for d in range(2):
                    for hd in range(H):
                        r = (2 * d) * H + hd
                        pp, bp = self.ps[(2 * hd) % 4], self.psb[(2 * hd) % 4]
                        P.mm(pp[:, :nt], sel[:, r * 128:(r + 1) * 128], g2[:, :nt], True, True, [b_c, b_g2], [bp])
                        P.act(ksc[:, d, hd, :nt], pp[:, :nt], AF.Exp, [bp], [b_ksc])
                        r = (2 * d + 1) * H + hd
                        pp, bp = self.ps[(2 * hd + 1) % 4], self.psb[(2 * hd + 1) % 4]
                        P.mm(pp[:, :nt], sel[:, r * 128:(r + 1) * 128], g2[:, :nt], True, True, [b_c, b_g2], [bp])
                        L_, bL_ = lfs[si % 2], b_lfs[si % 2]
                        P.cp("dve", L_[:, :nt], pp[:, :nt], [bp], [bL_])
                        P.dma("sp", self.mlf[d][hd, :, t0:t0 + nt], L_[:, :nt], "mlf%d" % (si % 2), reads=[bL_])
                        si += 1
                P.ts("dve", ksc[:, :, :, :nt], ksc[:, :, :, :nt], float((D // H) ** -0.5), None, ALU.mult, None, [b_ksc], [b_ksc])
                outs = [self.mq, None, self.mv, self.mog]
                for sec in range(4):
                    if sec == 1:
                        S0, S1 = stg[0], stg[1]
                    else:
                        S0 = stg[sec % 2]
                    for c in range(DC):
                        oc = sec * DC + c
                        W, bW = ws[wi % 3], b_ws[wi % 3]
                        P.dma("sp", W[:], wv[oc], "mw%d" % (wi % 3), writes=[bW])
                        wi += 1
                        pp, bp = self.ps[4 + oc % 3], self.psb[4 + oc % 3]
                        for kc in range(DC):
                            P.mm(pp[:, :nt], W[:, kc, :], hT[:, kc, :nt], kc == 0, kc == DC - 1, [bW, b_h], [bp])
                        if sec == 0 or sec == 2:
                            P.cp("dve" if c % 2 else "act", S0[:, c, :nt], pp[:, :nt], [bp], [b_stg[sec % 2]]) if False else \
                                P.act(S0[:, c, :nt], pp[:, :nt], AF.Identity, [bp], [b_stg[sec % 2]])
                        elif sec == 3:
                            P.act(S0[:, c, :nt], pp[:, :nt], AF.Sigmoid, [bp], [b_stg[sec % 2]])
                        else:
                            hd = c // CPH
                            P.tt("dve", stg[0][:, c, :nt], pp[:, :nt], ksc[:, 0, hd, :nt], ALU.mult, [bp, b_ksc], [b_stg[0]])
                            P.tt("dve", stg[1][:, c, :nt], pp[:, :nt], ksc[:, 1, hd, :nt], ALU.mult, [bp, b_ksc], [b_stg[1]])
                    if sec == 1:
                        for d in range(2):
                            P.dma("sp", self.mk[d].rearrange("c p t -> p c t")[:, :, t0:t0 + nt], stg[d][:, :, :nt],
                                  "ms%d" % d, reads=[b_stg[d]])
                    else:
                        P.dma("sp", outs[sec].rearrange("c p t -> p c t")[:, :, t0:t0 + nt], S0[:, :, :nt],
                              "ms%d" % (sec % 2), reads=[b_stg[sec % 2]])
            P.barrier()
            P.flush()

    def mixer_phase(self, b, li, kind, cur, dst, ctx_out):
        if kind == 1:
            return self.pool_phase(b, li, cur, dst, ctx_out)
        if kind == 0:
            self.hg_proj_phase(b, li, cur)
            gla_phase(self, dict(nh=self.cfg.DC, nk=1, nv=1, ones=0, CL=32, nlf=1, q=self.hq, k=self.hk, lf=self.hlf,
                                 v=self.hv, out=self.ho, dbuf=(os.environ.get("E1") is None)))
            return self.hg_readout_phase(b, li, cur, dst, ctx_out)
        if kind == 2:
            H = self.cfg.ml_heads
            self.ml_proj_phase(b, li, cur)
            if getattr(self.cfg, "dbg", None) == "proj":
                raise StopBuild()
            gla_phase(self, dict(nh=H, nk=2, nv=2, ones=int(os.environ.get("E2", "1")), CL=128, nlf=1, q=self.mq, k=self.mk, lf=self.mlf,
                                 v=self.mv, out=self.mo, dbuf=False))
            if getattr(self.cfg, "dbg", None) == "scan":
                raise StopBuild()
            return self.readout_phase(b, li, cur, dst, ctx_out, self.mo, self.mog, self.ml_ngT, self.ml_wob, self.cfg.DC // H)
        if kind == 3:
            self.s5_scan_phase(b, li, cur)
            return self.s5_glu_phase(b, li, cur, dst, ctx_out)
        raise NotImplementedError

    def s5_scan_phase(self, b, li, cur):
        cfg, P = self.cfg, self.P
        D, DC, TC, TT, T = cfg.D, cfg.DC, cfg.TC, cfg.TT, cfg.T
        NT = 512
        s = 1
        PI = float(np.pi)
        MAGIC = 12582912.0
        tiles = [(0, TC)] + [(TC + i * NT, NT) for i in range(T // NT)]
        segs = [(0, TC), (TC, TT)]
        NCH = 4
        with contextlib.ExitStack() as st:
            u32 = self.sb(st, "u32", [128, TT], F32)
            ub = self.sb(st, "ub", [128, TT], BF16)
            yacc = self.sb(st, "yacc", [128, TT], F32)
            gy = self.sb(st, "gyo", [128, TT], BF16)
            iota = self.sb(st, "iota", [128, 512], F32)
            lam = self.sb(st, "lam", [128, 3, 2, DC, 4], F32)
            dsk = self.sb(st, "dsk", [128, DC], F32)
            tab = [[self.sb(st, "tab%d_%d" % (j, k), [128, 512], F32) for k in range(4)] for j in range(4)]
            Bf = [self.sb(st, "Bf%d" % i, [128, 4, 128], F32) for i in range(2)]
            Cf = [self.sb(st, "Cf%d" % i, [128, 4, 128], F32) for i in range(2)]
            Bb = [self.sb(st, "Bb%d" % i, [128, 4, 128], BF16) for i in range(2)]
            Cb = [self.sb(st, "Cb%d" % i, [128, 4, 128], BF16) for i in range(2)]
            lp = self.sb(st, "lp", [128, 4, 16], F32)
            x0 = self.sb(st, "x0", [128, 4, 2], F32)
            tmp = [[self.sb(st, "s5t%d_%d" % (j, k), [128, 512], F32) for k in range(6)] for j in range(NCH)]
            xb = [[self.sb(st, "s5x%d_%d" % (j, k), [128, 512], BF16) for k in range(2)] for j in range(NCH)]
            b_u, b_ub, b_y, b_gy, b_c, b_lam = Buf(), Buf(), Buf(), Buf(), Buf(), Buf()
            b_tab = bufs(4)
            b_B, b_C, b_lp = Buf(), Buf(), bufs(4)
            b_x0 = bufs(4)
            b_tmp = [bufs(6) for _ in range(NCH)]
            b_xb = [bufs(2) for _ in range(NCH)]
            P.dma("sp", iota[:], self.s5_iota[:, :], "c0", writes=[b_c])
            P.dma("sp", lam[:].rearrange("p a d c j -> p (a d c j)"), self.s5_lam[:, :], "c1", writes=[b_lam])
            P.dma("sp", dsk[:], self.s5_dT[:, :], "c2", writes=[b_c])
            for fc in range(DC):
                if cur is not None:
                    P.dma("sp", u32[:], cur[fc * 128:(fc + 1) * 128, :], "s5u", writes=[b_u])
                else:
                    P.dma("sp", u32[:, 0:TC], self.ctxT.rearrange("(b c p) t -> b c p t", p=128, c=DC)[b][fc], "s5u", writes=[b_u])
                    P.dma("sp", u32[:, TC:TT], self.xT.rearrange("(b c p) t -> b c p t", p=128, c=DC)[b][fc], "s5u", writes=[b_u])
                for (a0, a1), col in zip(segs, (1, 0)):
                    P.ts("dve", u32[:, a0:a1], u32[:, a0:a1], self.mod_col(li, 4, fc, col), self.mod_col(li, 3, fc, col),
                         ALU.mult, ALU.add, [b_u, self.b_const], [b_u])
                for d in range(2):
                    for (a0, a1) in segs:
                        src = u32[:, a0:a1]
                        if d == 1:
                            src = src[:, ::-1]
                        P.cp("pool", ub[:, a0:a1], src, [b_u], [b_ub])
                    r0 = (d * DC + fc) * 128
                    for i in range(2):
                        P.dma("sp", Bf[i][:].rearrange("p j l -> p (j l)"), self.s5_Bb[i][r0:r0 + 128, :], "s5b%d" % i, writes=[b_B])
                        P.dma("sp", Cf[i][:].rearrange("p j l -> p (j l)"), self.s5_Cb[i][r0:r0 + 128, :], "s5c%d" % i, writes=[b_C])
                    P.cp("dve", Bb[0][:], Bf[0][:], [b_B], [b_B])
                    P.cp("dve", Bb[1][:], Bf[1][:], [b_B], [b_B])
                    P.cp("pool", Cb[0][:], Cf[0][:], [b_C], [b_C])
                    P.act(Cb[1][:], Cf[1][:], AF.Identity, [b_C], [b_C], scale=-1.0)
                    for j in range(4):
                        L_ = lp[:, j, :]
                        bl = b_lp[j]
                        lr = lam[:, 0, d, fc, j:j + 1]
                        lim = lam[:, 1, d, fc, j:j + 1]
                        ldt = lam[:, 2, d, fc, j:j + 1]
                        dt, th, rr = L_[:, 0:1], L_[:, 1:2], L_[:, 2:3]
                        P.act(dt, ldt, AF.Exp, [b_lam], [bl])
                        P.tt("dve", th, lim, dt, ALU.mult, [b_lam, bl], [bl])
                        P.tt("dve", rr, lr, dt, ALU.mult, [b_lam, bl], [bl])
                        P.act(rr, rr, AF.Exp, [bl], [bl])
                        Rc, Rs, Tr, Ti = tab[j]
                        bt = b_tab[j]
                        for (dstt, off) in ((Rs, 0.0), (Rc, PI / 2)):
                            P.ts("dve", Tr[:], iota[:], th, off, ALU.mult, ALU.add, [b_c, bl], [bt])
                            P.ts("dve", Ti[:], Tr[:], 1.0 / (2 * PI), MAGIC, ALU.mult, ALU.add, [bt], [bt])
                            P.ts("dve", Ti[:], Ti[:], -MAGIC, None, ALU.add, None, [bt], [bt])
                            P.stt("dve", Tr[:], Ti[:], -2 * PI, Tr[:], ALU.mult, ALU.add, [bt], [bt])
                            P.ts("dve", Tr[:], Tr[:], -PI, PI, ALU.max, ALU.min, [bt], [bt])
                            P.act(dstt[:], Tr[:], AF.Sin, [bt], [bt])
                        nr, ni, den, fr, fi, t1, t2 = (L_[:, k:k + 1] for k in range(3, 10))
                        P.tt("dve", nr, rr, Rc[:, 0:1], ALU.mult, [bl, bt], [bl])
                        P.ts("dve", nr, nr, -1.0, None, ALU.add, None, [bl], [bl])
                        P.tt("dve", ni, rr, Rs[:, 0:1], ALU.mult, [bl, bt], [bl])
                        P.tt("dve", den, lr, lr, ALU.mult, [b_lam], [bl])
                        P.tt("dve", t1, lim, lim, ALU.mult, [b_lam], [bl])
                        P.tt("dve", den, den, t1, ALU.add, [bl], [bl])
                        P.op("dve", lambda e, den=den: e.reciprocal(out=den, in_=den), [bl], [bl])
                        P.tt("dve", t1, nr, lr, ALU.mult, [bl, b_lam], [bl])
                        P.tt("dve", t2, ni, lim, ALU.mult, [bl, b_lam], [bl])
                        P.tt("dve", fr, t1, t2, ALU.add, [bl], [bl])
                        P.tt("dve", fr, fr, den, ALU.mult, [bl], [bl])
                        P.tt("dve", t1, ni, lr, ALU.mult, [bl, b_lam], [bl])
                        P.tt("dve", t2, nr, lim, ALU.mult, [bl, b_lam], [bl])
                        P.tt("dve", fi, t1, t2, ALU.subtract, [bl], [bl])
                        P.tt("dve", fi, fi, den, ALU.mult, [bl], [bl])
                        P.ts("dve", Tr[:], Rc[:], fr, None, ALU.mult, None, [bt, bl], [bt])
                        P.stt("dve", Tr[:], Rs[:], fi, Tr[:], ALU.mult, ALU.add, [bt, bl], [bt])
                        P.ts("dve", Ti[:], Rs[:], fr, None, ALU.mult, None, [bt, bl], [bt])
                        P.stt("dve", Ti[:], Rc[:], fi, Ti[:], ALU.mult, ALU.subtract, [bt, bl], [bt])
                        P.memset("dve", x0[:, j, :], 0.0, [b_x0[j]])
                    for (t0, nt) in tiles:
                        pY, bY = self.ps[7], self.psb[7]
                        def make_chain(j, t0=t0, nt=nt, pY=pY, bY=bY):
                            Rc, Rs, Tr, Ti = tab[j]
                            bt, bl = b_tab[j], b_lp[j]
                            rr = lp[:, j, 2:3]
                            pr, bpr = self.ps[j % 3 * 2], self.psb[j % 3 * 2]
                            pi_, bpi = self.ps[j % 3 * 2 + 1], self.psb[j % 3 * 2 + 1]
                            tm, btm = tmp[j], b_tmp[j]
                            rb = rr.to_broadcast([128, nt])

                            def sA():
                                P.mm(pr[:, :nt], Bb[0][:, j, :], ub[:, t0:t0 + nt], True, True, [b_B, b_ub], [bpr])
                                P.mm(pi_[:, :nt], Bb[1][:, j, :], ub[:, t0:t0 + nt], True, True, [b_B, b_ub], [bpi])

                            def sB():
                                P.tt("dve", tm[0][:, :nt], pr[:, :nt], Tr[:, :nt], ALU.mult, [bpr, bt], [btm[0]])
                                P.tt("dve", tm[1][:, :nt], pi_[:, :nt], Ti[:, :nt], ALU.mult, [bpi, bt], [btm[1]])
                                P.tt("dve", tm[2][:, :nt], pi_[:, :nt], Tr[:, :nt], ALU.mult, [bpi, bt], [btm[2]])
                                P.tt("dve", tm[3][:, :nt], pr[:, :nt], Ti[:, :nt], ALU.mult, [bpr, bt], [btm[3]])

                            def sC():
                                P.tt("pool", tm[0][:, :nt], tm[0][:, :nt], tm[1][:, :nt], ALU.subtract, [btm[0], btm[1]], [btm[0]])
                                P.tt("pool", tm[2][:, :nt], tm[2][:, :nt], tm[3][:, :nt], ALU.add, [btm[2], btm[3]], [btm[2]])

                            def sD():
                                P.scan(tm[4][:, :nt], rb, tm[0][:, :nt], x0[:, j, 0:1], ALU.mult, ALU.add, [bl, btm[0], b_x0[j]], [btm[4]])
                                P.scan(tm[5][:, :nt], rb, tm[2][:, :nt], x0[:, j, 1:2], ALU.mult, ALU.add, [bl, btm[2], b_x0[j]], [btm[5]])

                            def sE():
                                P.tt("pool", tm[0][:, :nt], tm[4][:, :nt], Rc[:, :nt], ALU.mult, [btm[4], bt], [btm[0]])
                                P.tt("pool", tm[1][:, :nt], tm[5][:, :nt], Rs[:, :nt], ALU.mult, [btm[5], bt], [btm[1]])
                                P.tt("dve", tm[2][:, :nt], tm[5][:, :nt], Rc[:, :nt], ALU.mult, [btm[5], bt], [btm[2]])
                                P.tt("dve", tm[3][:, :nt], tm[4][:, :nt], Rs[:, :nt], ALU.mult, [btm[4], bt], [btm[3]])

                            def sF():
                                P.tt("pool", xb[j][0][:, :nt], tm[0][:, :nt], tm[1][:, :nt], ALU.subtract, [btm[0], btm[1]], [b_xb[j][0]])
                                P.tt("dve", xb[j][1][:, :nt], tm[2][:, :nt], tm[3][:, :nt], ALU.add, [btm[2], btm[3]], [b_xb[j][1]])
                                P.tt("dve", x0[:, j, 0:1], tm[0][:, nt - 1:nt], tm[1][:, nt - 1:nt], ALU.subtract, [btm[0], btm[1]], [b_x0[j]])
                                P.tt("dve", x0[:, j, 1:2], tm[2][:, nt - 1:nt], tm[3][:, nt - 1:nt], ALU.add, [btm[2], btm[3]], [b_x0[j]])

                            def sG():
                                P.mm(pY[:, :nt], Cb[0][:, j, :], xb[j][0][:, :nt], j == 0, False, [b_C, b_xb[j][0]], [bY])
                                P.mm(pY[:, :nt], Cb[1][:, j, :], xb[j][1][:, :nt], False, j == 3, [b_C, b_xb[j][1]], [bY])
                            return [sA, sB, sC, sD, sE, sF, sG]

                        for j0 in (0, 2):
                            pair = [make_chain(j0), make_chain(j0 + 1)]
                            for si in range(len(pair[0])):
                                for ch in pair:
                                    ch[si]()
                        if d == 0:
                            P.cp("dve", yacc[:, t0:t0 + nt], pY[:, :nt], [bY], [b_y])
                        else:
                            a0, a1 = (0, TC) if t0 < TC else (TC, TT)
                            n0 = a0 + a1 - (t0 + nt)
                            dst = yacc[:, n0:n0 + nt][:, ::-1]
                            P.tt("dve", dst, dst, pY[:, :nt], ALU.add, [bY, b_y], [b_y])
                for (a0, a1) in [(0, TC)] + [(TC + i * 1024, TC + min(T, (i + 1) * 1024)) for i in range((T + 1023) // 1024)]:
                    a1 = min(a1, TT)
                    ya, ua = yacc[:, a0:a1], u32[:, a0:a1]
                    P.stt("dve", ya, ua, dsk[:, fc:fc + 1], ya, ALU.mult, ALU.add, [b_u, b_y, b_c], [b_y])
                    P.act(ua, ya, AF.Square, [b_y, b_u], [b_u])
                    P.ts("dve", ua, ua, 0.044715, 1.0, ALU.mult, ALU.add, [b_u], [b_u])
                    P.tt("dve", ua, ua, ya, ALU.mult, [b_u, b_y], [b_u])
                    P.act(ua, ua, AF.Sigmoid, [b_u], [b_u], scale=float(2.0 * np.sqrt(2.0 / np.pi)))
                    P.tt("dve", gy[:, a0:a1], ya, ua, ALU.mult, [b_y, b_u], [b_gy])
                P.dma("sp", self.sgy[fc, :, :], gy[:], "s5o", reads=[b_gy])
            P.barrier()
            P.flush()

    def s5_glu_phase(self, b, li, cur, dst, ctx_out):
        cfg, P = self.cfg, self.P
        D, DC, TC, NT = cfg.D, cfg.DC, cfg.TC, cfg.NT
        s = 1
        with contextlib.ExitStack() as st:
            xt = [self.sb(st, "xt%d" % i, [128, DC, NT], F32) for i in range(2)]
            gt = self.sb(st, "gyT", [128, DC, NT], BF16)
            ws = [self.sb(st, "gws%d" % i, [128, DC, 128], BF16) for i in range(4)]
            sg = [self.sb(st, "gsg%d" % i, [128, NT], F32) for i in range(2)]
            yv = [self.sb(st, "gyv%d" % i, [128, NT], F32) for i in range(2)]
            b_x = [bufs(DC) for _ in range(2)]
            b_gt = Buf()
            b_ws, b_sg, b_yv = bufs(4), bufs(2), bufs(2)
            pn = PN(self, st, NT)
            wv = self.s5_wglub.rearrange("(r p) (k n) -> r p k n", p=128, n=128)
            wi = 0
            tiles = cfg.tiles if ctx_out else cfg.tiles[1:]
            for ti, (t0, nt) in enumerate(tiles):
                col = 0 if t0 >= TC else 1
                X, bX = xt[ti % 2], b_x[ti % 2]
                P.dma("sp", X[:, :, :nt], self.src_ap(b, cur, t0, nt), "xl%d" % (ti % 2), writes=bX)
                P.dma("sp", gt[:, :, :nt], self.sgy.rearrange("c p t -> p c t")[:, :, t0:t0 + nt], "rg", writes=[b_gt])
                pn.begin(li, s, X, bX, nt, col)
                for m in range(DC):
                    pa, ba = self.ps[m % 2], self.psb[m % 2]
                    pg, bg = self.ps[2 + m % 2], self.psb[2 + m % 2]
                    for (oc, pp, bp) in ((m, pa, ba), (DC + m, pg, bg)):
                        W, bW = ws[wi % 4], b_ws[wi % 4]
                        P.dma("sp", W[:], wv[oc], "gw%d" % (wi % 4), writes=[bW])
                        wi += 1
                        for kc in range(DC):
                            P.mm(pp[:, :nt], W[:, kc, :], gt[:, kc, :nt], kc == 0, kc == DC - 1, [bW, b_gt], [bp])
                    P.act(sg[m % 2][:, :nt], pg[:, :nt], AF.Sigmoid, [bg], [b_sg[m % 2]])
                    P.tt("dve", yv[m % 2][:, :nt], pa[:, :nt], sg[m % 2][:, :nt], ALU.mult, [ba, b_sg[m % 2]], [b_yv[m % 2]])
                    pn.add(m, yv[m % 2][:, :nt], [b_yv[m % 2]], self.mod_col(li, 5, m, col))
                pn.finish()
                P.dma("sp", dst.rearrange("(c p) t -> p c t", p=128)[:, :, t0:t0 + nt], X[:, :, :nt],
                      "xst%d" % (ti % 2), reads=bX)
            P.barrier()
            P.flush()

    def hg_proj_phase(self, b, li, cur):
        cfg, P = self.cfg, self.P
        D, DC, TC, NT = cfg.D, cfg.DC, cfg.TC, cfg.NT
        s = 1
        with contextlib.ExitStack() as st:
            xt1 = self.sb(st, "xt", [128, DC, NT], F32)
            xt = [xt1, xt1]
            hT = self.sb(st, "hT", [128, DC, NT], BF16)
            ws = [self.sb(st, "hws%d" % i, [128, DC, 128], BF16) for i in range(3)]
            oq2 = [self.sb(st, "oq%d" % i, [128, DC, NT], BF16) for i in range(2)]
            oq = [oq2[i % 2] for i in range(5)]
            olf = [self.sb(st, "olf%d" % i, [128, DC, NT], F32) for i in range(2)]
            t1 = [self.sb(st, "ht1_%d" % i, [128, NT], F32) for i in range(2)]
            t2 = [self.sb(st, "ht2_%d" % i, [128, NT], F32) for i in range(2)]
            lg = self.sb(st, "lg", [128, 2, 4, DC], F32)
            oml = self.sb(st, "oml", [128, 2, DC], F32)
            den = self.sb(st, "lden", [128, 2, DC], F32)
            onec = self.sb(st, "onec", [128, 1], F32)
            b_x1 = bufs(DC)
            b_x = [b_x1, b_x1]
            b_h, b_lb = Buf(), Buf()
            b_ws, b_t1, b_t2 = bufs(3), bufs(2), bufs(2)
            b_oq2 = bufs(2)
            b_oq = [b_oq2[i % 2] for i in range(5)]
            b_olf = bufs(2)
            P.dma("sp", lg[:].rearrange("p d l c -> p (d l c)"), self.hg_lbT[:, :], "c0", writes=[b_lb])
            P.memset("dve", onec[:], 1.0, [b_lb])
            P.act(lg[:], lg[:], AF.Exp, [b_lb], [b_lb])
            P.cp("dve", den[:], lg[:, :, 0, :], [b_lb], [b_lb])
            for l in range(1, 4):
                P.tt("dve", den[:], den[:], lg[:, :, l, :], ALU.add, [b_lb], [b_lb])
            P.cp("dve", oml[:], lg[:, :, 0, :], [b_lb], [b_lb])
            for l in range(1, li + 1):
                P.tt("dve", oml[:], oml[:], lg[:, :, l, :], ALU.add, [b_lb], [b_lb])
            P.op("dve", lambda e: e.reciprocal(out=den[:], in_=den[:]), [b_lb], [b_lb])
            P.tt("dve", oml[:], oml[:], den[:], ALU.mult, [b_lb], [b_lb])
            P.ts("dve", oml[:], oml[:], -1.0, 1.0, ALU.mult, ALU.add, [b_lb], [b_lb])
            wv = self.hg_winb.rearrange("(r p) (k n) -> r p k n", p=128, n=128)
            wi = 0
            outs = [self.hq, self.hv, self.hgt, self.hk[0], self.hk[1]]
            for ti, (t0, nt) in enumerate(cfg.tiles):
                col = 0 if t0 >= TC else 1
                X, bX = xt[ti % 2], b_x[ti % 2]
                P.dma("sp", X[:, :, :nt], self.src_ap(b, cur, t0, nt), "xl%d" % (ti % 2), writes=bX)
                for c in range(DC):
                    P.ts("pool", hT[:, c, :nt], X[:, c, :nt], self.mod_col(li, 3 * s + 1, c, col),
                         self.mod_col(li, 3 * s + 0, c, col), ALU.mult, ALU.add, [bX[c], self.b_const], [b_h])
                for sec in range(5):
                    for c in range(DC):
                        oc = sec * DC + c
                        W, bW = ws[wi % 3], b_ws[wi % 3]
                        P.dma("sp", W[:], wv[oc], "hw%d" % (wi % 3), writes=[bW])
                        wi += 1
                        pp, bp = self.ps[oc % 4], self.psb[oc % 4]
                        for kc in range(DC):
                            P.mm(pp[:, :nt], W[:, kc, :], hT[:, kc, :nt], kc == 0, kc == DC - 1, [bW, b_h], [bp])
                        if sec == 0 or sec == 2:
                            P.act(oq[sec][:, c, :nt], pp[:, :nt], AF.Silu, [bp], [b_oq[sec]])
                        elif sec == 1:
                            P.cp("dve", oq[1][:, c, :nt], pp[:, :nt], [bp], [b_oq[1]])
                        else:
                            d = sec - 3
                            T1, T2 = t1[c % 2], t2[c % 2]
                            P.act(T1[:, :nt], pp[:, :nt], AF.Sigmoid, [bp], [b_t1[c % 2]], scale=-1.0)
                            P.ts("dve", T2[:, :nt], T1[:, :nt], oml[:, d, c:c + 1], None, ALU.mult, None,
                                 [b_t1[c % 2], b_lb], [b_t2[c % 2]])
                            P.cp("pool", oq[sec][:, c, :nt], T2[:, :nt], [b_t2[c % 2]], [b_oq[sec]])
                            P.act(olf[d][:, c, :nt], T2[:, :nt], AF.Ln, [b_t2[c % 2], b_lb], [b_olf[d]], scale=-1.0, bias=onec[:, 0:1])
                    dstT = outs[sec].rearrange("c p t -> p c t")[:, :, t0:t0 + nt]
                    P.dma("sp", dstT, oq[sec][:, :, :nt], "hs%d" % sec, reads=[b_oq[sec]])
                    if sec >= 3:
                        P.dma("sp", self.hlf[sec - 3].rearrange("c p t -> p c t")[:, :, t0:t0 + nt], olf[sec - 3][:, :, :nt],
                              "hl%d" % (sec - 3), reads=[b_olf[sec - 3]])
            P.barrier()
            P.flush()

    def hg_readout_phase(self, b, li, cur, dst, ctx_out):
        self.readout_phase(b, li, cur, dst, ctx_out, self.ho, self.hgt, self.hg_ngT, self.hg_wob, 1)

    def readout_phase(self, b, li, cur, dst, ctx_out, o_dirs, gate_s, ng_in, wob, hc):
        cfg, P = self.cfg, self.P
        D, DC, TC, NT = cfg.D, cfg.DC, cfg.TC, cfg.NT
        s = 1
        with contextlib.ExitStack() as st:
            xt1 = self.sb(st, "xt", [128, DC, NT], F32)
            xt = [xt1, xt1]
            o0 = self.sb(st, "ro0", [128, DC, NT], F32)
            o1 = self.sb(st, "ro1", [128, DC, NT], F32)
            gt = self.sb(st, "rgt", [128, DC, NT], BF16)
            yT = self.sb(st, "ryT", [128, DC, NT], BF16)
            sq = [self.sb(st, "rsq%d" % i, [128, hc, NT], BF16) for i in range(2)]
            rs = [self.sb(st, "rrs%d" % i, [128, NT], F32) for i in range(2)]
            ws = [self.sb(st, "rws%d" % i, [128, DC, 128], BF16) for i in range(3)]
            ng = self.sb(st, "rng", [128, DC], F32)
            b_x1 = bufs(DC)
            b_x = [b_x1, b_x1]
            b_o0, b_o1, b_gt = bufs(DC), bufs(DC), Buf()
            b_y = bufs(DC)
            b_sq, b_rs, b_ws = bufs(2), bufs(2), bufs(3)
            b_ng = Buf()
            pn = PN(self, st, NT)
            P.dma("sp", ng[:], ng_in[:, :], "c0", writes=[b_ng])
            wv = wob.rearrange("(r p) (k n) -> r p k n", p=128, n=128)
            wi = 0
            tiles = cfg.tiles if ctx_out else cfg.tiles[1:]
            for ti, (t0, nt) in enumerate(tiles):
                col = 0 if t0 >= TC else 1
                X, bX = xt[ti % 2], b_x[ti % 2]
                P.dma("sp", X[:, :, :nt], self.src_ap(b, cur, t0, nt), "xl%d" % (ti % 2), writes=bX)
                P.dma("sp", o0[:, :, :nt], o_dirs[0].rearrange("c p t -> p c t")[:, :, t0:t0 + nt], "ro0", writes=b_o0)
                P.dma("sp", o1[:, :, :nt], o_dirs[1].rearrange("c p t -> p c t")[:, :, t0:t0 + nt], "ro1", writes=b_o1)
                P.dma("sp", gt[:, :, :nt], gate_s.rearrange("c p t -> p c t")[:, :, t0:t0 + nt], "rg", writes=[b_gt])
                for hd in range(DC // hc):
                    r = hd % 2
                    for cc in range(hc):
                        c = hd * hc + cc
                        P.tt("dve", o0[:, c, :nt], o0[:, c, :nt], o1[:, c, :nt], ALU.add, [b_o0[c], b_o1[c]], [b_o0[c]])
                        P.act(sq[r][:, cc, :nt], o0[:, c, :nt], AF.Square, [b_o0[c]], [b_sq[r]])
                    pp, bp = self.ps[hd % 4], self.psb[hd % 4]
                    for cc in range(hc):
                        P.mm(pp[:, :nt], self.ones[:], sq[r][:, cc, :nt], cc == 0, cc == hc - 1, [b_sq[r], self.b_const], [bp])
                    P.ts("dve", rs[r][:, :nt], pp[:, :nt], 1.0 / (128 * hc), RMS_EPS, ALU.mult, ALU.add, [bp], [b_rs[r]])
                    P.act(rs[r][:, :nt], rs[r][:, :nt], AF.Sqrt, [b_rs[r]], [b_rs[r]])
                    P.op("dve", lambda e, a=rs[r], nt=nt: e.reciprocal(out=a[:, :nt], in_=a[:, :nt]), [b_rs[r]], [b_rs[r]])
                    for cc in range(hc):
                        c = hd * hc + cc
                        P.tt("pool", o0[:, c, :nt], o0[:, c, :nt], rs[r][:, :nt], ALU.mult, [b_o0[c], b_rs[r]], [b_o0[c]])
                        P.stt("dve", yT[:, c, :nt], o0[:, c, :nt], ng[:, c:c + 1], gt[:, c, :nt], ALU.mult, ALU.mult,
                              [b_o0[c], b_ng, b_gt], [b_y[c]])
                pn.begin(li, s, X, bX, nt, col)
                for m in range(DC):
                    W, bW = ws[wi % 3], b_ws[wi % 3]
                    P.dma("sp", W[:], wv[m], "rw%d" % (wi % 3), writes=[bW])
                    wi += 1
                    po, bo = self.ps[4 + m % 2], self.psb[4 + m % 2]
                    for kc in range(DC):
                        P.mm(po[:, :nt], W[:, kc, :], yT[:, kc, :nt], kc == 0, kc == DC - 1, [bW, b_y[kc]], [bo])
                    pn.add(m, po[:, :nt], [bo], self.mod_col(li, 5, m, col))
                pn.finish()
                P.dma("sp", dst.rearrange("(c p) t -> p c t", p=128)[:, :, t0:t0 + nt], X[:, :, :nt],
                      "xst%d" % (ti % 2), reads=bX)
            if not ctx_out:
                pass
            P.barrier()
            P.flush()

    def pool_phase(self, b, li, cur, dst, ctx_out):
        cfg = self.cfg
        P = self.P
        D, DC, TC, NT = cfg.D, cfg.DC, cfg.TC, cfg.NT
        NJ = 2
        CPG = DC // 4
        W = D // 4
        s = 1
        with contextlib.ExitStack() as st:
            xt = [self.sb(st, "xt%d" % i, [128, DC, NT], F32) for i in range(2)]
            PAD = 8
            HW = NT + 2 * PAD * (NT // GRID_W)
            hf = self.sb(st, "hf", [128, DC, HW], F32)
            pdT = self.sb(st, "pdT", [128, DC, NT], BF16)
            pw = self.sb(st, "pw", [128, 4, CPG, W], BF16)
            psc = self.sb(st, "psc", [128, DC], F32)
            sg = self.sb(st, "sg", [128, 2, DC], F32)
            invl = self.sb(st, "invl", [128, 4, GRID_W], F32)
            invc = self.sb(st, "invc", [128, 4, TC], F32)
            tmp = {e: [self.sb(st, "ptmp_%s%d" % (e, i), [128, HW], F32) for i in range(2)] for e in ("dve", "pool")}
            b_tmp = {e: bufs(2) for e in ("dve", "pool")}
            b_xc = [bufs(DC) for _ in range(2)]
            b_hf, b_pd = bufs(DC), bufs(DC)
            b_pw, b_psc, b_sg, b_inv = Buf(), Buf(), Buf(), Buf()
            pn = PN(self, st, NT)
            P.dma("sp", pw[:].rearrange("p g k o -> p (g k o)"), self.pool_wBb[:, :], "c0", writes=[b_pw])
            P.dma("sp", psc[:], self.pool_scT[:, :], "c1", writes=[b_psc])
            P.dma("sp", invl[:].rearrange("p g w -> p (g w)"), self.pool_inv_lat[:, :], "c2", writes=[b_inv])
            P.dma("sp", invc[:].rearrange("p g w -> p (g w)"), self.pool_inv_ctx[:, :], "c3", writes=[b_inv])
            for ci, col in enumerate((0, 1)):
                o0 = ((li * 9 + 5) * DC) * NJ
                gv = self.modT[:, o0:o0 + DC * NJ].rearrange("p (c j) -> p c j", j=NJ)[:, :, col]
                P.tt("dve", sg[:, ci, :], gv, psc[:], ALU.mult, [self.b_const, b_psc], [b_sg])
            tiles = cfg.tiles if ctx_out else cfg.tiles[1:]
            for ti, (t0, nt) in enumerate(tiles):
                lat = t0 >= TC
                col = 0 if lat else 1
                ci = 0 if lat else 1
                R = GRID_W if lat else TC
                inv = invl if lat else invc
                X = xt[ti % 2]
                bX = b_xc[ti % 2]
                P.dma("sp", X[:, :, :nt], self.src_ap(b, cur, t0, nt), "xl%d" % (ti % 2), writes=bX)
                RP = R + 2 * PAD
                rows = nt // R
                if ti < 2:
                    P.memset("dve", hf[:], 0.0, b_hf)
                for c in range(DC):
                    eng = "dve" if c % 2 == 0 else "pool"

                    def vp(ap):
                        return ap[:, :rows * RP].rearrange("p (r w) -> p r w", w=RP)

                    def v(ap):
                        return ap.rearrange("p (r w) -> p r w", w=R)
                    hp = vp(hf[:, c, :])
                    hin = hp[:, :, PAD:PAD + R]
                    P.ts(eng, hin, v(X[:, c, :nt]), self.mod_col(li, 3 * s + 1, c, col),
                         self.mod_col(li, 3 * s + 0, c, col), ALU.mult, ALU.add, [bX[c], self.b_const], [b_hf[c]])
                    wi = c // CPG
                    w = cfg.pool_windows[wi]
                    A, Bt = vp(tmp[eng][0]), vp(tmp[eng][1])
                    bA, bB = b_tmp[eng]
                    P.tt(eng, A[:, :, 1:RP], hp[:, :, 1:RP], hp[:, :, 0:RP - 1], ALU.add, [b_hf[c]], [bA])
                    srcv, bs, dstv, bd = A, bA, Bt, bB
                    d = 1
                    while 4 * d <= w:
                        P.tt(eng, dstv[:, :, d:RP - d], srcv[:, :, 0:RP - 2 * d], srcv[:, :, 2 * d:RP], ALU.add, [bs], [bd])
                        srcv, bs, dstv, bd = dstv, bd, srcv, bs
                        d *= 2
                    ib = inv[:, wi, :].unsqueeze(1).to_broadcast([128, rows, R])
                    P.tt(eng, srcv[:, :, PAD:PAD + R], srcv[:, :, PAD:PAD + R], ib, ALU.mult, [bs, b_inv], [bs])
                    P.tt(eng, v(pdT[:, c, :nt]), srcv[:, :, PAD:PAD + R], hin, ALU.subtract, [bs, b_hf[c]], [b_pd[c]])
                pn.begin(li, s, X, bX, nt, col)
                for m in range(DC):
                    g = m // CPG
                    po, bo = self.ps[4 + m % 2], self.psb[4 + m % 2]
                    for kk in range(CPG):
                        kc = g * CPG + kk
                        P.mm(po[:, :nt], pw[:, g, kk, (m % CPG) * 128:(m % CPG + 1) * 128], pdT[:, kc, :nt],
                             kk == 0, kk == CPG - 1, [b_pw, b_pd[kc]], [bo])
                    pn.add(m, po[:, :nt], [bo, b_sg], sg[:, ci, m:m + 1])
                pn.finish()
                P.dma("sp", dst.rearrange("(c p) t -> p c t", p=128)[:, :, t0:t0 + nt], X[:, :, :nt],
                      "xst%d" % (ti % 2), reads=bX)
            if not ctx_out:
                pass
            P.barrier()
            P.flush()


def gla_phase(B, spec):
    cfg, P = B.cfg, B.P
    TC, TT = cfg.TC, cfg.TT
    nh, nk, nv, ones, CL, nlf = spec["nh"], spec["nk"], spec["nv"], spec["ones"], spec["CL"], spec["nlf"]
    nvt = nv + ones
    NSUB = 128 // CL
    BL = 128
    nblk = TT // BL
    ncb = TC // BL
    NL = 2 if spec.get("dbuf", True) else 1
    order = [list(range(nblk)), list(range(ncb - 1, -1, -1)) + list(range(nblk - 1, ncb - 1, -1))]
    with contextlib.ExitStack() as st:
        ident = B.sb(st, "ident", [128, 128], BF16)
        identf = B.sb(st, "identf", [128, 128], F32)
        cmask = B.sb(st, "cmask", [128, 128], F32)
        rmask = B.sb(st, "rmask", [128, NSUB], F32)
        reset = B.sb(st, "reset", [128, 128], F32)
        b_cst = Buf()
        P.dma("sp", identf[:], B.c_ident[:, :], "c0", writes=[b_cst])
        P.dma("sp", cmask[:], B.c_cmask[CL][:, :], "c1", writes=[b_cst])
        P.dma("sp", rmask[:], B.c_rmask[CL][:, :], "c2", writes=[b_cst])
        P.dma("sp", reset[:], B.c_reset[CL][:, :], "c3", writes=[b_cst])
        P.cp("dve", ident[:], identf[:], [b_cst], [b_cst])
        D_ = {}
        R2 = 2
        for d in range(2):
            t = {}
            t["Lq"] = [B.sb(st, "Lq%d_%d" % (d, i), [128, nh * nk, BL], BF16) for i in range(NL)]
            t["Lk"] = [B.sb(st, "Lk%d_%d" % (d, i), [128, nh * nk, BL], BF16) for i in range(NL)]
            t["Ll"] = [B.sb(st, "Ll%d_%d" % (d, i), [128, nh * nlf, BL], F32) for i in range(NL)]
            t["Lv"] = [B.sb(st, "Lv%d_%d" % (d, i), [128, nh * nv, BL], BF16) for i in range(NL)]
            t["bL"] = [bufs(4) for _ in range(NL)]
            t["Oo"] = [B.sb(st, "Oo%d_%d" % (d, i), [128, nh * nv, BL], F32) for i in range(NL)]
            t["bOo"] = bufs(NL)
            shapes = {"b": ([128, nk, BL], F32), "eb": ([128, nk, BL], F32), "enb": ([128, nk, BL], F32),
                      "qe": ([128, nk, BL], BF16), "ke": ([128, nk, BL], BF16), "vb": ([128, nv, BL], BF16),
                      "attm": ([128, BL], BF16), "vtok": ([128, nvt * 128], BF16), "ktok": ([128, NSUB, nk * 128], BF16),
                      "ud": ([128, nvt * 128], F32), "rc": ([128, BL], F32)}
            for nm, (shp, dt) in shapes.items():
                t[nm] = [B.sb(st, "g%s%d_%d" % (nm, d, i), shp, dt) for i in range(R2)]
                t["B" + nm] = bufs(R2)
            t["S"] = B.sb(st, "gS%d" % d, [128, nh, nk, nvt * 128], F32)
            t["Sb"] = B.sb(st, "gSb%d" % d, [128, nh, NSUB, nk * nvt * 128], BF16)
            t["bS"] = bufs(nh)
            t["bSb"] = bufs(nh)
            P.memset("dve", t["S"][:], 0.0, t["bS"])
            P.memset("pool", t["Sb"][:], 0.0, t["bSb"])
            if ones:
                for i in range(R2):
                    P.memset("pool", t["vtok"][i][:, nv * 128:], 1.0, [t["Bvtok"][i]])
            t["pb"] = d * 4
            t["bAatt"] = t["bAkt"] = B.psb[d * 4]
            t["bBv"] = B.psb[d * 4 + 1]
            t["bO"] = [B.psb[d * 4 + 1], B.psb[d * 4 + 1]]
            t["bU"] = [B.psb[d * 4 + 2], B.psb[d * 4 + 3]]
            t["oi"] = 0
            D_[d] = t
        STG = int(os.environ.get("E4", "9"))
        for step in range(min(nblk, int(os.environ.get("E3", "100000")))):
            for d in range(2):
                t = D_[d]
                blk = order[d][step]
                t0 = blk * BL
                rev = (d == 1)
                li = step % NL
                Lq, Lk, Ll, Lv = t["Lq"][li], t["Lk"][li], t["Ll"][li], t["Lv"][li]
                bLq, bLk, bLl, bLv = t["bL"][li]
                P.dma("sp", Lq[:], spec["q"].rearrange("c p t -> p c t")[:, :, t0:t0 + BL], "gq%d%d" % (d, li), writes=[bLq])
                P.dma("sp", Lk[:], spec["k"][d].rearrange("c p t -> p c t")[:, :, t0:t0 + BL], "gk%d%d" % (d, li), writes=[bLk])
                P.dma("sp", Ll[:], spec["lf"][d].rearrange("c p t -> p c t")[:, :, t0:t0 + BL], "gl%d%d" % (d, li), writes=[bLl])
                P.dma("sp", Lv[:], spec["v"].rearrange("c p t -> p c t")[:, :, t0:t0 + BL], "gv%d%d" % (d, li), writes=[bLv])
                Oo, bOo = t["Oo"][li], t["bOo"][li]

                def rv(ap, rev=rev):
                    return ap[:, ::-1] if rev else ap
                pb = t["pb"]
                pA, pBk, pU = B.ps[pb], B.ps[pb + 1], [B.ps[pb + 2], B.ps[pb + 3]]
                bAatt, bAkt, bBv, bO, bU = t["bAatt"], t["bAkt"], t["bBv"], t["bO"], t["bU"]
                S, Sb = t["S"], t["Sb"]
                def make_head(h, t=t, rv=rv, Lq=Lq, Lk=Lk, Ll=Ll, Lv=Lv, bLq=bLq, bLk=bLk, bLl=bLl, bLv=bLv, Oo=Oo, bOo=bOo, pb=pb):
                    r = h % R2
                    g = {nm: t[nm][r] for nm in ("b", "eb", "enb", "qe", "ke", "vb", "attm", "vtok", "ktok", "ud", "rc")}
                    G = {nm: t["B" + nm][r] for nm in g}
                    bS, bSb = t["bS"][h], t["bSb"][h]
                    S, Sb = t["S"], t["Sb"]
                    b_, eb, enb, qe, ke, vb = g["b"], g["eb"], g["enb"], g["qe"], g["ke"], g["vb"]
                    attm, vtok, ktok, ud, rc = g["attm"], g["vtok"], g["ktok"], g["ud"], g["rc"]
                    if nk == 1:
                        ia, ib = pb + 2 * r, pb + 2 * r + 1
                        pA, pBk = B.ps[ia], B.ps[ib]
                        bA, bBk = B.psb[ia], B.psb[ib]
                        pUr = [pA[:, 384:512]]
                        bU = [bA]
                    else:
                        pA, pBk = B.ps[pb], B.ps[pb + 1]
                        bA, bBk = B.psb[pb], B.psb[pb + 1]
                        pUr = [B.ps[pb + 2][:, 0:nvt * 128], B.ps[pb + 3][:, 0:nvt * 128]]
                        bU = [B.psb[pb + 2], B.psb[pb + 3]]

                    def s_prep():
                        for kc in range(nk):
                            lfi = h * nlf + (kc if nlf == nk else 0)
                            if kc == 0 or nlf == nk:
                                P.scan(b_[:, kc, :], reset[:], rv(Ll[:, lfi, :]), 0.0, ALU.mult, ALU.add, [b_cst, bLl], [G["b"]])
                            src_b = b_[:, kc if nlf == nk else 0, :]
                            P.act(eb[:, kc, :], src_b, AF.Exp, [G["b"]], [G["eb"]])
                            P.act(enb[:, kc, :], src_b, AF.Exp, [G["b"]], [G["enb"]], scale=-1.0)
                            P.tt("dve", qe[:, kc, :], rv(Lq[:, h * nk + kc, :]), eb[:, kc, :], ALU.mult, [bLq, G["eb"]], [G["qe"]])
                            P.tt("pool", ke[:, kc, :], rv(Lk[:, h * nk + kc, :]), enb[:, kc, :], ALU.mult, [bLk, G["enb"]], [G["ke"]])
                        for vc in range(nv):
                            P.cp("pool", vb[:, vc, :], rv(Lv[:, h * nv + vc, :]), [bLv], [G["vb"]])

                    def s_pe1():
                        for kc in range(nk):
                            P.mm(pA[:, 0:128], ke[:, kc, :], qe[:, kc, :], kc == 0, kc == nk - 1, [G["ke"], G["qe"]], [bA])
                        for kc in range(nk):
                            P.mm(pA[:, 128 + kc * 128:256 + kc * 128], ke[:, kc, :], ident[:], True, True, [G["ke"], b_cst], [bA])
                        for vc in range(nv):
                            P.mm(pBk[:, vc * 128:(vc + 1) * 128], vb[:, vc, :], ident[:], True, True, [G["vb"], b_cst], [bBk])

                    def s_evac():
                        P.tt("dve", attm[:], pA[:, 0:128], cmask[:], ALU.mult, [bA, b_cst], [G["attm"]])
                        P.act(vtok[:, 0:nv * 128], pBk[:, 0:nv * 128], AF.Identity, [bBk], [G["vtok"]])
                        for j in range(NSUB):
                            P.act(ktok[:, j, :], pA[:, 128:128 + nk * 128], AF.Identity, [bA, b_cst], [G["ktok"]],
                                  scale=rmask[:, j:j + 1])

                    def chain(j):
                        for kc in range(nk):
                            P.mm(pUr[kc], ktok[:, j, kc * 128:(kc + 1) * 128], vtok[:], True, True,
                                 [G["ktok"], G["vtok"]], [bU[kc]])
                        for kc in range(nk):
                            dec = eb[:, kc, (j + 1) * CL - 1:(j + 1) * CL]
                            P.act(ud[:], pUr[kc], AF.Identity, [bU[kc], G["eb"]], [G["ud"]], scale=dec)
                            P.stt("dve", S[:, h, kc, :], S[:, h, kc, :], dec, ud[:], ALU.mult, ALU.add,
                                  [bS, G["ud"], G["eb"]], [bS])
                            jn = (j + 1) % NSUB
                            P.cp("pool", Sb[:, h, jn, kc * nvt * 128:(kc + 1) * nvt * 128], S[:, h, kc, :], [bS], [bSb])

                    def s_out():
                        vorder = ([nv] if ones else []) + list(range(nv))
                        for vc in vorder:
                            oi = t["oi"] % 2
                            t["oi"] += 1
                            pO = pBk[:, 256 + oi * 128:384 + oi * 128]
                            P.mm(pO, vtok[:, vc * 128:(vc + 1) * 128], attm[:], True, False, [G["vtok"], G["attm"]], [bBk])
                            n_in = NSUB * nk
                            ii = 0
                            for j in range(NSUB):
                                for kc in range(nk):
                                    ii += 1
                                    o0 = kc * nvt * 128 + vc * 128
                                    P.mm(pBk[:, 256 + oi * 128 + j * CL:256 + oi * 128 + (j + 1) * CL],
                                         Sb[:, h, j, o0:o0 + 128], qe[:, kc, j * CL:(j + 1) * CL], False, ii == n_in,
                                         [bSb, G["qe"]], [bBk])
                            if ones and vc == nv:
                                P.act(rc[:], pO, AF.Abs, [bBk], [G["rc"]])
                                P.ts("dve", rc[:], rc[:], 1.0, None, ALU.max, None, [G["rc"]], [G["rc"]])
                                P.op("dve", lambda e, rc=rc: e.reciprocal(out=rc[:], in_=rc[:]), [G["rc"]], [G["rc"]])
                            elif ones:
                                P.tt("dve", rv(Oo[:, h * nv + vc, :]), pO, rc[:], ALU.mult, [bBk, G["rc"]], [bOo])
                            else:
                                P.cp("dve", rv(Oo[:, h * nv + vc, :]), pO, [bBk], [bOo])

                    stages = [s_prep, s_pe1, s_evac]
                    for j in range(NSUB - 1):
                        stages.append(lambda j=j: chain(j))
                    stages.append(s_out)
                    stages.append(lambda: chain(NSUB - 1))
                    return stages

                PW = 2 if nk == 1 else 1
                for h0 in range(0, nh, PW):
                    pair = [make_head(h) for h in range(h0, min(nh, h0 + PW))]
                    for si in range(len(pair[0])):
                        for stg in pair:
                            stg[si]()
                P.dma("sp", spec["out"][d].rearrange("c p t -> p c t")[:, :, t0:t0 + BL], Oo[:], "go%d%d" % (d, li), reads=[bOo])
        P.barrier()
        P.flush()


def fm_cols(v):
    v = np.asarray(v, np.float32)
    lead = v.shape[:-1]
    n = v.shape[-1] // 128
    v = v.reshape(lead + (n, 128))
    v = np.moveaxis(v, -1, 0)
    return np.ascontiguousarray(v.reshape(128, -1))


def pool_inv_table(n, windows):
    out = np.zeros((len(windows), n), np.float32)
    pos = np.arange(n)
    for i, w in enumerate(windows):
        lo = np.clip(pos - w // 2, 0, n - 1)
        hi = np.clip(pos + w - w // 2 - 1, 0, n - 1)
        out[i] = 1.0 / (hi - lo + 1)
    return out


def prep_shared(cfg, inp):
    D, F, DC, FC, L = cfg.D, cfg.F, cfg.DC, cfg.FC, cfg.L
    f32 = lambda a: np.asarray(a, np.float32)
    m = {}
    m["mod_w"] = np.ascontiguousarray(f32(inp["mod_w"]).reshape(L * D, 9 * D))
    m["mod_bT"] = fm_cols(f32(inp["mod_b"]))
    m["ln_gT"] = fm_cols(f32(inp["ln_g"]))
    m["ln_bT"] = fm_cols(f32(inp["ln_b"]))
    w13 = np.stack([f32(inp["ffn1_w13"]), f32(inp["ffn2_w13"])], 1)
    w13 = w13.reshape(L, 2, DC, 128, 2, FC, 128)
    w13 = w13.transpose(0, 1, 5, 3, 2, 4, 6)
    m["w13"] = np.ascontiguousarray(w13).reshape(L * 2 * FC * 128, DC * 256)
    w2 = np.stack([f32(inp["ffn1_w2"]), f32(inp["ffn2_w2"])], 1)
    w2 = w2.reshape(L, 2, FC, 128, DC, 128)
    w2 = w2.transpose(0, 1, 4, 3, 2, 5)
    m["w2"] = np.ascontiguousarray(w2).reshape(L * 2 * DC * 128, FC * 128)
    if 1 in cfg.kinds:
        W = D // 4
        CPG = DC // 4
        pw = f32(inp["pool_w"])[0].reshape(4, CPG, 128, W).transpose(2, 0, 1, 3)
        m["pool_wB"] = np.ascontiguousarray(pw).reshape(128, -1)
        m["pool_scT"] = fm_cols(f32(inp["pool_scale"])[0])
        m["pool_inv_lat"] = np.ascontiguousarray(np.broadcast_to(pool_inv_table(GRID_W, cfg.pool_windows).reshape(1, -1), (128, 4 * GRID_W)))
        m["pool_inv_ctx"] = np.ascontiguousarray(np.broadcast_to(pool_inv_table(cfg.TC, cfg.pool_windows).reshape(1, -1), (128, 4 * cfg.TC)))
    if 0 in cfg.kinds or 2 in cfg.kinds:
        m["c_ident"] = np.eye(128, dtype=np.float32)
        for CL in ((32,) if 0 in cfg.kinds else ()) + ((128,) if 2 in cfg.kinds else ()):
            i = np.arange(128)
            same = (i[:, None] // CL) == (i[None, :] // CL)
            m["c_cmask%d" % CL] = (same & (i[:, None] <= i[None, :])).astype(np.float32)
            m["c_rmask%d" % CL] = ((i[:, None] // CL) == np.arange(128 // CL)[None, :]).astype(np.float32)
            m["c_reset%d" % CL] = np.ascontiguousarray(np.broadcast_to(((i % CL) != 0).astype(np.float32)[None, :], (128, 128)))
    if 0 in cfg.kinds:
        w = f32(inp["hg_w_in"])[0].reshape(DC, 128, 5 * DC, 128).transpose(2, 1, 0, 3)
        m["hg_win"] = np.ascontiguousarray(w).reshape(5 * DC * 128, DC * 128)
        w = f32(inp["hg_w_o"])[0].reshape(DC, 128, DC, 128).transpose(2, 1, 0, 3)
        m["hg_wo"] = np.ascontiguousarray(w).reshape(DC * 128, DC * 128)
        m["hg_lbT"] = fm_cols(f32(inp["hg_lb_logits"]))
        m["hg_ngT"] = fm_cols(f32(inp["hg_norm_g"])[0])
    if 2 in cfg.kinds:
        H = cfg.ml_heads
        win = f32(inp["ml_w_in"])[0]
        w = win[:, :4 * D].reshape(DC, 128, 4 * DC, 128).transpose(2, 1, 0, 3)
        m["ml_win"] = np.ascontiguousarray(w).reshape(4 * DC * 128, DC * 128)
        w = f32(inp["ml_w_o"])[0].reshape(DC, 128, DC, 128).transpose(2, 1, 0, 3)
        m["ml_wo"] = np.ascontiguousarray(w).reshape(DC * 128, DC * 128)
        wg = win[:, 4 * D:].reshape(DC, 128, 4 * H).transpose(1, 0, 2)
        m["ml_wg"] = np.ascontiguousarray(wg).reshape(128, DC * 4 * H)
        gb = f32(inp["ml_gate_b"])[0].reshape(4 * H)
        isf = np.repeat(np.array([0.0, 1.0, 0.0, 1.0], np.float32), H)
        m["ml_gcol"] = np.ascontiguousarray(np.stack([gb, np.float32(1.0) - isf, -isf], 1).astype(np.float32))
        sel = np.zeros((4 * H, 4 * H, 128), np.float32)
        for r in range(4 * H):
            sel[r, r, :] = 1.0
        m["ml_sel"] = sel.reshape(4 * H, 4 * H * 128)
        m["ml_ngT"] = fm_cols(f32(inp["ml_norm_g"])[0])
    if 3 in cfg.kinds:
        G = D // 16

        def lanes(a):
            a = f32(a).reshape(2, DC, 4, 2, 64).transpose(3, 4, 0, 1, 2).reshape(128, 2, DC, 4)
            return a
        lre, lim = lanes(inp["s5_lam_re"][0]), lanes(inp["s5_lam_im"][0])
        ldt = lanes(np.repeat(f32(inp["s5_log_dt"][0])[:, :, None], 64, axis=2))
        m["s5_lam"] = np.ascontiguousarray(np.stack([lre, lim, ldt], 1)).reshape(128, -1)

        def bblk(bm):
            bm = f32(bm).reshape(2, DC, 4, 2, 64, 16)
            out = np.zeros((2, DC, 8, 16, 4, 2, 64), np.float32)
            for j in range(4):
                for gl in range(2):
                    out[:, :, 2 * j + gl, :, j, gl, :] = bm[:, :, j, gl].transpose(0, 1, 3, 2)
            return out.reshape(2 * DC * 128, 4 * 128)

        def cblk(cm):
            cm = f32(cm).reshape(2, DC, 4, 2, 16, 64)
            out = np.zeros((2, DC, 2, 64, 4, 8, 16), np.float32)
            for j in range(4):
                for gl in range(2):
                    out[:, :, gl, :, j, 2 * j + gl, :] = cm[:, :, j, gl].transpose(0, 1, 3, 2)
            return out.reshape(2 * DC * 128, 4 * 128)
        m["s5_Bb0"], m["s5_Bb1"] = bblk(inp["s5_b_re"][0]), bblk(inp["s5_b_im"][0])
        m["s5_Cb0"], m["s5_Cb1"] = cblk(inp["s5_c_re"][0]), cblk(inp["s5_c_im"][0])
        m["s5_dT"] = fm_cols(f32(inp["s5_d"])[0])
        m["s5_iota"] = np.ascontiguousarray(np.broadcast_to(np.arange(1, 513, dtype=np.float32)[None, :], (128, 512)))
        w = f32(inp["s5_w_glu"])[0].reshape(DC, 128, 2 * DC, 128).transpose(2, 1, 0, 3)
        m["s5_wglu"] = np.ascontiguousarray(w).reshape(2 * DC * 128, DC * 128)
    return m


def prep_core(cfg, inp, batches):
    D, DC = cfg.D, cfg.DC
    f32 = lambda a: np.asarray(a, np.float32)
    m = {}
    m["xT"] = np.ascontiguousarray(np.concatenate([f32(inp["x"][b]).T for b in batches], 0))
    m["ctxT"] = np.ascontiguousarray(np.concatenate([f32(inp["ctx"][b]).T for b in batches], 0))
    cnds = []
    for b in batches:
        cs = [f32(inp["c"][b]), f32(inp["c_ctx"])]
        cnds.append(np.stack(cs, -1).reshape(DC, 128, 2).transpose(1, 0, 2).reshape(128, -1))
    m["cond"] = np.ascontiguousarray(np.concatenate(cnds, 0))
    return m


def run(cfg, inp, n_cores):
    nb = cfg.nb
    bld = Builder(cfg)
    nc = bld.build()
    shared = prep_shared(cfg, inp)
    in_maps = []
    for c in range(n_cores):
        m = dict(shared)
        m.update(prep_core(cfg, inp, list(range(c * nb, (c + 1) * nb))))
        in_maps.append({k: m[k] for k in bld.din})
    res = run_bass_kernel_spmd(nc, in_maps, core_ids=list(range(n_cores)))
    outs = []
    for c in range(n_cores):
        o = res.results[c]["outT"].reshape(nb, cfg.D, cfg.T)
        outs.append(np.transpose(o, (0, 2, 1)))
    return np.ascontiguousarray(np.concatenate(outs, 0))


N_CORES = 8


def kernel(**inputs):
    n_cores = N_CORES
    cfg = Cfg(nb=8 // n_cores)
    return run(cfg, inputs, n_cores).astype(np.float32)
```
